# Optimizing a Trainium2 kernel written in Bass

```python
import jax, jax.numpy as jnp
from jax import lax
import numpy as np

D_MODEL = 1024
BATCH = 2
SEQ = 8192
DEPTH = 2

GRID_W = 64
CTX_LEN = 256
HEAD_DIM = 64
A_Q_HEADS = 6
A_KV_HEADS = 2
WINDOW = 128
BLOCK = 128
B_Q_HEADS = 6
B_KV_HEADS = 2
C_HEADS = 6
NA_KH_MAX = 8
NA_KW = 16
NA_QW = 16
NA_SLAB = NA_QW + NA_KW
N_BRANCH = 3
A_WIDTH = A_Q_HEADS * HEAD_DIM
B_WIDTH = B_Q_HEADS * HEAD_DIM
C_WIDTH = C_HEADS * HEAD_DIM
MLP_HIDDEN = 4 * D_MODEL
ROPE_THETA = 10000.0
NORM_EPS = 1e-6
NEG_INF = -1e30
SPLIT_SIZES = (A_Q_HEADS * HEAD_DIM, A_KV_HEADS * HEAD_DIM, A_KV_HEADS * HEAD_DIM,
               B_Q_HEADS * HEAD_DIM, B_KV_HEADS * HEAD_DIM, B_KV_HEADS * HEAD_DIM,
               C_HEADS * HEAD_DIM, C_HEADS * HEAD_DIM, C_HEADS * HEAD_DIM,
               N_BRANCH * D_MODEL)
IN_COLS = (A_Q_HEADS + 2 * A_KV_HEADS + B_Q_HEADS + 2 * B_KV_HEADS + 3 * C_HEADS) * HEAD_DIM + N_BRANCH * D_MODEL

kernel_name = "hybrid_parallel_gated_dit_block"


def rms_norm(x, gain):
    x32 = x.astype(jnp.float32)
    y = x32 * lax.rsqrt(jnp.mean(x32 * x32, axis=-1, keepdims=True) + NORM_EPS)
    return (y * gain.astype(jnp.float32)).astype(x.dtype)


def modulate(h, shift, scale):
    return h * (1 + scale) + shift


def axial_angles(n_tok):
    pos = jnp.arange(n_tok)
    row = (pos // GRID_W).astype(jnp.float32)
    col = (pos % GRID_W).astype(jnp.float32)
    n_freq = HEAD_DIM // 4
    freqs = ROPE_THETA ** (-jnp.arange(n_freq, dtype=jnp.float32) / n_freq)
    ang = jnp.concatenate([row[:, None] * freqs, col[:, None] * freqs], axis=-1)
    return jnp.cos(ang), jnp.sin(ang)


def rope_2d(x, cos, sin):
    xp = x.reshape(x.shape[:-1] + (HEAD_DIM // 2, 2))
    x0, x1 = xp[..., 0], xp[..., 1]
    c = cos[:, None, :].astype(x.dtype)
    s = sin[:, None, :].astype(x.dtype)
    return jnp.stack([x0 * c - x1 * s, x0 * s + x1 * c], axis=-1).reshape(x.shape)


def to_heads(t, n_heads):
    return t.reshape(t.shape[:2] + (n_heads, HEAD_DIM))


def split_cols(p):
    out = []
    start = 0
    for n in SPLIT_SIZES:
        out.append(p[..., start:start + n])
        start += n
    return out


def dense_context_attn(q, k, v, sink):
    bsz, lq, hkv, g, _ = q.shape
    s = jnp.einsum('bqkgd,bmkd->bkgqm', q, k).astype(jnp.float32) * (HEAD_DIM ** -0.5)
    if sink is not None:
        sink_col = jnp.broadcast_to(sink.astype(jnp.float32).reshape(hkv, g)[None, :, :, None, None], s.shape[:-1] + (1,))
        s = jnp.concatenate([s, sink_col], axis=-1)
    p = jax.nn.softmax(s, axis=-1)
    if sink is not None:
        p = p[..., :-1]
    o = jnp.einsum('bkgqm,bmkd->bqkgd', p.astype(v.dtype), v)
    return o.reshape(bsz, lq, hkv * g * HEAD_DIM)


def window_attention_latent(q, k, v, k_ctx, v_ctx, sink):
    bsz, seq = q.shape[:2]
    nb = seq // BLOCK
    g = A_Q_HEADS // A_KV_HEADS
    scale = HEAD_DIM ** -0.5
    qb = q.reshape(bsz, nb, BLOCK, A_KV_HEADS, g, HEAD_DIM)

    def band(t):
        tp = jnp.pad(t, ((0, 0), (BLOCK, BLOCK), (0, 0), (0, 0))).reshape(bsz, nb + 2, BLOCK, A_KV_HEADS, HEAD_DIM)
        return jnp.concatenate([tp[:, :-2], tp[:, 1:-1], tp[:, 2:]], axis=2)

    kb, vb = band(k), band(v)
    blk = jnp.arange(nb) * BLOCK
    qpos = blk[:, None, None] + jnp.arange(BLOCK)[None, :, None]
    kpos = (blk - BLOCK)[:, None, None] + jnp.arange(3 * BLOCK)[None, None, :]
    valid = (jnp.abs(qpos - kpos) <= WINDOW) & (kpos >= 0) & (kpos < seq)
    s_win = jnp.einsum('bnqkgd,bnmkd->bnkgqm', qb, kb).astype(jnp.float32) * scale
    s_win = jnp.where(valid[None, :, None, None], s_win, NEG_INF)
    s_ctx = jnp.einsum('bnqkgd,blkd->bnkgql', qb, k_ctx).astype(jnp.float32) * scale
    sink_col = jnp.broadcast_to(sink.astype(jnp.float32).reshape(A_KV_HEADS, g)[None, None, :, :, None, None], s_win.shape[:-1] + (1,))
    p = jax.nn.softmax(jnp.concatenate([s_win, s_ctx, sink_col], axis=-1), axis=-1)
    n_win = 3 * BLOCK
    n_ctx = k_ctx.shape[1]
    p_win = p[..., :n_win].astype(v.dtype)
    p_ctx = p[..., n_win:n_win + n_ctx].astype(v.dtype)
    o = jnp.einsum('bnkgqm,bnmkd->bnqkgd', p_win, vb) + jnp.einsum('bnkgql,blkd->bnqkgd', p_ctx, v_ctx)
    return o.reshape(bsz, seq, A_WIDTH)


def global_attention_latent(q, k, v, k_ctx, v_ctx):
    bsz, seq = q.shape[:2]
    nb = seq // BLOCK
    g = B_Q_HEADS // B_KV_HEADS
    scale = HEAD_DIM ** -0.5
    k_all = jnp.concatenate([k_ctx, k], axis=1)
    v_all = jnp.concatenate([v_ctx, v], axis=1)
    q_blocks = q.reshape(bsz, nb, BLOCK, B_KV_HEADS, g, HEAD_DIM).swapaxes(0, 1)

    def one_block(qb):
        s = jnp.einsum('bqkgd,bmkd->bkgqm', qb, k_all).astype(jnp.float32) * scale
        p = jax.nn.softmax(s, axis=-1).astype(v_all.dtype)
        return jnp.einsum('bkgqm,bmkd->bqkgd', p, v_all)

    o = lax.map(one_block, q_blocks)
    return o.swapaxes(0, 1).reshape(bsz, seq, B_WIDTH)


def neighbourhood_attention_latent(q, k, v, k_ctx, v_ctx, rpb):
    bsz, seq = q.shape[:2]
    rows = seq // GRID_W
    kh = min(NA_KH_MAX, rows)
    ncb = GRID_W // NA_QW
    scale = HEAD_DIM ** -0.5
    q_g = q.reshape(bsz, rows, ncb, NA_QW, C_HEADS, HEAD_DIM)
    k_g = k.reshape(bsz, rows, GRID_W, C_HEADS, HEAD_DIM)
    v_g = v.reshape(bsz, rows, GRID_W, C_HEADS, HEAD_DIM)
    r = jnp.arange(rows)
    row_start = jnp.clip(r - kh // 2, 0, rows - kh)
    row_idx = row_start[:, None] + jnp.arange(kh)[None, :]
    slab_start = jnp.clip(jnp.arange(ncb) * NA_QW - NA_KW // 2, 0, GRID_W - NA_SLAB)
    col_idx = slab_start[:, None] + jnp.arange(NA_SLAB)[None, :]

    def gather(t):
        return jnp.take(jnp.take(t, col_idx, axis=2), row_idx, axis=1)

    k_blk, v_blk = gather(k_g), gather(v_g)
    q_col = jnp.arange(ncb)[:, None] * NA_QW + jnp.arange(NA_QW)[None, :]
    win_start = jnp.clip(q_col - NA_KW // 2, 0, GRID_W - NA_KW)
    key_col = col_idx[:, None, :]
    col_valid = (key_col >= win_start[..., None]) & (key_col < win_start[..., None] + NA_KW)
    col_off = jnp.clip(key_col - q_col[..., None] + NA_KW - 1, 0, 2 * NA_KW - 2)
    row_off = row_idx - r[:, None] + NA_KH_MAX - 1
    bias = jnp.take(jnp.take(rpb, row_off, axis=1), col_off, axis=3)
    bias = bias.transpose(1, 3, 0, 4, 2, 5).astype(jnp.float32)
    s_nb = jnp.einsum('brcqhd,brkcshd->brchqks', q_g, k_blk).astype(jnp.float32) * scale + bias[None]
    s_nb = jnp.where(col_valid[None, None, :, None, :, None, :], s_nb, NEG_INF)
    s_ctx = jnp.einsum('brcqhd,blhd->brchql', q_g, k_ctx).astype(jnp.float32) * scale
    n_nb = kh * NA_SLAB
    p = jax.nn.softmax(jnp.concatenate([s_nb.reshape(s_nb.shape[:5] + (n_nb,)), s_ctx], axis=-1), axis=-1)
    p_nb = p[..., :n_nb].reshape(s_nb.shape).astype(v.dtype)
    p_ctx = p[..., n_nb:].astype(v.dtype)
    o = jnp.einsum('brchqks,brkcshd->brcqhd', p_nb, v_blk) + jnp.einsum('brchql,blhd->brcqhd', p_ctx, v_ctx)
    return o.reshape(bsz, seq, C_WIDTH)


def gated_merge(o_a, o_b, o_c, gates, w_br_a, w_br_b, w_br_c, w_out):
    g_a, g_b, g_c = jnp.split(gates, N_BRANCH, axis=-1)
    merged = (jax.nn.sigmoid(g_a) * (o_a @ w_br_a)
              + jax.nn.sigmoid(g_b) * (o_b @ w_br_b)
              + jax.nn.sigmoid(g_c) * (o_c @ w_br_c))
    return merged @ w_out


def token_mixer(h_lat, h_ctx, cos, sin, w_in, sink_a, qnorm_b, knorm_b, rpb_c,
                w_br_a, w_br_b, w_br_c, w_out, with_ctx_out):
    qa, ka, va, qb, kb, vb, qc, kc, vc, gates = split_cols(h_lat @ w_in)
    qa_c, ka_c, va_c, qb_c, kb_c, vb_c, qc_c, kc_c, vc_c, gates_c = split_cols(h_ctx @ w_in)
    bsz, n_ctx = h_ctx.shape[:2]
    qa = rope_2d(to_heads(qa, A_Q_HEADS), cos, sin)
    ka = rope_2d(to_heads(ka, A_KV_HEADS), cos, sin)
    va = to_heads(va, A_KV_HEADS)
    ka_c, va_c = to_heads(ka_c, A_KV_HEADS), to_heads(va_c, A_KV_HEADS)
    o_a = window_attention_latent(qa, ka, va, ka_c, va_c, sink_a)
    qb = rope_2d(rms_norm(to_heads(qb, B_Q_HEADS), qnorm_b), cos, sin)
    kb = rope_2d(rms_norm(to_heads(kb, B_KV_HEADS), knorm_b), cos, sin)
    vb = to_heads(vb, B_KV_HEADS)
    kb_c = rms_norm(to_heads(kb_c, B_KV_HEADS), knorm_b)
    vb_c = to_heads(vb_c, B_KV_HEADS)
    o_b = global_attention_latent(qb, kb, vb, kb_c, vb_c)
    qc, kc, vc = to_heads(qc, C_HEADS), to_heads(kc, C_HEADS), to_heads(vc, C_HEADS)
    kc_c, vc_c = to_heads(kc_c, C_HEADS), to_heads(vc_c, C_HEADS)
    o_c = neighbourhood_attention_latent(qc, kc, vc, kc_c, vc_c, rpb_c)
    y_lat = gated_merge(o_a, o_b, o_c, gates, w_br_a, w_br_b, w_br_c, w_out)
    if not with_ctx_out:
        return y_lat, None
    qa_c = to_heads(qa_c, A_Q_HEADS).reshape(bsz, n_ctx, A_KV_HEADS, A_Q_HEADS // A_KV_HEADS, HEAD_DIM)
    o_a_c = dense_context_attn(qa_c, ka_c, va_c, sink_a)
    qb_c = rms_norm(to_heads(qb_c, B_Q_HEADS), qnorm_b).reshape(bsz, n_ctx, B_KV_HEADS, B_Q_HEADS // B_KV_HEADS, HEAD_DIM)
    o_b_c = dense_context_attn(qb_c, kb_c, vb_c, None)
    qc_c = to_heads(qc_c, C_HEADS)[:, :, :, None, :]
    o_c_c = dense_context_attn(qc_c, kc_c, vc_c, None)
    y_ctx = gated_merge(o_a_c, o_b_c, o_c_c, gates_c, w_br_a, w_br_b, w_br_c, w_out)
    return y_lat, y_ctx


def sq_relu_mlp(h, w1, w2):
    a = jax.nn.relu(h @ w1)
    return (a * a) @ w2


def setup_inputs(seed: int = 0) -> dict:
    key = jax.random.key(seed)
    ks = jax.random.split(key, 24)

    def nrm(k, shape, scale):
        return jax.random.normal(k, shape, jnp.float32) * scale

    def gain(k, shape):
        return 1.0 + 0.05 * jax.random.normal(k, shape, jnp.float32)

    return {
        "x": nrm(ks[0], (BATCH, SEQ, D_MODEL), 1.0),
        "c": nrm(ks[1], (BATCH, D_MODEL), 1.0),
        "ctx": nrm(ks[2], (BATCH, CTX_LEN, D_MODEL), 1.0),
        "c_ctx": nrm(ks[3], (D_MODEL,), 1.0),
        "w_ada": nrm(ks[4], (DEPTH, D_MODEL, 6 * D_MODEL), 0.5 * D_MODEL ** -0.5),
        "b_ada": nrm(ks[5], (DEPTH, 6 * D_MODEL), 0.02),
        "norm_mix_pre": gain(ks[6], (DEPTH, D_MODEL)),
        "norm_mix_post": gain(ks[7], (DEPTH, D_MODEL)),
        "w_in": nrm(ks[8], (DEPTH, D_MODEL, IN_COLS), D_MODEL ** -0.5),
        "sink_a": nrm(ks[9], (DEPTH, A_Q_HEADS), 0.5),
        "qnorm_b": gain(ks[10], (DEPTH, HEAD_DIM)),
        "knorm_b": gain(ks[11], (DEPTH, HEAD_DIM)),
        "rpb_c": nrm(ks[12], (DEPTH, C_HEADS, 2 * NA_KH_MAX - 1, 2 * NA_KW - 1), 0.2),
        "w_br_a": nrm(ks[13], (DEPTH, A_WIDTH, D_MODEL), A_WIDTH ** -0.5),
        "w_br_b": nrm(ks[14], (DEPTH, B_WIDTH, D_MODEL), B_WIDTH ** -0.5),
        "w_br_c": nrm(ks[15], (DEPTH, C_WIDTH, D_MODEL), C_WIDTH ** -0.5),
        "w_out": nrm(ks[16], (DEPTH, D_MODEL, D_MODEL), D_MODEL ** -0.5),
        "norm_mlp_pre": gain(ks[17], (DEPTH, D_MODEL)),
        "norm_mlp_post": gain(ks[18], (DEPTH, D_MODEL)),
        "w_mlp_in": nrm(ks[19], (DEPTH, D_MODEL, MLP_HIDDEN), D_MODEL ** -0.5),
        "w_mlp_out": nrm(ks[20], (DEPTH, MLP_HIDDEN, D_MODEL), MLP_HIDDEN ** -0.5),
    }


def reference(x, c, ctx, c_ctx, w_ada, b_ada, norm_mix_pre, norm_mix_post, w_in, sink_a,
              qnorm_b, knorm_b, rpb_c, w_br_a, w_br_b, w_br_c, w_out, norm_mlp_pre,
              norm_mlp_post, w_mlp_in, w_mlp_out):
    seq = x.shape[1]
    cos, sin = axial_angles(seq)
    silu_c = jax.nn.silu(c)
    silu_cc = jax.nn.silu(c_ctx)
    x_lat, x_ctx = x, ctx
    for l in range(DEPTH):
        last = l == DEPTH - 1
        mod_lat = (silu_c @ w_ada[l] + b_ada[l])[:, None, :]
        mod_ctx = (silu_cc @ w_ada[l] + b_ada[l])[None, None, :]
        sh1, sc1, g1, sh2, sc2, g2 = jnp.split(mod_lat, 6, axis=-1)
        csh1, csc1, cg1, csh2, csc2, cg2 = jnp.split(mod_ctx, 6, axis=-1)
        h_lat = modulate(rms_norm(x_lat, norm_mix_pre[l]), sh1, sc1)
        h_ctx = modulate(rms_norm(x_ctx, norm_mix_pre[l]), csh1, csc1)
        y_lat, y_ctx = token_mixer(h_lat, h_ctx, cos, sin, w_in[l], sink_a[l], qnorm_b[l], knorm_b[l],
                                   rpb_c[l], w_br_a[l], w_br_b[l], w_br_c[l], w_out[l], not last)
        x_lat = x_lat + g1 * rms_norm(y_lat, norm_mix_post[l])
        h2 = modulate(rms_norm(x_lat, norm_mlp_pre[l]), sh2, sc2)
        x_lat = x_lat + g2 * rms_norm(sq_relu_mlp(h2, w_mlp_in[l], w_mlp_out[l]), norm_mlp_post[l])
        if not last:
            x_ctx = x_ctx + cg1 * rms_norm(y_ctx, norm_mix_post[l])
            h2c = modulate(rms_norm(x_ctx, norm_mlp_pre[l]), csh2, csc2)
            x_ctx = x_ctx + cg2 * rms_norm(sq_relu_mlp(h2c, w_mlp_in[l], w_mlp_out[l]), norm_mlp_post[l])
    return x_lat
```

```python
import numpy as np
from contextlib import ExitStack
import concourse.bass as bass
import concourse.mybir as mybir
from concourse.bass_utils import run_bass_kernel_spmd

F32 = mybir.dt.float32
BF16 = mybir.dt.bfloat16
AF = mybir.ActivationFunctionType
ALU = mybir.AluOpType

NT = 2048
NCX = 256
NTOT = NT + NCX
NEG = -30000.0
EPS = 1e-6
HP = [0, 3, 1, 4, 2, 5]
C_QA, C_QAS, C_KA, C_KAS = 0, 384, 768, 896
C_QB, C_QBS, C_KB, C_KBS = 1024, 1408, 1792, 1920
C_QC, C_KC, C_V, C_G = 2048, 2432, 2816, 3456
NWIN = 6528
O_KB, O_VB, O_KAH, O_KAT, O_VAH, O_VAT = 0, 262144, 524288, 540672, 557056, 573440
O_KCH, O_KCT, O_VCH, O_VCT, CONTRIB = 589824, 688128, 786432, 884736, 983040


class Trk:
    ENGS = ("pe", "act", "dve", "pool", "sp")
    NDSEM = 12

    def __init__(self, nc, es):
        self.nc = nc
        self.esem = {e: es.enter_context(nc.semaphore("s_" + e)) for e in self.ENGS}
        self.ecnt = {e: 0 for e in self.ENGS}
        self.dsem = {q: [es.enter_context(nc.semaphore("d_%s%d" % (q, i))) for i in range(self.NDSEM)]
                     for q in ("sp", "pool")}
        self.dval = {q: [0] * self.NDSEM for q in ("sp", "pool")}
        self.dcnt = {"sp": 0, "pool": 0}
        self.ccsem = es.enter_context(nc.semaphore("s_cc"))
        self.ccval = 0
        self.ops = []
        self.bar = None
        self.waited = {e: {} for e in self.ENGS}

    def add(self, eng, fn, r=(), w=(), dma=False, cc=False, accw=False):
        self.ops.append(dict(eng=eng, fn=fn, r=tuple(r), w=tuple(w), dma=dma, cc=cc, accw=accw))

    def flush(self):
        ops = self.ops
        self.ops = []
        if not ops:
            return
        last_w, readers = {}, {}

        def is_acc(tok):
            t0_ = tok[0] if isinstance(tok, tuple) else tok
            return isinstance(t0_, str) and t0_.startswith("+")
        for i, op in enumerate(ops):
            deps = set()
            for r in op["r"]:
                if r in last_w:
                    deps.update(last_w[r])
            for w in op["w"]:
                if w in last_w:
                    if is_acc(w) and (op["dma"] or op["accw"]):
                        deps.add(last_w[w][0])
                    else:
                        deps.update(last_w[w])
                for rd in readers.get(w, {}).values():
                    if isinstance(rd, list):
                        deps.update(rd)
                    else:
                        deps.add(rd)
            deps.discard(i)
            if op["eng"] == "pe":
                deps = {d for d in deps if ops[d]["dma"] or ops[d]["eng"] != "pe"}
            op["deps"] = deps
            for r in op["r"]:
                rr = readers.setdefault(r, {})
                if op["dma"]:
                    rr.setdefault("dma", []).append(i)
                else:
                    rr[op["eng"]] = i
            for w in op["w"]:
                if is_acc(w) and (op["dma"] or op["accw"]) and w in last_w:
                    last_w[w] = last_w[w] + [i]
                else:
                    last_w[w] = [i]
                readers[w] = {}
        for op in ops:
            op["sig"] = False
        for op in ops:
            for d in op["deps"]:
                ops[d]["sig"] = True
        last_of = {}
        for i, op in enumerate(ops):
            if not op["dma"]:
                last_of[op["eng"]] = i
        for i in last_of.values():
            ops[i]["sig"] = True
        for op in ops:
            if op["cc"]:
                self.ccval += 1
                op["done"] = (self.ccsem, self.ccval)
                op["pre"] = None
            elif op["dma"]:
                q = op["eng"]
                k = self.dcnt[q] % self.NDSEM
                self.dcnt[q] += 1
                prev = self.dval[q][k]
                self.dval[q][k] += 16
                op["done"] = (self.dsem[q][k], self.dval[q][k])
                op["pre"] = (self.dsem[q][k], prev) if prev > 0 else None
            elif op["sig"]:
                self.ecnt[op["eng"]] += 1
                op["done"] = (self.esem[op["eng"]], self.ecnt[op["eng"]])
                op["pre"] = None
            else:
                op["done"] = None
                op["pre"] = None
        per = {e: [] for e in self.ENGS}
        for op in ops:
            per[op["eng"]].append(op)
        bar = self.bar
        waited = self.waited

        def emit(ename, e):
            wd = waited[ename]

            def wait(sem, val):
                key = id(sem)
                if wd.get(key, 0) < val:
                    e.wait_ge(sem, val)
                    wd[key] = val
            if bar and per[ename]:
                for sem, val in bar:
                    wait(sem, val)
            for op in per[ename]:
                for d in op["deps"]:
                    sem, val = ops[d]["done"]
                    wait(sem, val)
                if op["pre"] is not None:
                    wait(*op["pre"])
                if op["fn"] is None:
                    continue
                ins = op["fn"](e)
                if op["done"] is not None:
                    sem, val = op["done"]
                    if op["cc"]:
                        ins.then_inc(sem, 1)
                    elif op["dma"]:
                        ins.then_inc(sem, 16)
                    else:
                        ins.then_inc(sem, 1)

        with self.nc.Block() as block:
            @block.sync
            def _(e):
                emit("sp", e)

            @block.scalar
            def _(e):
                emit("act", e)

            @block.vector
            def _(e):
                emit("dve", e)

            @block.gpsimd
            def _(e):
                emit("pool", e)

            @block.tensor
            def _(e):
                emit("pe", e)
        nb = []
        for e in self.ENGS:
            if self.ecnt[e] > 0:
                nb.append((self.esem[e], self.ecnt[e]))
        for q in ("sp", "pool"):
            for k in range(self.NDSEM):
                if self.dval[q][k] > 0:
                    nb.append((self.dsem[q][k], self.dval[q][k]))
        if self.ccval > 0:
            nb.append((self.ccsem, self.ccval))
        self.bar = nb

    def final_wait(self):
        bar = self.bar
        with self.nc.Block() as block:
            @block.sync
            def _(e):
                for sem, val in bar:
                    e.wait_ge(sem, val)


def bcast(ap, pos, n):
    l = [list(x) for x in ap.ap]
    l.insert(pos, [0, n])
    return bass.AP(ap.tensor, ap.offset, l)


def dview(t, off, dims):
    return bass.AP(t, off, [list(d) for d in dims])


class K:
    def __init__(self, n_layers=2, taps=(), stop=None):
        self.nl = n_layers
        self.taps = set(taps)
        self.stop = stop
        self.nc = bass.Bass("TRN2", target_bir_lowering=False)
        self.uid = 0

    def din(self, name, shape, dt=F32):
        return self.nc.dram_tensor(name, list(shape), dt, kind="ExternalInput")

    def dscr(self, name, shape, dt=BF16, tap=False):
        if tap and name in self.taps:
            return self.nc.dram_tensor(name, list(shape), dt, kind="ExternalOutput")
        return self.nc.dram_tensor(name, list(shape), dt)

    def sb(self, es, name, shape, dt):
        self.uid += 1
        return es.enter_context(self.nc.sbuf_tensor("%s_%d" % (name, self.uid), list(shape), dt))

    def mm(self, out, lhsT, rhs, start, stop, r, w):
        self.T.add("pe", lambda e: e.matmul(out, lhsT=lhsT, rhs=rhs, start=start, stop=stop), r, w)

    def act(self, out, in_, func, r, w, bias=None, scale=None):
        kw = {}
        if bias is not None:
            kw["bias"] = bias
        if scale is not None:
            kw["scale"] = scale
        self.T.add("act", lambda e: e.activation(out=out, in_=in_, func=func, **kw), r, w)

    def tt(self, eng, out, in0, in1, op, r, w):
        self.T.add(eng, lambda e: e.tensor_tensor(out=out, in0=in0, in1=in1, op=op), r, w)

    def ts(self, eng, out, in0, s1, s2, op0, op1, r, w):
        if op1 is None:
            self.T.add(eng, lambda e: e.tensor_scalar(out=out, in0=in0, scalar1=s1, scalar2=None, op0=op0), r, w)
        else:
            self.T.add(eng, lambda e: e.tensor_scalar(out=out, in0=in0, scalar1=s1, scalar2=s2, op0=op0, op1=op1), r, w)

    def stt(self, out, in0, scalar, in1, op0, op1, r, w):
        self.T.add("dve", lambda e: e.scalar_tensor_tensor(out=out, in0=in0, scalar=scalar, in1=in1, op0=op0, op1=op1), r, w)

    def cp(self, eng, out, in_, r, w):
        self.T.add(eng, lambda e: e.tensor_copy(out=out, in_=in_), r, w)

    def memset(self, eng, ap, val, w):
        self.T.add(eng, lambda e: e.memset(ap, val), (), w)

    def recip(self, out, in_, r, w):
        self.T.add("dve", lambda e: e.reciprocal(out=out, in_=in_), r, w)

    def dma(self, out, in_, r, w, q="sp"):
        self.T.add(q, lambda e: e.dma_start(out=out, in_=in_), r, w, dma=True)

    def newbank(self):
        if self.split:
            b = self.bankc % self.nsb
        else:
            b = self.bankc % 8
        self.bankc += 1
        return b

    def newacc(self):
        b = self.nsb + self.accc % (8 - self.nsb)
        self.accc += 1
        return b

    def alloc_norm(self, es, nmax):
        self.sq_buf = self.sb(es, "sqbuf", [128, 8, nmax], BF16)
        self.lnt = self.sb(es, "lnt", [128, nmax], F32)
        self.rstd = self.sb(es, "rstd", [128, nmax], F32)
        self.xn = self.sb(es, "xn", [128, 8, nmax], F32)

    def cast(self, dst, src, r, w):
        engs = self.cast_engs
        e = engs[self.cast_i % len(engs)]
        self.cast_i += 1
        if e == "act":
            self.T.add("act", lambda e_: e_.activation(out=dst, in_=src, func=AF.Copy), r, w, accw=True)
        else:
            self.T.add(e, lambda e_: e_.tensor_copy(out=dst, in_=src), r, w, accw=True)

    def load_w(self, dst, src, tok_w, nk, ncols):
        CH = 1024
        per = max(1, CH // ncols)
        k = 0
        while k < nk:
            kk = min(per, nk - k)
            if ncols > CH:
                assert per == 1
                c = 0
                while c < ncols:
                    cc = min(CH, ncols - c)
                    s = self.stg_i % self.nstg
                    self.stg_i += 1
                    st = self.stg[:, s, 0:cc]
                    self.dma(st, src[:, k, c:c + cc], (), [("stg", self.nstg, s)])
                    self.cast(dst[:, k, c:c + cc], st, [("stg", self.nstg, s)], [tok_w])
                    c += cc
            else:
                s = self.stg_i % self.nstg
                self.stg_i += 1
                st = self.stg[:, s, 0:kk * ncols].rearrange("p (k c) -> p k c", k=kk)
                self.dma(st, src[:, k:k + kk, :], (), [("stg", self.nstg, s)])
                self.cast(dst[:, k:k + kk, :], st, [("stg", self.nstg, s)], [tok_w])
            k += kk

    def rstd_of(self, es_tmp, src, nk, n, div, tokr, name):
        sq = self.sq_buf[:, 0:nk, 0:n]
        self.act(sq, src, AF.Square, [tokr], ["sqbuf"])
        b = self.newbank()
        for k in range(nk):
            self.mm(self.ps[b][:, 0:n], self.ones[:], sq[:, k, :], k == 0, k == nk - 1, ["sqbuf", "ones"], [("ps", b)])
        self.act(self.lnt[:, 0:n], self.ps[b][:, 0:n], AF.Ln, [("ps", b)], ["lnt"], bias=self.epsc[:, 0:1], scale=1.0 / div)
        self.act(self.rstd[:, 0:n], self.lnt[:, 0:n], AF.Exp, ["lnt"], ["rstd"], scale=-0.5)
        return self.rstd[:, 0:n]

    def norm_mod(self, xch, n, GG, SH, hT, tokx, tokh):
        rs = self.rstd_of(None, xch[:, :, 0:n], 8, n, 1024.0, tokx, "nm")
        self.tt("dve", self.xn[:, :, 0:n], xch[:, :, 0:n], bcast(rs, 1, 8), ALU.mult, [tokx, "rstd"], ["xn"])
        for k in range(8):
            self.ts("dve", hT[:, k, 0:n], self.xn[:, k, 0:n], GG[:, k:k + 1], SH[:, k:k + 1], ALU.mult, ALU.add,
                    ["xn", "mods"], [tokh])

    def build(self):
        nc = self.nc
        NL = self.nl
        xT = self.din("xT", [128, 8, NT])
        ctxT = self.din("ctxT", [128, 8, NCX])
        ccT = self.din("ccT", [128, 8, 2])
        ropeC = self.din("ropeC", [128, NT])
        ropeS = self.din("ropeS", [128, NT])
        maskA = self.din("maskA", [128, 10, 128])
        rm01 = self.din("rm01", [128, 44, 8])
        L = []
        for l in range(NL):
            d = dict(
                wada=self.din("wada%d" % l, [1024, 6144]),
                badaT=self.din("badaT%d" % l, [128, 48]),
                gains=self.din("gains%d" % l, [128, 4, 8]),
                win=self.din("win%d" % l, [1024, NWIN]),
                bgain=self.din("bgain%d" % l, [128, 4]),
                sinkT=self.din("sinkT%d" % l, [128, 6]),
                TB=self.din("TB%d" % l, [128, 2, 6 * 22 * 64]),
                wbr=self.din("wbr%d" % l, [3, 384, 1024]),
                wout=self.din("wout%d" % l, [1024, 1024]),
                w1=self.din("w1_%d" % l, [1024, 4096]),
                w2=self.din("w2_%d" % l, [4096, 1024]),
            )
            L.append(d)
        yT = nc.dram_tensor("yT", [128, 8, NT], F32, kind="ExternalOutput")
        xs = [self.dscr("xs%d" % i, [128, 8, NTOT], F32, tap=True) for i in range(2 * NL)]
        q_d = [self.dscr("q_d%d" % i, [128, 3, NTOT], BF16, tap=True) for i in range(3)]
        kaT_d = self.dscr("kaT_d", [128, NTOT], BF16, tap=True)
        va_d = self.dscr("va_d", [NTOT, 128], BF16, tap=True)
        kcT_d = self.dscr("kcT_d", [128, 3, NTOT], BF16, tap=True)
        vc_d = self.dscr("vc_d", [NTOT, 384], BF16, tap=True)
        kbcT_d = self.dscr("kbcT_d", [128, NCX], BF16, tap=True)
        vbc_d = self.dscr("vbc_d", [NCX, 128], BF16, tap=True)
        contrib = self.dscr("contrib", [960, 1024], BF16, tap=True)
        gathB = self.dscr("gathB", [4 * 512, 1024], BF16)
        gathH = self.dscr("gathH", [4 * 448, 1024], BF16)

        class G:
            @staticmethod
            def view(r, off, dims):
                if off < 524288:
                    return dview(gathB, r * 524288 + off, dims)
                return dview(gathH, r * 458752 + (off - 524288), dims)
        gath = G
        self.gathB, self.gathH = gathB, gathH
        o_d = [self.dscr("o_d%d" % i, [128, 3, NTOT], BF16, tap=True) for i in range(3)]
        mods_d = self.dscr("mods_d", [128, NL, 48, 2], F32, tap=True)

        with ExitStack() as es0:
            self.T = Trk(nc, es0)
            T = self.T
            self.bankc = 0
            self.accc = 0
            self.nsb = 4
            self.split = False
            self.stg_i = 0
            self.cast_i = 0
            self.cast_engs = ("pool",)
            self.psall = es0.enter_context(nc.psum_tensor("psall", [128, 4096], F32))
            self.ps = [self.psall[:, i * 512:(i + 1) * 512] for i in range(8)]
            self.ones = self.sb(es0, "ones", [128, 128], BF16)
            self.bd = self.sb(es0, "bd", [128, 128], BF16)
            self.epsc = self.sb(es0, "epsc", [128, 1], F32)
            self.modsb = self.sb(es0, "modsb", [128, NL, 48, 2], F32)
            self.dvec = self.sb(es0, "dvec", [128, NL, 2, 6, 8], F32)
            self.gn = self.sb(es0, "gn", [128, NL, 4, 8], F32)
            self.stg = self.sb(es0, "stg", [128, 3, 1024], F32)
            self.nstg = 3

            with ExitStack() as es:
                self.memset("pool", self.ones[:], 1.0, ["ones"])
                self.memset("pool", self.bd[:], 0.0, ["bd"])
                self.memset("pool", self.bd[0:64, 0:64], 1.0, ["bd"])
                self.memset("pool", self.bd[64:128, 64:128], 1.0, ["bd"])
                self.memset("pool", self.epsc[:], EPS, ["epsc"])
                cc_s = self.sb(es, "cc_s", [128, 8, 2], F32)
                sil = self.sb(es, "sil", [128, 8, 2], F32)
                self.dma(cc_s[:], ccT.ap(), (), ["cc_s"])
                self.act(sil[:], cc_s[:], AF.Silu, ["cc_s"], ["sil"])
                wa = self.sb(es, "wa", [128, 2, 8, 768], F32)
                bad = self.sb(es, "bad", [128, NL, 48], F32)
                for l in range(NL):
                    self.dma(bad[:, l, :], L[l]["badaT"].ap(), (), ["bad"])
                    self.dma(self.gn[:, l], L[l]["gains"].ap(), (), ["gn"])
                    wsrc = L[l]["wada"].ap().rearrange("(k p) n -> p k n", p=128)
                    b = self.newbank()
                    for g in range(8):
                        s = g % 2
                        self.dma(wa[:, s], wsrc[:, :, g * 768:(g + 1) * 768], (), [("wa", s)])
                        for mm_ in range(6):
                            m = g * 6 + mm_
                            for k in range(8):
                                self.mm(self.ps[b][:, 2 * m:2 * m + 2], wa[:, s, k, mm_ * 128:(mm_ + 1) * 128], sil[:, k, :],
                                        k == 0, k == 7, [("wa", s), "sil"], [("ps", b)])
                    self.tt("dve", self.modsb[:, l], self.ps[b][:, 0:96].rearrange("p (m t) -> p m t", t=2),
                            bcast(bad[:, l, :], 2, 2), ALU.add, [("ps", b), "bad"], ["modsb"])
                    for t in range(2):
                        mv = self.modsb[:, l, :, t]
                        dv = self.dvec[:, l, t]
                        self.stt(dv[:, 0, :], mv[:, 8:16], 1.0, self.gn[:, l, 0, :], ALU.add, ALU.mult, ["modsb", "gn"], ["mods"])
                        self.cp("dve", dv[:, 1, :], mv[:, 0:8], ["modsb"], ["mods"])
                        self.tt("dve", dv[:, 2, :], mv[:, 16:24], self.gn[:, l, 1, :], ALU.mult, ["modsb", "gn"], ["mods"])
                        self.stt(dv[:, 3, :], mv[:, 32:40], 1.0, self.gn[:, l, 2, :], ALU.add, ALU.mult, ["modsb", "gn"], ["mods"])
                        self.cp("dve", dv[:, 4, :], mv[:, 24:32], ["modsb"], ["mods"])
                        self.tt("dve", dv[:, 5, :], mv[:, 40:48], self.gn[:, l, 3, :], ALU.mult, ["modsb", "gn"], ["mods"])
                if "mods_d" in self.taps:
                    self.dma(mods_d.ap(), self.modsb[:], ["modsb"], ["mods_d"])
                T.flush()

            for l in range(NL):
                last = (l == NL - 1)
                if l == 0:
                    xin_lat = lambda c0, n: xT.ap()[:, :, c0:c0 + n]
                    xin_ctx = ctxT.ap()
                else:
                    xin_lat = (lambda xs_: (lambda c0, n: xs_.ap()[:, :, c0:c0 + n]))(xs[2 * l - 1])
                    xin_ctx = xs[2 * l - 1].ap()[:, :, NT:NTOT]
                if self.stop == "mods":
                    break
                self.phase_proj(L[l], l, last, xin_lat, xin_ctx, ropeC, ropeS, q_d, kaT_d, va_d, kcT_d, vc_d, kbcT_d, vbc_d, contrib)
                if self.stop == "proj%d" % l or (self.stop or "").startswith("proj0"):
                    break
                self.phase_gather(contrib, gath)
                self.phase_attn_a(L[l], l, last, q_d[0], kaT_d, va_d, gath, maskA, o_d[0])
                if self.stop == "attn_a%d" % l:
                    break
                self.phase_attn_b(L[l], l, last, q_d[1], kbcT_d, vbc_d, gath, o_d[1])
                if self.stop == "attn_b%d" % l:
                    break
                self.phase_attn_c(L[l], l, last, q_d[2], kcT_d, vc_d, gath, rm01, o_d[2])
                if self.stop == "attn_c%d" % l:
                    break
                self.phase_mix(L[l], l, last, xin_lat, xin_ctx, o_d, xs[2 * l])
                if self.stop == "mix%d" % l:
                    break
                out_lat = yT if last else xs[2 * l + 1]
                self.phase_mlp(L[l], l, last, xs[2 * l], out_lat, xs[2 * l + 1])
                if self.stop == "mlp%d" % l:
                    break
            T.final_wait()
        return nc

    def phase_proj(self, Ld, l, last, xin_lat, xin_ctx, ropeC, ropeS, q_d, kaT_d, va_d, kcT_d, vc_d, kbcT_d, vbc_d, contrib):
        T = self.T
        with ExitStack() as es:
            self.split = False
            self.alloc_norm(es, 512)
            hT = self.sb(es, "hT", [128, 8, NTOT], BF16)
            xch = self.sb(es, "xch", [128, 2, 8, 512], F32)
            rC = self.sb(es, "rC", [128, NT], F32)
            rS = self.sb(es, "rS", [128, NT], F32)
            bg = self.sb(es, "bg", [128, 4], F32)
            wt = self.sb(es, "wt", [128, 2, 8, 640], BF16)
            ost = self.sb(es, "ost", [128, 4, 640], BF16)
            t1 = self.sb(es, "t1", [128, 2, 512], F32)
            t2 = self.sb(es, "t2", [128, 2, 512], F32)
            sqh = self.sb(es, "sqh", [128, 2, 512], BF16)
            lnh = self.sb(es, "lnh", [128, 2, 512], F32)
            rsh = self.sb(es, "rsh", [128, 2, 512], F32)
            self.dma(rC[:], ropeC.ap(), (), ["rC"])
            self.dma(rS[:], ropeS.ap(), (), ["rS"])
            self.dma(bg[:], Ld["bgain"].ap(), (), ["bg"])
            chunks = [(i * 512, 512, False) for i in range(4)] + [(NT, NCX, True)]
            for ci, (c0, n, isc) in enumerate(chunks):
                s = ci % 2
                src = xin_ctx if isc else xin_lat(c0, n)
                self.dma(xch[:, s, :, 0:n], src, (), [("xch", s)])
                dv = self.dvec[:, l, 1 if isc else 0]
                self.norm_mod(xch[:, s], n, dv[:, 0, :], dv[:, 1, :], hT[:, :, c0:c0 + n], ("xch", s), "hT")
            if self.stop == "proj0a":
                T.flush()
                return
            win = Ld["win"].ap().rearrange("(k p) n -> p k n", p=128)
            units = []
            units.append(("KB", 0, [C_KB, C_KBS]))
            units.append(("V", 0, None))
            units.append(("KA", 0, [C_KA, C_KAS]))
            for mi in range(3):
                units.append(("KC", mi, [C_KC + mi * 128]))
            n_kv_units = len(units)
            for mi in range(3):
                units.append(("QB", mi, [C_QB + mi * 128, C_QBS + mi * 128]))
            for mi in range(3):
                units.append(("QA", mi, [C_QA + mi * 128, C_QAS + mi * 128]))
            for mi in range(3):
                units.append(("QC", mi, [C_QC + mi * 128]))
            ctr = dict(ost=0, t=0)

            ctoks = []

            def store(srcs_dsts, tok):
                for dst, src in srcs_dsts:
                    ctr["st"] = ctr.get("st", 0) + 1
                    wtk = ("dout", ctr["st"])
                    if dst.tensor.name == contrib.name:
                        ctoks.append(wtk)
                    self.dma(dst, src, [tok], [wtk])

            if self.stop and self.stop.startswith("proj0u"):
                sel = [int(x) for x in self.stop[6:].split("_")]
                units = [units[i] for i in sel]
            def load_unit(ui):
                kind, mi, cols = units[ui]
                ws = ui % 2
                wtok = ("+wt", ws)
                if kind == "V":
                    self.load_w(wt[:, ws, :, 0:640], win[:, :, C_V:C_V + 640], wtok, 8, 640)
                else:
                    for j, c in enumerate(cols):
                        self.load_w(wt[:, ws, :, j * 128:(j + 1) * 128], win[:, :, c:c + 128], wtok, 8, 128)
            load_unit(0)
            for ui, (kind, mi, cols) in enumerate(units):
                ws = ui % 2
                wtok = ("+wt", ws)
                if ui == n_kv_units and not (self.stop or "").startswith("proj0u"):
                    self.emit_gather(contrib, list(ctoks))
                if ui + 1 < len(units):
                    load_unit(ui + 1)
                if kind == "V":
                    for tile in range(NTOT // 128):
                        t0 = tile * 128
                        b0, b1 = self.newbank(), self.newbank()
                        for k in range(8):
                            self.mm(self.ps[b0][:, 0:512], hT[:, k, t0:t0 + 128], wt[:, ws, k, 0:512], k == 0, k == 7, ["hT", wtok], [("ps", b0)])
                        for k in range(8):
                            self.mm(self.ps[b1][:, 0:128], hT[:, k, t0:t0 + 128], wt[:, ws, k, 512:640], k == 0, k == 7, ["hT", wtok], [("ps", b1)])
                        o = ctr["ost"] % 4
                        ctr["ost"] += 1
                        otok = ("ost", o)
                        self.act(ost[:, o, 0:512], self.ps[b0][:, 0:512], AF.Copy, [("ps", b0)], [otok])
                        self.cp("dve", ost[:, o, 512:640], self.ps[b1][:, 0:128], [("ps", b1)], [otok])
                        cb = contrib
                        dl = [(va_d.ap()[t0:t0 + 128, :], ost[:, o, 0:128]),
                              (vc_d.ap()[t0:t0 + 128, :], ost[:, o, 256:640])]
                        if tile < 16:
                            dl.append((dview(cb, O_VB + t0 * 128, [[128, 128], [1, 128]]), ost[:, o, 128:256]))
                            if tile == 0:
                                dl.append((dview(cb, O_VAH, [[128, 128], [1, 128]]), ost[:, o, 0:128]))
                            if tile == 15:
                                dl.append((dview(cb, O_VAT, [[128, 128], [1, 128]]), ost[:, o, 0:128]))
                            if tile < 2:
                                dl.append((dview(cb, O_VCH + tile * 128 * 384, [[384, 128], [1, 384]]), ost[:, o, 256:640]))
                            if tile >= 14:
                                dl.append((dview(cb, O_VCT + (tile - 14) * 128 * 384, [[384, 128], [1, 384]]), ost[:, o, 256:640]))
                        else:
                            dl.append((vbc_d.ap()[t0 - NT:t0 - NT + 128, :], ost[:, o, 128:256]))
                        store(dl, otok)
                    continue
                for ci, (c0, n, isc) in enumerate(chunks):
                    if isc and last and kind in ("QA", "QB", "QC"):
                        continue
                    hs = [hT[:, k, c0:c0 + n] for k in range(8)]
                    bq = self.newbank()
                    for k in range(8):
                        self.mm(self.ps[bq][:, 0:n], wt[:, ws, k, 0:128], hs[k], k == 0, k == 7, ["hT", wtok], [("ps", bq)])
                    pq = self.ps[bq][:, 0:n]
                    o = ctr["ost"] % 4
                    ctr["ost"] += 1
                    otok = ("ost", o)
                    oo = ost[:, o, 0:n]
                    need_sw = (len(cols) == 2) and not isc
                    if need_sw:
                        bs = self.newbank()
                        for k in range(8):
                            self.mm(self.ps[bs][:, 0:n], wt[:, ws, k, 128:256], hs[k], k == 0, k == 7, ["hT", wtok], [("ps", bs)])
                        psw = self.ps[bs][:, 0:n]
                    tt_ = ctr["t"] % 2
                    ctr["t"] += 1
                    a1, a2 = t1[:, tt_, 0:n], t2[:, tt_, 0:n]
                    k1, k2 = ("t1", tt_), ("t2", tt_)
                    if kind in ("QA", "KA"):
                        if isc:
                            self.act(oo, pq, AF.Copy, [("ps", bq)], [otok])
                        else:
                            self.tt("dve", a1, pq, rC[:, c0:c0 + n], ALU.mult, [("ps", bq), "rC"], [k1])
                            self.tt("dve", a2, psw, rS[:, c0:c0 + n], ALU.mult, [("ps", bs), "rS"], [k2])
                            self.tt("pool", oo, a1, a2, ALU.add, [k1, k2], [otok])
                    elif kind in ("QB", "KB"):
                        gi = 0 if kind == "QB" else 2
                        self.act(sqh[:, tt_, 0:n], pq, AF.Square, [("ps", bq)], [("sqh", tt_)])
                        bss = self.newbank()
                        self.mm(self.ps[bss][:, 0:n], self.bd[:], sqh[:, tt_, 0:n], True, True, [("sqh", tt_), "bd"], [("ps", bss)])
                        self.act(lnh[:, tt_, 0:n], self.ps[bss][:, 0:n], AF.Ln, [("ps", bss)], [("lnh", tt_)], bias=self.epsc[:, 0:1], scale=1.0 / 64)
                        self.act(rsh[:, tt_, 0:n], lnh[:, tt_, 0:n], AF.Exp, [("lnh", tt_)], [("rsh", tt_)], scale=-0.5)
                        if isc:
                            self.stt(oo, pq, bg[:, gi:gi + 1], rsh[:, tt_, 0:n], ALU.mult, ALU.mult, [("ps", bq), "bg", ("rsh", tt_)], [otok])
                        else:
                            self.stt(a1, pq, bg[:, gi:gi + 1], rC[:, c0:c0 + n], ALU.mult, ALU.mult, [("ps", bq), "bg", "rC", ("sqh", tt_)], [k1])
                            self.stt(a2, psw, bg[:, gi + 1:gi + 2], rS[:, c0:c0 + n], ALU.mult, ALU.mult, [("ps", bs), "bg", "rS"], [k2])
                            self.tt("pool", a1, a1, a2, ALU.add, [k1, k2], [k1])
                            self.tt("pool", oo, a1, rsh[:, tt_, 0:n], ALU.mult, [k1, ("rsh", tt_)], [otok])
                    else:
                        self.act(oo, pq, AF.Copy, [("ps", bq)], [otok])
                    dl = []
                    cb = contrib
                    if kind == "QA":
                        dl.append((q_d[0].ap()[:, mi, c0:c0 + n], oo))
                    elif kind == "QB":
                        dl.append((q_d[1].ap()[:, mi, c0:c0 + n], oo))
                    elif kind == "QC":
                        dl.append((q_d[2].ap()[:, mi, c0:c0 + n], oo))
                    elif kind == "KA":
                        dl.append((kaT_d.ap()[:, c0:c0 + n], oo))
                        if ci == 0:
                            dl.append((dview(cb, O_KAH, [[128, 128], [1, 128]]), ost[:, o, 0:128]))
                        if ci == 3:
                            dl.append((dview(cb, O_KAT, [[128, 128], [1, 128]]), ost[:, o, 384:512]))
                    elif kind == "KB":
                        if isc:
                            dl.append((kbcT_d.ap(), oo))
                        else:
                            dl.append((dview(cb, O_KB + c0, [[2048, 128], [1, n]]), oo))
                    elif kind == "KC":
                        dl.append((kcT_d.ap()[:, mi, c0:c0 + n], oo))
                        if ci == 0:
                            dl.append((dview(cb, O_KCH + mi * 256, [[768, 128], [1, 256]]), ost[:, o, 0:256]))
                        if ci == 3:
                            dl.append((dview(cb, O_KCT + mi * 256, [[768, 128], [1, 256]]), ost[:, o, 256:512]))
                    store(dl, otok)
            T.flush()

    def phase_gather(self, contrib, gath):
        return

    def emit_gather(self, contrib, rtoks):
        T = self.T
        gB, gH = self.gathB, self.gathH
        T.add("pool", lambda e: e.collective_compute("AllGather", ALU.bypass, replica_groups=[[0, 1, 2, 3], [4, 5, 6, 7]],
                                                     ins=[contrib.ap()[0:512, :]], outs=[gB.ap()]), rtoks, ["gathB"], cc=True)
        T.add("pool", lambda e: e.collective_compute("AllGather", ALU.bypass, replica_groups=[[0, 1, 2, 3], [4, 5, 6, 7]],
                                                     ins=[contrib.ap()[512:960, :]], outs=[gH.ap()]), rtoks, ["gathH"], cc=True)

    def attn_fin(self, bank, n, base, out_ap, rc, rctok, extra=None, extra_tok=None, shape3=None):
        ob = 64 - base
        pso = self.ps[bank]
        den = pso[ob:ob + 64, 0:n]
        num = pso[base:base + 64, 0:n]
        rcp = rc[base:base + 64, 0:n]
        if shape3 is not None:
            a, b_ = shape3
            den = den.rearrange("p (a b) -> p a b", a=a)
            num = num.rearrange("p (a b) -> p a b", a=a)
            rcp = rcp.rearrange("p (a b) -> p a b", a=a)
        if extra is not None:
            self.tt("dve", rcp, den, extra, ALU.add, [("ps", bank), extra_tok], [rctok])
            self.recip(rcp, rcp, [rctok], [rctok])
        else:
            self.recip(rcp, den, [("ps", bank)], [rctok])
        self.tt("dve", out_ap, num, rcp, ALU.mult, [("ps", bank), rctok], ["oT"])

    def pipeline(self, items, look=3, group=1):
        groups = [items[i:i + group] for i in range(0, len(items), group)]
        banks = {}

        def qks(gi):
            for k, itm in enumerate(groups[gi]):
                banks[(gi, k)] = self.newbank()
                itm[0](banks[(gi, k)])
        for gi in range(min(look, len(groups))):
            qks(gi)
        for gi in range(len(groups)):
            if gi + look < len(groups):
                qks(gi + look)
            for k, itm in enumerate(groups[gi]):
                itm[1](banks[(gi, k)])
            for k, itm in enumerate(groups[gi]):
                itm[2]()
                if itm[3] is not None:
                    itm[3]()

    def load_vaug(self, dst, src, tokn):
        self.dma(dst, src, (), [tokn])

    def phase_attn_a(self, Ld, l, last, qa_d, kaT_d, va_d, gath, maskA, oa_d):
        T = self.T
        with ExitStack() as es:
            self.split = True
            QT = self.sb(es, "QT", [128, 3, NTOT], BF16)
            KT = self.sb(es, "KT", [128, NTOT], BF16)
            KH = self.sb(es, "KH", [128, 8, 128], BF16)
            VA = self.sb(es, "VA", [128, 18, 2, 128], BF16)
            VH = self.sb(es, "VH", [128, 8, 2, 128], BF16)
            MA = self.sb(es, "MA", [128, 10, 128], F32)
            ES = self.sb(es, "ES", [128, 6, 128], F32)
            sk = self.sb(es, "sk", [128, 6], F32)
            PT = self.sb(es, "PT", [128, 4, 512], BF16)
            tm = self.sb(es, "tm", [128, 2, 384], F32)
            rc = self.sb(es, "rc", [128, 2, 512], F32)
            oT = self.sb(es, "oT", [128, 3, NTOT], BF16)
            nq = NT if last else NTOT
            self.dma(QT[:, :, 0:nq], qa_d.ap()[:, :, 0:nq], (), ["+QT"])
            self.dma(KT[:], kaT_d.ap(), (), ["+KT"])
            self.memset("pool", VA[:], 1.0, ["+VA"])
            self.memset("pool", VH[:], 1.0, ["+VH"])
            vsrc = va_d.ap().rearrange("(t p) c -> p t c", p=128)
            self.dma(VA[:, :, 0, 0:64], vsrc[:, :, 0:64], (), ["+VA"])
            self.dma(VA[:, :, 1, 64:128], vsrc[:, :, 64:128], (), ["+VA"])
            for r in range(4):
                pass
                self.dma(KH[:, r, :], gath.view(r, O_KAT, [[128, 128], [1, 128]]), ["gath"], ["+KH"])
                self.dma(KH[:, 4 + r, :], gath.view(r, O_KAH, [[128, 128], [1, 128]]), ["gath"], ["+KH"])
                self.dma(VH[:, r, 0, 0:64], gath.view(r, O_VAT, [[128, 128], [1, 64]]), ["gath"], ["+VH"])
                self.dma(VH[:, r, 1, 64:128], gath.view(r, O_VAT + 64, [[128, 128], [1, 64]]), ["gath"], ["+VH"])
                self.dma(VH[:, 4 + r, 0, 0:64], gath.view(r, O_VAH, [[128, 128], [1, 64]]), ["gath"], ["+VH"])
                self.dma(VH[:, 4 + r, 1, 64:128], gath.view(r, O_VAH + 64, [[128, 128], [1, 64]]), ["gath"], ["+VH"])
            self.dma(MA[:], maskA.ap(), (), ["MA"])
            self.dma(sk[:], Ld["sinkT"].ap(), (), ["sk"])
            self.act(sk[:], sk[:], AF.Exp, ["sk"], ["sk"])
            self.cp("dve", ES[:], bcast(sk[:], 2, 128), ["sk"], ["ES"])
            it = dict(p=0, t=0, r=0)
            items = []

            def run(qcols, nqc, base, tiles, shape3, out_ap, es_ap):
                n = 3 * nqc
                rhs = QT[base:base + 64, :, qcols:qcols + nqc]
                bo = self.newacc()
                nt = len(tiles)
                for ti, (kap, vap, mk) in enumerate(tiles):
                    p = it["p"] % 4
                    it["p"] += 1
                    t = it["t"] % 2
                    if mk is not None:
                        it["t"] += 1

                    def qk(b, kap=kap):
                        self.mm(self.ps[b][:, 0:n].rearrange("p (a b) -> p a b", a=3), kap, rhs, True, True, ["+QT", "+KT", "+KH"], [("ps", b)])

                    def sm(b, mk=mk, p=p, t=t):
                        pt = PT[:, p, 0:n]
                        if mk is not None:
                            self.stt(tm[:, t, 0:n].rearrange("p (a b) -> p a b", a=3), self.ps[b][:, 0:n].rearrange("p (a b) -> p a b", a=3),
                                     0.125, bcast(mk, 1, 3), ALU.mult, ALU.add, [("ps", b), "MA"], [("tm", t)])
                            self.act(pt, tm[:, t, 0:n], AF.Exp, [("tm", t)], [("PT", p)])
                        else:
                            self.act(pt, self.ps[b][:, 0:n], AF.Exp, [("ps", b)], [("PT", p)], scale=0.125)

                    def pv(vap=vap, p=p, ti=ti):
                        self.mm(self.ps[bo][:, 0:n], vap, PT[:, p, 0:n], ti == 0, ti == nt - 1, [("PT", p), "+VA", "+VH"], [("ps", bo)])
                    fin = None
                    if ti == nt - 1:
                        r_ = it["r"] % 2
                        it["r"] += 1

                        def fin(r_=r_):
                            self.attn_fin(bo, n, base, out_ap, rc[:, r_], ("rc", r_), extra=es_ap, extra_tok="ES", shape3=shape3)
                    items.append((qk, sm, pv, fin))

            for half in range(2):
                base = half * 64
                ob = 64 - base
                esl = ES[ob:ob + 64, 3 * half:3 * half + 3, :]
                for j in range(16):
                    tiles = []
                    if j > 0:
                        tiles.append((KT[base:base + 64, (j - 1) * 128:j * 128], VA[:, j - 1, half, :], MA[:, 0, :]))
                    else:
                        for r in range(4):
                            tiles.append((KH[base:base + 64, r, :], VH[:, r, half, :], MA[:, 2 + r, :]))
                    tiles.append((KT[base:base + 64, j * 128:(j + 1) * 128], VA[:, j, half, :], None))
                    if j < 15:
                        tiles.append((KT[base:base + 64, (j + 1) * 128:(j + 2) * 128], VA[:, j + 1, half, :], MA[:, 1, :]))
                    else:
                        for r in range(4):
                            tiles.append((KH[base:base + 64, 4 + r, :], VH[:, 4 + r, half, :], MA[:, 6 + r, :]))
                    for c in range(2):
                        tiles.append((KT[base:base + 64, NT + c * 128:NT + (c + 1) * 128], VA[:, 16 + c, half, :], None))
                    run(j * 128, 128, base, tiles, (3, 128), oT[base:base + 64, :, j * 128:(j + 1) * 128], esl)
                if not last:
                    for cq in range(2):
                        tiles = [(KT[base:base + 64, NT + c * 128:NT + (c + 1) * 128], VA[:, 16 + c, half, :], None) for c in range(2)]
                        q0 = NT + cq * 128
                        run(q0, 128, base, tiles, (3, 128), oT[base:base + 64, :, q0:q0 + 128], esl)
            self.pipeline(items, 3)
            self.dma(oa_d.ap()[:, :, 0:nq], oT[:, :, 0:nq], ["oT"], ["oa_d"])
            T.flush()

    def phase_attn_b(self, Ld, l, last, qb_d, kbcT_d, vbc_d, gath, ob_d):
        T = self.T
        with ExitStack() as es:
            NK = NCX + 4 * NT
            self.split = True
            self.nsb = 6
            self.bankc = 0
            QT = self.sb(es, "QTb", [128, 3, NTOT], BF16)
            KT = self.sb(es, "KTb", [128, NK], BF16)
            VB = self.sb(es, "VBb", [128, 66, 2, 128], BF16)
            PT = self.sb(es, "PTb", [128, 6, 512], BF16)
            rc = self.sb(es, "rcb", [128, 2, 512], F32)
            oT = self.sb(es, "oTb", [128, 3, NTOT], BF16)
            nq = NT if last else NTOT
            self.dma(QT[:, :, 0:nq], qb_d.ap()[:, :, 0:nq], (), ["+QT"])
            self.dma(KT[:, 0:NCX], kbcT_d.ap(), (), ["+KT"])
            self.memset("pool", VB[:, 0:33], 1.0, ["+VB"])
            self.memset("pool", VB[:, 33:66], 1.0, ["+VB"])
            vcs = vbc_d.ap().rearrange("(t p) c -> p t c", p=128)
            self.dma(VB[:, 0:2, 0, 0:64], vcs[:, :, 0:64], (), ["+VB"])
            self.dma(VB[:, 0:2, 1, 64:128], vcs[:, :, 64:128], (), ["+VB"])
            for r in range(4):
                pass
                self.dma(KT[:, NCX + r * NT:NCX + (r + 1) * NT], gath.view(r, O_KB, [[2048, 128], [1, 2048]]), ["gath"], ["+KT"])
                self.dma(VB[:, 2 + r * 16:2 + (r + 1) * 16, 0, 0:64], gath.view(r, O_VB, [[128, 128], [128 * 128, 16], [1, 64]]), ["gath"], ["+VB"])
                self.dma(VB[:, 2 + r * 16:2 + (r + 1) * 16, 1, 64:128], gath.view(r, O_VB + 64, [[128, 128], [128 * 128, 16], [1, 64]]), ["gath"], ["+VB"])
            it = dict(p=0, r=0)
            items = []

            def run(mi, q0, n, ktiles):
                bo = [self.newacc(), self.newacc()]
                nt = len(ktiles)
                for ti, kt in enumerate(ktiles):
                    for half in range(2):
                        base = half * 64
                        p = it["p"] % 6
                        it["p"] += 1

                        def qk(b, base=base, kt=kt):
                            self.mm(self.ps[b][:, 0:n], KT[base:base + 64, kt * 128:(kt + 1) * 128], QT[base:base + 64, mi, q0:q0 + n],
                                    True, True, ["+QT", "+KT"], [("ps", b)])

                        def sm(b, p=p, half=half):
                            if half == 1:
                                return
                            assert b % 2 == 0 and p % 2 == 0
                            src = self.psall[:, b * 512:(b + 2) * 512].rearrange("p (a c) -> p a c", a=2)[:, :, 0:n]
                            self.act(PT[:, p:p + 2, 0:n], src, AF.Exp, [("ps", b), ("ps", b + 1)], [("PT", p), ("PT", p + 1)], scale=0.125)

                        def pv(p=p, half=half, kt=kt, ti=ti):
                            self.mm(self.ps[bo[half]][:, 0:n], VB[:, kt, half, :], PT[:, p, 0:n], ti == 0, ti == nt - 1,
                                    [("PT", p), "+VB"], [("ps", bo[half])])
                        fin = None
                        if ti == nt - 1:
                            r_ = it["r"] % 2
                            it["r"] += 1

                            def fin(r_=r_, half=half, base=base):
                                self.attn_fin(bo[half], n, base, oT[base:base + 64, mi, q0:q0 + n], rc[:, r_], ("rc", r_))
                        items.append((qk, sm, pv, fin))

            for mi in range(3):
                for qc in range(4):
                    run(mi, qc * 512, 512, list(range(66)))
                if not last:
                    run(mi, NT, NCX, [0, 1])
            self.pipeline(items, 2, 2)
            self.nsb = 4
            self.dma(ob_d.ap()[:, :, 0:nq], oT[:, :, 0:nq], ["oT"], ["ob_d"])
            T.flush()

    def phase_attn_c(self, Ld, l, last, qc_d, kcT_d, vc_d, gath, rm01, oc_d):
        T = self.T
        with ExitStack() as es:
            self.split = True
            QT = self.sb(es, "QTc", [128, 3, NTOT], BF16)
            KT = self.sb(es, "KTc", [128, 3, NTOT], BF16)
            KH = self.sb(es, "KHc", [128, 3, 8, 256], BF16)
            VC = self.sb(es, "VCc", [128, 18, 6, 128], BF16)
            VH = self.sb(es, "VHc", [128, 16, 6, 128], BF16)
            EB = self.sb(es, "EB", [128, 2, 6, 22, 64], BF16)
            RM = self.sb(es, "RM", [128, 44, 8], F32)
            RMb = self.sb(es, "RMb", [128, 44, 8], BF16)
            PT = self.sb(es, "PTc", [128, 4, 512], BF16)
            rc = self.sb(es, "rcc", [128, 2, 512], F32)
            oT = self.sb(es, "oTc", [128, 3, NTOT], BF16)
            nq = NT if last else NTOT
            self.dma(QT[:, :, 0:nq], qc_d.ap()[:, :, 0:nq], (), ["+QT"])
            self.dma(KT[:], kcT_d.ap(), (), ["+KT"])
            self.memset("pool", VC[:], 1.0, ["+VC"])
            self.memset("pool", VH[:], 1.0, ["+VH"])
            vsrc = vc_d.ap().rearrange("(t p) (h d) -> p t h d", p=128, d=64)
            for h in range(6):
                o = (h % 2) * 64
                self.dma(VC[:, :, h, o:o + 64], vsrc[:, :, h, :], (), ["+VC"])
            for r in range(4):
                pass
                self.dma(KH[:, :, r, :], gath.view(r, O_KCT, [[768, 128], [256, 3], [1, 256]]), ["gath"], ["+KH"])
                self.dma(KH[:, :, 4 + r, :], gath.view(r, O_KCH, [[768, 128], [256, 3], [1, 256]]), ["gath"], ["+KH"])
                for h in range(6):
                    o = (h % 2) * 64
                    self.dma(VH[:, 2 * r:2 * r + 2, h, o:o + 64], gath.view(r, O_VCT + h * 64, [[384, 128], [128 * 384, 2], [1, 64]]), ["gath"], ["+VH"])
                    self.dma(VH[:, 8 + 2 * r:8 + 2 * r + 2, h, o:o + 64], gath.view(r, O_VCH + h * 64, [[384, 128], [128 * 384, 2], [1, 64]]), ["gath"], ["+VH"])
            TBd = Ld["TB"].ap()
            for tbl in range(2):
                ebf = EB[:, tbl].rearrange("p h u q -> p (h u q)")
                c = 0
                while c < 6 * 22 * 64:
                    cc_ = min(1024, 6 * 22 * 64 - c)
                    sl_ = self.stg_i % self.nstg
                    self.stg_i += 1
                    st = self.stg[:, sl_, 0:cc_]
                    self.dma(st, TBd[:, tbl, c:c + cc_], (), [("stg", self.nstg, sl_)])
                    self.T.add("act", (lambda e_, o_=ebf[:, c:c + cc_], i_=st: e_.activation(out=o_, in_=i_, func=AF.Exp)),
                               [("stg", self.nstg, sl_)], ["+EB"], accw=True)
                    c += cc_
            self.dma(RM[:], rm01.ap(), (), ["RM"])
            self.cp("dve", RMb[:], RM[:], ["RM"], ["RMb"])
            it = dict(p=0, t=0, r=0)
            items = []

            def crun(q0, n, tiles, h, mi, base):
                rhs = QT[base:base + 64, mi, q0:q0 + n]
                bo = self.newacc()
                nt = len(tiles)
                for ti, (kap, vap, j, ri) in enumerate(tiles):
                    p = it["p"] % 4
                    it["p"] += 1
                    t = it["t"] % 3
                    if j is not None:
                        it["t"] += 1

                    def qk(b, kap=kap):
                        self.mm(self.ps[b][:, 0:n], kap, rhs, True, True, ["+QT", "+KT", "+KH"], [("ps", b)])

                    def sm(b, j=j, ri=ri, p=p, t=t):
                        pt = PT[:, p, 0:n]
                        self.act(pt, self.ps[b][:, 0:n], AF.Exp, [("ps", b)], [("PT", p)], scale=0.125)
                        if j is not None:
                            u0 = 14 - 2 * j
                            g_ = q0 // 512
                            tbl = 1 if g_ in (1, 2) else 0
                            pt3 = pt.rearrange("p (a b) -> p a b", a=8)
                            self.tt("dve", pt3, pt3, EB[:, tbl, h, u0:u0 + 8, :], ALU.mult, [("PT", p), "+EB"], [("PT", p)])
                            if tbl == 0:
                                self.tt("pool", pt3, pt3, bcast(RMb[:, ri, :], 2, 64), ALU.mult, [("PT", p), "RMb"], [("PT", p)])

                    def pv(vap=vap, p=p, ti=ti):
                        self.mm(self.ps[bo][:, 0:n], vap, PT[:, p, 0:n], ti == 0, ti == nt - 1, [("PT", p), "+VC", "+VH"], [("ps", bo)])
                    fin = None
                    if ti == nt - 1:
                        r_ = it["r"] % 2
                        it["r"] += 1

                        def fin(r_=r_):
                            self.attn_fin(bo, n, base, oT[base:base + 64, mi, q0:q0 + n], rc[:, r_], ("rc", r_))
                    items.append((qk, sm, pv, fin))

            for h in range(6):
                mi, half = h // 2, h % 2
                base = half * 64
                rmi = 0
                for g in range(4):
                    tiles = []
                    for j in range(8):
                        lt = g * 512 - 256 + 128 * j
                        if g == 0 and j < 2:
                            for r in range(4):
                                tiles.append((KH[base:base + 64, mi, r, j * 128:(j + 1) * 128], VH[:, 2 * r + j, h, :], j, rmi))
                                rmi += 1
                        elif g == 3 and j >= 6:
                            for r in range(4):
                                tiles.append((KH[base:base + 64, mi, 4 + r, (j - 6) * 128:(j - 5) * 128], VH[:, 8 + 2 * r + (j - 6), h, :], j, rmi))
                                rmi += 1
                        else:
                            tiles.append((KT[base:base + 64, mi, lt:lt + 128], VC[:, lt // 128, h, :], j, rmi))
                            rmi += 1
                    for c in range(2):
                        tiles.append((KT[base:base + 64, mi, NT + c * 128:NT + (c + 1) * 128], VC[:, 16 + c, h, :], None, None))
                    crun(g * 512, 512, tiles, h, mi, base)
                if not last:
                    tiles = [(KT[base:base + 64, mi, NT + c * 128:NT + (c + 1) * 128], VC[:, 16 + c, h, :], None, None) for c in range(2)]
                    crun(NT, NCX, tiles, h, mi, base)
            self.pipeline(items, 3)
            self.dma(oc_d.ap()[:, :, 0:nq], oT[:, :, 0:nq], ["oT"], ["oc_d"])
            T.flush()

    def post_norm_res(self, yTb, n, PG, xsrc, xdst, tok_y, tok_x, tok_o, tmpb):
        rs = self.rstd_of(None, yTb[:, :, 0:n], 8, n, 1024.0, tok_y, "pn")
        for m in range(8):
            self.stt(tmpb[:, m, 0:n], yTb[:, m, 0:n], PG[:, m:m + 1], rs, ALU.mult, ALU.mult, [tok_y, "rstd", "mods"], ["xn"])
            self.tt("pool", xdst[:, m, 0:n], xsrc[:, m, 0:n], tmpb[:, m, 0:n], ALU.add, [tok_x, "xn"], [tok_o])

    def phase_mix(self, Ld, l, last, xin_lat, xin_ctx, o_d, xs1):
        T = self.T
        with ExitStack() as es:
            self.split = False
            self.alloc_norm(es, 256)
            N = 256
            wg = self.sb(es, "wg", [128, 8, 3072], BF16)
            wbr = self.sb(es, "wbr", [128, 3, 3, 1024], BF16)
            wo = self.sb(es, "wo", [128, 8, 1024], BF16)
            xch = self.sb(es, "xchm", [128, 2, 8, N], F32)
            hT = self.sb(es, "hTm", [128, 2, 8, N], BF16)
            oc = self.sb(es, "ocm", [128, 2, 3, 3, N], BF16)
            sg = self.sb(es, "sg", [128, 2, 3, N], F32)
            ta = self.sb(es, "ta", [128, 2, 3, N], F32)
            mg = self.sb(es, "mg", [128, 8, N], BF16)
            yTb = self.sb(es, "yTb", [128, 8, N], F32)
            stg2 = self.sb(es, "stg2", [128, 6, 1024], F32)
            chunks = [(i * N, N, False) for i in range(NT // N)]
            if not last:
                chunks.append((NT, NCX, True))

            def prep(ci):
                c0, n, isc = chunks[ci]
                s_ = ci % 2
                src = xin_ctx if isc else xin_lat(c0, n)
                self.dma(xch[:, s_, :, 0:n], src, (), [("xch", s_)])
                for i in range(3):
                    self.dma(oc[:, s_, i, :, 0:n], o_d[i].ap()[:, :, c0:c0 + n], (), [("+oc", s_)])
                dv_ = self.dvec[:, l, 1 if isc else 0]
                self.norm_mod(xch[:, s_], n, dv_[:, 0, :], dv_[:, 1, :], hT[:, s_], ("xch", s_), ("hTm", s_))
            prep(0)
            win = Ld["win"].ap().rearrange("(k p) n -> p k n", p=128)
            self.cast_engs = ("pool", "dve", "act")
            old_stg, old_n = self.stg, self.nstg
            self.stg, self.nstg = stg2, 6
            for i in range(3):
                self.load_w(wg[:, :, i * 1024:(i + 1) * 1024], win[:, :, C_G + i * 1024:C_G + (i + 1) * 1024], ("+wg", i), 8, 1024)
                self.load_w(wbr[:, i], Ld["wbr"].ap()[i].rearrange("(k p) n -> p k n", p=128), ("+wbr", i), 3, 1024)
            self.load_w(wo[:], Ld["wout"].ap().rearrange("(k p) n -> p k n", p=128), "+wo", 8, 1024)
            self.stg, self.nstg = old_stg, old_n
            self.cast_engs = ("pool",)
            for ci, (c0, n, isc) in enumerate(chunks):
                s = ci % 2
                dv = self.dvec[:, l, 1 if isc else 0]
                for m in range(8):
                    q = m % 2
                    gb_, yb_ = [], []
                    for i in range(3):
                        b = self.newbank()
                        gb_.append(b)
                        for k in range(8):
                            self.mm(self.ps[b][:, 0:n], wg[:, k, i * 1024 + m * 128:i * 1024 + (m + 1) * 128], hT[:, s, k, 0:n], k == 0, k == 7,
                                    [("+wg", i), ("hTm", s)], [("ps", b)])
                        b = self.newbank()
                        yb_.append(b)
                        for k in range(3):
                            self.mm(self.ps[b][:, 0:n], wbr[:, i, k, m * 128:(m + 1) * 128], oc[:, s, i, k, 0:n], k == 0, k == 2,
                                    [("+wbr", i), ("+oc", s)], [("ps", b)])
                    for i in range(3):
                        self.act(sg[:, q, i, 0:n], self.ps[gb_[i]][:, 0:n], AF.Sigmoid, [("ps", gb_[i])], [("sg", q, i)])
                        self.tt("dve", ta[:, q, i, 0:n], sg[:, q, i, 0:n], self.ps[yb_[i]][:, 0:n], ALU.mult, [("sg", q, i), ("ps", yb_[i])], [("ta", q, i)])
                    self.tt("pool", ta[:, q, 0, 0:n], ta[:, q, 0, 0:n], ta[:, q, 1, 0:n], ALU.add, [("ta", q, 0), ("ta", q, 1)], [("ta", q, 0)])
                    self.tt("pool", mg[:, m, 0:n], ta[:, q, 0, 0:n], ta[:, q, 2, 0:n], ALU.add, [("ta", q, 0), ("ta", q, 2)], [("mg", m)])
                if ci + 1 < len(chunks):
                    prep(ci + 1)
                for m in range(8):
                    b = self.newbank()
                    for k in range(8):
                        self.mm(self.ps[b][:, 0:n], wo[:, k, m * 128:(m + 1) * 128], mg[:, k, 0:n], k == 0, k == 7, ["+wo", ("mg", k)], [("ps", b)])
                    self.act(yTb[:, m, 0:n], self.ps[b][:, 0:n], AF.Copy, [("ps", b)], ["yTb"])
                self.post_norm_res(yTb, n, dv[:, 2, :], xch[:, s], xch[:, s], "yTb", ("xch", s), ("xch", s), self.xn)
                self.dma(xs1.ap()[:, :, c0:c0 + n], xch[:, s, :, 0:n], [("xch", s)], [("xs1", ci)])
            T.flush()

    def phase_mlp(self, Ld, l, last, xs1, out_lat, xs2):
        T = self.T
        with ExitStack() as es:
            self.split = False
            self.alloc_norm(es, 256)
            w1 = self.sb(es, "w1", [128, 8, 4096], BF16)
            w2 = self.sb(es, "w2", [128, 32, 1024], BF16)
            xch = self.sb(es, "xchp", [128, 2, 8, 256], F32)
            h2 = self.sb(es, "h2", [128, 2, 8, 256], BF16)
            rl = self.sb(es, "rl", [128, 2, 256], F32)
            aT = self.sb(es, "aT", [128, 32, 256], BF16)
            y2 = self.sb(es, "y2", [128, 8, 256], F32)
            n = 256
            chunks = [(i * 256, False) for i in range(8)]
            if not last:
                chunks.append((NT, True))

            def prep(ci):
                c0, isc = chunks[ci]
                s_ = ci % 2
                self.dma(xch[:, s_], xs1.ap()[:, :, c0:c0 + n], ["xs1"], [("xch", s_)])
                dv_ = self.dvec[:, l, 1 if isc else 0]
                self.norm_mod(xch[:, s_], n, dv_[:, 3, :], dv_[:, 4, :], h2[:, s_], ("xch", s_), ("h2", s_))
            prep(0)
            self.cast_engs = ("pool", "dve", "act")
            w1src = Ld["w1"].ap().rearrange("(k p) n -> p k n", p=128)
            for cc_ in range(4):
                self.load_w(w1[:, :, cc_ * 1024:(cc_ + 1) * 1024], w1src[:, :, cc_ * 1024:(cc_ + 1) * 1024], ("+w1", cc_), 8, 1024)
            self.load_w(w2[:], Ld["w2"].ap().rearrange("(k p) n -> p k n", p=128), "+w2", 32, 1024)
            self.cast_engs = ("pool",)
            for ci, (c0, isc) in enumerate(chunks):
                s = ci % 2
                dv = self.dvec[:, l, 1 if isc else 0]
                for j in range(32):
                    b = self.newbank()
                    for k in range(8):
                        self.mm(self.ps[b][:, 0:n], w1[:, k, j * 128:(j + 1) * 128], h2[:, s, k, :], k == 0, k == 7, [("+w1", j // 8), ("h2", s)], [("ps", b)])
                    r_ = j % 2
                    self.act(rl[:, r_, :], self.ps[b][:, 0:n], AF.Relu, [("ps", b)], [("rl", r_)])
                    self.tt("pool", aT[:, j, :], rl[:, r_, :], rl[:, r_, :], ALU.mult, [("rl", r_)], [("aT", j)])
                if ci + 1 < len(chunks):
                    prep(ci + 1)
                for m in range(8):
                    b = self.newbank()
                    for j in range(32):
                        self.mm(self.ps[b][:, 0:n], w2[:, j, m * 128:(m + 1) * 128], aT[:, j, :], j == 0, j == 31, ["+w2", ("aT", j)], [("ps", b)])
                    self.act(y2[:, m, :], self.ps[b][:, 0:n], AF.Copy, [("ps", b)], ["y2"])
                self.post_norm_res(y2, n, dv[:, 5, :], xch[:, s], xch[:, s], "y2", ("xch", s), ("xch", s), self.xn)
                if isc:
                    dst = xs2.ap()[:, :, c0:c0 + n]
                else:
                    dst = out_lat.ap()[:, :, c0:c0 + n]
                self.dma(dst, xch[:, s], [("xch", s)], [("out", ci)])
            T.flush()


def _fm(v):
    v = np.asarray(v, np.float32)
    return np.ascontiguousarray(v.reshape(-1, 128).T)


def _perm_cols():
    def hc(base, heads, swap):
        out = []
        for h in heads:
            d = np.arange(64)
            if swap:
                d = d ^ 1
            out += list(base + h * 64 + d)
        return out
    p = []
    p += hc(0, HP, False) + hc(0, HP, True)
    p += hc(384, [0, 1], False) + hc(384, [0, 1], True)
    p += hc(640, HP, False) + hc(640, HP, True)
    p += hc(1024, [0, 1], False) + hc(1024, [0, 1], True)
    p += list(range(1280, 1664)) + list(range(1664, 2048))
    p += list(range(512, 640)) + list(range(1152, 1280)) + list(range(2048, 2432))
    p += list(range(2432, 5504))
    assert len(p) == NWIN
    return np.array(p)


def _rope_tables(tok0):
    pos = np.arange(tok0, tok0 + NT)
    row = (pos // 64).astype(np.float32)
    col = (pos % 64).astype(np.float32)
    freqs = (np.float32(10000.0) ** (-np.arange(16, dtype=np.float32) / np.float32(16))).astype(np.float32)
    ang = np.concatenate([row[:, None] * freqs, col[:, None] * freqs], axis=-1).astype(np.float32)
    cos, sin = np.cos(ang).astype(np.float32), np.sin(ang).astype(np.float32)
    d = np.arange(128) % 64
    C = cos[:, d // 2].T
    S = sin[:, d // 2].T * np.where(d % 2 == 0, -1.0, 1.0)[:, None]
    return np.ascontiguousarray(C, np.float32), np.ascontiguousarray(S, np.float32)


def _mask_a(rank):
    ki = np.arange(128)[:, None]
    qi = np.arange(128)[None, :]
    prev = np.where(qi <= ki, 0.0, NEG).astype(np.float32)
    nxt = np.where(ki <= qi, 0.0, NEG).astype(np.float32)
    allneg = np.full((128, 128), NEG, np.float32)
    m = np.zeros((128, 10, 128), np.float32)
    m[:, 0], m[:, 1] = prev, nxt
    for r in range(4):
        m[:, 2 + r] = prev if r == rank - 1 else allneg
        m[:, 6 + r] = nxt if r == rank + 1 else allneg
    return m


def _rm01(rank):
    out = np.zeros((128, 44, 8), np.float32)
    idx = 0
    kl = (np.arange(128) // 64)[:, None]
    ql = np.arange(8)[None, :]
    for g in range(4):
        R0 = rank * 32 + g * 8
        for j in range(8):
            kr = R0 - 4 + 2 * j + kl
            qr = R0 + ql
            rs = np.clip(qr - 4, 0, 120)
            valid = (kr >= rs) & (kr < rs + 8) & (kr >= 0) & (kr < 128)
            if g == 0 and j < 2:
                for r in range(4):
                    out[:, idx] = valid if r == rank - 1 else 0.0
                    idx += 1
            elif g == 3 and j >= 6:
                for r in range(4):
                    out[:, idx] = valid if r == rank + 1 else 0.0
                    idx += 1
            else:
                out[:, idx] = valid
                idx += 1
    assert idx == 44
    return out


def _tb_table(rpb, interior=False):
    rpb = np.asarray(rpb, np.float32)
    kc = np.arange(64)[:, None]
    qc = np.arange(64)[None, :]
    ws = np.clip(qc - 8, 0, 48)
    colv = (kc >= ws) & (kc < ws + 16)
    dc = np.clip(kc - qc + 15, 0, 30)
    tb = np.zeros((2, 64, 6, 22, 64), np.float32)
    for kl in range(2):
        for u in range(22):
            dr = 17 + kl - u
            for h in range(6):
                if 0 <= dr <= 14:
                    v = rpb[h, dr][dc]
                else:
                    v = np.zeros((64, 64), np.float32)
                if interior and not (3 <= dr <= 10):
                    tb[kl, :, h, u, :] = NEG
                else:
                    tb[kl, :, h, u, :] = np.where(colv, v, NEG)
    return np.ascontiguousarray(tb.reshape(128, 6, 22, 64))


_CACHE = {}


def _get_nc(n_layers=2, taps=()):
    key = (n_layers, tuple(taps))
    if key not in _CACHE:
        _CACHE[key] = K(n_layers, taps).build()
    return _CACHE[key]


def make_in_maps(inputs, n_layers=2):
    f = lambda a: np.asarray(a, np.float32)
    x, c, ctx, c_ctx = f(inputs["x"]), f(inputs["c"]), f(inputs["ctx"]), f(inputs["c_ctx"])
    perm = _perm_cols()
    rows_ab = np.concatenate([np.arange(h * 64, (h + 1) * 64) for h in HP])
    shared = {}
    for l in range(n_layers):
        shared["wada%d" % l] = np.ascontiguousarray(f(inputs["w_ada"])[l])
        shared["badaT%d" % l] = _fm(f(inputs["b_ada"])[l])
        shared["gains%d" % l] = np.ascontiguousarray(np.stack(
            [_fm(f(inputs[k])[l]) for k in ("norm_mix_pre", "norm_mix_post", "norm_mlp_pre", "norm_mlp_post")], axis=1))
        shared["win%d" % l] = np.ascontiguousarray(f(inputs["w_in"])[l][:, perm])
        gq, gk = f(inputs["qnorm_b"])[l], f(inputs["knorm_b"])[l]
        d = np.arange(128) % 64
        shared["bgain%d" % l] = np.ascontiguousarray(np.stack([gq[d], gq[d ^ 1], gk[d], gk[d ^ 1]], axis=1))
        shared["sinkT%d" % l] = np.ascontiguousarray(np.broadcast_to(f(inputs["sink_a"])[l][None, :], (128, 6)))
        shared["TB%d" % l] = np.ascontiguousarray(np.stack(
            [_tb_table(f(inputs["rpb_c"])[l]), _tb_table(f(inputs["rpb_c"])[l], True)], axis=1))
        shared["wbr%d" % l] = np.ascontiguousarray(np.stack(
            [f(inputs["w_br_a"])[l][rows_ab], f(inputs["w_br_b"])[l][rows_ab], f(inputs["w_br_c"])[l]], axis=0))
        shared["wout%d" % l] = np.ascontiguousarray(f(inputs["w_out"])[l])
        shared["w1_%d" % l] = np.ascontiguousarray(f(inputs["w_mlp_in"])[l])
        shared["w2_%d" % l] = np.ascontiguousarray(f(inputs["w_mlp_out"])[l])
    in_maps = []
    for core in range(8):
        b, rank = core // 4, core % 4
        tok0 = rank * NT
        m = dict(shared)
        xs = x[b, tok0:tok0 + NT, :]
        m["xT"] = np.ascontiguousarray(xs.T.reshape(8, 128, NT).transpose(1, 0, 2))
        m["ctxT"] = np.ascontiguousarray(ctx[b].T.reshape(8, 128, NCX).transpose(1, 0, 2))
        m["ccT"] = np.ascontiguousarray(np.stack([_fm(c[b]), _fm(c_ctx)], axis=2))
        C, S = _rope_tables(tok0)
        m["ropeC"], m["ropeS"] = C, S
        m["maskA"] = _mask_a(rank)
        m["rm01"] = _rm01(rank)
        in_maps.append(m)
    return in_maps


def kernel(**inputs):
    nc = _get_nc(2)
    in_maps = make_in_maps(inputs, 2)
    res = run_bass_kernel_spmd(nc, in_maps, core_ids=list(range(8)))
    out = np.zeros((2, 4 * NT, 1024), np.float32)
    for core in range(8):
        b, rank = core // 4, core % 4
        yT = np.asarray(res.results[core]["yT"])
        out[b, rank * NT:(rank + 1) * NT, :] = yT.transpose(1, 0, 2).reshape(1024, NT).T
    return out
```

```python
import numpy as np
from contextlib import ExitStack
import concourse.bass as bass
import concourse.mybir as mybir
from concourse.bass_utils import run_bass_kernel_spmd

F32 = mybir.dt.float32
BF16 = mybir.dt.bfloat16
AF = mybir.ActivationFunctionType
ALU = mybir.AluOpType

NT = 2048
NCX = 256
NTOT = NT + NCX
NEG = -30000.0
EPS = 1e-6
HP = [0, 3, 1, 4, 2, 5]
C_QA, C_QAS, C_KA, C_KAS = 0, 384, 768, 896
C_QB, C_QBS, C_KB, C_KBS = 1024, 1408, 1792, 1920
C_QC, C_KC, C_V, C_G = 2048, 2432, 2816, 3456
NWIN = 6528
O_KB, O_VB, O_KAH, O_KAT, O_VAH, O_VAT = 0, 262144, 524288, 540672, 557056, 573440
O_KCH, O_KCT, O_VCH, O_VCT, CONTRIB = 589824, 688128, 786432, 884736, 983040


class Trk:
    ENGS = ("pe", "act", "dve", "pool", "sp")
    NDSEM = 12

    def __init__(self, nc, es):
        self.nc = nc
        self.esem = {e: es.enter_context(nc.semaphore("s_" + e)) for e in self.ENGS}
        self.ecnt = {e: 0 for e in self.ENGS}
        self.dsem = {q: [es.enter_context(nc.semaphore("d_%s%d" % (q, i))) for i in range(self.NDSEM)]
                     for q in ("sp", "pool")}
        self.dval = {q: [0] * self.NDSEM for q in ("sp", "pool")}
        self.dcnt = {"sp": 0, "pool": 0}
        self.ccsem = es.enter_context(nc.semaphore("s_cc"))
        self.ccval = 0
        self.ops = []
        self.bar = None
        self.waited = {e: {} for e in self.ENGS}

    def add(self, eng, fn, r=(), w=(), dma=False, cc=False, accw=False):
        self.ops.append(dict(eng=eng, fn=fn, r=tuple(r), w=tuple(w), dma=dma, cc=cc, accw=accw))

    def flush(self):
        ops = self.ops
        self.ops = []
        if not ops:
            return
        last_w, readers = {}, {}

        def is_acc(tok):
            t0_ = tok[0] if isinstance(tok, tuple) else tok
            return isinstance(t0_, str) and t0_.startswith("+")
        for i, op in enumerate(ops):
            deps = set()
            for r in op["r"]:
                if r in last_w:
                    deps.update(last_w[r])
            for w in op["w"]:
                if w in last_w:
                    if is_acc(w) and (op["dma"] or op["accw"]):
                        deps.add(last_w[w][0])
                    else:
                        deps.update(last_w[w])
                for rd in readers.get(w, {}).values():
                    if isinstance(rd, list):
                        deps.update(rd)
                    else:
                        deps.add(rd)
            deps.discard(i)
            if op["eng"] == "pe":
                deps = {d for d in deps if ops[d]["dma"] or ops[d]["eng"] != "pe"}
            op["deps"] = deps
            for r in op["r"]:
                rr = readers.setdefault(r, {})
                if op["dma"]:
                    rr.setdefault("dma", []).append(i)
                else:
                    rr[op["eng"]] = i
            for w in op["w"]:
                if is_acc(w) and (op["dma"] or op["accw"]) and w in last_w:
                    last_w[w] = last_w[w] + [i]
                else:
                    last_w[w] = [i]
                readers[w] = {}
        for op in ops:
            op["sig"] = False
        for op in ops:
            for d in op["deps"]:
                ops[d]["sig"] = True
        last_of = {}
        for i, op in enumerate(ops):
            if not op["dma"]:
                last_of[op["eng"]] = i
        for i in last_of.values():
            ops[i]["sig"] = True
        for op in ops:
            if op["cc"]:
                self.ccval += 1
                op["done"] = (self.ccsem, self.ccval)
                op["pre"] = None
            elif op["dma"]:
                q = op["eng"]
                k = self.dcnt[q] % self.NDSEM
                self.dcnt[q] += 1
                prev = self.dval[q][k]
                self.dval[q][k] += 16
                op["done"] = (self.dsem[q][k], self.dval[q][k])
                op["pre"] = (self.dsem[q][k], prev) if prev > 0 else None
            elif op["sig"]:
                self.ecnt[op["eng"]] += 1
                op["done"] = (self.esem[op["eng"]], self.ecnt[op["eng"]])
                op["pre"] = None
            else:
                op["done"] = None
                op["pre"] = None
        per = {e: [] for e in self.ENGS}
        for op in ops:
            per[op["eng"]].append(op)
        bar = self.bar
        waited = self.waited

        def emit(ename, e):
            wd = waited[ename]

            def wait(sem, val):
                key = id(sem)
                if wd.get(key, 0) < val:
                    e.wait_ge(sem, val)
                    wd[key] = val
            if bar and per[ename]:
                for sem, val in bar:
                    wait(sem, val)
            for op in per[ename]:
                for d in op["deps"]:
                    sem, val = ops[d]["done"]
                    wait(sem, val)
                if op["pre"] is not None:
                    wait(*op["pre"])
                if op["fn"] is None:
                    continue
                ins = op["fn"](e)
                if op["done"] is not None:
                    sem, val = op["done"]
                    if op["cc"]:
                        ins.then_inc(sem, 1)
                    elif op["dma"]:
                        ins.then_inc(sem, 16)
                    else:
                        ins.then_inc(sem, 1)

        with self.nc.Block() as block:
            @block.sync
            def _(e):
                emit("sp", e)

            @block.scalar
            def _(e):
                emit("act", e)

            @block.vector
            def _(e):
                emit("dve", e)

            @block.gpsimd
            def _(e):
                emit("pool", e)

            @block.tensor
            def _(e):
                emit("pe", e)
        nb = []
        for e in self.ENGS:
            if self.ecnt[e] > 0:
                nb.append((self.esem[e], self.ecnt[e]))
        for q in ("sp", "pool"):
            for k in range(self.NDSEM):
                if self.dval[q][k] > 0:
                    nb.append((self.dsem[q][k], self.dval[q][k]))
        if self.ccval > 0:
            nb.append((self.ccsem, self.ccval))
        self.bar = nb

    def final_wait(self):
        bar = self.bar
        with self.nc.Block() as block:
            @block.sync
            def _(e):
                for sem, val in bar:
                    e.wait_ge(sem, val)


def bcast(ap, pos, n):
    l = [list(x) for x in ap.ap]
    l.insert(pos, [0, n])
    return bass.AP(ap.tensor, ap.offset, l)


def dview(t, off, dims):
    return bass.AP(t, off, [list(d) for d in dims])


class K:
    def __init__(self, n_layers=2, taps=(), stop=None):
        self.nl = n_layers
        self.taps = set(taps)
        self.stop = stop
        self.nc = bass.Bass("TRN2", target_bir_lowering=False)
        self.uid = 0

    def din(self, name, shape, dt=F32):
        return self.nc.dram_tensor(name, list(shape), dt, kind="ExternalInput")

    def dscr(self, name, shape, dt=BF16, tap=False):
        if tap and name in self.taps:
            return self.nc.dram_tensor(name, list(shape), dt, kind="ExternalOutput")
        return self.nc.dram_tensor(name, list(shape), dt)

    def sb(self, es, name, shape, dt):
        self.uid += 1
        return es.enter_context(self.nc.sbuf_tensor("%s_%d" % (name, self.uid), list(shape), dt))

    def mm(self, out, lhsT, rhs, start, stop, r, w):
        self.T.add("pe", lambda e: e.matmul(out, lhsT=lhsT, rhs=rhs, start=start, stop=stop), r, w)

    def act(self, out, in_, func, r, w, bias=None, scale=None):
        kw = {}
        if bias is not None:
            kw["bias"] = bias
        if scale is not None:
            kw["scale"] = scale
        self.T.add("act", lambda e: e.activation(out=out, in_=in_, func=func, **kw), r, w)

    def tt(self, eng, out, in0, in1, op, r, w):
        self.T.add(eng, lambda e: e.tensor_tensor(out=out, in0=in0, in1=in1, op=op), r, w)

    def ts(self, eng, out, in0, s1, s2, op0, op1, r, w):
        if op1 is None:
            self.T.add(eng, lambda e: e.tensor_scalar(out=out, in0=in0, scalar1=s1, scalar2=None, op0=op0), r, w)
        else:
            self.T.add(eng, lambda e: e.tensor_scalar(out=out, in0=in0, scalar1=s1, scalar2=s2, op0=op0, op1=op1), r, w)

    def stt(self, out, in0, scalar, in1, op0, op1, r, w):
        self.T.add("dve", lambda e: e.scalar_tensor_tensor(out=out, in0=in0, scalar=scalar, in1=in1, op0=op0, op1=op1), r, w)

    def cp(self, eng, out, in_, r, w):
        self.T.add(eng, lambda e: e.tensor_copy(out=out, in_=in_), r, w)

    def memset(self, eng, ap, val, w):
        self.T.add(eng, lambda e: e.memset(ap, val), (), w)

    def recip(self, out, in_, r, w):
        self.T.add("dve", lambda e: e.reciprocal(out=out, in_=in_), r, w)

    def dma(self, out, in_, r, w, q="sp"):
        self.T.add(q, lambda e: e.dma_start(out=out, in_=in_), r, w, dma=True)

    def newbank(self):
        if self.split:
            b = self.bankc % self.nsb
        else:
            b = self.bankc % 8
        self.bankc += 1
        return b

    def newacc(self):
        b = self.nsb + self.accc % (8 - self.nsb)
        self.accc += 1
        return b

    def alloc_norm(self, es, nmax):
        self.sq_buf = self.sb(es, "sqbuf", [128, 8, nmax], BF16)
        self.lnt = self.sb(es, "lnt", [128, nmax], F32)
        self.rstd = self.sb(es, "rstd", [128, nmax], F32)
        self.xn = self.sb(es, "xn", [128, 8, nmax], F32)

    def cast(self, dst, src, r, w):
        engs = self.cast_engs
        e = engs[self.cast_i % len(engs)]
        self.cast_i += 1
        if e == "act":
            self.T.add("act", lambda e_: e_.activation(out=dst, in_=src, func=AF.Copy), r, w, accw=True)
        else:
            self.T.add(e, lambda e_: e_.tensor_copy(out=dst, in_=src), r, w, accw=True)

    def load_w(self, dst, src, tok_w, nk, ncols):
        CH = 1024
        per = max(1, CH // ncols)
        k = 0
        while k < nk:
            kk = min(per, nk - k)
            if ncols > CH:
                assert per == 1
                c = 0
                while c < ncols:
                    cc = min(CH, ncols - c)
                    s = self.stg_i % self.nstg
                    self.stg_i += 1
                    st = self.stg[:, s, 0:cc]
                    self.dma(st, src[:, k, c:c + cc], (), [("stg", self.nstg, s)])
                    self.cast(dst[:, k, c:c + cc], st, [("stg", self.nstg, s)], [tok_w])
                    c += cc
            else:
                s = self.stg_i % self.nstg
                self.stg_i += 1
                st = self.stg[:, s, 0:kk * ncols].rearrange("p (k c) -> p k c", k=kk)
                self.dma(st, src[:, k:k + kk, :], (), [("stg", self.nstg, s)])
                self.cast(dst[:, k:k + kk, :], st, [("stg", self.nstg, s)], [tok_w])
            k += kk

    def rstd_of(self, es_tmp, src, nk, n, div, tokr, name):
        sq = self.sq_buf[:, 0:nk, 0:n]
        self.act(sq, src, AF.Square, [tokr], ["sqbuf"])
        b = self.newbank()
        for k in range(nk):
            self.mm(self.ps[b][:, 0:n], self.ones[:], sq[:, k, :], k == 0, k == nk - 1, ["sqbuf", "ones"], [("ps", b)])
        self.act(self.lnt[:, 0:n], self.ps[b][:, 0:n], AF.Ln, [("ps", b)], ["lnt"], bias=self.epsc[:, 0:1], scale=1.0 / div)
        self.act(self.rstd[:, 0:n], self.lnt[:, 0:n], AF.Exp, ["lnt"], ["rstd"], scale=-0.5)
        return self.rstd[:, 0:n]

    def norm_mod(self, xch, n, GG, SH, hT, tokx, tokh):
        rs = self.rstd_of(None, xch[:, :, 0:n], 8, n, 1024.0, tokx, "nm")
        self.tt("dve", self.xn[:, :, 0:n], xch[:, :, 0:n], bcast(rs, 1, 8), ALU.mult, [tokx, "rstd"], ["xn"])
        for k in range(8):
            self.ts("dve", hT[:, k, 0:n], self.xn[:, k, 0:n], GG[:, k:k + 1], SH[:, k:k + 1], ALU.mult, ALU.add,
                    ["xn", "mods"], [tokh])

    def build(self):
        nc = self.nc
        NL = self.nl
        xT = self.din("xT", [128, 8, NT])
        ctxT = self.din("ctxT", [128, 8, NCX])
        ccT = self.din("ccT", [128, 8, 2])
        ropeC = self.din("ropeC", [128, NT])
        ropeS = self.din("ropeS", [128, NT])
        maskA = self.din("maskA", [128, 10, 128])
        rm01 = self.din("rm01", [128, 44, 8])
        L = []
        for l in range(NL):
            d = dict(
                wada=self.din("wada%d" % l, [1024, 6144]),
                badaT=self.din("badaT%d" % l, [128, 48]),
                gains=self.din("gains%d" % l, [128, 4, 8]),
                win=self.din("win%d" % l, [1024, NWIN]),
                bgain=self.din("bgain%d" % l, [128, 4]),
                sinkT=self.din("sinkT%d" % l, [128, 6]),
                TB=self.din("TB%d" % l, [128, 2, 6 * 22 * 64]),
                wbr=self.din("wbr%d" % l, [3, 384, 1024]),
                wout=self.din("wout%d" % l, [1024, 1024]),
                w1=self.din("w1_%d" % l, [1024, 4096]),
                w2=self.din("w2_%d" % l, [4096, 1024]),
            )
            L.append(d)
        yT = nc.dram_tensor("yT", [128, 8, NT], F32, kind="ExternalOutput")
        xs = [self.dscr("xs%d" % i, [128, 8, NTOT], F32, tap=True) for i in range(2 * NL)]
        q_d = [self.dscr("q_d%d" % i, [128, 3, NTOT], BF16, tap=True) for i in range(3)]
        kaT_d = self.dscr("kaT_d", [128, NTOT], BF16, tap=True)
        va_d = self.dscr("va_d", [NTOT, 128], BF16, tap=True)
        kcT_d = self.dscr("kcT_d", [128, 3, NTOT], BF16, tap=True)
        vc_d = self.dscr("vc_d", [NTOT, 384], BF16, tap=True)
        kbcT_d = self.dscr("kbcT_d", [128, NCX], BF16, tap=True)
        vbc_d = self.dscr("vbc_d", [NCX, 128], BF16, tap=True)
        contrib = self.dscr("contrib", [960, 1024], BF16, tap=True)
        gathB = self.dscr("gathB", [4 * 512, 1024], BF16)
        gathH = self.dscr("gathH", [4 * 448, 1024], BF16)

        class G:
            @staticmethod
            def view(r, off, dims):
                if off < 524288:
                    return dview(gathB, r * 524288 + off, dims)
                return dview(gathH, r * 458752 + (off - 524288), dims)
        gath = G
        self.gathB, self.gathH = gathB, gathH
        o_d = [self.dscr("o_d%d" % i, [128, 3, NTOT], BF16, tap=True) for i in range(3)]
        mods_d = self.dscr("mods_d", [128, NL, 48, 2], F32, tap=True)

        with ExitStack() as es0:
            self.T = Trk(nc, es0)
            T = self.T
            self.bankc = 0
            self.accc = 0
            self.nsb = 4
            self.split = False
            self.stg_i = 0
            self.cast_i = 0
            self.cast_engs = ("pool",)
            self.psall = es0.enter_context(nc.psum_tensor("psall", [128, 4096], F32))
            self.ps = [self.psall[:, i * 512:(i + 1) * 512] for i in range(8)]
            self.ones = self.sb(es0, "ones", [128, 128], BF16)
            self.bd = self.sb(es0, "bd", [128, 128], BF16)
            self.epsc = self.sb(es0, "epsc", [128, 1], F32)
            self.modsb = self.sb(es0, "modsb", [128, NL, 48, 2], F32)
            self.dvec = self.sb(es0, "dvec", [128, NL, 2, 6, 8], F32)
            self.gn = self.sb(es0, "gn", [128, NL, 4, 8], F32)
            self.stg = self.sb(es0, "stg", [128, 3, 1024], F32)
            self.nstg = 3

            with ExitStack() as es:
                self.memset("pool", self.ones[:], 1.0, ["ones"])
                self.memset("pool", self.bd[:], 0.0, ["bd"])
                self.memset("pool", self.bd[0:64, 0:64], 1.0, ["bd"])
                self.memset("pool", self.bd[64:128, 64:128], 1.0, ["bd"])
                self.memset("pool", self.epsc[:], EPS, ["epsc"])
                cc_s = self.sb(es, "cc_s", [128, 8, 2], F32)
                sil = self.sb(es, "sil", [128, 8, 2], F32)
                self.dma(cc_s[:], ccT.ap(), (), ["cc_s"])
                self.act(sil[:], cc_s[:], AF.Silu, ["cc_s"], ["sil"])
                wa = self.sb(es, "wa", [128, 2, 8, 768], F32)
                bad = self.sb(es, "bad", [128, NL, 48], F32)
                for l in range(NL):
                    self.dma(bad[:, l, :], L[l]["badaT"].ap(), (), ["bad"])
                    self.dma(self.gn[:, l], L[l]["gains"].ap(), (), ["gn"])
                    wsrc = L[l]["wada"].ap().rearrange("(k p) n -> p k n", p=128)
                    b = self.newbank()
                    for g in range(8):
                        s = g % 2
                        self.dma(wa[:, s], wsrc[:, :, g * 768:(g + 1) * 768], (), [("wa", s)])
                        for mm_ in range(6):
                            m = g * 6 + mm_
                            for k in range(8):
                                self.mm(self.ps[b][:, 2 * m:2 * m + 2], wa[:, s, k, mm_ * 128:(mm_ + 1) * 128], sil[:, k, :],
                                        k == 0, k == 7, [("wa", s), "sil"], [("ps", b)])
                    self.tt("dve", self.modsb[:, l], self.ps[b][:, 0:96].rearrange("p (m t) -> p m t", t=2),
                            bcast(bad[:, l, :], 2, 2), ALU.add, [("ps", b), "bad"], ["modsb"])
                    for t in range(2):
                        mv = self.modsb[:, l, :, t]
                        dv = self.dvec[:, l, t]
                        self.stt(dv[:, 0, :], mv[:, 8:16], 1.0, self.gn[:, l, 0, :], ALU.add, ALU.mult, ["modsb", "gn"], ["mods"])
                        self.cp("dve", dv[:, 1, :], mv[:, 0:8], ["modsb"], ["mods"])
                        self.tt("dve", dv[:, 2, :], mv[:, 16:24], self.gn[:, l, 1, :], ALU.mult, ["modsb", "gn"], ["mods"])
                        self.stt(dv[:, 3, :], mv[:, 32:40], 1.0, self.gn[:, l, 2, :], ALU.add, ALU.mult, ["modsb", "gn"], ["mods"])
                        self.cp("dve", dv[:, 4, :], mv[:, 24:32], ["modsb"], ["mods"])
                        self.tt("dve", dv[:, 5, :], mv[:, 40:48], self.gn[:, l, 3, :], ALU.mult, ["modsb", "gn"], ["mods"])
                if "mods_d" in self.taps:
                    self.dma(mods_d.ap(), self.modsb[:], ["modsb"], ["mods_d"])
                T.flush()

            for l in range(NL):
                last = (l == NL - 1)
                if l == 0:
                    xin_lat = lambda c0, n: xT.ap()[:, :, c0:c0 + n]
                    xin_ctx = ctxT.ap()
                else:
                    xin_lat = (lambda xs_: (lambda c0, n: xs_.ap()[:, :, c0:c0 + n]))(xs[2 * l - 1])
                    xin_ctx = xs[2 * l - 1].ap()[:, :, NT:NTOT]
                if self.stop == "mods":
                    break
                self.phase_proj(L[l], l, last, xin_lat, xin_ctx, ropeC, ropeS, q_d, kaT_d, va_d, kcT_d, vc_d, kbcT_d, vbc_d, contrib)
                if self.stop == "proj%d" % l or (self.stop or "").startswith("proj0"):
                    break
                self.phase_gather(contrib, gath)
                self.phase_attn_a(L[l], l, last, q_d[0], kaT_d, va_d, gath, maskA, o_d[0])
                if self.stop == "attn_a%d" % l:
                    break
                self.phase_attn_b(L[l], l, last, q_d[1], kbcT_d, vbc_d, gath, o_d[1])
                if self.stop == "attn_b%d" % l:
                    break
                self.phase_attn_c(L[l], l, last, q_d[2], kcT_d, vc_d, gath, rm01, o_d[2])
                if self.stop == "attn_c%d" % l:
                    break
                self.phase_mix(L[l], l, last, xin_lat, xin_ctx, o_d, xs[2 * l])
                if self.stop == "mix%d" % l:
                    break
                out_lat = yT if last else xs[2 * l + 1]
                self.phase_mlp(L[l], l, last, xs[2 * l], out_lat, xs[2 * l + 1])
                if self.stop == "mlp%d" % l:
                    break
            T.final_wait()
        return nc

    def phase_proj(self, Ld, l, last, xin_lat, xin_ctx, ropeC, ropeS, q_d, kaT_d, va_d, kcT_d, vc_d, kbcT_d, vbc_d, contrib):
        T = self.T
        with ExitStack() as es:
            self.split = False
            self.alloc_norm(es, 512)
            hT = self.sb(es, "hT", [128, 8, NTOT], BF16)
            xch = self.sb(es, "xch", [128, 2, 8, 512], F32)
            rC = self.sb(es, "rC", [128, NT], F32)
            rS = self.sb(es, "rS", [128, NT], F32)
            bg = self.sb(es, "bg", [128, 4], F32)
            wt = self.sb(es, "wt", [128, 2, 8, 640], BF16)
            ost = self.sb(es, "ost", [128, 4, 640], BF16)
            t1 = self.sb(es, "t1", [128, 2, 512], F32)
            t2 = self.sb(es, "t2", [128, 2, 512], F32)
            sqh = self.sb(es, "sqh", [128, 2, 512], BF16)
            lnh = self.sb(es, "lnh", [128, 2, 512], F32)
            rsh = self.sb(es, "rsh", [128, 2, 512], F32)
            self.dma(rC[:], ropeC.ap(), (), ["rC"])
            self.dma(rS[:], ropeS.ap(), (), ["rS"])
            self.dma(bg[:], Ld["bgain"].ap(), (), ["bg"])
            chunks = [(i * 512, 512, False) for i in range(4)] + [(NT, NCX, True)]
            for ci, (c0, n, isc) in enumerate(chunks):
                s = ci % 2
                src = xin_ctx if isc else xin_lat(c0, n)
                self.dma(xch[:, s, :, 0:n], src, (), [("xch", s)])
                dv = self.dvec[:, l, 1 if isc else 0]
                self.norm_mod(xch[:, s], n, dv[:, 0, :], dv[:, 1, :], hT[:, :, c0:c0 + n], ("xch", s), "hT")
            if self.stop == "proj0a":
                T.flush()
                return
            win = Ld["win"].ap().rearrange("(k p) n -> p k n", p=128)
            units = []
            units.append(("KB", 0, [C_KB, C_KBS]))
            units.append(("V", 0, None))
            units.append(("KA", 0, [C_KA, C_KAS]))
            for mi in range(3):
                units.append(("KC", mi, [C_KC + mi * 128]))
            n_kv_units = len(units)
            for mi in range(3):
                units.append(("QB", mi, [C_QB + mi * 128, C_QBS + mi * 128]))
            for mi in range(3):
                units.append(("QA", mi, [C_QA + mi * 128, C_QAS + mi * 128]))
            for mi in range(3):
                units.append(("QC", mi, [C_QC + mi * 128]))
            ctr = dict(ost=0, t=0)

            ctoks = []

            def store(srcs_dsts, tok):
                for dst, src in srcs_dsts:
                    ctr["st"] = ctr.get("st", 0) + 1
                    wtk = ("dout", ctr["st"])
                    if dst.tensor.name == contrib.name:
                        ctoks.append(wtk)
                    self.dma(dst, src, [tok], [wtk])

            if self.stop and self.stop.startswith("proj0u"):
                sel = [int(x) for x in self.stop[6:].split("_")]
                units = [units[i] for i in sel]
            def load_unit(ui):
                kind, mi, cols = units[ui]
                ws = ui % 2
                wtok = ("+wt", ws)
                if kind == "V":
                    self.load_w(wt[:, ws, :, 0:640], win[:, :, C_V:C_V + 640], wtok, 8, 640)
                else:
                    for j, c in enumerate(cols):
                        self.load_w(wt[:, ws, :, j * 128:(j + 1) * 128], win[:, :, c:c + 128], wtok, 8, 128)
            load_unit(0)
            for ui, (kind, mi, cols) in enumerate(units):
                ws = ui % 2
                wtok = ("+wt", ws)
                if ui == n_kv_units and not (self.stop or "").startswith("proj0u"):
                    self.emit_gather(contrib, list(ctoks))
                if ui + 1 < len(units):
                    load_unit(ui + 1)
                if kind == "V":
                    for tile in range(NTOT // 128):
                        t0 = tile * 128
                        b0, b1 = self.newbank(), self.newbank()
                        for k in range(8):
                            self.mm(self.ps[b0][:, 0:512], hT[:, k, t0:t0 + 128], wt[:, ws, k, 0:512], k == 0, k == 7, ["hT", wtok], [("ps", b0)])
                        for k in range(8):
                            self.mm(self.ps[b1][:, 0:128], hT[:, k, t0:t0 + 128], wt[:, ws, k, 512:640], k == 0, k == 7, ["hT", wtok], [("ps", b1)])
                        o = ctr["ost"] % 4
                        ctr["ost"] += 1
                        otok = ("ost", o)
                        self.act(ost[:, o, 0:512], self.ps[b0][:, 0:512], AF.Copy, [("ps", b0)], [otok])
                        self.cp("dve", ost[:, o, 512:640], self.ps[b1][:, 0:128], [("ps", b1)], [otok])
                        cb = contrib
                        dl = [(va_d.ap()[t0:t0 + 128, :], ost[:, o, 0:128]),
                              (vc_d.ap()[t0:t0 + 128, :], ost[:, o, 256:640])]
                        if tile < 16:
                            dl.append((dview(cb, O_VB + t0 * 128, [[128, 128], [1, 128]]), ost[:, o, 128:256]))
                            if tile == 0:
                                dl.append((dview(cb, O_VAH, [[128, 128], [1, 128]]), ost[:, o, 0:128]))
                            if tile == 15:
                                dl.append((dview(cb, O_VAT, [[128, 128], [1, 128]]), ost[:, o, 0:128]))
                            if tile < 2:
                                dl.append((dview(cb, O_VCH + tile * 128 * 384, [[384, 128], [1, 384]]), ost[:, o, 256:640]))
                            if tile >= 14:
                                dl.append((dview(cb, O_VCT + (tile - 14) * 128 * 384, [[384, 128], [1, 384]]), ost[:, o, 256:640]))
                        else:
                            dl.append((vbc_d.ap()[t0 - NT:t0 - NT + 128, :], ost[:, o, 128:256]))
                        store(dl, otok)
                    continue
                for ci, (c0, n, isc) in enumerate(chunks):
                    if isc and last and kind in ("QA", "QB", "QC"):
                        continue
                    hs = [hT[:, k, c0:c0 + n] for k in range(8)]
                    bq = self.newbank()
                    for k in range(8):
                        self.mm(self.ps[bq][:, 0:n], wt[:, ws, k, 0:128], hs[k], k == 0, k == 7, ["hT", wtok], [("ps", bq)])
                    pq = self.ps[bq][:, 0:n]
                    o = ctr["ost"] % 4
                    ctr["ost"] += 1
                    otok = ("ost", o)
                    oo = ost[:, o, 0:n]
                    need_sw = (len(cols) == 2) and not isc
                    if need_sw:
                        bs = self.newbank()
                        for k in range(8):
                            self.mm(self.ps[bs][:, 0:n], wt[:, ws, k, 128:256], hs[k], k == 0, k == 7, ["hT", wtok], [("ps", bs)])
                        psw = self.ps[bs][:, 0:n]
                    tt_ = ctr["t"] % 2
                    ctr["t"] += 1
                    a1, a2 = t1[:, tt_, 0:n], t2[:, tt_, 0:n]
                    k1, k2 = ("t1", tt_), ("t2", tt_)
                    if kind in ("QA", "KA"):
                        if isc:
                            self.act(oo, pq, AF.Copy, [("ps", bq)], [otok])
                        else:
                            self.tt("dve", a1, pq, rC[:, c0:c0 + n], ALU.mult, [("ps", bq), "rC"], [k1])
                            self.tt("dve", a2, psw, rS[:, c0:c0 + n], ALU.mult, [("ps", bs), "rS"], [k2])
                            self.tt("pool", oo, a1, a2, ALU.add, [k1, k2], [otok])
                    elif kind in ("QB", "KB"):
                        gi = 0 if kind == "QB" else 2
                        self.act(sqh[:, tt_, 0:n], pq, AF.Square, [("ps", bq)], [("sqh", tt_)])
                        bss = self.newbank()
                        self.mm(self.ps[bss][:, 0:n], self.bd[:], sqh[:, tt_, 0:n], True, True, [("sqh", tt_), "bd"], [("ps", bss)])
                        self.act(lnh[:, tt_, 0:n], self.ps[bss][:, 0:n], AF.Ln, [("ps", bss)], [("lnh", tt_)], bias=self.epsc[:, 0:1], scale=1.0 / 64)
                        self.act(rsh[:, tt_, 0:n], lnh[:, tt_, 0:n], AF.Exp, [("lnh", tt_)], [("rsh", tt_)], scale=-0.5)
                        if isc:
                            self.stt(oo, pq, bg[:, gi:gi + 1], rsh[:, tt_, 0:n], ALU.mult, ALU.mult, [("ps", bq), "bg", ("rsh", tt_)], [otok])
                        else:
                            self.stt(a1, pq, bg[:, gi:gi + 1], rC[:, c0:c0 + n], ALU.mult, ALU.mult, [("ps", bq), "bg", "rC", ("sqh", tt_)], [k1])
                            self.stt(a2, psw, bg[:, gi + 1:gi + 2], rS[:, c0:c0 + n], ALU.mult, ALU.mult, [("ps", bs), "bg", "rS"], [k2])
                            self.tt("pool", a1, a1, a2, ALU.add, [k1, k2], [k1])
                            self.tt("pool", oo, a1, rsh[:, tt_, 0:n], ALU.mult, [k1, ("rsh", tt_)], [otok])
                    else:
                        self.act(oo, pq, AF.Copy, [("ps", bq)], [otok])
                    dl = []
                    cb = contrib
                    if kind == "QA":
                        dl.append((q_d[0].ap()[:, mi, c0:c0 + n], oo))
                    elif kind == "QB":
                        dl.append((q_d[1].ap()[:, mi, c0:c0 + n], oo))
                    elif kind == "QC":
                        dl.append((q_d[2].ap()[:, mi, c0:c0 + n], oo))
                    elif kind == "KA":
                        dl.append((kaT_d.ap()[:, c0:c0 + n], oo))
                        if ci == 0:
                            dl.append((dview(cb, O_KAH, [[128, 128], [1, 128]]), ost[:, o, 0:128]))
                        if ci == 3:
                            dl.append((dview(cb, O_KAT, [[128, 128], [1, 128]]), ost[:, o, 384:512]))
                    elif kind == "KB":
                        if isc:
                            dl.append((kbcT_d.ap(), oo))
                        else:
                            dl.append((dview(cb, O_KB + c0, [[2048, 128], [1, n]]), oo))
                    elif kind == "KC":
                        dl.append((kcT_d.ap()[:, mi, c0:c0 + n], oo))
                        if ci == 0:
                            dl.append((dview(cb, O_KCH + mi * 256, [[768, 128], [1, 256]]), ost[:, o, 0:256]))
                        if ci == 3:
                            dl.append((dview(cb, O_KCT + mi * 256, [[768, 128], [1, 256]]), ost[:, o, 256:512]))
                    store(dl, otok)
            T.flush()

    def phase_gather(self, contrib, gath):
        return

    def emit_gather(self, contrib, rtoks):
        T = self.T
        gB, gH = self.gathB, self.gathH
        T.add("pool", lambda e: e.collective_compute("AllGather", ALU.bypass, replica_groups=[[0, 1, 2, 3], [4, 5, 6, 7]],
                                                     ins=[contrib.ap()[0:512, :]], outs=[gB.ap()]), rtoks, ["gathB"], cc=True)
        T.add("pool", lambda e: e.collective_compute("AllGather", ALU.bypass, replica_groups=[[0, 1, 2, 3], [4, 5, 6, 7]],
                                                     ins=[contrib.ap()[512:960, :]], outs=[gH.ap()]), rtoks, ["gathH"], cc=True)

    def attn_fin(self, bank, n, base, out_ap, rc, rctok, extra=None, extra_tok=None, shape3=None):
        ob = 64 - base
        pso = self.ps[bank]
        den = pso[ob:ob + 64, 0:n]
        num = pso[base:base + 64, 0:n]
        rcp = rc[base:base + 64, 0:n]
        if shape3 is not None:
            a, b_ = shape3
            den = den.rearrange("p (a b) -> p a b", a=a)
            num = num.rearrange("p (a b) -> p a b", a=a)
            rcp = rcp.rearrange("p (a b) -> p a b", a=a)
        if extra is not None:
            self.tt("dve", rcp, den, extra, ALU.add, [("ps", bank), extra_tok], [rctok])
            self.recip(rcp, rcp, [rctok], [rctok])
        else:
            self.recip(rcp, den, [("ps", bank)], [rctok])
        self.tt("dve", out_ap, num, rcp, ALU.mult, [("ps", bank), rctok], ["oT"])

    def pipeline(self, items, look=3, group=1):
        groups = [items[i:i + group] for i in range(0, len(items), group)]
        banks = {}

        def qks(gi):
            for k, itm in enumerate(groups[gi]):
                banks[(gi, k)] = self.newbank()
                itm[0](banks[(gi, k)])
        for gi in range(min(look, len(groups))):
            qks(gi)
        for gi in range(len(groups)):
            if gi + look < len(groups):
                qks(gi + look)
            for k, itm in enumerate(groups[gi]):
                itm[1](banks[(gi, k)])
            for k, itm in enumerate(groups[gi]):
                itm[2]()
                if itm[3] is not None:
                    itm[3]()

    def load_vaug(self, dst, src, tokn):
        self.dma(dst, src, (), [tokn])

    def phase_attn_a(self, Ld, l, last, qa_d, kaT_d, va_d, gath, maskA, oa_d):
        T = self.T
        with ExitStack() as es:
            self.split = True
            self.nsb = 6
            self.bankc = 0
            QT = self.sb(es, "QT", [128, 3, NTOT], BF16)
            KT = self.sb(es, "KT", [128, NTOT], BF16)
            KH = self.sb(es, "KH", [128, 8, 128], BF16)
            VA = self.sb(es, "VA", [128, 18, 2, 128], BF16)
            VH = self.sb(es, "VH", [128, 8, 2, 128], BF16)
            MA = self.sb(es, "MA", [128, 10, 128], F32)
            ES = self.sb(es, "ES", [128, 6, 128], F32)
            sk = self.sb(es, "sk", [128, 6], F32)
            PT = self.sb(es, "PT", [128, 8, 512], BF16)
            tm = self.sb(es, "tm", [128, 4, 384], F32)
            rc = self.sb(es, "rc", [128, 2, 512], F32)
            oT = self.sb(es, "oT", [128, 3, NTOT], BF16)
            nq = NT if last else NTOT
            self.dma(QT[:, :, 0:nq], qa_d.ap()[:, :, 0:nq], (), ["+QT"])
            self.dma(KT[:], kaT_d.ap(), (), ["+KT"])
            self.memset("pool", VA[:], 1.0, ["+VA"])
            self.memset("pool", VH[:], 1.0, ["+VH"])
            vsrc = va_d.ap().rearrange("(t p) c -> p t c", p=128)
            self.dma(VA[:, :, 0, 0:64], vsrc[:, :, 0:64], (), ["+VA"])
            self.dma(VA[:, :, 1, 64:128], vsrc[:, :, 64:128], (), ["+VA"])
            for r in range(4):
                pass
                self.dma(KH[:, r, :], gath.view(r, O_KAT, [[128, 128], [1, 128]]), ["gath"], ["+KH"])
                self.dma(KH[:, 4 + r, :], gath.view(r, O_KAH, [[128, 128], [1, 128]]), ["gath"], ["+KH"])
                self.dma(VH[:, r, 0, 0:64], gath.view(r, O_VAT, [[128, 128], [1, 64]]), ["gath"], ["+VH"])
                self.dma(VH[:, r, 1, 64:128], gath.view(r, O_VAT + 64, [[128, 128], [1, 64]]), ["gath"], ["+VH"])
                self.dma(VH[:, 4 + r, 0, 0:64], gath.view(r, O_VAH, [[128, 128], [1, 64]]), ["gath"], ["+VH"])
                self.dma(VH[:, 4 + r, 1, 64:128], gath.view(r, O_VAH + 64, [[128, 128], [1, 64]]), ["gath"], ["+VH"])
            self.dma(MA[:], maskA.ap(), (), ["MA"])
            self.dma(sk[:], Ld["sinkT"].ap(), (), ["sk"])
            self.act(sk[:], sk[:], AF.Exp, ["sk"], ["sk"])
            self.cp("dve", ES[:], bcast(sk[:], 2, 128), ["sk"], ["ES"])
            it = dict(p=0, t=0, r=0)
            items = []

            def run(qcols, nqc, base, tiles, shape3, out_ap, es_ap):
                n = 3 * nqc
                rhs = QT[base:base + 64, :, qcols:qcols + nqc]
                bo = self.newacc()
                nt = len(tiles)
                for ti, (kap, vap, mk) in enumerate(tiles):
                    p = it["p"] % 8
                    it["p"] += 1
                    t = it["t"] % 4
                    if mk is not None:
                        it["t"] += 1

                    def qk(b, kap=kap):
                        self.mm(self.ps[b][:, 0:n].rearrange("p (a b) -> p a b", a=3), kap, rhs, True, True, ["+QT", "+KT", "+KH"], [("ps", b)])

                    def sm(b, mk=mk, p=p, t=t):
                        pt = PT[:, p, 0:n]
                        if mk is not None:
                            self.stt(tm[:, t, 0:n].rearrange("p (a b) -> p a b", a=3), self.ps[b][:, 0:n].rearrange("p (a b) -> p a b", a=3),
                                     0.125, bcast(mk, 1, 3), ALU.mult, ALU.add, [("ps", b), "MA"], [("tm", t)])
                            self.act(pt, tm[:, t, 0:n], AF.Exp, [("tm", t)], [("PT", p)])
                        else:
                            self.act(pt, self.ps[b][:, 0:n], AF.Exp, [("ps", b)], [("PT", p)], scale=0.125)

                    def pv(vap=vap, p=p, ti=ti):
                        self.mm(self.ps[bo][:, 0:n], vap, PT[:, p, 0:n], ti == 0, ti == nt - 1, [("PT", p), "+VA", "+VH"], [("ps", bo)])
                    fin = None
                    if ti == nt - 1:
                        r_ = it["r"] % 2
                        it["r"] += 1

                        def fin(r_=r_):
                            self.attn_fin(bo, n, base, out_ap, rc[:, r_], ("rc", r_), extra=es_ap, extra_tok="ES", shape3=shape3)
                    items.append((qk, sm, pv, fin))

            for half in range(2):
                base = half * 64
                ob = 64 - base
                esl = ES[ob:ob + 64, 3 * half:3 * half + 3, :]
                for j in range(16):
                    tiles = []
                    if j > 0:
                        tiles.append((KT[base:base + 64, (j - 1) * 128:j * 128], VA[:, j - 1, half, :], MA[:, 0, :]))
                    else:
                        for r in range(4):
                            tiles.append((KH[base:base + 64, r, :], VH[:, r, half, :], MA[:, 2 + r, :]))
                    tiles.append((KT[base:base + 64, j * 128:(j + 1) * 128], VA[:, j, half, :], None))
                    if j < 15:
                        tiles.append((KT[base:base + 64, (j + 1) * 128:(j + 2) * 128], VA[:, j + 1, half, :], MA[:, 1, :]))
                    else:
                        for r in range(4):
                            tiles.append((KH[base:base + 64, 4 + r, :], VH[:, 4 + r, half, :], MA[:, 6 + r, :]))
                    for c in range(2):
                        tiles.append((KT[base:base + 64, NT + c * 128:NT + (c + 1) * 128], VA[:, 16 + c, half, :], None))
                    run(j * 128, 128, base, tiles, (3, 128), oT[base:base + 64, :, j * 128:(j + 1) * 128], esl)
                if not last:
                    for cq in range(2):
                        tiles = [(KT[base:base + 64, NT + c * 128:NT + (c + 1) * 128], VA[:, 16 + c, half, :], None) for c in range(2)]
                        q0 = NT + cq * 128
                        run(q0, 128, base, tiles, (3, 128), oT[base:base + 64, :, q0:q0 + 128], esl)
            self.pipeline(items, 5)
            self.nsb = 4
            self.dma(oa_d.ap()[:, :, 0:nq], oT[:, :, 0:nq], ["oT"], ["oa_d"])
            T.flush()

    def phase_attn_b(self, Ld, l, last, qb_d, kbcT_d, vbc_d, gath, ob_d):
        T = self.T
        with ExitStack() as es:
            NK = NCX + 4 * NT
            self.split = True
            self.nsb = 6
            self.bankc = 0
            QT = self.sb(es, "QTb", [128, 3, NTOT], BF16)
            KT = self.sb(es, "KTb", [128, NK], BF16)
            VB = self.sb(es, "VBb", [128, 66, 2, 128], BF16)
            PT = self.sb(es, "PTb", [128, 6, 512], BF16)
            rc = self.sb(es, "rcb", [128, 2, 512], F32)
            oT = self.sb(es, "oTb", [128, 3, NTOT], BF16)
            nq = NT if last else NTOT
            self.dma(QT[:, :, 0:nq], qb_d.ap()[:, :, 0:nq], (), ["+QT"])
            self.dma(KT[:, 0:NCX], kbcT_d.ap(), (), ["+KT"])
            self.memset("pool", VB[:, 0:33], 1.0, ["+VB"])
            self.memset("pool", VB[:, 33:66], 1.0, ["+VB"])
            vcs = vbc_d.ap().rearrange("(t p) c -> p t c", p=128)
            self.dma(VB[:, 0:2, 0, 0:64], vcs[:, :, 0:64], (), ["+VB"])
            self.dma(VB[:, 0:2, 1, 64:128], vcs[:, :, 64:128], (), ["+VB"])
            for r in range(4):
                pass
                self.dma(KT[:, NCX + r * NT:NCX + (r + 1) * NT], gath.view(r, O_KB, [[2048, 128], [1, 2048]]), ["gath"], ["+KT"])
                self.dma(VB[:, 2 + r * 16:2 + (r + 1) * 16, 0, 0:64], gath.view(r, O_VB, [[128, 128], [128 * 128, 16], [1, 64]]), ["gath"], ["+VB"])
                self.dma(VB[:, 2 + r * 16:2 + (r + 1) * 16, 1, 64:128], gath.view(r, O_VB + 64, [[128, 128], [128 * 128, 16], [1, 64]]), ["gath"], ["+VB"])
            it = dict(p=0, r=0)
            items = []

            def run(mi, q0, n, ktiles):
                bo = [self.newacc(), self.newacc()]
                nt = len(ktiles)
                for ti, kt in enumerate(ktiles):
                    for half in range(2):
                        base = half * 64
                        p = it["p"] % 6
                        it["p"] += 1

                        def qk(b, base=base, kt=kt):
                            self.mm(self.ps[b][:, 0:n], KT[base:base + 64, kt * 128:(kt + 1) * 128], QT[base:base + 64, mi, q0:q0 + n],
                                    True, True, ["+QT", "+KT"], [("ps", b)])

                        def sm(b, p=p, half=half):
                            if half == 1:
                                return
                            assert b % 2 == 0 and p % 2 == 0
                            src = self.psall[:, b * 512:(b + 2) * 512].rearrange("p (a c) -> p a c", a=2)[:, :, 0:n]
                            self.act(PT[:, p:p + 2, 0:n], src, AF.Exp, [("ps", b), ("ps", b + 1)], [("PT", p), ("PT", p + 1)], scale=0.125)

                        def pv(p=p, half=half, kt=kt, ti=ti):
                            self.mm(self.ps[bo[half]][:, 0:n], VB[:, kt, half, :], PT[:, p, 0:n], ti == 0, ti == nt - 1,
                                    [("PT", p), "+VB"], [("ps", bo[half])])
                        fin = None
                        if ti == nt - 1:
                            r_ = it["r"] % 2
                            it["r"] += 1

                            def fin(r_=r_, half=half, base=base):
                                self.attn_fin(bo[half], n, base, oT[base:base + 64, mi, q0:q0 + n], rc[:, r_], ("rc", r_))
                        items.append((qk, sm, pv, fin))

            for mi in range(3):
                for qc in range(4):
                    run(mi, qc * 512, 512, list(range(66)))
                if not last:
                    run(mi, NT, NCX, [0, 1])
            self.pipeline(items, 2, 2)
            self.nsb = 4
            self.dma(ob_d.ap()[:, :, 0:nq], oT[:, :, 0:nq], ["oT"], ["ob_d"])
            T.flush()

    def phase_attn_c(self, Ld, l, last, qc_d, kcT_d, vc_d, gath, rm01, oc_d):
        T = self.T
        with ExitStack() as es:
            self.split = True
            self.nsb = 6
            self.bankc = 0
            QT = self.sb(es, "QTc", [128, 3, NTOT], BF16)
            KT = self.sb(es, "KTc", [128, 3, NTOT], BF16)
            KH = self.sb(es, "KHc", [128, 3, 8, 256], BF16)
            VC = self.sb(es, "VCc", [128, 18, 6, 128], BF16)
            VH = self.sb(es, "VHc", [128, 16, 6, 128], BF16)
            EB = self.sb(es, "EB", [128, 2, 6, 22, 64], BF16)
            RM = self.sb(es, "RM", [128, 44, 8], F32)
            RMb = self.sb(es, "RMb", [128, 44, 8], BF16)
            PT = self.sb(es, "PTc", [128, 8, 512], BF16)
            rc = self.sb(es, "rcc", [128, 2, 512], F32)
            oT = self.sb(es, "oTc", [128, 3, NTOT], BF16)
            nq = NT if last else NTOT
            self.dma(QT[:, :, 0:nq], qc_d.ap()[:, :, 0:nq], (), ["+QT"])
            self.dma(KT[:], kcT_d.ap(), (), ["+KT"])
            self.memset("pool", VC[:], 1.0, ["+VC"])
            self.memset("pool", VH[:], 1.0, ["+VH"])
            vsrc = vc_d.ap().rearrange("(t p) (h d) -> p t h d", p=128, d=64)
            for h in range(6):
                o = (h % 2) * 64
                self.dma(VC[:, :, h, o:o + 64], vsrc[:, :, h, :], (), ["+VC"])
            for r in range(4):
                pass
                self.dma(KH[:, :, r, :], gath.view(r, O_KCT, [[768, 128], [256, 3], [1, 256]]), ["gath"], ["+KH"])
                self.dma(KH[:, :, 4 + r, :], gath.view(r, O_KCH, [[768, 128], [256, 3], [1, 256]]), ["gath"], ["+KH"])
                for h in range(6):
                    o = (h % 2) * 64
                    self.dma(VH[:, 2 * r:2 * r + 2, h, o:o + 64], gath.view(r, O_VCT + h * 64, [[384, 128], [128 * 384, 2], [1, 64]]), ["gath"], ["+VH"])
                    self.dma(VH[:, 8 + 2 * r:8 + 2 * r + 2, h, o:o + 64], gath.view(r, O_VCH + h * 64, [[384, 128], [128 * 384, 2], [1, 64]]), ["gath"], ["+VH"])
            TBd = Ld["TB"].ap()
            for tbl in range(2):
                ebf = EB[:, tbl].rearrange("p h u q -> p (h u q)")
                c = 0
                while c < 6 * 22 * 64:
                    cc_ = min(1024, 6 * 22 * 64 - c)
                    sl_ = self.stg_i % self.nstg
                    self.stg_i += 1
                    st = self.stg[:, sl_, 0:cc_]
                    self.dma(st, TBd[:, tbl, c:c + cc_], (), [("stg", self.nstg, sl_)])
                    self.T.add("act", (lambda e_, o_=ebf[:, c:c + cc_], i_=st: e_.activation(out=o_, in_=i_, func=AF.Exp)),
                               [("stg", self.nstg, sl_)], ["+EB"], accw=True)
                    c += cc_
            self.dma(RM[:], rm01.ap(), (), ["RM"])
            self.cp("dve", RMb[:], RM[:], ["RM"], ["RMb"])
            it = dict(p=0, t=0, r=0)
            items = []

            def crun(q0, n, tiles, h, mi, base):
                rhs = QT[base:base + 64, mi, q0:q0 + n]
                bo = self.newacc()
                nt = len(tiles)
                for ti, (kap, vap, j, ri) in enumerate(tiles):
                    p = it["p"] % 8
                    it["p"] += 1
                    t = it["t"] % 3
                    if j is not None:
                        it["t"] += 1

                    def qk(b, kap=kap):
                        self.mm(self.ps[b][:, 0:n], kap, rhs, True, True, ["+QT", "+KT", "+KH"], [("ps", b)])

                    def sm(b, j=j, ri=ri, p=p, t=t):
                        pt = PT[:, p, 0:n]
                        self.act(pt, self.ps[b][:, 0:n], AF.Exp, [("ps", b)], [("PT", p)], scale=0.125)
                        if j is not None:
                            u0 = 14 - 2 * j
                            g_ = q0 // 512
                            tbl = 1 if g_ in (1, 2) else 0
                            pt3 = pt.rearrange("p (a b) -> p a b", a=8)
                            self.tt("dve", pt3, pt3, EB[:, tbl, h, u0:u0 + 8, :], ALU.mult, [("PT", p), "+EB"], [("PT", p)])
                            if tbl == 0:
                                self.tt("pool", pt3, pt3, bcast(RMb[:, ri, :], 2, 64), ALU.mult, [("PT", p), "RMb"], [("PT", p)])

                    def pv(vap=vap, p=p, ti=ti):
                        self.mm(self.ps[bo][:, 0:n], vap, PT[:, p, 0:n], ti == 0, ti == nt - 1, [("PT", p), "+VC", "+VH"], [("ps", bo)])
                    fin = None
                    if ti == nt - 1:
                        r_ = it["r"] % 2
                        it["r"] += 1

                        def fin(r_=r_):
                            self.attn_fin(bo, n, base, oT[base:base + 64, mi, q0:q0 + n], rc[:, r_], ("rc", r_))
                    items.append((qk, sm, pv, fin))

            for h in range(6):
                mi, half = h // 2, h % 2
                base = half * 64
                rmi = 0
                for g in range(4):
                    tiles = []
                    for j in range(8):
                        lt = g * 512 - 256 + 128 * j
                        if g == 0 and j < 2:
                            for r in range(4):
                                tiles.append((KH[base:base + 64, mi, r, j * 128:(j + 1) * 128], VH[:, 2 * r + j, h, :], j, rmi))
                                rmi += 1
                        elif g == 3 and j >= 6:
                            for r in range(4):
                                tiles.append((KH[base:base + 64, mi, 4 + r, (j - 6) * 128:(j - 5) * 128], VH[:, 8 + 2 * r + (j - 6), h, :], j, rmi))
                                rmi += 1
                        else:
                            tiles.append((KT[base:base + 64, mi, lt:lt + 128], VC[:, lt // 128, h, :], j, rmi))
                            rmi += 1
                    for c in range(2):
                        tiles.append((KT[base:base + 64, mi, NT + c * 128:NT + (c + 1) * 128], VC[:, 16 + c, h, :], None, None))
                    crun(g * 512, 512, tiles, h, mi, base)
                if not last:
                    tiles = [(KT[base:base + 64, mi, NT + c * 128:NT + (c + 1) * 128], VC[:, 16 + c, h, :], None, None) for c in range(2)]
                    crun(NT, NCX, tiles, h, mi, base)
            self.pipeline(items, 5)
            self.nsb = 4
            self.dma(oc_d.ap()[:, :, 0:nq], oT[:, :, 0:nq], ["oT"], ["oc_d"])
            T.flush()

    def post_norm_res(self, yTb, n, PG, xsrc, xdst, tok_y, tok_x, tok_o, tmpb):
        rs = self.rstd_of(None, yTb[:, :, 0:n], 8, n, 1024.0, tok_y, "pn")
        for m in range(8):
            self.stt(tmpb[:, m, 0:n], yTb[:, m, 0:n], PG[:, m:m + 1], rs, ALU.mult, ALU.mult, [tok_y, "rstd", "mods"], ["xn"])
            self.tt("pool", xdst[:, m, 0:n], xsrc[:, m, 0:n], tmpb[:, m, 0:n], ALU.add, [tok_x, "xn"], [tok_o])

    def phase_mix(self, Ld, l, last, xin_lat, xin_ctx, o_d, xs1):
        T = self.T
        with ExitStack() as es:
            self.split = False
            self.alloc_norm(es, 256)
            N = 256
            wg = self.sb(es, "wg", [128, 8, 3072], BF16)
            wbr = self.sb(es, "wbr", [128, 3, 3, 1024], BF16)
            wo = self.sb(es, "wo", [128, 8, 1024], BF16)
            xch = self.sb(es, "xchm", [128, 2, 8, N], F32)
            hT = self.sb(es, "hTm", [128, 2, 8, N], BF16)
            oc = self.sb(es, "ocm", [128, 2, 3, 3, N], BF16)
            sg = self.sb(es, "sg", [128, 2, 3, N], F32)
            ta = self.sb(es, "ta", [128, 2, 3, N], F32)
            mg = self.sb(es, "mg", [128, 8, N], BF16)
            yTb = self.sb(es, "yTb", [128, 8, N], F32)
            stg2 = self.sb(es, "stg2", [128, 6, 1024], F32)
            chunks = [(i * N, N, False) for i in range(NT // N)]
            if not last:
                chunks.append((NT, NCX, True))

            def prep(ci):
                c0, n, isc = chunks[ci]
                s_ = ci % 2
                src = xin_ctx if isc else xin_lat(c0, n)
                self.dma(xch[:, s_, :, 0:n], src, (), [("xch", s_)])
                for i in range(3):
                    self.dma(oc[:, s_, i, :, 0:n], o_d[i].ap()[:, :, c0:c0 + n], (), [("+oc", s_)])
                dv_ = self.dvec[:, l, 1 if isc else 0]
                self.norm_mod(xch[:, s_], n, dv_[:, 0, :], dv_[:, 1, :], hT[:, s_], ("xch", s_), ("hTm", s_))
            prep(0)
            win = Ld["win"].ap().rearrange("(k p) n -> p k n", p=128)
            self.cast_engs = ("pool", "dve", "act")
            old_stg, old_n = self.stg, self.nstg
            self.stg, self.nstg = stg2, 6
            for i in range(3):
                self.load_w(wg[:, :, i * 1024:(i + 1) * 1024], win[:, :, C_G + i * 1024:C_G + (i + 1) * 1024], ("+wg", i), 8, 1024)
                self.load_w(wbr[:, i], Ld["wbr"].ap()[i].rearrange("(k p) n -> p k n", p=128), ("+wbr", i), 3, 1024)
            self.load_w(wo[:], Ld["wout"].ap().rearrange("(k p) n -> p k n", p=128), "+wo", 8, 1024)
            self.stg, self.nstg = old_stg, old_n
            self.cast_engs = ("pool",)
            for ci, (c0, n, isc) in enumerate(chunks):
                s = ci % 2
                dv = self.dvec[:, l, 1 if isc else 0]
                for m in range(8):
                    q = m % 2
                    gb_, yb_ = [], []
                    for i in range(3):
                        b = self.newbank()
                        gb_.append(b)
                        for k in range(8):
                            self.mm(self.ps[b][:, 0:n], wg[:, k, i * 1024 + m * 128:i * 1024 + (m + 1) * 128], hT[:, s, k, 0:n], k == 0, k == 7,
                                    [("+wg", i), ("hTm", s)], [("ps", b)])
                        b = self.newbank()
                        yb_.append(b)
                        for k in range(3):
                            self.mm(self.ps[b][:, 0:n], wbr[:, i, k, m * 128:(m + 1) * 128], oc[:, s, i, k, 0:n], k == 0, k == 2,
                                    [("+wbr", i), ("+oc", s)], [("ps", b)])
                    for i in range(3):
                        self.act(sg[:, q, i, 0:n], self.ps[gb_[i]][:, 0:n], AF.Sigmoid, [("ps", gb_[i])], [("sg", q, i)])
                        self.tt("dve", ta[:, q, i, 0:n], sg[:, q, i, 0:n], self.ps[yb_[i]][:, 0:n], ALU.mult, [("sg", q, i), ("ps", yb_[i])], [("ta", q, i)])
                    self.tt("pool", ta[:, q, 0, 0:n], ta[:, q, 0, 0:n], ta[:, q, 1, 0:n], ALU.add, [("ta", q, 0), ("ta", q, 1)], [("ta", q, 0)])
                    self.tt("pool", mg[:, m, 0:n], ta[:, q, 0, 0:n], ta[:, q, 2, 0:n], ALU.add, [("ta", q, 0), ("ta", q, 2)], [("mg", m)])
                if ci + 1 < len(chunks):
                    prep(ci + 1)
                for m in range(8):
                    b = self.newbank()
                    for k in range(8):
                        self.mm(self.ps[b][:, 0:n], wo[:, k, m * 128:(m + 1) * 128], mg[:, k, 0:n], k == 0, k == 7, ["+wo", ("mg", k)], [("ps", b)])
                    self.act(yTb[:, m, 0:n], self.ps[b][:, 0:n], AF.Copy, [("ps", b)], ["yTb"])
                self.post_norm_res(yTb, n, dv[:, 2, :], xch[:, s], xch[:, s], "yTb", ("xch", s), ("xch", s), self.xn)
                self.dma(xs1.ap()[:, :, c0:c0 + n], xch[:, s, :, 0:n], [("xch", s)], [("xs1", ci)])
            T.flush()

    def phase_mlp(self, Ld, l, last, xs1, out_lat, xs2):
        T = self.T
        with ExitStack() as es:
            self.split = False
            self.alloc_norm(es, 256)
            w1 = self.sb(es, "w1", [128, 8, 4096], BF16)
            w2 = self.sb(es, "w2", [128, 32, 1024], BF16)
            xch = self.sb(es, "xchp", [128, 2, 8, 256], F32)
            h2 = self.sb(es, "h2", [128, 2, 8, 256], BF16)
            rl = self.sb(es, "rl", [128, 2, 256], F32)
            aT = self.sb(es, "aT", [128, 32, 256], BF16)
            y2 = self.sb(es, "y2", [128, 8, 256], F32)
            n = 256
            chunks = [(i * 256, False) for i in range(8)]
            if not last:
                chunks.append((NT, True))

            def prep(ci):
                c0, isc = chunks[ci]
                s_ = ci % 2
                self.dma(xch[:, s_], xs1.ap()[:, :, c0:c0 + n], ["xs1"], [("xch", s_)])
                dv_ = self.dvec[:, l, 1 if isc else 0]
                self.norm_mod(xch[:, s_], n, dv_[:, 3, :], dv_[:, 4, :], h2[:, s_], ("xch", s_), ("h2", s_))
            prep(0)
            self.cast_engs = ("pool", "dve", "act")
            w1src = Ld["w1"].ap().rearrange("(k p) n -> p k n", p=128)
            for cc_ in range(4):
                self.load_w(w1[:, :, cc_ * 1024:(cc_ + 1) * 1024], w1src[:, :, cc_ * 1024:(cc_ + 1) * 1024], ("+w1", cc_), 8, 1024)
            self.load_w(w2[:], Ld["w2"].ap().rearrange("(k p) n -> p k n", p=128), "+w2", 32, 1024)
            self.cast_engs = ("pool",)
            for ci, (c0, isc) in enumerate(chunks):
                s = ci % 2
                dv = self.dvec[:, l, 1 if isc else 0]
                for j in range(32):
                    b = self.newbank()
                    for k in range(8):
                        self.mm(self.ps[b][:, 0:n], w1[:, k, j * 128:(j + 1) * 128], h2[:, s, k, :], k == 0, k == 7, [("+w1", j // 8), ("h2", s)], [("ps", b)])
                    r_ = j % 2
                    self.act(rl[:, r_, :], self.ps[b][:, 0:n], AF.Relu, [("ps", b)], [("rl", r_)])
                    self.tt("pool", aT[:, j, :], rl[:, r_, :], rl[:, r_, :], ALU.mult, [("rl", r_)], [("aT", j)])
                if ci + 1 < len(chunks):
                    prep(ci + 1)
                for m in range(8):
                    b = self.newbank()
                    for j in range(32):
                        self.mm(self.ps[b][:, 0:n], w2[:, j, m * 128:(m + 1) * 128], aT[:, j, :], j == 0, j == 31, ["+w2", ("aT", j)], [("ps", b)])
                    self.act(y2[:, m, :], self.ps[b][:, 0:n], AF.Copy, [("ps", b)], ["y2"])
                self.post_norm_res(y2, n, dv[:, 5, :], xch[:, s], xch[:, s], "y2", ("xch", s), ("xch", s), self.xn)
                if isc:
                    dst = xs2.ap()[:, :, c0:c0 + n]
                else:
                    dst = out_lat.ap()[:, :, c0:c0 + n]
                self.dma(dst, xch[:, s], [("xch", s)], [("out", ci)])
            T.flush()


def _fm(v):
    v = np.asarray(v, np.float32)
    return np.ascontiguousarray(v.reshape(-1, 128).T)


def _perm_cols():
    def hc(base, heads, swap):
        out = []
        for h in heads:
            d = np.arange(64)
            if swap:
                d = d ^ 1
            out += list(base + h * 64 + d)
        return out
    p = []
    p += hc(0, HP, False) + hc(0, HP, True)
    p += hc(384, [0, 1], False) + hc(384, [0, 1], True)
    p += hc(640, HP, False) + hc(640, HP, True)
    p += hc(1024, [0, 1], False) + hc(1024, [0, 1], True)
    p += list(range(1280, 1664)) + list(range(1664, 2048))
    p += list(range(512, 640)) + list(range(1152, 1280)) + list(range(2048, 2432))
    p += list(range(2432, 5504))
    assert len(p) == NWIN
    return np.array(p)


def _rope_tables(tok0):
    pos = np.arange(tok0, tok0 + NT)
    row = (pos // 64).astype(np.float32)
    col = (pos % 64).astype(np.float32)
    freqs = (np.float32(10000.0) ** (-np.arange(16, dtype=np.float32) / np.float32(16))).astype(np.float32)
    ang = np.concatenate([row[:, None] * freqs, col[:, None] * freqs], axis=-1).astype(np.float32)
    cos, sin = np.cos(ang).astype(np.float32), np.sin(ang).astype(np.float32)
    d = np.arange(128) % 64
    C = cos[:, d // 2].T
    S = sin[:, d // 2].T * np.where(d % 2 == 0, -1.0, 1.0)[:, None]
    return np.ascontiguousarray(C, np.float32), np.ascontiguousarray(S, np.float32)


def _mask_a(rank):
    ki = np.arange(128)[:, None]
    qi = np.arange(128)[None, :]
    prev = np.where(qi <= ki, 0.0, NEG).astype(np.float32)
    nxt = np.where(ki <= qi, 0.0, NEG).astype(np.float32)
    allneg = np.full((128, 128), NEG, np.float32)
    m = np.zeros((128, 10, 128), np.float32)
    m[:, 0], m[:, 1] = prev, nxt
    for r in range(4):
        m[:, 2 + r] = prev if r == rank - 1 else allneg
        m[:, 6 + r] = nxt if r == rank + 1 else allneg
    return m


def _rm01(rank):
    out = np.zeros((128, 44, 8), np.float32)
    idx = 0
    kl = (np.arange(128) // 64)[:, None]
    ql = np.arange(8)[None, :]
    for g in range(4):
        R0 = rank * 32 + g * 8
        for j in range(8):
            kr = R0 - 4 + 2 * j + kl
            qr = R0 + ql
            rs = np.clip(qr - 4, 0, 120)
            valid = (kr >= rs) & (kr < rs + 8) & (kr >= 0) & (kr < 128)
            if g == 0 and j < 2:
                for r in range(4):
                    out[:, idx] = valid if r == rank - 1 else 0.0
                    idx += 1
            elif g == 3 and j >= 6:
                for r in range(4):
                    out[:, idx] = valid if r == rank + 1 else 0.0
                    idx += 1
            else:
                out[:, idx] = valid
                idx += 1
    assert idx == 44
    return out


def _tb_table(rpb, interior=False):
    rpb = np.asarray(rpb, np.float32)
    kc = np.arange(64)[:, None]
    qc = np.arange(64)[None, :]
    ws = np.clip(qc - 8, 0, 48)
    colv = (kc >= ws) & (kc < ws + 16)
    dc = np.clip(kc - qc + 15, 0, 30)
    tb = np.zeros((2, 64, 6, 22, 64), np.float32)
    for kl in range(2):
        for u in range(22):
            dr = 17 + kl - u
            for h in range(6):
                if 0 <= dr <= 14:
                    v = rpb[h, dr][dc]
                else:
                    v = np.zeros((64, 64), np.float32)
                if interior and not (3 <= dr <= 10):
                    tb[kl, :, h, u, :] = NEG
                else:
                    tb[kl, :, h, u, :] = np.where(colv, v, NEG)
    return np.ascontiguousarray(tb.reshape(128, 6, 22, 64))


_CACHE = {}


def _get_nc(n_layers=2, taps=()):
    key = (n_layers, tuple(taps))
    if key not in _CACHE:
        _CACHE[key] = K(n_layers, taps).build()
    return _CACHE[key]


def make_in_maps(inputs, n_layers=2):
    f = lambda a: np.asarray(a, np.float32)
    x, c, ctx, c_ctx = f(inputs["x"]), f(inputs["c"]), f(inputs["ctx"]), f(inputs["c_ctx"])
    perm = _perm_cols()
    rows_ab = np.concatenate([np.arange(h * 64, (h + 1) * 64) for h in HP])
    shared = {}
    for l in range(n_layers):
        shared["wada%d" % l] = np.ascontiguousarray(f(inputs["w_ada"])[l])
        shared["badaT%d" % l] = _fm(f(inputs["b_ada"])[l])
        shared["gains%d" % l] = np.ascontiguousarray(np.stack(
            [_fm(f(inputs[k])[l]) for k in ("norm_mix_pre", "norm_mix_post", "norm_mlp_pre", "norm_mlp_post")], axis=1))
        shared["win%d" % l] = np.ascontiguousarray(f(inputs["w_in"])[l][:, perm])
        gq, gk = f(inputs["qnorm_b"])[l], f(inputs["knorm_b"])[l]
        d = np.arange(128) % 64
        shared["bgain%d" % l] = np.ascontiguousarray(np.stack([gq[d], gq[d ^ 1], gk[d], gk[d ^ 1]], axis=1))
        shared["sinkT%d" % l] = np.ascontiguousarray(np.broadcast_to(f(inputs["sink_a"])[l][None, :], (128, 6)))
        shared["TB%d" % l] = np.ascontiguousarray(np.stack(
            [_tb_table(f(inputs["rpb_c"])[l]), _tb_table(f(inputs["rpb_c"])[l], True)], axis=1))
        shared["wbr%d" % l] = np.ascontiguousarray(np.stack(
            [f(inputs["w_br_a"])[l][rows_ab], f(inputs["w_br_b"])[l][rows_ab], f(inputs["w_br_c"])[l]], axis=0))
        shared["wout%d" % l] = np.ascontiguousarray(f(inputs["w_out"])[l])
        shared["w1_%d" % l] = np.ascontiguousarray(f(inputs["w_mlp_in"])[l])
        shared["w2_%d" % l] = np.ascontiguousarray(f(inputs["w_mlp_out"])[l])
    in_maps = []
    for core in range(8):
        b, rank = core // 4, core % 4
        tok0 = rank * NT
        m = dict(shared)
        xs = x[b, tok0:tok0 + NT, :]
        m["xT"] = np.ascontiguousarray(xs.T.reshape(8, 128, NT).transpose(1, 0, 2))
        m["ctxT"] = np.ascontiguousarray(ctx[b].T.reshape(8, 128, NCX).transpose(1, 0, 2))
        m["ccT"] = np.ascontiguousarray(np.stack([_fm(c[b]), _fm(c_ctx)], axis=2))
        C, S = _rope_tables(tok0)
        m["ropeC"], m["ropeS"] = C, S
        m["maskA"] = _mask_a(rank)
        m["rm01"] = _rm01(rank)
        in_maps.append(m)
    return in_maps


def kernel(**inputs):
    nc = _get_nc(2)
    in_maps = make_in_maps(inputs, 2)
    res = run_bass_kernel_spmd(nc, in_maps, core_ids=list(range(8)))
    out = np.zeros((2, 4 * NT, 1024), np.float32)
    for core in range(8):
        b, rank = core // 4, core % 4
        yT = np.asarray(res.results[core]["yT"])
        out[b, rank * NT:(rank + 1) * NT, :] = yT.transpose(1, 0, 2).reshape(1024, NT).T
    return out
```

```python
import numpy as np
from contextlib import ExitStack
import concourse.bass as bass
import concourse.mybir as mybir
from concourse.bass_utils import run_bass_kernel_spmd

F32 = mybir.dt.float32
BF16 = mybir.dt.bfloat16
AF = mybir.ActivationFunctionType
ALU = mybir.AluOpType

NT = 2048
NCX = 256
NTOT = NT + NCX
NEG = -30000.0
EPS = 1e-6
HP = [0, 3, 1, 4, 2, 5]
C_QA, C_QAS, C_KA, C_KAS = 0, 384, 768, 896
C_QB, C_QBS, C_KB, C_KBS = 1024, 1408, 1792, 1920
C_QC, C_KC, C_V, C_G = 2048, 2432, 2816, 3456
NWIN = 6528
O_KB, O_VB, O_KAH, O_KAT, O_VAH, O_VAT = 0, 262144, 524288, 540672, 557056, 573440
O_KCH, O_KCT, O_VCH, O_VCT, CONTRIB = 589824, 688128, 786432, 884736, 983040


class Trk:
    ENGS = ("pe", "act", "dve", "pool", "sp")
    NDSEM = 12

    def __init__(self, nc, es):
        self.nc = nc
        self.esem = {e: es.enter_context(nc.semaphore("s_" + e)) for e in self.ENGS}
        self.ecnt = {e: 0 for e in self.ENGS}
        self.dsem = {q: [es.enter_context(nc.semaphore("d_%s%d" % (q, i))) for i in range(self.NDSEM)]
                     for q in ("sp", "pool")}
        self.dval = {q: [0] * self.NDSEM for q in ("sp", "pool")}
        self.dcnt = {"sp": 0, "pool": 0}
        self.ccsem = es.enter_context(nc.semaphore("s_cc"))
        self.ccval = 0
        self.ops = []
        self.bar = None
        self.waited = {e: {} for e in self.ENGS}

    def add(self, eng, fn, r=(), w=(), dma=False, cc=False, accw=False):
        self.ops.append(dict(eng=eng, fn=fn, r=tuple(r), w=tuple(w), dma=dma, cc=cc, accw=accw))

    def flush(self):
        ops = self.ops
        self.ops = []
        if not ops:
            return
        last_w, readers = {}, {}

        def is_acc(tok):
            t0_ = tok[0] if isinstance(tok, tuple) else tok
            return isinstance(t0_, str) and t0_.startswith("+")
        for i, op in enumerate(ops):
            deps = set()
            for r in op["r"]:
                if r in last_w:
                    deps.update(last_w[r])
            for w in op["w"]:
                if w in last_w:
                    if is_acc(w) and (op["dma"] or op["accw"]):
                        deps.add(last_w[w][0])
                    else:
                        deps.update(last_w[w])
                for rd in readers.get(w, {}).values():
                    if isinstance(rd, list):
                        deps.update(rd)
                    else:
                        deps.add(rd)
            deps.discard(i)
            if op["eng"] == "pe":
                deps = {d for d in deps if ops[d]["dma"] or ops[d]["eng"] != "pe"}
            op["deps"] = deps
            for r in op["r"]:
                rr = readers.setdefault(r, {})
                if op["dma"]:
                    rr.setdefault("dma", []).append(i)
                else:
                    rr[op["eng"]] = i
            for w in op["w"]:
                if is_acc(w) and (op["dma"] or op["accw"]) and w in last_w:
                    last_w[w] = last_w[w] + [i]
                else:
                    last_w[w] = [i]
                readers[w] = {}
        for op in ops:
            op["sig"] = False
        for op in ops:
            for d in op["deps"]:
                ops[d]["sig"] = True
        last_of = {}
        for i, op in enumerate(ops):
            if not op["dma"]:
                last_of[op["eng"]] = i
        for i in last_of.values():
            ops[i]["sig"] = True
        for op in ops:
            if op["cc"]:
                self.ccval += 1
                op["done"] = (self.ccsem, self.ccval)
                op["pre"] = None
            elif op["dma"]:
                q = op["eng"]
                k = self.dcnt[q] % self.NDSEM
                self.dcnt[q] += 1
                prev = self.dval[q][k]
                self.dval[q][k] += 16
                op["done"] = (self.dsem[q][k], self.dval[q][k])
                op["pre"] = (self.dsem[q][k], prev) if prev > 0 else None
            elif op["sig"]:
                self.ecnt[op["eng"]] += 1
                op["done"] = (self.esem[op["eng"]], self.ecnt[op["eng"]])
                op["pre"] = None
            else:
                op["done"] = None
                op["pre"] = None
        per = {e: [] for e in self.ENGS}
        for op in ops:
            per[op["eng"]].append(op)
        bar = self.bar
        waited = self.waited

        def emit(ename, e):
            wd = waited[ename]

            def wait(sem, val):
                key = id(sem)
                if wd.get(key, 0) < val:
                    e.wait_ge(sem, val)
                    wd[key] = val
            if bar and per[ename]:
                for sem, val in bar:
                    wait(sem, val)
            for op in per[ename]:
                for d in op["deps"]:
                    sem, val = ops[d]["done"]
                    wait(sem, val)
                if op["pre"] is not None:
                    wait(*op["pre"])
                if op["fn"] is None:
                    continue
                ins = op["fn"](e)
                if op["done"] is not None:
                    sem, val = op["done"]
                    if op["cc"]:
                        ins.then_inc(sem, 1)
                    elif op["dma"]:
                        ins.then_inc(sem, 16)
                    else:
                        ins.then_inc(sem, 1)

        with self.nc.Block() as block:
            @block.sync
            def _(e):
                emit("sp", e)

            @block.scalar
            def _(e):
                emit("act", e)

            @block.vector
            def _(e):
                emit("dve", e)

            @block.gpsimd
            def _(e):
                emit("pool", e)

            @block.tensor
            def _(e):
                emit("pe", e)
        nb = []
        for e in self.ENGS:
            if self.ecnt[e] > 0:
                nb.append((self.esem[e], self.ecnt[e]))
        for q in ("sp", "pool"):
            for k in range(self.NDSEM):
                if self.dval[q][k] > 0:
                    nb.append((self.dsem[q][k], self.dval[q][k]))
        if self.ccval > 0:
            nb.append((self.ccsem, self.ccval))
        self.bar = nb

    def final_wait(self):
        bar = self.bar
        with self.nc.Block() as block:
            @block.sync
            def _(e):
                for sem, val in bar:
                    e.wait_ge(sem, val)


def bcast(ap, pos, n):
    l = [list(x) for x in ap.ap]
    l.insert(pos, [0, n])
    return bass.AP(ap.tensor, ap.offset, l)


def dview(t, off, dims):
    return bass.AP(t, off, [list(d) for d in dims])


class K:
    def __init__(self, n_layers=2, taps=(), stop=None):
        self.nl = n_layers
        self.taps = set(taps)
        self.stop = stop
        self.nc = bass.Bass("TRN2", target_bir_lowering=False)
        self.uid = 0

    def din(self, name, shape, dt=F32):
        return self.nc.dram_tensor(name, list(shape), dt, kind="ExternalInput")

    def dscr(self, name, shape, dt=BF16, tap=False):
        if tap and name in self.taps:
            return self.nc.dram_tensor(name, list(shape), dt, kind="ExternalOutput")
        return self.nc.dram_tensor(name, list(shape), dt)

    def sb(self, es, name, shape, dt):
        self.uid += 1
        return es.enter_context(self.nc.sbuf_tensor("%s_%d" % (name, self.uid), list(shape), dt))

    def mm(self, out, lhsT, rhs, start, stop, r, w):
        self.T.add("pe", lambda e: e.matmul(out, lhsT=lhsT, rhs=rhs, start=start, stop=stop), r, w)

    def act(self, out, in_, func, r, w, bias=None, scale=None):
        kw = {}
        if bias is not None:
            kw["bias"] = bias
        if scale is not None:
            kw["scale"] = scale
        self.T.add("act", lambda e: e.activation(out=out, in_=in_, func=func, **kw), r, w)

    def tt(self, eng, out, in0, in1, op, r, w):
        self.T.add(eng, lambda e: e.tensor_tensor(out=out, in0=in0, in1=in1, op=op), r, w)

    def ts(self, eng, out, in0, s1, s2, op0, op1, r, w):
        if op1 is None:
            self.T.add(eng, lambda e: e.tensor_scalar(out=out, in0=in0, scalar1=s1, scalar2=None, op0=op0), r, w)
        else:
            self.T.add(eng, lambda e: e.tensor_scalar(out=out, in0=in0, scalar1=s1, scalar2=s2, op0=op0, op1=op1), r, w)

    def stt(self, out, in0, scalar, in1, op0, op1, r, w):
        self.T.add("dve", lambda e: e.scalar_tensor_tensor(out=out, in0=in0, scalar=scalar, in1=in1, op0=op0, op1=op1), r, w)

    def cp(self, eng, out, in_, r, w):
        self.T.add(eng, lambda e: e.tensor_copy(out=out, in_=in_), r, w)

    def memset(self, eng, ap, val, w):
        self.T.add(eng, lambda e: e.memset(ap, val), (), w)

    def recip(self, out, in_, r, w):
        self.T.add("dve", lambda e: e.reciprocal(out=out, in_=in_), r, w)

    def dma(self, out, in_, r, w, q="sp"):
        self.T.add(q, lambda e: e.dma_start(out=out, in_=in_), r, w, dma=True)

    def newbank(self):
        if self.split:
            b = self.bankc % self.nsb
        else:
            b = self.bankc % 8
        self.bankc += 1
        return b

    def newacc(self):
        b = self.nsb + self.accc % (8 - self.nsb)
        self.accc += 1
        return b

    def alloc_norm(self, es, nmax):
        self.sq_buf = self.sb(es, "sqbuf", [128, 8, nmax], BF16)
        self.lnt = self.sb(es, "lnt", [128, nmax], F32)
        self.rstd = self.sb(es, "rstd", [128, nmax], F32)
        self.xn = self.sb(es, "xn", [128, 8, nmax], F32)

    def cast(self, dst, src, r, w):
        engs = self.cast_engs
        e = engs[self.cast_i % len(engs)]
        self.cast_i += 1
        if e == "act":
            self.T.add("act", lambda e_: e_.activation(out=dst, in_=src, func=AF.Copy), r, w, accw=True)
        else:
            self.T.add(e, lambda e_: e_.tensor_copy(out=dst, in_=src), r, w, accw=True)

    def load_w(self, dst, src, tok_w, nk, ncols):
        CH = 1024
        per = max(1, CH // ncols)
        k = 0
        while k < nk:
            kk = min(per, nk - k)
            if ncols > CH:
                assert per == 1
                c = 0
                while c < ncols:
                    cc = min(CH, ncols - c)
                    s = self.stg_i % self.nstg
                    self.stg_i += 1
                    st = self.stg[:, s, 0:cc]
                    self.dma(st, src[:, k, c:c + cc], (), [("stg", self.nstg, s)])
                    self.cast(dst[:, k, c:c + cc], st, [("stg", self.nstg, s)], [tok_w])
                    c += cc
            else:
                s = self.stg_i % self.nstg
                self.stg_i += 1
                st = self.stg[:, s, 0:kk * ncols].rearrange("p (k c) -> p k c", k=kk)
                self.dma(st, src[:, k:k + kk, :], (), [("stg", self.nstg, s)])
                self.cast(dst[:, k:k + kk, :], st, [("stg", self.nstg, s)], [tok_w])
            k += kk

    def rstd_of(self, es_tmp, src, nk, n, div, tokr, name):
        sq = self.sq_buf[:, 0:nk, 0:n]
        self.act(sq, src, AF.Square, [tokr], ["sqbuf"])
        b = self.newbank()
        for k in range(nk):
            self.mm(self.ps[b][:, 0:n], self.ones[:], sq[:, k, :], k == 0, k == nk - 1, ["sqbuf", "ones"], [("ps", b)])
        self.act(self.lnt[:, 0:n], self.ps[b][:, 0:n], AF.Ln, [("ps", b)], ["lnt"], bias=self.epsc[:, 0:1], scale=1.0 / div)
        self.act(self.rstd[:, 0:n], self.lnt[:, 0:n], AF.Exp, ["lnt"], ["rstd"], scale=-0.5)
        return self.rstd[:, 0:n]

    def norm_mod(self, xch, n, GG, SH, hT, tokx, tokh):
        rs = self.rstd_of(None, xch[:, :, 0:n], 8, n, 1024.0, tokx, "nm")
        self.tt("dve", self.xn[:, :, 0:n], xch[:, :, 0:n], bcast(rs, 1, 8), ALU.mult, [tokx, "rstd"], ["xn"])
        for k in range(8):
            self.ts("dve", hT[:, k, 0:n], self.xn[:, k, 0:n], GG[:, k:k + 1], SH[:, k:k + 1], ALU.mult, ALU.add,
                    ["xn", "mods"], [tokh])

    def build(self):
        nc = self.nc
        NL = self.nl
        xT = self.din("xT", [128, 8, NT])
        ctxT = self.din("ctxT", [128, 8, NCX])
        ccT = self.din("ccT", [128, 8, 2])
        ropeC = self.din("ropeC", [128, NT])
        ropeS = self.din("ropeS", [128, NT])
        maskA = self.din("maskA", [128, 10, 128])
        rm01 = self.din("rm01", [128, 44, 8])
        L = []
        for l in range(NL):
            d = dict(
                wada=self.din("wada%d" % l, [1024, 6144]),
                badaT=self.din("badaT%d" % l, [128, 48]),
                gains=self.din("gains%d" % l, [128, 4, 8]),
                win=self.din("win%d" % l, [1024, NWIN]),
                bgain=self.din("bgain%d" % l, [128, 4]),
                sinkT=self.din("sinkT%d" % l, [128, 6]),
                TB=self.din("TB%d" % l, [128, 2, 6 * 22 * 64]),
                wbr=self.din("wbr%d" % l, [3, 384, 1024]),
                wout=self.din("wout%d" % l, [1024, 1024]),
                w1=self.din("w1_%d" % l, [1024, 4096]),
                w2=self.din("w2_%d" % l, [4096, 1024]),
            )
            L.append(d)
        yT = nc.dram_tensor("yT", [128, 8, NT], F32, kind="ExternalOutput")
        xs = [self.dscr("xs%d" % i, [128, 8, NTOT], F32, tap=True) for i in range(2 * NL)]
        q_d = [self.dscr("q_d%d" % i, [128, 3, NTOT], BF16, tap=True) for i in range(3)]
        kaT_d = self.dscr("kaT_d", [128, NTOT], BF16, tap=True)
        va_d = self.dscr("va_d", [NTOT, 128], BF16, tap=True)
        kcT_d = self.dscr("kcT_d", [128, 3, NTOT], BF16, tap=True)
        vc_d = self.dscr("vc_d", [NTOT, 384], BF16, tap=True)
        kbcT_d = self.dscr("kbcT_d", [128, NCX], BF16, tap=True)
        vbc_d = self.dscr("vbc_d", [NCX, 128], BF16, tap=True)
        contrib = self.dscr("contrib", [960, 1024], BF16, tap=True)
        gathB = self.dscr("gathB", [4 * 512, 1024], BF16)
        gathH = self.dscr("gathH", [4 * 448, 1024], BF16)

        class G:
            @staticmethod
            def view(r, off, dims):
                if off < 524288:
                    return dview(gathB, r * 524288 + off, dims)
                return dview(gathH, r * 458752 + (off - 524288), dims)
        gath = G
        self.gathB, self.gathH = gathB, gathH
        o_d = [self.dscr("o_d%d" % i, [128, 3, NTOT], BF16, tap=True) for i in range(3)]
        mods_d = self.dscr("mods_d", [128, NL, 48, 2], F32, tap=True)

        with ExitStack() as es0:
            self.T = Trk(nc, es0)
            T = self.T
            self.bankc = 0
            self.accc = 0
            self.nsb = 4
            self.split = False
            self.stg_i = 0
            self.cast_i = 0
            self.cast_engs = ("pool",)
            self.psall = es0.enter_context(nc.psum_tensor("psall", [128, 4096], F32))
            self.ps = [self.psall[:, i * 512:(i + 1) * 512] for i in range(8)]
            self.ones = self.sb(es0, "ones", [128, 128], BF16)
            self.bd = self.sb(es0, "bd", [128, 128], BF16)
            self.epsc = self.sb(es0, "epsc", [128, 1], F32)
            self.modsb = self.sb(es0, "modsb", [128, NL, 48, 2], F32)
            self.dvec = self.sb(es0, "dvec", [128, NL, 2, 6, 8], F32)
            self.gn = self.sb(es0, "gn", [128, NL, 4, 8], F32)
            self.stg = self.sb(es0, "stg", [128, 3, 1024], F32)
            self.nstg = 3

            with ExitStack() as es:
                self.memset("pool", self.ones[:], 1.0, ["ones"])
                self.memset("pool", self.bd[:], 0.0, ["bd"])
                self.memset("pool", self.bd[0:64, 0:64], 1.0, ["bd"])
                self.memset("pool", self.bd[64:128, 64:128], 1.0, ["bd"])
                self.memset("pool", self.epsc[:], EPS, ["epsc"])
                cc_s = self.sb(es, "cc_s", [128, 8, 2], F32)
                sil = self.sb(es, "sil", [128, 8, 2], F32)
                self.dma(cc_s[:], ccT.ap(), (), ["cc_s"])
                self.act(sil[:], cc_s[:], AF.Silu, ["cc_s"], ["sil"])
                wa = self.sb(es, "wa", [128, 2, 8, 768], F32)
                bad = self.sb(es, "bad", [128, NL, 48], F32)
                for l in range(NL):
                    self.dma(bad[:, l, :], L[l]["badaT"].ap(), (), ["bad"])
                    self.dma(self.gn[:, l], L[l]["gains"].ap(), (), ["gn"])
                    wsrc = L[l]["wada"].ap().rearrange("(k p) n -> p k n", p=128)
                    b = self.newbank()
                    for g in range(8):
                        s = g % 2
                        self.dma(wa[:, s], wsrc[:, :, g * 768:(g + 1) * 768], (), [("wa", s)])
                        for mm_ in range(6):
                            m = g * 6 + mm_
                            for k in range(8):
                                self.mm(self.ps[b][:, 2 * m:2 * m + 2], wa[:, s, k, mm_ * 128:(mm_ + 1) * 128], sil[:, k, :],
                                        k == 0, k == 7, [("wa", s), "sil"], [("ps", b)])
                    self.tt("dve", self.modsb[:, l], self.ps[b][:, 0:96].rearrange("p (m t) -> p m t", t=2),
                            bcast(bad[:, l, :], 2, 2), ALU.add, [("ps", b), "bad"], ["modsb"])
                    for t in range(2):
                        mv = self.modsb[:, l, :, t]
                        dv = self.dvec[:, l, t]
                        self.stt(dv[:, 0, :], mv[:, 8:16], 1.0, self.gn[:, l, 0, :], ALU.add, ALU.mult, ["modsb", "gn"], ["mods"])
                        self.cp("dve", dv[:, 1, :], mv[:, 0:8], ["modsb"], ["mods"])
                        self.tt("dve", dv[:, 2, :], mv[:, 16:24], self.gn[:, l, 1, :], ALU.mult, ["modsb", "gn"], ["mods"])
                        self.stt(dv[:, 3, :], mv[:, 32:40], 1.0, self.gn[:, l, 2, :], ALU.add, ALU.mult, ["modsb", "gn"], ["mods"])
                        self.cp("dve", dv[:, 4, :], mv[:, 24:32], ["modsb"], ["mods"])
                        self.tt("dve", dv[:, 5, :], mv[:, 40:48], self.gn[:, l, 3, :], ALU.mult, ["modsb", "gn"], ["mods"])
                if "mods_d" in self.taps:
                    self.dma(mods_d.ap(), self.modsb[:], ["modsb"], ["mods_d"])
                T.flush()

            for l in range(NL):
                last = (l == NL - 1)
                if l == 0:
                    xin_lat = lambda c0, n: xT.ap()[:, :, c0:c0 + n]
                    xin_ctx = ctxT.ap()
                else:
                    xin_lat = (lambda xs_: (lambda c0, n: xs_.ap()[:, :, c0:c0 + n]))(xs[2 * l - 1])
                    xin_ctx = xs[2 * l - 1].ap()[:, :, NT:NTOT]
                if self.stop == "mods":
                    break
                self.phase_proj(L[l], l, last, xin_lat, xin_ctx, ropeC, ropeS, q_d, kaT_d, va_d, kcT_d, vc_d, kbcT_d, vbc_d, contrib)
                if self.stop == "proj%d" % l or (self.stop or "").startswith("proj0"):
                    break
                self.phase_gather(contrib, gath)
                self.phase_attn_a(L[l], l, last, q_d[0], kaT_d, va_d, gath, maskA, o_d[0])
                if self.stop == "attn_a%d" % l:
                    break
                self.phase_attn_b(L[l], l, last, q_d[1], kbcT_d, vbc_d, gath, o_d[1])
                if self.stop == "attn_b%d" % l:
                    break
                self.phase_attn_c(L[l], l, last, q_d[2], kcT_d, vc_d, gath, rm01, o_d[2])
                if self.stop == "attn_c%d" % l:
                    break
                self.phase_mix(L[l], l, last, xin_lat, xin_ctx, o_d, xs[2 * l])
                if self.stop == "mix%d" % l:
                    break
                out_lat = yT if last else xs[2 * l + 1]
                self.phase_mlp(L[l], l, last, xs[2 * l], out_lat, xs[2 * l + 1])
                if self.stop == "mlp%d" % l:
                    break
            T.final_wait()
        return nc

    def phase_proj(self, Ld, l, last, xin_lat, xin_ctx, ropeC, ropeS, q_d, kaT_d, va_d, kcT_d, vc_d, kbcT_d, vbc_d, contrib):
        T = self.T
        with ExitStack() as es:
            self.split = False
            self.alloc_norm(es, 512)
            hT = self.sb(es, "hT", [128, 8, NTOT], BF16)
            xch = self.sb(es, "xch", [128, 2, 8, 512], F32)
            rC = self.sb(es, "rC", [128, NT], F32)
            rS = self.sb(es, "rS", [128, NT], F32)
            bg = self.sb(es, "bg", [128, 4], F32)
            wt = self.sb(es, "wt", [128, 2, 8, 640], BF16)
            ost = self.sb(es, "ost", [128, 4, 640], BF16)
            t1 = self.sb(es, "t1", [128, 2, 512], F32)
            t2 = self.sb(es, "t2", [128, 2, 512], F32)
            sqh = self.sb(es, "sqh", [128, 2, 512], BF16)
            lnh = self.sb(es, "lnh", [128, 2, 512], F32)
            rsh = self.sb(es, "rsh", [128, 2, 512], F32)
            self.dma(rC[:], ropeC.ap(), (), ["rC"])
            self.dma(rS[:], ropeS.ap(), (), ["rS"])
            self.dma(bg[:], Ld["bgain"].ap(), (), ["bg"])
            chunks = [(i * 512, 512, False) for i in range(4)] + [(NT, NCX, True)]
            for ci, (c0, n, isc) in enumerate(chunks):
                s = ci % 2
                src = xin_ctx if isc else xin_lat(c0, n)
                self.dma(xch[:, s, :, 0:n], src, (), [("xch", s)])
                dv = self.dvec[:, l, 1 if isc else 0]
                self.norm_mod(xch[:, s], n, dv[:, 0, :], dv[:, 1, :], hT[:, :, c0:c0 + n], ("xch", s), "hT")
            if self.stop == "proj0a":
                T.flush()
                return
            win = Ld["win"].ap().rearrange("(k p) n -> p k n", p=128)
            units = []
            units.append(("KB", 0, [C_KB, C_KBS]))
            units.append(("V", 0, None))
            units.append(("KA", 0, [C_KA, C_KAS]))
            for mi in range(3):
                units.append(("KC", mi, [C_KC + mi * 128]))
            n_kv_units = len(units)
            for mi in range(3):
                units.append(("QB", mi, [C_QB + mi * 128, C_QBS + mi * 128]))
            for mi in range(3):
                units.append(("QA", mi, [C_QA + mi * 128, C_QAS + mi * 128]))
            for mi in range(3):
                units.append(("QC", mi, [C_QC + mi * 128]))
            ctr = dict(ost=0, t=0)

            ctoks = []

            def store(srcs_dsts, tok):
                for dst, src in srcs_dsts:
                    ctr["st"] = ctr.get("st", 0) + 1
                    wtk = ("dout", ctr["st"])
                    if dst.tensor.name == contrib.name:
                        ctoks.append(wtk)
                    self.dma(dst, src, [tok], [wtk])

            if self.stop and self.stop.startswith("proj0u"):
                sel = [int(x) for x in self.stop[6:].split("_")]
                units = [units[i] for i in sel]
            def load_unit(ui):
                kind, mi, cols = units[ui]
                ws = ui % 2
                wtok = ("+wt", ws)
                if kind == "V":
                    self.load_w(wt[:, ws, :, 0:640], win[:, :, C_V:C_V + 640], wtok, 8, 640)
                else:
                    for j, c in enumerate(cols):
                        self.load_w(wt[:, ws, :, j * 128:(j + 1) * 128], win[:, :, c:c + 128], wtok, 8, 128)
            load_unit(0)
            for ui, (kind, mi, cols) in enumerate(units):
                ws = ui % 2
                wtok = ("+wt", ws)
                if ui == n_kv_units and not (self.stop or "").startswith("proj0u"):
                    self.emit_gather(contrib, list(ctoks))
                if ui + 1 < len(units):
                    load_unit(ui + 1)
                if kind == "V":
                    for tile in range(NTOT // 128):
                        t0 = tile * 128
                        b0, b1 = self.newbank(), self.newbank()
                        for k in range(8):
                            self.mm(self.ps[b0][:, 0:512], hT[:, k, t0:t0 + 128], wt[:, ws, k, 0:512], k == 0, k == 7, ["hT", wtok], [("ps", b0)])
                        for k in range(8):
                            self.mm(self.ps[b1][:, 0:128], hT[:, k, t0:t0 + 128], wt[:, ws, k, 512:640], k == 0, k == 7, ["hT", wtok], [("ps", b1)])
                        o = ctr["ost"] % 4
                        ctr["ost"] += 1
                        otok = ("ost", o)
                        self.act(ost[:, o, 0:512], self.ps[b0][:, 0:512], AF.Copy, [("ps", b0)], [otok])
                        self.cp("dve", ost[:, o, 512:640], self.ps[b1][:, 0:128], [("ps", b1)], [otok])
                        cb = contrib
                        dl = [(va_d.ap()[t0:t0 + 128, :], ost[:, o, 0:128]),
                              (vc_d.ap()[t0:t0 + 128, :], ost[:, o, 256:640])]
                        if tile < 16:
                            dl.append((dview(cb, O_VB + t0 * 128, [[128, 128], [1, 128]]), ost[:, o, 128:256]))
                            if tile == 0:
                                dl.append((dview(cb, O_VAH, [[128, 128], [1, 128]]), ost[:, o, 0:128]))
                            if tile == 15:
                                dl.append((dview(cb, O_VAT, [[128, 128], [1, 128]]), ost[:, o, 0:128]))
                            if tile < 2:
                                dl.append((dview(cb, O_VCH + tile * 128 * 384, [[384, 128], [1, 384]]), ost[:, o, 256:640]))
                            if tile >= 14:
                                dl.append((dview(cb, O_VCT + (tile - 14) * 128 * 384, [[384, 128], [1, 384]]), ost[:, o, 256:640]))
                        else:
                            dl.append((vbc_d.ap()[t0 - NT:t0 - NT + 128, :], ost[:, o, 128:256]))
                        store(dl, otok)
                    continue
                for ci, (c0, n, isc) in enumerate(chunks):
                    if isc and last and kind in ("QA", "QB", "QC"):
                        continue
                    hs = [hT[:, k, c0:c0 + n] for k in range(8)]
                    bq = self.newbank()
                    for k in range(8):
                        self.mm(self.ps[bq][:, 0:n], wt[:, ws, k, 0:128], hs[k], k == 0, k == 7, ["hT", wtok], [("ps", bq)])
                    pq = self.ps[bq][:, 0:n]
                    o = ctr["ost"] % 4
                    ctr["ost"] += 1
                    otok = ("ost", o)
                    oo = ost[:, o, 0:n]
                    need_sw = (len(cols) == 2) and not isc
                    if need_sw:
                        bs = self.newbank()
                        for k in range(8):
                            self.mm(self.ps[bs][:, 0:n], wt[:, ws, k, 128:256], hs[k], k == 0, k == 7, ["hT", wtok], [("ps", bs)])
                        psw = self.ps[bs][:, 0:n]
                    tt_ = ctr["t"] % 2
                    ctr["t"] += 1
                    a1, a2 = t1[:, tt_, 0:n], t2[:, tt_, 0:n]
                    k1, k2 = ("t1", tt_), ("t2", tt_)
                    if kind in ("QA", "KA"):
                        if isc:
                            self.act(oo, pq, AF.Copy, [("ps", bq)], [otok])
                        else:
                            self.tt("dve", a1, pq, rC[:, c0:c0 + n], ALU.mult, [("ps", bq), "rC"], [k1])
                            self.tt("dve", a2, psw, rS[:, c0:c0 + n], ALU.mult, [("ps", bs), "rS"], [k2])
                            self.tt("pool", oo, a1, a2, ALU.add, [k1, k2], [otok])
                    elif kind in ("QB", "KB"):
                        gi = 0 if kind == "QB" else 2
                        self.act(sqh[:, tt_, 0:n], pq, AF.Square, [("ps", bq)], [("sqh", tt_)])
                        bss = self.newbank()
                        self.mm(self.ps[bss][:, 0:n], self.bd[:], sqh[:, tt_, 0:n], True, True, [("sqh", tt_), "bd"], [("ps", bss)])
                        self.act(lnh[:, tt_, 0:n], self.ps[bss][:, 0:n], AF.Ln, [("ps", bss)], [("lnh", tt_)], bias=self.epsc[:, 0:1], scale=1.0 / 64)
                        self.act(rsh[:, tt_, 0:n], lnh[:, tt_, 0:n], AF.Exp, [("lnh", tt_)], [("rsh", tt_)], scale=-0.5)
                        if isc:
                            self.stt(oo, pq, bg[:, gi:gi + 1], rsh[:, tt_, 0:n], ALU.mult, ALU.mult, [("ps", bq), "bg", ("rsh", tt_)], [otok])
                        else:
                            self.stt(a1, pq, bg[:, gi:gi + 1], rC[:, c0:c0 + n], ALU.mult, ALU.mult, [("ps", bq), "bg", "rC", ("sqh", tt_)], [k1])
                            self.stt(a2, psw, bg[:, gi + 1:gi + 2], rS[:, c0:c0 + n], ALU.mult, ALU.mult, [("ps", bs), "bg", "rS"], [k2])
                            self.tt("pool", a1, a1, a2, ALU.add, [k1, k2], [k1])
                            self.tt("pool", oo, a1, rsh[:, tt_, 0:n], ALU.mult, [k1, ("rsh", tt_)], [otok])
                    else:
                        self.act(oo, pq, AF.Copy, [("ps", bq)], [otok])
                    dl = []
                    cb = contrib
                    if kind == "QA":
                        dl.append((q_d[0].ap()[:, mi, c0:c0 + n], oo))
                    elif kind == "QB":
                        dl.append((q_d[1].ap()[:, mi, c0:c0 + n], oo))
                    elif kind == "QC":
                        dl.append((q_d[2].ap()[:, mi, c0:c0 + n], oo))
                    elif kind == "KA":
                        dl.append((kaT_d.ap()[:, c0:c0 + n], oo))
                        if ci == 0:
                            dl.append((dview(cb, O_KAH, [[128, 128], [1, 128]]), ost[:, o, 0:128]))
                        if ci == 3:
                            dl.append((dview(cb, O_KAT, [[128, 128], [1, 128]]), ost[:, o, 384:512]))
                    elif kind == "KB":
                        if isc:
                            dl.append((kbcT_d.ap(), oo))
                        else:
                            dl.append((dview(cb, O_KB + c0, [[2048, 128], [1, n]]), oo))
                    elif kind == "KC":
                        dl.append((kcT_d.ap()[:, mi, c0:c0 + n], oo))
                        if ci == 0:
                            dl.append((dview(cb, O_KCH + mi * 256, [[768, 128], [1, 256]]), ost[:, o, 0:256]))
                        if ci == 3:
                            dl.append((dview(cb, O_KCT + mi * 256, [[768, 128], [1, 256]]), ost[:, o, 256:512]))
                    store(dl, otok)
            T.flush()

    def phase_gather(self, contrib, gath):
        return

    def emit_gather(self, contrib, rtoks):
        T = self.T
        gB, gH = self.gathB, self.gathH
        T.add("pool", lambda e: e.collective_compute("AllGather", ALU.bypass, replica_groups=[[0, 1, 2, 3], [4, 5, 6, 7]],
                                                     ins=[contrib.ap()[0:512, :]], outs=[gB.ap()]), rtoks, ["gathB"], cc=True)
        T.add("pool", lambda e: e.collective_compute("AllGather", ALU.bypass, replica_groups=[[0, 1, 2, 3], [4, 5, 6, 7]],
                                                     ins=[contrib.ap()[512:960, :]], outs=[gH.ap()]), rtoks, ["gathH"], cc=True)

    def attn_fin(self, bank, n, base, out_ap, rc, rctok, extra=None, extra_tok=None, shape3=None, act_recip=True):
        ob = 64 - base
        k = self.fc_i % 2
        self.fc_i += 1
        fct = ("fc", k)
        self.cp("dve", self.fc[:, k, 0:n], self.ps[bank][:, 0:n], [("ps", bank)], [fct])
        den = self.fc[ob:ob + 64, k, 0:n]
        num = self.fc[base:base + 64, k, 0:n]
        rcp = rc[base:base + 64, 0:n]
        tmp = rc[ob:ob + 64, 0:n]
        if shape3 is not None:
            a, b_ = shape3
            den = den.rearrange("p (a b) -> p a b", a=a)
            num = num.rearrange("p (a b) -> p a b", a=a)
            rcp = rcp.rearrange("p (a b) -> p a b", a=a)
            tmp = tmp.rearrange("p (a b) -> p a b", a=a)
        src, srct = den, fct
        if extra is not None:
            self.tt("dve", tmp, den, extra, ALU.add, [fct, extra_tok], [rctok])
            src, srct = tmp, rctok
        if act_recip:
            self.act(rcp, src, AF.Ln, [srct], [rctok])
            self.act(rcp, rcp, AF.Exp, [rctok], [rctok], scale=-1.0)
        else:
            self.recip(rcp, src, [srct], [rctok])
        self.tt("dve", out_ap, num, rcp, ALU.mult, [fct, rctok], ["oT"])

    def pipeline(self, items, look=3, group=1):
        groups = [items[i:i + group] for i in range(0, len(items), group)]
        banks = {}

        def qks(gi):
            for k, itm in enumerate(groups[gi]):
                banks[(gi, k)] = self.newbank()
                itm[0](banks[(gi, k)])
        for gi in range(min(look, len(groups))):
            qks(gi)
        for gi in range(len(groups)):
            if gi + look < len(groups):
                qks(gi + look)
            for k, itm in enumerate(groups[gi]):
                itm[1](banks[(gi, k)])
            for k, itm in enumerate(groups[gi]):
                itm[2]()
                if itm[3] is not None:
                    itm[3]()

    def load_vaug(self, dst, src, tokn):
        self.dma(dst, src, (), [tokn])

    def phase_attn_a(self, Ld, l, last, qa_d, kaT_d, va_d, gath, maskA, oa_d):
        T = self.T
        with ExitStack() as es:
            self.split = True
            self.nsb = 6
            self.bankc = 0
            QT = self.sb(es, "QT", [128, 3, NTOT], BF16)
            KT = self.sb(es, "KT", [128, NTOT], BF16)
            KH = self.sb(es, "KH", [128, 8, 128], BF16)
            VA = self.sb(es, "VA", [128, 18, 2, 128], BF16)
            VH = self.sb(es, "VH", [128, 8, 2, 128], BF16)
            MA = self.sb(es, "MA", [128, 10, 128], F32)
            ES = self.sb(es, "ES", [128, 6, 128], F32)
            sk = self.sb(es, "sk", [128, 6], F32)
            PT = self.sb(es, "PT", [128, 8, 512], BF16)
            tm = self.sb(es, "tm", [128, 4, 384], F32)
            rc = self.sb(es, "rc", [128, 2, 512], F32)
            self.fc = self.sb(es, "fc", [128, 2, 512], F32)
            self.fc_i = 0
            oT = self.sb(es, "oT", [128, 3, NTOT], BF16)
            nq = NT if last else NTOT
            self.dma(QT[:, :, 0:nq], qa_d.ap()[:, :, 0:nq], (), ["+QT"])
            self.dma(KT[:], kaT_d.ap(), (), ["+KT"])
            self.memset("pool", VA[:], 1.0, ["+VA"])
            self.memset("pool", VH[:], 1.0, ["+VH"])
            vsrc = va_d.ap().rearrange("(t p) c -> p t c", p=128)
            self.dma(VA[:, :, 0, 0:64], vsrc[:, :, 0:64], (), ["+VA"])
            self.dma(VA[:, :, 1, 64:128], vsrc[:, :, 64:128], (), ["+VA"])
            for r in range(4):
                pass
                self.dma(KH[:, r, :], gath.view(r, O_KAT, [[128, 128], [1, 128]]), ["gath"], ["+KH"])
                self.dma(KH[:, 4 + r, :], gath.view(r, O_KAH, [[128, 128], [1, 128]]), ["gath"], ["+KH"])
                self.dma(VH[:, r, 0, 0:64], gath.view(r, O_VAT, [[128, 128], [1, 64]]), ["gath"], ["+VH"])
                self.dma(VH[:, r, 1, 64:128], gath.view(r, O_VAT + 64, [[128, 128], [1, 64]]), ["gath"], ["+VH"])
                self.dma(VH[:, 4 + r, 0, 0:64], gath.view(r, O_VAH, [[128, 128], [1, 64]]), ["gath"], ["+VH"])
                self.dma(VH[:, 4 + r, 1, 64:128], gath.view(r, O_VAH + 64, [[128, 128], [1, 64]]), ["gath"], ["+VH"])
            self.dma(MA[:], maskA.ap(), (), ["MA"])
            self.dma(sk[:], Ld["sinkT"].ap(), (), ["sk"])
            self.act(sk[:], sk[:], AF.Exp, ["sk"], ["sk"])
            self.cp("dve", ES[:], bcast(sk[:], 2, 128), ["sk"], ["ES"])
            it = dict(p=0, t=0, r=0)
            items = []

            def run(qcols, nqc, base, tiles, shape3, out_ap, es_ap):
                n = 3 * nqc
                rhs = QT[base:base + 64, :, qcols:qcols + nqc]
                bo = self.newacc()
                nt = len(tiles)
                for ti, (kap, vap, mk) in enumerate(tiles):
                    p = it["p"] % 8
                    it["p"] += 1
                    t = it["t"] % 4
                    if mk is not None:
                        it["t"] += 1

                    def qk(b, kap=kap):
                        self.mm(self.ps[b][:, 0:n].rearrange("p (a b) -> p a b", a=3), kap, rhs, True, True, ["+QT", "+KT", "+KH"], [("ps", b)])

                    def sm(b, mk=mk, p=p, t=t):
                        pt = PT[:, p, 0:n]
                        if mk is not None:
                            self.stt(tm[:, t, 0:n].rearrange("p (a b) -> p a b", a=3), self.ps[b][:, 0:n].rearrange("p (a b) -> p a b", a=3),
                                     0.125, bcast(mk, 1, 3), ALU.mult, ALU.add, [("ps", b), "MA"], [("tm", t)])
                            self.act(pt, tm[:, t, 0:n], AF.Exp, [("tm", t)], [("PT", p)])
                        else:
                            self.act(pt, self.ps[b][:, 0:n], AF.Exp, [("ps", b)], [("PT", p)], scale=0.125)

                    def pv(vap=vap, p=p, ti=ti):
                        self.mm(self.ps[bo][:, 0:n], vap, PT[:, p, 0:n], ti == 0, ti == nt - 1, [("PT", p), "+VA", "+VH"], [("ps", bo)])
                    fin = None
                    if ti == nt - 1:
                        r_ = it["r"] % 2
                        it["r"] += 1

                        def fin(r_=r_):
                            self.attn_fin(bo, n, base, out_ap, rc[:, r_], ("rc", r_), extra=es_ap, extra_tok="ES", shape3=shape3)
                    items.append((qk, sm, pv, fin))

            for half in range(2):
                base = half * 64
                ob = 64 - base
                esl = ES[ob:ob + 64, 3 * half:3 * half + 3, :]
                for j in range(16):
                    tiles = []
                    if j > 0:
                        tiles.append((KT[base:base + 64, (j - 1) * 128:j * 128], VA[:, j - 1, half, :], MA[:, 0, :]))
                    else:
                        for r in range(4):
                            tiles.append((KH[base:base + 64, r, :], VH[:, r, half, :], MA[:, 2 + r, :]))
                    tiles.append((KT[base:base + 64, j * 128:(j + 1) * 128], VA[:, j, half, :], None))
                    if j < 15:
                        tiles.append((KT[base:base + 64, (j + 1) * 128:(j + 2) * 128], VA[:, j + 1, half, :], MA[:, 1, :]))
                    else:
                        for r in range(4):
                            tiles.append((KH[base:base + 64, 4 + r, :], VH[:, 4 + r, half, :], MA[:, 6 + r, :]))
                    for c in range(2):
                        tiles.append((KT[base:base + 64, NT + c * 128:NT + (c + 1) * 128], VA[:, 16 + c, half, :], None))
                    run(j * 128, 128, base, tiles, (3, 128), oT[base:base + 64, :, j * 128:(j + 1) * 128], esl)
                if not last:
                    for cq in range(2):
                        tiles = [(KT[base:base + 64, NT + c * 128:NT + (c + 1) * 128], VA[:, 16 + c, half, :], None) for c in range(2)]
                        q0 = NT + cq * 128
                        run(q0, 128, base, tiles, (3, 128), oT[base:base + 64, :, q0:q0 + 128], esl)
            self.pipeline(items, 5)
            self.nsb = 4
            self.dma(oa_d.ap()[:, :, 0:nq], oT[:, :, 0:nq], ["oT"], ["oa_d"])
            T.flush()

    def phase_attn_b(self, Ld, l, last, qb_d, kbcT_d, vbc_d, gath, ob_d):
        T = self.T
        with ExitStack() as es:
            NK = NCX + 4 * NT
            self.split = True
            self.nsb = 6
            self.bankc = 0
            QT = self.sb(es, "QTb", [128, 3, NTOT], BF16)
            KT = self.sb(es, "KTb", [128, NK], BF16)
            VB = self.sb(es, "VBb", [128, 66, 2, 128], BF16)
            PT = self.sb(es, "PTb", [128, 6, 512], BF16)
            rc = self.sb(es, "rcb", [128, 2, 512], F32)
            self.fc = self.sb(es, "fcb", [128, 2, 512], F32)
            self.fc_i = 0
            oT = self.sb(es, "oTb", [128, 3, NTOT], BF16)
            nq = NT if last else NTOT
            self.dma(QT[:, :, 0:nq], qb_d.ap()[:, :, 0:nq], (), ["+QT"])
            self.dma(KT[:, 0:NCX], kbcT_d.ap(), (), ["+KT"])
            self.memset("pool", VB[:, 0:33], 1.0, ["+VB"])
            self.memset("pool", VB[:, 33:66], 1.0, ["+VB"])
            vcs = vbc_d.ap().rearrange("(t p) c -> p t c", p=128)
            self.dma(VB[:, 0:2, 0, 0:64], vcs[:, :, 0:64], (), ["+VB"])
            self.dma(VB[:, 0:2, 1, 64:128], vcs[:, :, 64:128], (), ["+VB"])
            for r in range(4):
                pass
                self.dma(KT[:, NCX + r * NT:NCX + (r + 1) * NT], gath.view(r, O_KB, [[2048, 128], [1, 2048]]), ["gath"], ["+KT"])
                self.dma(VB[:, 2 + r * 16:2 + (r + 1) * 16, 0, 0:64], gath.view(r, O_VB, [[128, 128], [128 * 128, 16], [1, 64]]), ["gath"], ["+VB"])
                self.dma(VB[:, 2 + r * 16:2 + (r + 1) * 16, 1, 64:128], gath.view(r, O_VB + 64, [[128, 128], [128 * 128, 16], [1, 64]]), ["gath"], ["+VB"])
            it = dict(p=0, r=0)
            items = []

            def run(mi, q0, n, ktiles):
                bo = [self.newacc(), self.newacc()]
                nt = len(ktiles)
                for ti, kt in enumerate(ktiles):
                    for half in range(2):
                        base = half * 64
                        p = it["p"] % 6
                        it["p"] += 1

                        def qk(b, base=base, kt=kt):
                            self.mm(self.ps[b][:, 0:n], KT[base:base + 64, kt * 128:(kt + 1) * 128], QT[base:base + 64, mi, q0:q0 + n],
                                    True, True, ["+QT", "+KT"], [("ps", b)])

                        def sm(b, p=p, half=half):
                            if half == 1:
                                return
                            assert b % 2 == 0 and p % 2 == 0
                            src = self.psall[:, b * 512:(b + 2) * 512].rearrange("p (a c) -> p a c", a=2)[:, :, 0:n]
                            self.act(PT[:, p:p + 2, 0:n], src, AF.Exp, [("ps", b), ("ps", b + 1)], [("PT", p), ("PT", p + 1)], scale=0.125)

                        def pv(p=p, half=half, kt=kt, ti=ti):
                            self.mm(self.ps[bo[half]][:, 0:n], VB[:, kt, half, :], PT[:, p, 0:n], ti == 0, ti == nt - 1,
                                    [("PT", p), "+VB"], [("ps", bo[half])])
                        fin = None
                        if ti == nt - 1:
                            r_ = it["r"] % 2
                            it["r"] += 1

                            def fin(r_=r_, half=half, base=base):
                                self.attn_fin(bo[half], n, base, oT[base:base + 64, mi, q0:q0 + n], rc[:, r_], ("rc", r_), act_recip=False)
                        items.append((qk, sm, pv, fin))

            for mi in range(3):
                for qc in range(4):
                    run(mi, qc * 512, 512, list(range(66)))
                if not last:
                    run(mi, NT, NCX, [0, 1])
            self.pipeline(items, 2, 2)
            self.nsb = 4
            self.dma(ob_d.ap()[:, :, 0:nq], oT[:, :, 0:nq], ["oT"], ["ob_d"])
            T.flush()

    def phase_attn_c(self, Ld, l, last, qc_d, kcT_d, vc_d, gath, rm01, oc_d):
        T = self.T
        with ExitStack() as es:
            self.split = True
            self.nsb = 6
            self.bankc = 0
            QT = self.sb(es, "QTc", [128, 3, NTOT], BF16)
            KT = self.sb(es, "KTc", [128, 3, NTOT], BF16)
            KH = self.sb(es, "KHc", [128, 3, 8, 256], BF16)
            VC = self.sb(es, "VCc", [128, 18, 6, 128], BF16)
            VH = self.sb(es, "VHc", [128, 16, 6, 128], BF16)
            EB = self.sb(es, "EB", [128, 2, 6, 22, 64], BF16)
            RM = self.sb(es, "RM", [128, 44, 8], F32)
            RMb = self.sb(es, "RMb", [128, 44, 8], BF16)
            PT = self.sb(es, "PTc", [128, 8, 512], BF16)
            rc = self.sb(es, "rcc", [128, 2, 512], F32)
            self.fc = self.sb(es, "fcc", [128, 2, 512], F32)
            self.fc_i = 0
            oT = self.sb(es, "oTc", [128, 3, NTOT], BF16)
            nq = NT if last else NTOT
            self.dma(QT[:, :, 0:nq], qc_d.ap()[:, :, 0:nq], (), ["+QT"])
            self.dma(KT[:], kcT_d.ap(), (), ["+KT"])
            self.memset("pool", VC[:], 1.0, ["+VC"])
            self.memset("pool", VH[:], 1.0, ["+VH"])
            vsrc = vc_d.ap().rearrange("(t p) (h d) -> p t h d", p=128, d=64)
            for h in range(6):
                o = (h % 2) * 64
                self.dma(VC[:, :, h, o:o + 64], vsrc[:, :, h, :], (), ["+VC"])
            for r in range(4):
                pass
                self.dma(KH[:, :, r, :], gath.view(r, O_KCT, [[768, 128], [256, 3], [1, 256]]), ["gath"], ["+KH"])
                self.dma(KH[:, :, 4 + r, :], gath.view(r, O_KCH, [[768, 128], [256, 3], [1, 256]]), ["gath"], ["+KH"])
                for h in range(6):
                    o = (h % 2) * 64
                    self.dma(VH[:, 2 * r:2 * r + 2, h, o:o + 64], gath.view(r, O_VCT + h * 64, [[384, 128], [128 * 384, 2], [1, 64]]), ["gath"], ["+VH"])
                    self.dma(VH[:, 8 + 2 * r:8 + 2 * r + 2, h, o:o + 64], gath.view(r, O_VCH + h * 64, [[384, 128], [128 * 384, 2], [1, 64]]), ["gath"], ["+VH"])
            TBd = Ld["TB"].ap()
            for tbl in range(2):
                ebf = EB[:, tbl].rearrange("p h u q -> p (h u q)")
                c = 0
                while c < 6 * 22 * 64:
                    cc_ = min(1024, 6 * 22 * 64 - c)
                    sl_ = self.stg_i % self.nstg
                    self.stg_i += 1
                    st = self.stg[:, sl_, 0:cc_]
                    self.dma(st, TBd[:, tbl, c:c + cc_], (), [("stg", self.nstg, sl_)])
                    self.T.add("act", (lambda e_, o_=ebf[:, c:c + cc_], i_=st: e_.activation(out=o_, in_=i_, func=AF.Exp)),
                               [("stg", self.nstg, sl_)], ["+EB"], accw=True)
                    c += cc_
            self.dma(RM[:], rm01.ap(), (), ["RM"])
            self.cp("dve", RMb[:], RM[:], ["RM"], ["RMb"])
            it = dict(p=0, t=0, r=0)
            items = []

            def crun(q0, n, tiles, h, mi, base):
                rhs = QT[base:base + 64, mi, q0:q0 + n]
                bo = self.newacc()
                nt = len(tiles)
                for ti, (kap, vap, j, ri) in enumerate(tiles):
                    p = it["p"] % 8
                    it["p"] += 1
                    t = it["t"] % 3
                    if j is not None:
                        it["t"] += 1

                    def qk(b, kap=kap):
                        self.mm(self.ps[b][:, 0:n], kap, rhs, True, True, ["+QT", "+KT", "+KH"], [("ps", b)])

                    def sm(b, j=j, ri=ri, p=p, t=t):
                        pt = PT[:, p, 0:n]
                        self.act(pt, self.ps[b][:, 0:n], AF.Exp, [("ps", b)], [("PT", p)], scale=0.125)
                        if j is not None:
                            u0 = 14 - 2 * j
                            g_ = q0 // 512
                            tbl = 1 if g_ in (1, 2) else 0
                            pt3 = pt.rearrange("p (a b) -> p a b", a=8)
                            self.tt("dve", pt3, pt3, EB[:, tbl, h, u0:u0 + 8, :], ALU.mult, [("PT", p), "+EB"], [("PT", p)])
                            if tbl == 0:
                                self.tt("pool", pt3, pt3, bcast(RMb[:, ri, :], 2, 64), ALU.mult, [("PT", p), "RMb"], [("PT", p)])

                    def pv(vap=vap, p=p, ti=ti):
                        self.mm(self.ps[bo][:, 0:n], vap, PT[:, p, 0:n], ti == 0, ti == nt - 1, [("PT", p), "+VC", "+VH"], [("ps", bo)])
                    fin = None
                    if ti == nt - 1:
                        r_ = it["r"] % 2
                        it["r"] += 1

                        def fin(r_=r_):
                            self.attn_fin(bo, n, base, oT[base:base + 64, mi, q0:q0 + n], rc[:, r_], ("rc", r_))
                    items.append((qk, sm, pv, fin))

            for h in range(6):
                mi, half = h // 2, h % 2
                base = half * 64
                rmi = 0
                for g in range(4):
                    tiles = []
                    for j in range(8):
                        lt = g * 512 - 256 + 128 * j
                        if g == 0 and j < 2:
                            for r in range(4):
                                tiles.append((KH[base:base + 64, mi, r, j * 128:(j + 1) * 128], VH[:, 2 * r + j, h, :], j, rmi))
                                rmi += 1
                        elif g == 3 and j >= 6:
                            for r in range(4):
                                tiles.append((KH[base:base + 64, mi, 4 + r, (j - 6) * 128:(j - 5) * 128], VH[:, 8 + 2 * r + (j - 6), h, :], j, rmi))
                                rmi += 1
                        else:
                            tiles.append((KT[base:base + 64, mi, lt:lt + 128], VC[:, lt // 128, h, :], j, rmi))
                            rmi += 1
                    for c in range(2):
                        tiles.append((KT[base:base + 64, mi, NT + c * 128:NT + (c + 1) * 128], VC[:, 16 + c, h, :], None, None))
                    crun(g * 512, 512, tiles, h, mi, base)
                if not last:
                    tiles = [(KT[base:base + 64, mi, NT + c * 128:NT + (c + 1) * 128], VC[:, 16 + c, h, :], None, None) for c in range(2)]
                    crun(NT, NCX, tiles, h, mi, base)
            self.pipeline(items, 5)
            self.nsb = 4
            self.dma(oc_d.ap()[:, :, 0:nq], oT[:, :, 0:nq], ["oT"], ["oc_d"])
            T.flush()

    def post_norm_res(self, yTb, n, PG, xsrc, xdst, tok_y, tok_x, tok_o, tmpb):
        rs = self.rstd_of(None, yTb[:, :, 0:n], 8, n, 1024.0, tok_y, "pn")
        for m in range(8):
            self.stt(tmpb[:, m, 0:n], yTb[:, m, 0:n], PG[:, m:m + 1], rs, ALU.mult, ALU.mult, [tok_y, "rstd", "mods"], ["xn"])
            self.tt("pool", xdst[:, m, 0:n], xsrc[:, m, 0:n], tmpb[:, m, 0:n], ALU.add, [tok_x, "xn"], [tok_o])

    def phase_mix(self, Ld, l, last, xin_lat, xin_ctx, o_d, xs1):
        T = self.T
        with ExitStack() as es:
            self.split = False
            self.alloc_norm(es, 256)
            N = 256
            wg = self.sb(es, "wg", [128, 8, 3072], BF16)
            wbr = self.sb(es, "wbr", [128, 3, 3, 1024], BF16)
            wo = self.sb(es, "wo", [128, 8, 1024], BF16)
            xch = self.sb(es, "xchm", [128, 2, 8, N], F32)
            hT = self.sb(es, "hTm", [128, 2, 8, N], BF16)
            oc = self.sb(es, "ocm", [128, 2, 3, 3, N], BF16)
            sg = self.sb(es, "sg", [128, 2, 3, N], F32)
            ta = self.sb(es, "ta", [128, 2, 3, N], F32)
            mg = self.sb(es, "mg", [128, 8, N], BF16)
            yTb = self.sb(es, "yTb", [128, 8, N], F32)
            stg2 = self.sb(es, "stg2", [128, 6, 1024], F32)
            chunks = [(i * N, N, False) for i in range(NT // N)]
            if not last:
                chunks.append((NT, NCX, True))

            def prep(ci):
                c0, n, isc = chunks[ci]
                s_ = ci % 2
                src = xin_ctx if isc else xin_lat(c0, n)
                self.dma(xch[:, s_, :, 0:n], src, (), [("xch", s_)])
                for i in range(3):
                    self.dma(oc[:, s_, i, :, 0:n], o_d[i].ap()[:, :, c0:c0 + n], (), [("+oc", s_)])
                dv_ = self.dvec[:, l, 1 if isc else 0]
                self.norm_mod(xch[:, s_], n, dv_[:, 0, :], dv_[:, 1, :], hT[:, s_], ("xch", s_), ("hTm", s_))
            prep(0)
            win = Ld["win"].ap().rearrange("(k p) n -> p k n", p=128)
            self.cast_engs = ("pool", "dve", "act")
            old_stg, old_n = self.stg, self.nstg
            self.stg, self.nstg = stg2, 6
            for i in range(3):
                self.load_w(wg[:, :, i * 1024:(i + 1) * 1024], win[:, :, C_G + i * 1024:C_G + (i + 1) * 1024], ("+wg", i), 8, 1024)
                self.load_w(wbr[:, i], Ld["wbr"].ap()[i].rearrange("(k p) n -> p k n", p=128), ("+wbr", i), 3, 1024)
            self.load_w(wo[:], Ld["wout"].ap().rearrange("(k p) n -> p k n", p=128), "+wo", 8, 1024)
            self.stg, self.nstg = old_stg, old_n
            self.cast_engs = ("pool",)
            for ci, (c0, n, isc) in enumerate(chunks):
                s = ci % 2
                dv = self.dvec[:, l, 1 if isc else 0]
                for m in range(8):
                    q = m % 2
                    gb_, yb_ = [], []
                    for i in range(3):
                        b = self.newbank()
                        gb_.append(b)
                        for k in range(8):
                            self.mm(self.ps[b][:, 0:n], wg[:, k, i * 1024 + m * 128:i * 1024 + (m + 1) * 128], hT[:, s, k, 0:n], k == 0, k == 7,
                                    [("+wg", i), ("hTm", s)], [("ps", b)])
                        b = self.newbank()
                        yb_.append(b)
                        for k in range(3):
                            self.mm(self.ps[b][:, 0:n], wbr[:, i, k, m * 128:(m + 1) * 128], oc[:, s, i, k, 0:n], k == 0, k == 2,
                                    [("+wbr", i), ("+oc", s)], [("ps", b)])
                    for i in range(3):
                        self.act(sg[:, q, i, 0:n], self.ps[gb_[i]][:, 0:n], AF.Sigmoid, [("ps", gb_[i])], [("sg", q, i)])
                        self.tt("dve", ta[:, q, i, 0:n], sg[:, q, i, 0:n], self.ps[yb_[i]][:, 0:n], ALU.mult, [("sg", q, i), ("ps", yb_[i])], [("ta", q, i)])
                    self.tt("pool", ta[:, q, 0, 0:n], ta[:, q, 0, 0:n], ta[:, q, 1, 0:n], ALU.add, [("ta", q, 0), ("ta", q, 1)], [("ta", q, 0)])
                    self.tt("pool", mg[:, m, 0:n], ta[:, q, 0, 0:n], ta[:, q, 2, 0:n], ALU.add, [("ta", q, 0), ("ta", q, 2)], [("mg", m)])
                if ci + 1 < len(chunks):
                    prep(ci + 1)
                for m in range(8):
                    b = self.newbank()
                    for k in range(8):
                        self.mm(self.ps[b][:, 0:n], wo[:, k, m * 128:(m + 1) * 128], mg[:, k, 0:n], k == 0, k == 7, ["+wo", ("mg", k)], [("ps", b)])
                    self.act(yTb[:, m, 0:n], self.ps[b][:, 0:n], AF.Copy, [("ps", b)], ["yTb"])
                self.post_norm_res(yTb, n, dv[:, 2, :], xch[:, s], xch[:, s], "yTb", ("xch", s), ("xch", s), self.xn)
                self.dma(xs1.ap()[:, :, c0:c0 + n], xch[:, s, :, 0:n], [("xch", s)], [("xs1", ci)])
            T.flush()

    def phase_mlp(self, Ld, l, last, xs1, out_lat, xs2):
        T = self.T
        with ExitStack() as es:
            self.split = False
            self.alloc_norm(es, 256)
            w1 = self.sb(es, "w1", [128, 8, 4096], BF16)
            w2 = self.sb(es, "w2", [128, 32, 1024], BF16)
            xch = self.sb(es, "xchp", [128, 2, 8, 256], F32)
            h2 = self.sb(es, "h2", [128, 2, 8, 256], BF16)
            rl = self.sb(es, "rl", [128, 2, 256], F32)
            aT = self.sb(es, "aT", [128, 32, 256], BF16)
            y2 = self.sb(es, "y2", [128, 8, 256], F32)
            n = 256
            chunks = [(i * 256, False) for i in range(8)]
            if not last:
                chunks.append((NT, True))

            def prep(ci):
                c0, isc = chunks[ci]
                s_ = ci % 2
                self.dma(xch[:, s_], xs1.ap()[:, :, c0:c0 + n], ["xs1"], [("xch", s_)])
                dv_ = self.dvec[:, l, 1 if isc else 0]
                self.norm_mod(xch[:, s_], n, dv_[:, 3, :], dv_[:, 4, :], h2[:, s_], ("xch", s_), ("h2", s_))
            prep(0)
            self.cast_engs = ("pool", "dve", "act")
            w1src = Ld["w1"].ap().rearrange("(k p) n -> p k n", p=128)
            for cc_ in range(4):
                self.load_w(w1[:, :, cc_ * 1024:(cc_ + 1) * 1024], w1src[:, :, cc_ * 1024:(cc_ + 1) * 1024], ("+w1", cc_), 8, 1024)
            self.load_w(w2[:], Ld["w2"].ap().rearrange("(k p) n -> p k n", p=128), "+w2", 32, 1024)
            self.cast_engs = ("pool",)
            for ci, (c0, isc) in enumerate(chunks):
                s = ci % 2
                dv = self.dvec[:, l, 1 if isc else 0]
                for j in range(32):
                    b = self.newbank()
                    for k in range(8):
                        self.mm(self.ps[b][:, 0:n], w1[:, k, j * 128:(j + 1) * 128], h2[:, s, k, :], k == 0, k == 7, [("+w1", j // 8), ("h2", s)], [("ps", b)])
                    r_ = j % 2
                    self.act(rl[:, r_, :], self.ps[b][:, 0:n], AF.Relu, [("ps", b)], [("rl", r_)])
                    self.tt("pool", aT[:, j, :], rl[:, r_, :], rl[:, r_, :], ALU.mult, [("rl", r_)], [("aT", j)])
                if ci + 1 < len(chunks):
                    prep(ci + 1)
                for m in range(8):
                    b = self.newbank()
                    for j in range(32):
                        self.mm(self.ps[b][:, 0:n], w2[:, j, m * 128:(m + 1) * 128], aT[:, j, :], j == 0, j == 31, ["+w2", ("aT", j)], [("ps", b)])
                    self.act(y2[:, m, :], self.ps[b][:, 0:n], AF.Copy, [("ps", b)], ["y2"])
                self.post_norm_res(y2, n, dv[:, 5, :], xch[:, s], xch[:, s], "y2", ("xch", s), ("xch", s), self.xn)
                if isc:
                    dst = xs2.ap()[:, :, c0:c0 + n]
                else:
                    dst = out_lat.ap()[:, :, c0:c0 + n]
                self.dma(dst, xch[:, s], [("xch", s)], [("out", ci)])
            T.flush()


def _fm(v):
    v = np.asarray(v, np.float32)
    return np.ascontiguousarray(v.reshape(-1, 128).T)


def _perm_cols():
    def hc(base, heads, swap):
        out = []
        for h in heads:
            d = np.arange(64)
            if swap:
                d = d ^ 1
            out += list(base + h * 64 + d)
        return out
    p = []
    p += hc(0, HP, False) + hc(0, HP, True)
    p += hc(384, [0, 1], False) + hc(384, [0, 1], True)
    p += hc(640, HP, False) + hc(640, HP, True)
    p += hc(1024, [0, 1], False) + hc(1024, [0, 1], True)
    p += list(range(1280, 1664)) + list(range(1664, 2048))
    p += list(range(512, 640)) + list(range(1152, 1280)) + list(range(2048, 2432))
    p += list(range(2432, 5504))
    assert len(p) == NWIN
    return np.array(p)


def _rope_tables(tok0):
    pos = np.arange(tok0, tok0 + NT)
    row = (pos // 64).astype(np.float32)
    col = (pos % 64).astype(np.float32)
    freqs = (np.float32(10000.0) ** (-np.arange(16, dtype=np.float32) / np.float32(16))).astype(np.float32)
    ang = np.concatenate([row[:, None] * freqs, col[:, None] * freqs], axis=-1).astype(np.float32)
    cos, sin = np.cos(ang).astype(np.float32), np.sin(ang).astype(np.float32)
    d = np.arange(128) % 64
    C = cos[:, d // 2].T
    S = sin[:, d // 2].T * np.where(d % 2 == 0, -1.0, 1.0)[:, None]
    return np.ascontiguousarray(C, np.float32), np.ascontiguousarray(S, np.float32)


def _mask_a(rank):
    ki = np.arange(128)[:, None]
    qi = np.arange(128)[None, :]
    prev = np.where(qi <= ki, 0.0, NEG).astype(np.float32)
    nxt = np.where(ki <= qi, 0.0, NEG).astype(np.float32)
    allneg = np.full((128, 128), NEG, np.float32)
    m = np.zeros((128, 10, 128), np.float32)
    m[:, 0], m[:, 1] = prev, nxt
    for r in range(4):
        m[:, 2 + r] = prev if r == rank - 1 else allneg
        m[:, 6 + r] = nxt if r == rank + 1 else allneg
    return m


def _rm01(rank):
    out = np.zeros((128, 44, 8), np.float32)
    idx = 0
    kl = (np.arange(128) // 64)[:, None]
    ql = np.arange(8)[None, :]
    for g in range(4):
        R0 = rank * 32 + g * 8
        for j in range(8):
            kr = R0 - 4 + 2 * j + kl
            qr = R0 + ql
            rs = np.clip(qr - 4, 0, 120)
            valid = (kr >= rs) & (kr < rs + 8) & (kr >= 0) & (kr < 128)
            if g == 0 and j < 2:
                for r in range(4):
                    out[:, idx] = valid if r == rank - 1 else 0.0
                    idx += 1
            elif g == 3 and j >= 6:
                for r in range(4):
                    out[:, idx] = valid if r == rank + 1 else 0.0
                    idx += 1
            else:
                out[:, idx] = valid
                idx += 1
    assert idx == 44
    return out


def _tb_table(rpb, interior=False):
    rpb = np.asarray(rpb, np.float32)
    kc = np.arange(64)[:, None]
    qc = np.arange(64)[None, :]
    ws = np.clip(qc - 8, 0, 48)
    colv = (kc >= ws) & (kc < ws + 16)
    dc = np.clip(kc - qc + 15, 0, 30)
    tb = np.zeros((2, 64, 6, 22, 64), np.float32)
    for kl in range(2):
        for u in range(22):
            dr = 17 + kl - u
            for h in range(6):
                if 0 <= dr <= 14:
                    v = rpb[h, dr][dc]
                else:
                    v = np.zeros((64, 64), np.float32)
                if interior and not (3 <= dr <= 10):
                    tb[kl, :, h, u, :] = NEG
                else:
                    tb[kl, :, h, u, :] = np.where(colv, v, NEG)
    return np.ascontiguousarray(tb.reshape(128, 6, 22, 64))


_CACHE = {}


def _get_nc(n_layers=2, taps=()):
    key = (n_layers, tuple(taps))
    if key not in _CACHE:
        _CACHE[key] = K(n_layers, taps).build()
    return _CACHE[key]


def make_in_maps(inputs, n_layers=2):
    f = lambda a: np.asarray(a, np.float32)
    x, c, ctx, c_ctx = f(inputs["x"]), f(inputs["c"]), f(inputs["ctx"]), f(inputs["c_ctx"])
    perm = _perm_cols()
    rows_ab = np.concatenate([np.arange(h * 64, (h + 1) * 64) for h in HP])
    shared = {}
    for l in range(n_layers):
        shared["wada%d" % l] = np.ascontiguousarray(f(inputs["w_ada"])[l])
        shared["badaT%d" % l] = _fm(f(inputs["b_ada"])[l])
        shared["gains%d" % l] = np.ascontiguousarray(np.stack(
            [_fm(f(inputs[k])[l]) for k in ("norm_mix_pre", "norm_mix_post", "norm_mlp_pre", "norm_mlp_post")], axis=1))
        shared["win%d" % l] = np.ascontiguousarray(f(inputs["w_in"])[l][:, perm])
        gq, gk = f(inputs["qnorm_b"])[l], f(inputs["knorm_b"])[l]
        d = np.arange(128) % 64
        shared["bgain%d" % l] = np.ascontiguousarray(np.stack([gq[d], gq[d ^ 1], gk[d], gk[d ^ 1]], axis=1))
        shared["sinkT%d" % l] = np.ascontiguousarray(np.broadcast_to(f(inputs["sink_a"])[l][None, :], (128, 6)))
        shared["TB%d" % l] = np.ascontiguousarray(np.stack(
            [_tb_table(f(inputs["rpb_c"])[l]), _tb_table(f(inputs["rpb_c"])[l], True)], axis=1))
        shared["wbr%d" % l] = np.ascontiguousarray(np.stack(
            [f(inputs["w_br_a"])[l][rows_ab], f(inputs["w_br_b"])[l][rows_ab], f(inputs["w_br_c"])[l]], axis=0))
        shared["wout%d" % l] = np.ascontiguousarray(f(inputs["w_out"])[l])
        shared["w1_%d" % l] = np.ascontiguousarray(f(inputs["w_mlp_in"])[l])
        shared["w2_%d" % l] = np.ascontiguousarray(f(inputs["w_mlp_out"])[l])
    in_maps = []
    for core in range(8):
        b, rank = core // 4, core % 4
        tok0 = rank * NT
        m = dict(shared)
        xs = x[b, tok0:tok0 + NT, :]
        m["xT"] = np.ascontiguousarray(xs.T.reshape(8, 128, NT).transpose(1, 0, 2))
        m["ctxT"] = np.ascontiguousarray(ctx[b].T.reshape(8, 128, NCX).transpose(1, 0, 2))
        m["ccT"] = np.ascontiguousarray(np.stack([_fm(c[b]), _fm(c_ctx)], axis=2))
        C, S = _rope_tables(tok0)
        m["ropeC"], m["ropeS"] = C, S
        m["maskA"] = _mask_a(rank)
        m["rm01"] = _rm01(rank)
        in_maps.append(m)
    return in_maps


def kernel(**inputs):
    nc = _get_nc(2)
    in_maps = make_in_maps(inputs, 2)
    res = run_bass_kernel_spmd(nc, in_maps, core_ids=list(range(8)))
    out = np.zeros((2, 4 * NT, 1024), np.float32)
    for core in range(8):
        b, rank = core // 4, core % 4
        yT = np.asarray(res.results[core]["yT"])
        out[b, rank * NT:(rank + 1) * NT, :] = yT.transpose(1, 0, 2).reshape(1024, NT).T
    return out
```

```python
import numpy as np
from contextlib import ExitStack
import concourse.bass as bass
import concourse.mybir as mybir
from concourse.bass_utils import run_bass_kernel_spmd

F32 = mybir.dt.float32
BF16 = mybir.dt.bfloat16
AF = mybir.ActivationFunctionType
ALU = mybir.AluOpType

NT = 2048
NCX = 256
NTOT = NT + NCX
NEG = -30000.0
EPS = 1e-6
HP = [0, 3, 1, 4, 2, 5]
C_QA, C_QAS, C_KA, C_KAS = 0, 384, 768, 896
C_QB, C_QBS, C_KB, C_KBS = 1024, 1408, 1792, 1920
C_QC, C_KC, C_V, C_G = 2048, 2432, 2816, 3456
NWIN = 6528
O_KB, O_VB, O_KAH, O_KAT, O_VAH, O_VAT = 0, 262144, 524288, 540672, 557056, 573440
O_KCH, O_KCT, O_VCH, O_VCT, CONTRIB = 589824, 688128, 786432, 884736, 983040


class Trk:
    ENGS = ("pe", "act", "dve", "pool", "sp")
    NDSEM = 12

    def __init__(self, nc, es):
        self.nc = nc
        self.esem = {e: es.enter_context(nc.semaphore("s_" + e)) for e in self.ENGS}
        self.ecnt = {e: 0 for e in self.ENGS}
        self.dsem = {q: [es.enter_context(nc.semaphore("d_%s%d" % (q, i))) for i in range(self.NDSEM)]
                     for q in ("sp", "pool")}
        self.dval = {q: [0] * self.NDSEM for q in ("sp", "pool")}
        self.dcnt = {"sp": 0, "pool": 0}
        self.ccsem = es.enter_context(nc.semaphore("s_cc"))
        self.ccval = 0
        self.ops = []
        self.bar = None
        self.waited = {e: {} for e in self.ENGS}

    def add(self, eng, fn, r=(), w=(), dma=False, cc=False, accw=False):
        self.ops.append(dict(eng=eng, fn=fn, r=tuple(r), w=tuple(w), dma=dma, cc=cc, accw=accw))

    def flush(self):
        ops = self.ops
        self.ops = []
        if not ops:
            return
        last_w, readers = {}, {}

        def is_acc(tok):
            t0_ = tok[0] if isinstance(tok, tuple) else tok
            return isinstance(t0_, str) and t0_.startswith("+")
        for i, op in enumerate(ops):
            deps = set()
            for r in op["r"]:
                if r in last_w:
                    deps.update(last_w[r])
            for w in op["w"]:
                if w in last_w:
                    if is_acc(w) and (op["dma"] or op["accw"]):
                        deps.add(last_w[w][0])
                    else:
                        deps.update(last_w[w])
                for rd in readers.get(w, {}).values():
                    if isinstance(rd, list):
                        deps.update(rd)
                    else:
                        deps.add(rd)
            deps.discard(i)
            if op["eng"] == "pe":
                deps = {d for d in deps if ops[d]["dma"] or ops[d]["eng"] != "pe"}
            op["deps"] = deps
            for r in op["r"]:
                rr = readers.setdefault(r, {})
                if op["dma"]:
                    rr.setdefault("dma", []).append(i)
                else:
                    rr[op["eng"]] = i
            for w in op["w"]:
                if is_acc(w) and (op["dma"] or op["accw"]) and w in last_w:
                    last_w[w] = last_w[w] + [i]
                else:
                    last_w[w] = [i]
                readers[w] = {}
        for op in ops:
            op["sig"] = False
        for op in ops:
            for d in op["deps"]:
                ops[d]["sig"] = True
        last_of = {}
        for i, op in enumerate(ops):
            if not op["dma"]:
                last_of[op["eng"]] = i
        for i in last_of.values():
            ops[i]["sig"] = True
        for op in ops:
            if op["cc"]:
                self.ccval += 1
                op["done"] = (self.ccsem, self.ccval)
                op["pre"] = None
            elif op["dma"]:
                q = op["eng"]
                k = self.dcnt[q] % self.NDSEM
                self.dcnt[q] += 1
                prev = self.dval[q][k]
                self.dval[q][k] += 16
                op["done"] = (self.dsem[q][k], self.dval[q][k])
                op["pre"] = (self.dsem[q][k], prev) if prev > 0 else None
            elif op["sig"]:
                self.ecnt[op["eng"]] += 1
                op["done"] = (self.esem[op["eng"]], self.ecnt[op["eng"]])
                op["pre"] = None
            else:
                op["done"] = None
                op["pre"] = None
        per = {e: [] for e in self.ENGS}
        for op in ops:
            per[op["eng"]].append(op)
        bar = self.bar
        waited = self.waited

        def emit(ename, e):
            wd = waited[ename]

            def wait(sem, val):
                key = id(sem)
                if wd.get(key, 0) < val:
                    e.wait_ge(sem, val)
                    wd[key] = val
            if bar and per[ename]:
                for sem, val in bar:
                    wait(sem, val)
            for op in per[ename]:
                for d in op["deps"]:
                    sem, val = ops[d]["done"]
                    wait(sem, val)
                if op["pre"] is not None:
                    wait(*op["pre"])
                if op["fn"] is None:
                    continue
                ins = op["fn"](e)
                if op["done"] is not None:
                    sem, val = op["done"]
                    if op["cc"]:
                        ins.then_inc(sem, 1)
                    elif op["dma"]:
                        ins.then_inc(sem, 16)
                    else:
                        ins.then_inc(sem, 1)

        with self.nc.Block() as block:
            @block.sync
            def _(e):
                emit("sp", e)

            @block.scalar
            def _(e):
                emit("act", e)

            @block.vector
            def _(e):
                emit("dve", e)

            @block.gpsimd
            def _(e):
                emit("pool", e)

            @block.tensor
            def _(e):
                emit("pe", e)
        nb = []
        for e in self.ENGS:
            if self.ecnt[e] > 0:
                nb.append((self.esem[e], self.ecnt[e]))
        for q in ("sp", "pool"):
            for k in range(self.NDSEM):
                if self.dval[q][k] > 0:
                    nb.append((self.dsem[q][k], self.dval[q][k]))
        if self.ccval > 0:
            nb.append((self.ccsem, self.ccval))
        self.bar = nb

    def final_wait(self):
        bar = self.bar
        with self.nc.Block() as block:
            @block.sync
            def _(e):
                for sem, val in bar:
                    e.wait_ge(sem, val)


def bcast(ap, pos, n):
    l = [list(x) for x in ap.ap]
    l.insert(pos, [0, n])
    return bass.AP(ap.tensor, ap.offset, l)


def dview(t, off, dims):
    return bass.AP(t, off, [list(d) for d in dims])


class K:
    def __init__(self, n_layers=2, taps=(), stop=None):
        self.nl = n_layers
        self.taps = set(taps)
        self.stop = stop
        self.nc = bass.Bass("TRN2", target_bir_lowering=False)
        self.uid = 0

    def din(self, name, shape, dt=F32):
        return self.nc.dram_tensor(name, list(shape), dt, kind="ExternalInput")

    def dscr(self, name, shape, dt=BF16, tap=False):
        if tap and name in self.taps:
            return self.nc.dram_tensor(name, list(shape), dt, kind="ExternalOutput")
        return self.nc.dram_tensor(name, list(shape), dt)

    def sb(self, es, name, shape, dt):
        self.uid += 1
        return es.enter_context(self.nc.sbuf_tensor("%s_%d" % (name, self.uid), list(shape), dt))

    def mm(self, out, lhsT, rhs, start, stop, r, w):
        self.T.add("pe", lambda e: e.matmul(out, lhsT=lhsT, rhs=rhs, start=start, stop=stop), r, w)

    def act(self, out, in_, func, r, w, bias=None, scale=None):
        kw = {}
        if bias is not None:
            kw["bias"] = bias
        if scale is not None:
            kw["scale"] = scale
        self.T.add("act", lambda e: e.activation(out=out, in_=in_, func=func, **kw), r, w)

    def tt(self, eng, out, in0, in1, op, r, w):
        self.T.add(eng, lambda e: e.tensor_tensor(out=out, in0=in0, in1=in1, op=op), r, w)

    def ts(self, eng, out, in0, s1, s2, op0, op1, r, w):
        if op1 is None:
            self.T.add(eng, lambda e: e.tensor_scalar(out=out, in0=in0, scalar1=s1, scalar2=None, op0=op0), r, w)
        else:
            self.T.add(eng, lambda e: e.tensor_scalar(out=out, in0=in0, scalar1=s1, scalar2=s2, op0=op0, op1=op1), r, w)

    def stt(self, out, in0, scalar, in1, op0, op1, r, w):
        self.T.add("dve", lambda e: e.scalar_tensor_tensor(out=out, in0=in0, scalar=scalar, in1=in1, op0=op0, op1=op1), r, w)

    def cp(self, eng, out, in_, r, w):
        self.T.add(eng, lambda e: e.tensor_copy(out=out, in_=in_), r, w)

    def memset(self, eng, ap, val, w):
        self.T.add(eng, lambda e: e.memset(ap, val), (), w)

    def recip(self, out, in_, r, w):
        self.T.add("dve", lambda e: e.reciprocal(out=out, in_=in_), r, w)

    def dma(self, out, in_, r, w, q="sp"):
        self.T.add(q, lambda e: e.dma_start(out=out, in_=in_), r, w, dma=True)

    def newbank(self):
        if self.split:
            b = self.bankc % self.nsb
        else:
            b = self.bankc % 8
        self.bankc += 1
        return b

    def newacc(self):
        b = self.nsb + self.accc % (8 - self.nsb)
        self.accc += 1
        return b

    def alloc_norm(self, es, nmax):
        self.sq_buf = self.sb(es, "sqbuf", [128, 8, nmax], BF16)
        self.lnt = self.sb(es, "lnt", [128, nmax], F32)
        self.rstd = self.sb(es, "rstd", [128, nmax], F32)
        self.xn = self.sb(es, "xn", [128, 8, nmax], F32)

    def cast(self, dst, src, r, w):
        engs = self.cast_engs
        e = engs[self.cast_i % len(engs)]
        self.cast_i += 1
        if e == "act":
            self.T.add("act", lambda e_: e_.activation(out=dst, in_=src, func=AF.Copy), r, w, accw=True)
        else:
            self.T.add(e, lambda e_: e_.tensor_copy(out=dst, in_=src), r, w, accw=True)

    def load_w(self, dst, src, tok_w, nk, ncols):
        CH = 1024
        per = max(1, CH // ncols)
        k = 0
        while k < nk:
            kk = min(per, nk - k)
            if ncols > CH:
                assert per == 1
                c = 0
                while c < ncols:
                    cc = min(CH, ncols - c)
                    s = self.stg_i % self.nstg
                    self.stg_i += 1
                    st = self.stg[:, s, 0:cc]
                    self.dma(st, src[:, k, c:c + cc], (), [("stg", self.nstg, s)])
                    self.cast(dst[:, k, c:c + cc], st, [("stg", self.nstg, s)], [tok_w])
                    c += cc
            else:
                s = self.stg_i % self.nstg
                self.stg_i += 1
                st = self.stg[:, s, 0:kk * ncols].rearrange("p (k c) -> p k c", k=kk)
                self.dma(st, src[:, k:k + kk, :], (), [("stg", self.nstg, s)])
                self.cast(dst[:, k:k + kk, :], st, [("stg", self.nstg, s)], [tok_w])
            k += kk

    def rstd_of(self, es_tmp, src, nk, n, div, tokr, name):
        sq = self.sq_buf[:, 0:nk, 0:n]
        self.act(sq, src, AF.Square, [tokr], ["sqbuf"])
        b = self.newbank()
        for k in range(nk):
            self.mm(self.ps[b][:, 0:n], self.ones[:], sq[:, k, :], k == 0, k == nk - 1, ["sqbuf", "ones"], [("ps", b)])
        self.act(self.lnt[:, 0:n], self.ps[b][:, 0:n], AF.Ln, [("ps", b)], ["lnt"], bias=self.epsc[:, 0:1], scale=1.0 / div)
        self.act(self.rstd[:, 0:n], self.lnt[:, 0:n], AF.Exp, ["lnt"], ["rstd"], scale=-0.5)
        return self.rstd[:, 0:n]

    def norm_mod(self, xch, n, GG, SH, hT, tokx, tokh):
        rs = self.rstd_of(None, xch[:, :, 0:n], 8, n, 1024.0, tokx, "nm")
        self.tt("dve", self.xn[:, :, 0:n], xch[:, :, 0:n], bcast(rs, 1, 8), ALU.mult, [tokx, "rstd"], ["xn"])
        for k in range(8):
            self.ts("dve", hT[:, k, 0:n], self.xn[:, k, 0:n], GG[:, k:k + 1], SH[:, k:k + 1], ALU.mult, ALU.add,
                    ["xn", "mods"], [tokh])

    def build(self):
        nc = self.nc
        NL = self.nl
        xT = self.din("xT", [128, 8, NT])
        ctxT = self.din("ctxT", [128, 8, NCX])
        ccT = self.din("ccT", [128, 8, 2])
        ropeC = self.din("ropeC", [128, NT])
        ropeS = self.din("ropeS", [128, NT])
        maskA = self.din("maskA", [128, 10, 128])
        rm01 = self.din("rm01", [128, 44, 8])
        L = []
        for l in range(NL):
            d = dict(
                wada=self.din("wada%d" % l, [1024, 6144]),
                badaT=self.din("badaT%d" % l, [128, 48]),
                gains=self.din("gains%d" % l, [128, 4, 8]),
                win=self.din("win%d" % l, [1024, NWIN]),
                bgain=self.din("bgain%d" % l, [128, 4]),
                sinkT=self.din("sinkT%d" % l, [128, 6]),
                TB=self.din("TB%d" % l, [128, 2, 6 * 22 * 64]),
                wbr=self.din("wbr%d" % l, [3, 384, 1024]),
                wout=self.din("wout%d" % l, [1024, 1024]),
                w1=self.din("w1_%d" % l, [1024, 4096]),
                w2=self.din("w2_%d" % l, [4096, 1024]),
            )
            L.append(d)
        yT = nc.dram_tensor("yT", [128, 8, NT], F32, kind="ExternalOutput")
        xs = [self.dscr("xs%d" % i, [128, 8, NTOT], F32, tap=True) for i in range(2 * NL)]
        q_d = [self.dscr("q_d%d" % i, [128, 3, NTOT], BF16, tap=True) for i in range(3)]
        kaT_d = self.dscr("kaT_d", [128, NTOT], BF16, tap=True)
        va_d = self.dscr("va_d", [NTOT, 128], BF16, tap=True)
        kcT_d = self.dscr("kcT_d", [128, 3, NTOT], BF16, tap=True)
        vc_d = self.dscr("vc_d", [NTOT, 384], BF16, tap=True)
        kbcT_d = self.dscr("kbcT_d", [128, NCX], BF16, tap=True)
        vbc_d = self.dscr("vbc_d", [NCX, 128], BF16, tap=True)
        contrib = self.dscr("contrib", [960, 1024], BF16, tap=True)
        gathB = self.dscr("gathB", [4 * 512, 1024], BF16)
        gathH = self.dscr("gathH", [4 * 448, 1024], BF16)

        class G:
            @staticmethod
            def view(r, off, dims):
                if off < 524288:
                    return dview(gathB, r * 524288 + off, dims)
                return dview(gathH, r * 458752 + (off - 524288), dims)
        gath = G
        self.gathB, self.gathH = gathB, gathH
        o_d = [self.dscr("o_d%d" % i, [128, 3, NTOT], BF16, tap=True) for i in range(3)]
        mods_d = self.dscr("mods_d", [128, NL, 48, 2], F32, tap=True)

        with ExitStack() as es0:
            self.T = Trk(nc, es0)
            T = self.T
            self.bankc = 0
            self.accc = 0
            self.nsb = 4
            self.split = False
            self.stg_i = 0
            self.cast_i = 0
            self.cast_engs = ("act",)
            self.psall = es0.enter_context(nc.psum_tensor("psall", [128, 4096], F32))
            self.ps = [self.psall[:, i * 512:(i + 1) * 512] for i in range(8)]
            self.ones = self.sb(es0, "ones", [128, 128], BF16)
            self.bd = self.sb(es0, "bd", [128, 128], BF16)
            self.epsc = self.sb(es0, "epsc", [128, 1], F32)
            self.modsb = self.sb(es0, "modsb", [128, NL, 48, 2], F32)
            self.dvec = self.sb(es0, "dvec", [128, NL, 2, 6, 8], F32)
            self.gn = self.sb(es0, "gn", [128, NL, 4, 8], F32)
            self.stg = self.sb(es0, "stg", [128, 3, 1024], F32)
            self.nstg = 3

            with ExitStack() as es:
                self.memset("pool", self.ones[:], 1.0, ["ones"])
                self.memset("pool", self.bd[:], 0.0, ["bd"])
                self.memset("pool", self.bd[0:64, 0:64], 1.0, ["bd"])
                self.memset("pool", self.bd[64:128, 64:128], 1.0, ["bd"])
                self.memset("pool", self.epsc[:], EPS, ["epsc"])
                cc_s = self.sb(es, "cc_s", [128, 8, 2], F32)
                sil = self.sb(es, "sil", [128, 8, 2], F32)
                self.dma(cc_s[:], ccT.ap(), (), ["cc_s"])
                self.act(sil[:], cc_s[:], AF.Silu, ["cc_s"], ["sil"])
                wa = self.sb(es, "wa", [128, 2, 8, 768], F32)
                bad = self.sb(es, "bad", [128, NL, 48], F32)
                for l in range(NL):
                    self.dma(bad[:, l, :], L[l]["badaT"].ap(), (), ["bad"])
                    self.dma(self.gn[:, l], L[l]["gains"].ap(), (), ["gn"])
                    wsrc = L[l]["wada"].ap().rearrange("(k p) n -> p k n", p=128)
                    b = self.newbank()
                    for g in range(8):
                        s = g % 2
                        self.dma(wa[:, s], wsrc[:, :, g * 768:(g + 1) * 768], (), [("wa", s)])
                        for mm_ in range(6):
                            m = g * 6 + mm_
                            for k in range(8):
                                self.mm(self.ps[b][:, 2 * m:2 * m + 2], wa[:, s, k, mm_ * 128:(mm_ + 1) * 128], sil[:, k, :],
                                        k == 0, k == 7, [("wa", s), "sil"], [("ps", b)])
                    self.tt("dve", self.modsb[:, l], self.ps[b][:, 0:96].rearrange("p (m t) -> p m t", t=2),
                            bcast(bad[:, l, :], 2, 2), ALU.add, [("ps", b), "bad"], ["modsb"])
                    for t in range(2):
                        mv = self.modsb[:, l, :, t]
                        dv = self.dvec[:, l, t]
                        self.stt(dv[:, 0, :], mv[:, 8:16], 1.0, self.gn[:, l, 0, :], ALU.add, ALU.mult, ["modsb", "gn"], ["mods"])
                        self.cp("dve", dv[:, 1, :], mv[:, 0:8], ["modsb"], ["mods"])
                        self.tt("dve", dv[:, 2, :], mv[:, 16:24], self.gn[:, l, 1, :], ALU.mult, ["modsb", "gn"], ["mods"])
                        self.stt(dv[:, 3, :], mv[:, 32:40], 1.0, self.gn[:, l, 2, :], ALU.add, ALU.mult, ["modsb", "gn"], ["mods"])
                        self.cp("dve", dv[:, 4, :], mv[:, 24:32], ["modsb"], ["mods"])
                        self.tt("dve", dv[:, 5, :], mv[:, 40:48], self.gn[:, l, 3, :], ALU.mult, ["modsb", "gn"], ["mods"])
                if "mods_d" in self.taps:
                    self.dma(mods_d.ap(), self.modsb[:], ["modsb"], ["mods_d"])
                T.flush()

            for l in range(NL):
                last = (l == NL - 1)
                if l == 0:
                    xin_lat = lambda c0, n: xT.ap()[:, :, c0:c0 + n]
                    xin_ctx = ctxT.ap()
                else:
                    xin_lat = (lambda xs_: (lambda c0, n: xs_.ap()[:, :, c0:c0 + n]))(xs[2 * l - 1])
                    xin_ctx = xs[2 * l - 1].ap()[:, :, NT:NTOT]
                if self.stop == "mods":
                    break
                self.phase_proj(L[l], l, last, xin_lat, xin_ctx, ropeC, ropeS, q_d, kaT_d, va_d, kcT_d, vc_d, kbcT_d, vbc_d, contrib)
                if self.stop == "proj%d" % l or (self.stop or "").startswith("proj0"):
                    break
                self.phase_gather(contrib, gath)
                self.phase_attn_a(L[l], l, last, q_d[0], kaT_d, va_d, gath, maskA, o_d[0])
                if self.stop == "attn_a%d" % l:
                    break
                self.phase_attn_b(L[l], l, last, q_d[1], kbcT_d, vbc_d, gath, o_d[1])
                if self.stop == "attn_b%d" % l:
                    break
                self.phase_attn_c(L[l], l, last, q_d[2], kcT_d, vc_d, gath, rm01, o_d[2])
                if self.stop == "attn_c%d" % l:
                    break
                self.phase_mix(L[l], l, last, xin_lat, xin_ctx, o_d, xs[2 * l])
                if self.stop == "mix%d" % l:
                    break
                out_lat = yT if last else xs[2 * l + 1]
                self.phase_mlp(L[l], l, last, xs[2 * l], out_lat, xs[2 * l + 1])
                if self.stop == "mlp%d" % l:
                    break
            T.final_wait()
        return nc

    def phase_proj(self, Ld, l, last, xin_lat, xin_ctx, ropeC, ropeS, q_d, kaT_d, va_d, kcT_d, vc_d, kbcT_d, vbc_d, contrib):
        T = self.T
        with ExitStack() as es:
            self.split = False
            self.alloc_norm(es, 512)
            hT = self.sb(es, "hT", [128, 8, NTOT], BF16)
            xch = self.sb(es, "xch", [128, 2, 8, 512], F32)
            rC = self.sb(es, "rC", [128, NT], F32)
            rS = self.sb(es, "rS", [128, NT], F32)
            bg = self.sb(es, "bg", [128, 4], F32)
            wt = self.sb(es, "wt", [128, 2, 8, 640], BF16)
            ost = self.sb(es, "ost", [128, 4, 640], BF16)
            t1 = self.sb(es, "t1", [128, 2, 512], F32)
            t2 = self.sb(es, "t2", [128, 2, 512], F32)
            sqh = self.sb(es, "sqh", [128, 2, 512], BF16)
            lnh = self.sb(es, "lnh", [128, 2, 512], F32)
            rsh = self.sb(es, "rsh", [128, 2, 512], F32)
            self.dma(rC[:], ropeC.ap(), (), ["rC"])
            self.dma(rS[:], ropeS.ap(), (), ["rS"])
            self.dma(bg[:], Ld["bgain"].ap(), (), ["bg"])
            chunks = [(i * 512, 512, False) for i in range(4)] + [(NT, NCX, True)]
            for ci, (c0, n, isc) in enumerate(chunks):
                s = ci % 2
                src = xin_ctx if isc else xin_lat(c0, n)
                self.dma(xch[:, s, :, 0:n], src, (), [("xch", s)])
                dv = self.dvec[:, l, 1 if isc else 0]
                self.norm_mod(xch[:, s], n, dv[:, 0, :], dv[:, 1, :], hT[:, :, c0:c0 + n], ("xch", s), "hT")
            if self.stop == "proj0a":
                T.flush()
                return
            win = Ld["win"].ap().rearrange("(k p) n -> p k n", p=128)
            units = []
            units.append(("KB", 0, [C_KB, C_KBS]))
            units.append(("V", 0, None))
            units.append(("KA", 0, [C_KA, C_KAS]))
            for mi in range(3):
                units.append(("KC", mi, [C_KC + mi * 128]))
            n_kv_units = len(units)
            for mi in range(3):
                units.append(("QB", mi, [C_QB + mi * 128, C_QBS + mi * 128]))
            for mi in range(3):
                units.append(("QA", mi, [C_QA + mi * 128, C_QAS + mi * 128]))
            for mi in range(3):
                units.append(("QC", mi, [C_QC + mi * 128]))
            ctr = dict(ost=0, t=0)

            ctoks = []

            def store(srcs_dsts, tok):
                for dst, src in srcs_dsts:
                    ctr["st"] = ctr.get("st", 0) + 1
                    wtk = ("dout", ctr["st"])
                    if dst.tensor.name == contrib.name:
                        ctoks.append(wtk)
                    self.dma(dst, src, [tok], [wtk])

            if self.stop and self.stop.startswith("proj0u"):
                sel = [int(x) for x in self.stop[6:].split("_")]
                units = [units[i] for i in sel]
            def load_unit(ui):
                kind, mi, cols = units[ui]
                ws = ui % 2
                wtok = ("+wt", ws)
                if kind == "V":
                    self.load_w(wt[:, ws, :, 0:640], win[:, :, C_V:C_V + 640], wtok, 8, 640)
                else:
                    for j, c in enumerate(cols):
                        self.load_w(wt[:, ws, :, j * 128:(j + 1) * 128], win[:, :, c:c + 128], wtok, 8, 128)
            load_unit(0)
            for ui, (kind, mi, cols) in enumerate(units):
                ws = ui % 2
                wtok = ("+wt", ws)
                if ui == n_kv_units and not (self.stop or "").startswith("proj0u"):
                    self.emit_gather(contrib, list(ctoks))
                if ui + 1 < len(units):
                    load_unit(ui + 1)
                if kind == "V":
                    for tile in range(NTOT // 128):
                        t0 = tile * 128
                        b0, b1 = self.newbank(), self.newbank()
                        for k in range(8):
                            self.mm(self.ps[b0][:, 0:512], hT[:, k, t0:t0 + 128], wt[:, ws, k, 0:512], k == 0, k == 7, ["hT", wtok], [("ps", b0)])
                        for k in range(8):
                            self.mm(self.ps[b1][:, 0:128], hT[:, k, t0:t0 + 128], wt[:, ws, k, 512:640], k == 0, k == 7, ["hT", wtok], [("ps", b1)])
                        o = ctr["ost"] % 4
                        ctr["ost"] += 1
                        otok = ("ost", o)
                        self.act(ost[:, o, 0:512], self.ps[b0][:, 0:512], AF.Copy, [("ps", b0)], [otok])
                        self.cp("dve", ost[:, o, 512:640], self.ps[b1][:, 0:128], [("ps", b1)], [otok])
                        cb = contrib
                        dl = [(va_d.ap()[t0:t0 + 128, :], ost[:, o, 0:128]),
                              (vc_d.ap()[t0:t0 + 128, :], ost[:, o, 256:640])]
                        if tile < 16:
                            dl.append((dview(cb, O_VB + t0 * 128, [[128, 128], [1, 128]]), ost[:, o, 128:256]))
                            if tile == 0:
                                dl.append((dview(cb, O_VAH, [[128, 128], [1, 128]]), ost[:, o, 0:128]))
                            if tile == 15:
                                dl.append((dview(cb, O_VAT, [[128, 128], [1, 128]]), ost[:, o, 0:128]))
                            if tile < 2:
                                dl.append((dview(cb, O_VCH + tile * 128 * 384, [[384, 128], [1, 384]]), ost[:, o, 256:640]))
                            if tile >= 14:
                                dl.append((dview(cb, O_VCT + (tile - 14) * 128 * 384, [[384, 128], [1, 384]]), ost[:, o, 256:640]))
                        else:
                            dl.append((vbc_d.ap()[t0 - NT:t0 - NT + 128, :], ost[:, o, 128:256]))
                        store(dl, otok)
                    continue
                for ci, (c0, n, isc) in enumerate(chunks):
                    if isc and last and kind in ("QA", "QB", "QC"):
                        continue
                    hs = [hT[:, k, c0:c0 + n] for k in range(8)]
                    bq = self.newbank()
                    for k in range(8):
                        self.mm(self.ps[bq][:, 0:n], wt[:, ws, k, 0:128], hs[k], k == 0, k == 7, ["hT", wtok], [("ps", bq)])
                    pq = self.ps[bq][:, 0:n]
                    o = ctr["ost"] % 4
                    ctr["ost"] += 1
                    otok = ("ost", o)
                    oo = ost[:, o, 0:n]
                    need_sw = (len(cols) == 2) and not isc
                    if need_sw:
                        bs = self.newbank()
                        for k in range(8):
                            self.mm(self.ps[bs][:, 0:n], wt[:, ws, k, 128:256], hs[k], k == 0, k == 7, ["hT", wtok], [("ps", bs)])
                        psw = self.ps[bs][:, 0:n]
                    tt_ = ctr["t"] % 2
                    ctr["t"] += 1
                    a1, a2 = t1[:, tt_, 0:n], t2[:, tt_, 0:n]
                    k1, k2 = ("t1", tt_), ("t2", tt_)
                    if kind in ("QA", "KA"):
                        if isc:
                            self.act(oo, pq, AF.Copy, [("ps", bq)], [otok])
                        else:
                            self.tt("dve", a1, pq, rC[:, c0:c0 + n], ALU.mult, [("ps", bq), "rC"], [k1])
                            self.tt("dve", a2, psw, rS[:, c0:c0 + n], ALU.mult, [("ps", bs), "rS"], [k2])
                            self.tt("dve", oo, a1, a2, ALU.add, [k1, k2], [otok])
                    elif kind in ("QB", "KB"):
                        gi = 0 if kind == "QB" else 2
                        self.act(sqh[:, tt_, 0:n], pq, AF.Square, [("ps", bq)], [("sqh", tt_)])
                        bss = self.newbank()
                        self.mm(self.ps[bss][:, 0:n], self.bd[:], sqh[:, tt_, 0:n], True, True, [("sqh", tt_), "bd"], [("ps", bss)])
                        self.act(lnh[:, tt_, 0:n], self.ps[bss][:, 0:n], AF.Ln, [("ps", bss)], [("lnh", tt_)], bias=self.epsc[:, 0:1], scale=1.0 / 64)
                        self.act(rsh[:, tt_, 0:n], lnh[:, tt_, 0:n], AF.Exp, [("lnh", tt_)], [("rsh", tt_)], scale=-0.5)
                        if isc:
                            self.stt(oo, pq, bg[:, gi:gi + 1], rsh[:, tt_, 0:n], ALU.mult, ALU.mult, [("ps", bq), "bg", ("rsh", tt_)], [otok])
                        else:
                            self.stt(a1, pq, bg[:, gi:gi + 1], rC[:, c0:c0 + n], ALU.mult, ALU.mult, [("ps", bq), "bg", "rC", ("sqh", tt_)], [k1])
                            self.stt(a2, psw, bg[:, gi + 1:gi + 2], rS[:, c0:c0 + n], ALU.mult, ALU.mult, [("ps", bs), "bg", "rS"], [k2])
                            self.tt("dve", a1, a1, a2, ALU.add, [k1, k2], [k1])
                            self.tt("dve", oo, a1, rsh[:, tt_, 0:n], ALU.mult, [k1, ("rsh", tt_)], [otok])
                    else:
                        self.act(oo, pq, AF.Copy, [("ps", bq)], [otok])
                    dl = []
                    cb = contrib
                    if kind == "QA":
                        dl.append((q_d[0].ap()[:, mi, c0:c0 + n], oo))
                    elif kind == "QB":
                        dl.append((q_d[1].ap()[:, mi, c0:c0 + n], oo))
                    elif kind == "QC":
                        dl.append((q_d[2].ap()[:, mi, c0:c0 + n], oo))
                    elif kind == "KA":
                        dl.append((kaT_d.ap()[:, c0:c0 + n], oo))
                        if ci == 0:
                            dl.append((dview(cb, O_KAH, [[128, 128], [1, 128]]), ost[:, o, 0:128]))
                        if ci == 3:
                            dl.append((dview(cb, O_KAT, [[128, 128], [1, 128]]), ost[:, o, 384:512]))
                    elif kind == "KB":
                        if isc:
                            dl.append((kbcT_d.ap(), oo))
                        else:
                            dl.append((dview(cb, O_KB + c0, [[2048, 128], [1, n]]), oo))
                    elif kind == "KC":
                        dl.append((kcT_d.ap()[:, mi, c0:c0 + n], oo))
                        if ci == 0:
                            dl.append((dview(cb, O_KCH + mi * 256, [[768, 128], [1, 256]]), ost[:, o, 0:256]))
                        if ci == 3:
                            dl.append((dview(cb, O_KCT + mi * 256, [[768, 128], [1, 256]]), ost[:, o, 256:512]))
                    store(dl, otok)
            T.flush()

    def phase_gather(self, contrib, gath):
        return

    def emit_gather(self, contrib, rtoks):
        T = self.T
        gB, gH = self.gathB, self.gathH
        T.add("pool", lambda e: e.collective_compute("AllGather", ALU.bypass, replica_groups=[[0, 1, 2, 3], [4, 5, 6, 7]],
                                                     ins=[contrib.ap()[0:512, :]], outs=[gB.ap()]), rtoks, ["gathB"], cc=True)
        T.add("pool", lambda e: e.collective_compute("AllGather", ALU.bypass, replica_groups=[[0, 1, 2, 3], [4, 5, 6, 7]],
                                                     ins=[contrib.ap()[512:960, :]], outs=[gH.ap()]), rtoks, ["gathH"], cc=True)

    def attn_fin(self, bank, n, base, out_ap, rc, rctok, extra=None, extra_tok=None, shape3=None, act_recip=True):
        ob = 64 - base
        k = self.fc_i % 2
        self.fc_i += 1
        fct = ("fc", k)
        self.cp("dve", self.fc[:, k, 0:n], self.ps[bank][:, 0:n], [("ps", bank)], [fct])
        den = self.fc[ob:ob + 64, k, 0:n]
        num = self.fc[base:base + 64, k, 0:n]
        rcp = rc[base:base + 64, 0:n]
        tmp = rc[ob:ob + 64, 0:n]
        if shape3 is not None:
            a, b_ = shape3
            den = den.rearrange("p (a b) -> p a b", a=a)
            num = num.rearrange("p (a b) -> p a b", a=a)
            rcp = rcp.rearrange("p (a b) -> p a b", a=a)
            tmp = tmp.rearrange("p (a b) -> p a b", a=a)
        src, srct = den, fct
        if extra is not None:
            self.tt("dve", tmp, den, extra, ALU.add, [fct, extra_tok], [rctok])
            src, srct = tmp, rctok
        if act_recip:
            self.act(rcp, src, AF.Ln, [srct], [rctok])
            self.act(rcp, rcp, AF.Exp, [rctok], [rctok], scale=-1.0)
        else:
            self.recip(rcp, src, [srct], [rctok])
        self.tt("dve", out_ap, num, rcp, ALU.mult, [fct, rctok], ["oT"])

    def pipeline(self, items, look=3, group=1):
        groups = [items[i:i + group] for i in range(0, len(items), group)]
        banks = {}

        def qks(gi):
            for k, itm in enumerate(groups[gi]):
                banks[(gi, k)] = self.newbank()
                itm[0](banks[(gi, k)])
        for gi in range(min(look, len(groups))):
            qks(gi)
        for gi in range(len(groups)):
            if gi + look < len(groups):
                qks(gi + look)
            for k, itm in enumerate(groups[gi]):
                itm[1](banks[(gi, k)])
            for k, itm in enumerate(groups[gi]):
                itm[2]()
                if itm[3] is not None:
                    itm[3]()

    def load_vaug(self, dst, src, tokn):
        self.dma(dst, src, (), [tokn])

    def phase_attn_a(self, Ld, l, last, qa_d, kaT_d, va_d, gath, maskA, oa_d):
        T = self.T
        with ExitStack() as es:
            self.split = True
            self.nsb = 6
            self.bankc = 0
            QT = self.sb(es, "QT", [128, 3, NTOT], BF16)
            KT = self.sb(es, "KT", [128, NTOT], BF16)
            KH = self.sb(es, "KH", [128, 8, 128], BF16)
            VA = self.sb(es, "VA", [128, 18, 2, 128], BF16)
            VH = self.sb(es, "VH", [128, 8, 2, 128], BF16)
            MA = self.sb(es, "MA", [128, 10, 128], F32)
            ES = self.sb(es, "ES", [128, 6, 128], F32)
            sk = self.sb(es, "sk", [128, 6], F32)
            PT = self.sb(es, "PT", [128, 8, 512], BF16)
            tm = self.sb(es, "tm", [128, 4, 384], F32)
            rc = self.sb(es, "rc", [128, 2, 512], F32)
            self.fc = self.sb(es, "fc", [128, 2, 512], F32)
            self.fc_i = 0
            oT = self.sb(es, "oT", [128, 3, NTOT], BF16)
            nq = NT if last else NTOT
            self.dma(QT[:, :, 0:nq], qa_d.ap()[:, :, 0:nq], (), ["+QT"])
            self.dma(KT[:], kaT_d.ap(), (), ["+KT"])
            self.memset("pool", VA[:], 1.0, ["+VA"])
            self.memset("pool", VH[:], 1.0, ["+VH"])
            vsrc = va_d.ap().rearrange("(t p) c -> p t c", p=128)
            self.dma(VA[:, :, 0, 0:64], vsrc[:, :, 0:64], (), ["+VA"])
            self.dma(VA[:, :, 1, 64:128], vsrc[:, :, 64:128], (), ["+VA"])
            for r in range(4):
                pass
                self.dma(KH[:, r, :], gath.view(r, O_KAT, [[128, 128], [1, 128]]), ["gath"], ["+KH"])
                self.dma(KH[:, 4 + r, :], gath.view(r, O_KAH, [[128, 128], [1, 128]]), ["gath"], ["+KH"])
                self.dma(VH[:, r, 0, 0:64], gath.view(r, O_VAT, [[128, 128], [1, 64]]), ["gath"], ["+VH"])
                self.dma(VH[:, r, 1, 64:128], gath.view(r, O_VAT + 64, [[128, 128], [1, 64]]), ["gath"], ["+VH"])
                self.dma(VH[:, 4 + r, 0, 0:64], gath.view(r, O_VAH, [[128, 128], [1, 64]]), ["gath"], ["+VH"])
                self.dma(VH[:, 4 + r, 1, 64:128], gath.view(r, O_VAH + 64, [[128, 128], [1, 64]]), ["gath"], ["+VH"])
            self.dma(MA[:], maskA.ap(), (), ["MA"])
            self.dma(sk[:], Ld["sinkT"].ap(), (), ["sk"])
            self.act(sk[:], sk[:], AF.Exp, ["sk"], ["sk"])
            self.cp("dve", ES[:], bcast(sk[:], 2, 128), ["sk"], ["ES"])
            it = dict(p=0, t=0, r=0)
            items = []

            def run(qcols, nqc, base, tiles, shape3, out_ap, es_ap):
                n = 3 * nqc
                rhs = QT[base:base + 64, :, qcols:qcols + nqc]
                bo = self.newacc()
                nt = len(tiles)
                for ti, (kap, vap, mk) in enumerate(tiles):
                    p = it["p"] % 8
                    it["p"] += 1
                    t = it["t"] % 4
                    if mk is not None:
                        it["t"] += 1

                    def qk(b, kap=kap):
                        self.mm(self.ps[b][:, 0:n].rearrange("p (a b) -> p a b", a=3), kap, rhs, True, True, ["+QT", "+KT", "+KH"], [("ps", b)])

                    def sm(b, mk=mk, p=p, t=t):
                        pt = PT[:, p, 0:n]
                        if mk is not None:
                            self.stt(tm[:, t, 0:n].rearrange("p (a b) -> p a b", a=3), self.ps[b][:, 0:n].rearrange("p (a b) -> p a b", a=3),
                                     0.125, bcast(mk, 1, 3), ALU.mult, ALU.add, [("ps", b), "MA"], [("tm", t)])
                            self.act(pt, tm[:, t, 0:n], AF.Exp, [("tm", t)], [("PT", p)])
                        else:
                            self.act(pt, self.ps[b][:, 0:n], AF.Exp, [("ps", b)], [("PT", p)], scale=0.125)

                    def pv(vap=vap, p=p, ti=ti):
                        self.mm(self.ps[bo][:, 0:n], vap, PT[:, p, 0:n], ti == 0, ti == nt - 1, [("PT", p), "+VA", "+VH"], [("ps", bo)])
                    fin = None
                    if ti == nt - 1:
                        r_ = it["r"] % 2
                        it["r"] += 1

                        def fin(r_=r_):
                            self.attn_fin(bo, n, base, out_ap, rc[:, r_], ("rc", r_), extra=es_ap, extra_tok="ES", shape3=shape3)
                    items.append((qk, sm, pv, fin))

            for half in range(2):
                base = half * 64
                ob = 64 - base
                esl = ES[ob:ob + 64, 3 * half:3 * half + 3, :]
                for j in range(16):
                    tiles = []
                    if j > 0:
                        tiles.append((KT[base:base + 64, (j - 1) * 128:j * 128], VA[:, j - 1, half, :], MA[:, 0, :]))
                    else:
                        for r in range(4):
                            tiles.append((KH[base:base + 64, r, :], VH[:, r, half, :], MA[:, 2 + r, :]))
                    tiles.append((KT[base:base + 64, j * 128:(j + 1) * 128], VA[:, j, half, :], None))
                    if j < 15:
                        tiles.append((KT[base:base + 64, (j + 1) * 128:(j + 2) * 128], VA[:, j + 1, half, :], MA[:, 1, :]))
                    else:
                        for r in range(4):
                            tiles.append((KH[base:base + 64, 4 + r, :], VH[:, 4 + r, half, :], MA[:, 6 + r, :]))
                    for c in range(2):
                        tiles.append((KT[base:base + 64, NT + c * 128:NT + (c + 1) * 128], VA[:, 16 + c, half, :], None))
                    run(j * 128, 128, base, tiles, (3, 128), oT[base:base + 64, :, j * 128:(j + 1) * 128], esl)
                if not last:
                    for cq in range(2):
                        tiles = [(KT[base:base + 64, NT + c * 128:NT + (c + 1) * 128], VA[:, 16 + c, half, :], None) for c in range(2)]
                        q0 = NT + cq * 128
                        run(q0, 128, base, tiles, (3, 128), oT[base:base + 64, :, q0:q0 + 128], esl)
            self.pipeline(items, 5)
            self.nsb = 4
            self.dma(oa_d.ap()[:, :, 0:nq], oT[:, :, 0:nq], ["oT"], ["oa_d"])
            T.flush()

    def phase_attn_b(self, Ld, l, last, qb_d, kbcT_d, vbc_d, gath, ob_d):
        T = self.T
        with ExitStack() as es:
            NK = NCX + 4 * NT
            self.split = True
            self.nsb = 6
            self.bankc = 0
            QT = self.sb(es, "QTb", [128, 3, NTOT], BF16)
            KT = self.sb(es, "KTb", [128, NK], BF16)
            VB = self.sb(es, "VBb", [128, 66, 2, 128], BF16)
            PT = self.sb(es, "PTb", [128, 6, 512], BF16)
            rc = self.sb(es, "rcb", [128, 2, 512], F32)
            self.fc = self.sb(es, "fcb", [128, 2, 512], F32)
            self.fc_i = 0
            oT = self.sb(es, "oTb", [128, 3, NTOT], BF16)
            nq = NT if last else NTOT
            self.dma(QT[:, :, 0:nq], qb_d.ap()[:, :, 0:nq], (), ["+QT"])
            self.dma(KT[:, 0:NCX], kbcT_d.ap(), (), ["+KT"])
            self.memset("pool", VB[:, 0:33], 1.0, ["+VB"])
            self.memset("pool", VB[:, 33:66], 1.0, ["+VB"])
            vcs = vbc_d.ap().rearrange("(t p) c -> p t c", p=128)
            self.dma(VB[:, 0:2, 0, 0:64], vcs[:, :, 0:64], (), ["+VB"])
            self.dma(VB[:, 0:2, 1, 64:128], vcs[:, :, 64:128], (), ["+VB"])
            for r in range(4):
                pass
                self.dma(KT[:, NCX + r * NT:NCX + (r + 1) * NT], gath.view(r, O_KB, [[2048, 128], [1, 2048]]), ["gath"], ["+KT"])
                self.dma(VB[:, 2 + r * 16:2 + (r + 1) * 16, 0, 0:64], gath.view(r, O_VB, [[128, 128], [128 * 128, 16], [1, 64]]), ["gath"], ["+VB"])
                self.dma(VB[:, 2 + r * 16:2 + (r + 1) * 16, 1, 64:128], gath.view(r, O_VB + 64, [[128, 128], [128 * 128, 16], [1, 64]]), ["gath"], ["+VB"])
            it = dict(p=0, r=0)
            items = []

            def run(mi, q0, n, ktiles):
                bo = [self.newacc(), self.newacc()]
                nt = len(ktiles)
                for ti, kt in enumerate(ktiles):
                    for half in range(2):
                        base = half * 64
                        p = it["p"] % 6
                        it["p"] += 1

                        def qk(b, base=base, kt=kt):
                            self.mm(self.ps[b][:, 0:n], KT[base:base + 64, kt * 128:(kt + 1) * 128], QT[base:base + 64, mi, q0:q0 + n],
                                    True, True, ["+QT", "+KT"], [("ps", b)])

                        def sm(b, p=p, half=half):
                            if half == 1:
                                return
                            assert b % 2 == 0 and p % 2 == 0
                            src = self.psall[:, b * 512:(b + 2) * 512].rearrange("p (a c) -> p a c", a=2)[:, :, 0:n]
                            self.act(PT[:, p:p + 2, 0:n], src, AF.Exp, [("ps", b), ("ps", b + 1)], [("PT", p), ("PT", p + 1)], scale=0.125)

                        def pv(p=p, half=half, kt=kt, ti=ti):
                            self.mm(self.ps[bo[half]][:, 0:n], VB[:, kt, half, :], PT[:, p, 0:n], ti == 0, ti == nt - 1,
                                    [("PT", p), "+VB"], [("ps", bo[half])])
                        fin = None
                        if ti == nt - 1:
                            r_ = it["r"] % 2
                            it["r"] += 1

                            def fin(r_=r_, half=half, base=base):
                                self.attn_fin(bo[half], n, base, oT[base:base + 64, mi, q0:q0 + n], rc[:, r_], ("rc", r_), act_recip=False)
                        items.append((qk, sm, pv, fin))

            for mi in range(3):
                for qc in range(4):
                    run(mi, qc * 512, 512, list(range(66)))
                if not last:
                    run(mi, NT, NCX, [0, 1])
            self.pipeline(items, 2, 2)
            self.nsb = 4
            self.dma(ob_d.ap()[:, :, 0:nq], oT[:, :, 0:nq], ["oT"], ["ob_d"])
            T.flush()

    def phase_attn_c(self, Ld, l, last, qc_d, kcT_d, vc_d, gath, rm01, oc_d):
        T = self.T
        with ExitStack() as es:
            self.split = True
            self.nsb = 6
            self.bankc = 0
            QT = self.sb(es, "QTc", [128, 3, NTOT], BF16)
            KT = self.sb(es, "KTc", [128, 3, NTOT], BF16)
            KH = self.sb(es, "KHc", [128, 3, 8, 256], BF16)
            VC = self.sb(es, "VCc", [128, 18, 6, 128], BF16)
            VH = self.sb(es, "VHc", [128, 16, 6, 128], BF16)
            EB = self.sb(es, "EB", [128, 2, 6, 22, 64], BF16)
            RM = self.sb(es, "RM", [128, 44, 8], F32)
            RMb = self.sb(es, "RMb", [128, 44, 8], BF16)
            PT = self.sb(es, "PTc", [128, 8, 512], BF16)
            rc = self.sb(es, "rcc", [128, 2, 512], F32)
            self.fc = self.sb(es, "fcc", [128, 2, 512], F32)
            self.fc_i = 0
            oT = self.sb(es, "oTc", [128, 3, NTOT], BF16)
            nq = NT if last else NTOT
            self.dma(QT[:, :, 0:nq], qc_d.ap()[:, :, 0:nq], (), ["+QT"])
            self.dma(KT[:], kcT_d.ap(), (), ["+KT"])
            self.memset("pool", VC[:], 1.0, ["+VC"])
            self.memset("pool", VH[:], 1.0, ["+VH"])
            vsrc = vc_d.ap().rearrange("(t p) (h d) -> p t h d", p=128, d=64)
            for h in range(6):
                o = (h % 2) * 64
                self.dma(VC[:, :, h, o:o + 64], vsrc[:, :, h, :], (), ["+VC"])
            for r in range(4):
                pass
                self.dma(KH[:, :, r, :], gath.view(r, O_KCT, [[768, 128], [256, 3], [1, 256]]), ["gath"], ["+KH"])
                self.dma(KH[:, :, 4 + r, :], gath.view(r, O_KCH, [[768, 128], [256, 3], [1, 256]]), ["gath"], ["+KH"])
                for h in range(6):
                    o = (h % 2) * 64
                    self.dma(VH[:, 2 * r:2 * r + 2, h, o:o + 64], gath.view(r, O_VCT + h * 64, [[384, 128], [128 * 384, 2], [1, 64]]), ["gath"], ["+VH"])
                    self.dma(VH[:, 8 + 2 * r:8 + 2 * r + 2, h, o:o + 64], gath.view(r, O_VCH + h * 64, [[384, 128], [128 * 384, 2], [1, 64]]), ["gath"], ["+VH"])
            TBd = Ld["TB"].ap()
            for tbl in range(2):
                ebf = EB[:, tbl].rearrange("p h u q -> p (h u q)")
                c = 0
                while c < 6 * 22 * 64:
                    cc_ = min(1024, 6 * 22 * 64 - c)
                    sl_ = self.stg_i % self.nstg
                    self.stg_i += 1
                    st = self.stg[:, sl_, 0:cc_]
                    self.dma(st, TBd[:, tbl, c:c + cc_], (), [("stg", self.nstg, sl_)])
                    self.T.add("act", (lambda e_, o_=ebf[:, c:c + cc_], i_=st: e_.activation(out=o_, in_=i_, func=AF.Exp)),
                               [("stg", self.nstg, sl_)], ["+EB"], accw=True)
                    c += cc_
            self.dma(RM[:], rm01.ap(), (), ["RM"])
            self.cp("dve", RMb[:], RM[:], ["RM"], ["RMb"])
            it = dict(p=0, t=0, r=0)
            items = []

            def crun(q0, n, tiles, h, mi, base):
                rhs = QT[base:base + 64, mi, q0:q0 + n]
                bo = self.newacc()
                nt = len(tiles)
                for ti, (kap, vap, j, ri) in enumerate(tiles):
                    p = it["p"] % 8
                    it["p"] += 1
                    t = it["t"] % 3
                    if j is not None:
                        it["t"] += 1

                    def qk(b, kap=kap):
                        self.mm(self.ps[b][:, 0:n], kap, rhs, True, True, ["+QT", "+KT", "+KH"], [("ps", b)])

                    def sm(b, j=j, ri=ri, p=p, t=t):
                        pt = PT[:, p, 0:n]
                        self.act(pt, self.ps[b][:, 0:n], AF.Exp, [("ps", b)], [("PT", p)], scale=0.125)
                        if j is not None:
                            u0 = 14 - 2 * j
                            g_ = q0 // 512
                            tbl = 1 if g_ in (1, 2) else 0
                            pt3 = pt.rearrange("p (a b) -> p a b", a=8)
                            self.tt("dve", pt3, pt3, EB[:, tbl, h, u0:u0 + 8, :], ALU.mult, [("PT", p), "+EB"], [("PT", p)])
                            if tbl == 0:
                                self.tt("dve", pt3, pt3, bcast(RMb[:, ri, :], 2, 64), ALU.mult, [("PT", p), "RMb"], [("PT", p)])

                    def pv(vap=vap, p=p, ti=ti):
                        self.mm(self.ps[bo][:, 0:n], vap, PT[:, p, 0:n], ti == 0, ti == nt - 1, [("PT", p), "+VC", "+VH"], [("ps", bo)])
                    fin = None
                    if ti == nt - 1:
                        r_ = it["r"] % 2
                        it["r"] += 1

                        def fin(r_=r_):
                            self.attn_fin(bo, n, base, oT[base:base + 64, mi, q0:q0 + n], rc[:, r_], ("rc", r_))
                    items.append((qk, sm, pv, fin))

            for h in range(6):
                mi, half = h // 2, h % 2
                base = half * 64
                rmi = 0
                for g in range(4):
                    tiles = []
                    for j in range(8):
                        lt = g * 512 - 256 + 128 * j
                        if g == 0 and j < 2:
                            for r in range(4):
                                tiles.append((KH[base:base + 64, mi, r, j * 128:(j + 1) * 128], VH[:, 2 * r + j, h, :], j, rmi))
                                rmi += 1
                        elif g == 3 and j >= 6:
                            for r in range(4):
                                tiles.append((KH[base:base + 64, mi, 4 + r, (j - 6) * 128:(j - 5) * 128], VH[:, 8 + 2 * r + (j - 6), h, :], j, rmi))
                                rmi += 1
                        else:
                            tiles.append((KT[base:base + 64, mi, lt:lt + 128], VC[:, lt // 128, h, :], j, rmi))
                            rmi += 1
                    for c in range(2):
                        tiles.append((KT[base:base + 64, mi, NT + c * 128:NT + (c + 1) * 128], VC[:, 16 + c, h, :], None, None))
                    crun(g * 512, 512, tiles, h, mi, base)
                if not last:
                    tiles = [(KT[base:base + 64, mi, NT + c * 128:NT + (c + 1) * 128], VC[:, 16 + c, h, :], None, None) for c in range(2)]
                    crun(NT, NCX, tiles, h, mi, base)
            self.pipeline(items, 5)
            self.nsb = 4
            self.dma(oc_d.ap()[:, :, 0:nq], oT[:, :, 0:nq], ["oT"], ["oc_d"])
            T.flush()

    def post_norm_res(self, yTb, n, PG, xsrc, xdst, tok_y, tok_x, tok_o, tmpb):
        rs = self.rstd_of(None, yTb[:, :, 0:n], 8, n, 1024.0, tok_y, "pn")
        for m in range(8):
            self.stt(tmpb[:, m, 0:n], yTb[:, m, 0:n], PG[:, m:m + 1], rs, ALU.mult, ALU.mult, [tok_y, "rstd", "mods"], ["xn"])
            self.tt("dve", xdst[:, m, 0:n], xsrc[:, m, 0:n], tmpb[:, m, 0:n], ALU.add, [tok_x, "xn"], [tok_o])

    def phase_mix(self, Ld, l, last, xin_lat, xin_ctx, o_d, xs1):
        T = self.T
        with ExitStack() as es:
            self.split = False
            self.alloc_norm(es, 256)
            N = 256
            wg = self.sb(es, "wg", [128, 8, 3072], BF16)
            wbr = self.sb(es, "wbr", [128, 3, 3, 1024], BF16)
            wo = self.sb(es, "wo", [128, 8, 1024], BF16)
            xch = self.sb(es, "xchm", [128, 2, 8, N], F32)
            hT = self.sb(es, "hTm", [128, 2, 8, N], BF16)
            oc = self.sb(es, "ocm", [128, 2, 3, 3, N], BF16)
            sg = self.sb(es, "sg", [128, 2, 3, N], F32)
            ta = self.sb(es, "ta", [128, 2, 3, N], F32)
            mg = self.sb(es, "mg", [128, 8, N], BF16)
            yTb = self.sb(es, "yTb", [128, 8, N], F32)
            stg2 = self.sb(es, "stg2", [128, 6, 1024], F32)
            chunks = [(i * N, N, False) for i in range(NT // N)]
            if not last:
                chunks.append((NT, NCX, True))

            def prep(ci):
                c0, n, isc = chunks[ci]
                s_ = ci % 2
                src = xin_ctx if isc else xin_lat(c0, n)
                self.dma(xch[:, s_, :, 0:n], src, (), [("xch", s_)])
                for i in range(3):
                    self.dma(oc[:, s_, i, :, 0:n], o_d[i].ap()[:, :, c0:c0 + n], (), [("+oc", s_)])
                dv_ = self.dvec[:, l, 1 if isc else 0]
                self.norm_mod(xch[:, s_], n, dv_[:, 0, :], dv_[:, 1, :], hT[:, s_], ("xch", s_), ("hTm", s_))
            prep(0)
            win = Ld["win"].ap().rearrange("(k p) n -> p k n", p=128)
            self.cast_engs = ("act", "dve")
            old_stg, old_n = self.stg, self.nstg
            self.stg, self.nstg = stg2, 6
            for i in range(3):
                self.load_w(wg[:, :, i * 1024:(i + 1) * 1024], win[:, :, C_G + i * 1024:C_G + (i + 1) * 1024], ("+wg", i), 8, 1024)
                self.load_w(wbr[:, i], Ld["wbr"].ap()[i].rearrange("(k p) n -> p k n", p=128), ("+wbr", i), 3, 1024)
            self.load_w(wo[:], Ld["wout"].ap().rearrange("(k p) n -> p k n", p=128), "+wo", 8, 1024)
            self.stg, self.nstg = old_stg, old_n
            self.cast_engs = ("act",)
            for ci, (c0, n, isc) in enumerate(chunks):
                s = ci % 2
                dv = self.dvec[:, l, 1 if isc else 0]
                for m in range(8):
                    q = m % 2
                    gb_, yb_ = [], []
                    for i in range(3):
                        b = self.newbank()
                        gb_.append(b)
                        for k in range(8):
                            self.mm(self.ps[b][:, 0:n], wg[:, k, i * 1024 + m * 128:i * 1024 + (m + 1) * 128], hT[:, s, k, 0:n], k == 0, k == 7,
                                    [("+wg", i), ("hTm", s)], [("ps", b)])
                        b = self.newbank()
                        yb_.append(b)
                        for k in range(3):
                            self.mm(self.ps[b][:, 0:n], wbr[:, i, k, m * 128:(m + 1) * 128], oc[:, s, i, k, 0:n], k == 0, k == 2,
                                    [("+wbr", i), ("+oc", s)], [("ps", b)])
                    for i in range(3):
                        self.act(sg[:, q, i, 0:n], self.ps[gb_[i]][:, 0:n], AF.Sigmoid, [("ps", gb_[i])], [("sg", q, i)])
                        self.tt("dve", ta[:, q, i, 0:n], sg[:, q, i, 0:n], self.ps[yb_[i]][:, 0:n], ALU.mult, [("sg", q, i), ("ps", yb_[i])], [("ta", q, i)])
                    self.tt("dve", ta[:, q, 0, 0:n], ta[:, q, 0, 0:n], ta[:, q, 1, 0:n], ALU.add, [("ta", q, 0), ("ta", q, 1)], [("ta", q, 0)])
                    self.tt("dve", mg[:, m, 0:n], ta[:, q, 0, 0:n], ta[:, q, 2, 0:n], ALU.add, [("ta", q, 0), ("ta", q, 2)], [("mg", m)])
                if ci + 1 < len(chunks):
                    prep(ci + 1)
                for m in range(8):
                    b = self.newbank()
                    for k in range(8):
                        self.mm(self.ps[b][:, 0:n], wo[:, k, m * 128:(m + 1) * 128], mg[:, k, 0:n], k == 0, k == 7, ["+wo", ("mg", k)], [("ps", b)])
                    self.act(yTb[:, m, 0:n], self.ps[b][:, 0:n], AF.Copy, [("ps", b)], ["yTb"])
                self.post_norm_res(yTb, n, dv[:, 2, :], xch[:, s], xch[:, s], "yTb", ("xch", s), ("xch", s), self.xn)
                self.dma(xs1.ap()[:, :, c0:c0 + n], xch[:, s, :, 0:n], [("xch", s)], [("xs1", ci)])
            T.flush()

    def phase_mlp(self, Ld, l, last, xs1, out_lat, xs2):
        T = self.T
        with ExitStack() as es:
            self.split = False
            self.alloc_norm(es, 256)
            w1 = self.sb(es, "w1", [128, 8, 4096], BF16)
            w2 = self.sb(es, "w2", [128, 32, 1024], BF16)
            xch = self.sb(es, "xchp", [128, 2, 8, 256], F32)
            h2 = self.sb(es, "h2", [128, 2, 8, 256], BF16)
            rl = self.sb(es, "rl", [128, 2, 256], F32)
            aT = self.sb(es, "aT", [128, 32, 256], BF16)
            y2 = self.sb(es, "y2", [128, 8, 256], F32)
            n = 256
            chunks = [(i * 256, False) for i in range(8)]
            if not last:
                chunks.append((NT, True))

            def prep(ci):
                c0, isc = chunks[ci]
                s_ = ci % 2
                self.dma(xch[:, s_], xs1.ap()[:, :, c0:c0 + n], ["xs1"], [("xch", s_)])
                dv_ = self.dvec[:, l, 1 if isc else 0]
                self.norm_mod(xch[:, s_], n, dv_[:, 3, :], dv_[:, 4, :], h2[:, s_], ("xch", s_), ("h2", s_))
            prep(0)
            self.cast_engs = ("act", "dve")
            w1src = Ld["w1"].ap().rearrange("(k p) n -> p k n", p=128)
            for cc_ in range(4):
                self.load_w(w1[:, :, cc_ * 1024:(cc_ + 1) * 1024], w1src[:, :, cc_ * 1024:(cc_ + 1) * 1024], ("+w1", cc_), 8, 1024)
            self.load_w(w2[:], Ld["w2"].ap().rearrange("(k p) n -> p k n", p=128), "+w2", 32, 1024)
            self.cast_engs = ("act",)
            for ci, (c0, isc) in enumerate(chunks):
                s = ci % 2
                dv = self.dvec[:, l, 1 if isc else 0]
                for j in range(32):
                    b = self.newbank()
                    for k in range(8):
                        self.mm(self.ps[b][:, 0:n], w1[:, k, j * 128:(j + 1) * 128], h2[:, s, k, :], k == 0, k == 7, [("+w1", j // 8), ("h2", s)], [("ps", b)])
                    r_ = j % 2
                    self.act(rl[:, r_, :], self.ps[b][:, 0:n], AF.Relu, [("ps", b)], [("rl", r_)])
                    self.tt("dve", aT[:, j, :], rl[:, r_, :], rl[:, r_, :], ALU.mult, [("rl", r_)], [("aT", j)])
                if ci + 1 < len(chunks):
                    prep(ci + 1)
                for m in range(8):
                    b = self.newbank()
                    for j in range(32):
                        self.mm(self.ps[b][:, 0:n], w2[:, j, m * 128:(m + 1) * 128], aT[:, j, :], j == 0, j == 31, ["+w2", ("aT", j)], [("ps", b)])
                    self.act(y2[:, m, :], self.ps[b][:, 0:n], AF.Copy, [("ps", b)], ["y2"])
                self.post_norm_res(y2, n, dv[:, 5, :], xch[:, s], xch[:, s], "y2", ("xch", s), ("xch", s), self.xn)
                if isc:
                    dst = xs2.ap()[:, :, c0:c0 + n]
                else:
                    dst = out_lat.ap()[:, :, c0:c0 + n]
                self.dma(dst, xch[:, s], [("xch", s)], [("out", ci)])
            T.flush()


def _fm(v):
    v = np.asarray(v, np.float32)
    return np.ascontiguousarray(v.reshape(-1, 128).T)


def _perm_cols():
    def hc(base, heads, swap):
        out = []
        for h in heads:
            d = np.arange(64)
            if swap:
                d = d ^ 1
            out += list(base + h * 64 + d)
        return out
    p = []
    p += hc(0, HP, False) + hc(0, HP, True)
    p += hc(384, [0, 1], False) + hc(384, [0, 1], True)
    p += hc(640, HP, False) + hc(640, HP, True)
    p += hc(1024, [0, 1], False) + hc(1024, [0, 1], True)
    p += list(range(1280, 1664)) + list(range(1664, 2048))
    p += list(range(512, 640)) + list(range(1152, 1280)) + list(range(2048, 2432))
    p += list(range(2432, 5504))
    assert len(p) == NWIN
    return np.array(p)


def _rope_tables(tok0):
    pos = np.arange(tok0, tok0 + NT)
    row = (pos // 64).astype(np.float32)
    col = (pos % 64).astype(np.float32)
    freqs = (np.float32(10000.0) ** (-np.arange(16, dtype=np.float32) / np.float32(16))).astype(np.float32)
    ang = np.concatenate([row[:, None] * freqs, col[:, None] * freqs], axis=-1).astype(np.float32)
    cos, sin = np.cos(ang).astype(np.float32), np.sin(ang).astype(np.float32)
    d = np.arange(128) % 64
    C = cos[:, d // 2].T
    S = sin[:, d // 2].T * np.where(d % 2 == 0, -1.0, 1.0)[:, None]
    return np.ascontiguousarray(C, np.float32), np.ascontiguousarray(S, np.float32)


def _mask_a(rank):
    ki = np.arange(128)[:, None]
    qi = np.arange(128)[None, :]
    prev = np.where(qi <= ki, 0.0, NEG).astype(np.float32)
    nxt = np.where(ki <= qi, 0.0, NEG).astype(np.float32)
    allneg = np.full((128, 128), NEG, np.float32)
    m = np.zeros((128, 10, 128), np.float32)
    m[:, 0], m[:, 1] = prev, nxt
    for r in range(4):
        m[:, 2 + r] = prev if r == rank - 1 else allneg
        m[:, 6 + r] = nxt if r == rank + 1 else allneg
    return m


def _rm01(rank):
    out = np.zeros((128, 44, 8), np.float32)
    idx = 0
    kl = (np.arange(128) // 64)[:, None]
    ql = np.arange(8)[None, :]
    for g in range(4):
        R0 = rank * 32 + g * 8
        for j in range(8):
            kr = R0 - 4 + 2 * j + kl
            qr = R0 + ql
            rs = np.clip(qr - 4, 0, 120)
            valid = (kr >= rs) & (kr < rs + 8) & (kr >= 0) & (kr < 128)
            if g == 0 and j < 2:
                for r in range(4):
                    out[:, idx] = valid if r == rank - 1 else 0.0
                    idx += 1
            elif g == 3 and j >= 6:
                for r in range(4):
                    out[:, idx] = valid if r == rank + 1 else 0.0
                    idx += 1
            else:
                out[:, idx] = valid
                idx += 1
    assert idx == 44
    return out


def _tb_table(rpb, interior=False):
    rpb = np.asarray(rpb, np.float32)
    kc = np.arange(64)[:, None]
    qc = np.arange(64)[None, :]
    ws = np.clip(qc - 8, 0, 48)
    colv = (kc >= ws) & (kc < ws + 16)
    dc = np.clip(kc - qc + 15, 0, 30)
    tb = np.zeros((2, 64, 6, 22, 64), np.float32)
    for kl in range(2):
        for u in range(22):
            dr = 17 + kl - u
            for h in range(6):
                if 0 <= dr <= 14:
                    v = rpb[h, dr][dc]
                else:
                    v = np.zeros((64, 64), np.float32)
                if interior and not (3 <= dr <= 10):
                    tb[kl, :, h, u, :] = NEG
                else:
                    tb[kl, :, h, u, :] = np.where(colv, v, NEG)
    return np.ascontiguousarray(tb.reshape(128, 6, 22, 64))


_CACHE = {}


def _get_nc(n_layers=2, taps=()):
    key = (n_layers, tuple(taps))
    if key not in _CACHE:
        _CACHE[key] = K(n_layers, taps).build()
    return _CACHE[key]


def make_in_maps(inputs, n_layers=2):
    f = lambda a: np.asarray(a, np.float32)
    x, c, ctx, c_ctx = f(inputs["x"]), f(inputs["c"]), f(inputs["ctx"]), f(inputs["c_ctx"])
    perm = _perm_cols()
    rows_ab = np.concatenate([np.arange(h * 64, (h + 1) * 64) for h in HP])
    shared = {}
    for l in range(n_layers):
        shared["wada%d" % l] = np.ascontiguousarray(f(inputs["w_ada"])[l])
        shared["badaT%d" % l] = _fm(f(inputs["b_ada"])[l])
        shared["gains%d" % l] = np.ascontiguousarray(np.stack(
            [_fm(f(inputs[k])[l]) for k in ("norm_mix_pre", "norm_mix_post", "norm_mlp_pre", "norm_mlp_post")], axis=1))
        shared["win%d" % l] = np.ascontiguousarray(f(inputs["w_in"])[l][:, perm])
        gq, gk = f(inputs["qnorm_b"])[l], f(inputs["knorm_b"])[l]
        d = np.arange(128) % 64
        shared["bgain%d" % l] = np.ascontiguousarray(np.stack([gq[d], gq[d ^ 1], gk[d], gk[d ^ 1]], axis=1))
        shared["sinkT%d" % l] = np.ascontiguousarray(np.broadcast_to(f(inputs["sink_a"])[l][None, :], (128, 6)))
        shared["TB%d" % l] = np.ascontiguousarray(np.stack(
            [_tb_table(f(inputs["rpb_c"])[l]), _tb_table(f(inputs["rpb_c"])[l], True)], axis=1))
        shared["wbr%d" % l] = np.ascontiguousarray(np.stack(
            [f(inputs["w_br_a"])[l][rows_ab], f(inputs["w_br_b"])[l][rows_ab], f(inputs["w_br_c"])[l]], axis=0))
        shared["wout%d" % l] = np.ascontiguousarray(f(inputs["w_out"])[l])
        shared["w1_%d" % l] = np.ascontiguousarray(f(inputs["w_mlp_in"])[l])
        shared["w2_%d" % l] = np.ascontiguousarray(f(inputs["w_mlp_out"])[l])
    in_maps = []
    for core in range(8):
        b, rank = core // 4, core % 4
        tok0 = rank * NT
        m = dict(shared)
        xs = x[b, tok0:tok0 + NT, :]
        m["xT"] = np.ascontiguousarray(xs.T.reshape(8, 128, NT).transpose(1, 0, 2))
        m["ctxT"] = np.ascontiguousarray(ctx[b].T.reshape(8, 128, NCX).transpose(1, 0, 2))
        m["ccT"] = np.ascontiguousarray(np.stack([_fm(c[b]), _fm(c_ctx)], axis=2))
        C, S = _rope_tables(tok0)
        m["ropeC"], m["ropeS"] = C, S
        m["maskA"] = _mask_a(rank)
        m["rm01"] = _rm01(rank)
        in_maps.append(m)
    return in_maps


def kernel(**inputs):
    nc = _get_nc(2)
    in_maps = make_in_maps(inputs, 2)
    res = run_bass_kernel_spmd(nc, in_maps, core_ids=list(range(8)))
    out = np.zeros((2, 4 * NT, 1024), np.float32)
    for core in range(8):
        b, rank = core // 4, core % 4
        yT = np.asarray(res.results[core]["yT"])
        out[b, rank * NT:(rank + 1) * NT, :] = yT.transpose(1, 0, 2).reshape(1024, NT).T
    return out
```

```python
import numpy as np
from contextlib import ExitStack
import concourse.bass as bass
import concourse.mybir as mybir
from concourse.bass_utils import run_bass_kernel_spmd

F32 = mybir.dt.float32
BF16 = mybir.dt.bfloat16
AF = mybir.ActivationFunctionType
ALU = mybir.AluOpType

NT = 2048
NCX = 256
NTOT = NT + NCX
NEG = -30000.0
EPS = 1e-6
HP = [0, 3, 1, 4, 2, 5]
C_QA, C_QAS, C_KA, C_KAS = 0, 384, 768, 896
C_QB, C_QBS, C_KB, C_KBS = 1024, 1408, 1792, 1920
C_QC, C_KC, C_V, C_G = 2048, 2432, 2816, 3456
NWIN = 6528
O_KB, O_VB, O_KAH, O_KAT, O_VAH, O_VAT = 0, 262144, 524288, 540672, 557056, 573440
O_KCH, O_KCT, O_VCH, O_VCT, CONTRIB = 589824, 688128, 786432, 884736, 983040


class Trk:
    ENGS = ("pe", "act", "dve", "pool", "sp")
    NDSEM = 12

    def __init__(self, nc, es):
        self.nc = nc
        self.esem = {e: es.enter_context(nc.semaphore("s_" + e)) for e in self.ENGS}
        self.ecnt = {e: 0 for e in self.ENGS}
        self.dsem = {q: [es.enter_context(nc.semaphore("d_%s%d" % (q, i))) for i in range(self.NDSEM)]
                     for q in ("sp", "pool")}
        self.dval = {q: [0] * self.NDSEM for q in ("sp", "pool")}
        self.dcnt = {"sp": 0, "pool": 0}
        self.ccsem = es.enter_context(nc.semaphore("s_cc"))
        self.ccval = 0
        self.ops = []
        self.bar = None
        self.waited = {e: {} for e in self.ENGS}

    def add(self, eng, fn, r=(), w=(), dma=False, cc=False, accw=False):
        self.ops.append(dict(eng=eng, fn=fn, r=tuple(r), w=tuple(w), dma=dma, cc=cc, accw=accw))

    def flush(self):
        ops = self.ops
        self.ops = []
        if not ops:
            return
        last_w, readers = {}, {}

        def is_acc(tok):
            t0_ = tok[0] if isinstance(tok, tuple) else tok
            return isinstance(t0_, str) and t0_.startswith("+")
        for i, op in enumerate(ops):
            deps = set()
            for r in op["r"]:
                if r in last_w:
                    deps.update(last_w[r])
            for w in op["w"]:
                if w in last_w:
                    if is_acc(w) and (op["dma"] or op["accw"]):
                        deps.add(last_w[w][0])
                    else:
                        deps.update(last_w[w])
                for rd in readers.get(w, {}).values():
                    if isinstance(rd, list):
                        deps.update(rd)
                    else:
                        deps.add(rd)
            deps.discard(i)
            if op["eng"] == "pe":
                deps = {d for d in deps if ops[d]["dma"] or ops[d]["eng"] != "pe"}
            op["deps"] = deps
            for r in op["r"]:
                rr = readers.setdefault(r, {})
                if op["dma"]:
                    rr.setdefault("dma", []).append(i)
                else:
                    rr[op["eng"]] = i
            for w in op["w"]:
                if is_acc(w) and (op["dma"] or op["accw"]) and w in last_w:
                    last_w[w] = last_w[w] + [i]
                else:
                    last_w[w] = [i]
                readers[w] = {}
        for op in ops:
            op["sig"] = False
        for op in ops:
            for d in op["deps"]:
                ops[d]["sig"] = True
        last_of = {}
        for i, op in enumerate(ops):
            if not op["dma"]:
                last_of[op["eng"]] = i
        for i in last_of.values():
            ops[i]["sig"] = True
        for op in ops:
            if op["cc"]:
                self.ccval += 1
                op["done"] = (self.ccsem, self.ccval)
                op["pre"] = None
            elif op["dma"]:
                q = op["eng"]
                k = self.dcnt[q] % self.NDSEM
                self.dcnt[q] += 1
                prev = self.dval[q][k]
                self.dval[q][k] += 16
                op["done"] = (self.dsem[q][k], self.dval[q][k])
                op["pre"] = (self.dsem[q][k], prev) if prev > 0 else None
            elif op["sig"]:
                self.ecnt[op["eng"]] += 1
                op["done"] = (self.esem[op["eng"]], self.ecnt[op["eng"]])
                op["pre"] = None
            else:
                op["done"] = None
                op["pre"] = None
        per = {e: [] for e in self.ENGS}
        for op in ops:
            per[op["eng"]].append(op)
        bar = self.bar
        waited = self.waited

        def emit(ename, e):
            wd = waited[ename]

            def wait(sem, val):
                key = id(sem)
                if wd.get(key, 0) < val:
                    e.wait_ge(sem, val)
                    wd[key] = val
            if bar and per[ename]:
                for sem, val in bar:
                    wait(sem, val)
            for op in per[ename]:
                for d in op["deps"]:
                    sem, val = ops[d]["done"]
                    wait(sem, val)
                if op["pre"] is not None:
                    wait(*op["pre"])
                if op["fn"] is None:
                    continue
                ins = op["fn"](e)
                if op["done"] is not None:
                    sem, val = op["done"]
                    if op["cc"]:
                        ins.then_inc(sem, 1)
                    elif op["dma"]:
                        ins.then_inc(sem, 16)
                    else:
                        ins.then_inc(sem, 1)

        with self.nc.Block() as block:
            @block.sync
            def _(e):
                emit("sp", e)

            @block.scalar
            def _(e):
                emit("act", e)

            @block.vector
            def _(e):
                emit("dve", e)

            @block.gpsimd
            def _(e):
                emit("pool", e)

            @block.tensor
            def _(e):
                emit("pe", e)
        nb = []
        for e in self.ENGS:
            if self.ecnt[e] > 0:
                nb.append((self.esem[e], self.ecnt[e]))
        for q in ("sp", "pool"):
            for k in range(self.NDSEM):
                if self.dval[q][k] > 0:
                    nb.append((self.dsem[q][k], self.dval[q][k]))
        if self.ccval > 0:
            nb.append((self.ccsem, self.ccval))
        self.bar = nb

    def final_wait(self):
        bar = self.bar
        with self.nc.Block() as block:
            @block.sync
            def _(e):
                for sem, val in bar:
                    e.wait_ge(sem, val)


def bcast(ap, pos, n):
    l = [list(x) for x in ap.ap]
    l.insert(pos, [0, n])
    return bass.AP(ap.tensor, ap.offset, l)


def dview(t, off, dims):
    return bass.AP(t, off, [list(d) for d in dims])


class K:
    def __init__(self, n_layers=2, taps=(), stop=None):
        self.nl = n_layers
        self.taps = set(taps)
        self.stop = stop
        self.nc = bass.Bass("TRN2", target_bir_lowering=False)
        self.uid = 0

    def din(self, name, shape, dt=F32):
        return self.nc.dram_tensor(name, list(shape), dt, kind="ExternalInput")

    def dscr(self, name, shape, dt=BF16, tap=False):
        if tap and name in self.taps:
            return self.nc.dram_tensor(name, list(shape), dt, kind="ExternalOutput")
        return self.nc.dram_tensor(name, list(shape), dt)

    def sb(self, es, name, shape, dt):
        self.uid += 1
        return es.enter_context(self.nc.sbuf_tensor("%s_%d" % (name, self.uid), list(shape), dt))

    def mm(self, out, lhsT, rhs, start, stop, r, w):
        self.T.add("pe", lambda e: e.matmul(out, lhsT=lhsT, rhs=rhs, start=start, stop=stop), r, w)

    def act(self, out, in_, func, r, w, bias=None, scale=None):
        kw = {}
        if bias is not None:
            kw["bias"] = bias
        if scale is not None:
            kw["scale"] = scale
        self.T.add("act", lambda e: e.activation(out=out, in_=in_, func=func, **kw), r, w)

    def tt(self, eng, out, in0, in1, op, r, w):
        self.T.add(eng, lambda e: e.tensor_tensor(out=out, in0=in0, in1=in1, op=op), r, w)

    def ts(self, eng, out, in0, s1, s2, op0, op1, r, w):
        if op1 is None:
            self.T.add(eng, lambda e: e.tensor_scalar(out=out, in0=in0, scalar1=s1, scalar2=None, op0=op0), r, w)
        else:
            self.T.add(eng, lambda e: e.tensor_scalar(out=out, in0=in0, scalar1=s1, scalar2=s2, op0=op0, op1=op1), r, w)

    def stt(self, out, in0, scalar, in1, op0, op1, r, w):
        self.T.add("dve", lambda e: e.scalar_tensor_tensor(out=out, in0=in0, scalar=scalar, in1=in1, op0=op0, op1=op1), r, w)

    def cp(self, eng, out, in_, r, w):
        self.T.add(eng, lambda e: e.tensor_copy(out=out, in_=in_), r, w)

    def memset(self, eng, ap, val, w):
        self.T.add(eng, lambda e: e.memset(ap, val), (), w)

    def recip(self, out, in_, r, w):
        self.T.add("dve", lambda e: e.reciprocal(out=out, in_=in_), r, w)

    def dma(self, out, in_, r, w, q="sp"):
        self.T.add(q, lambda e: e.dma_start(out=out, in_=in_), r, w, dma=True)

    def newbank(self):
        if self.split:
            b = self.bankc % self.nsb
        else:
            b = self.bankc % 8
        self.bankc += 1
        return b

    def newacc(self):
        b = self.nsb + self.accc % (8 - self.nsb)
        self.accc += 1
        return b

    def alloc_norm(self, es, nmax):
        self.sq_buf = self.sb(es, "sqbuf", [128, 8, nmax], BF16)
        self.lnt = self.sb(es, "lnt", [128, nmax], F32)
        self.rstd = self.sb(es, "rstd", [128, nmax], F32)
        self.xn = self.sb(es, "xn", [128, 8, nmax], F32)

    def cast(self, dst, src, r, w):
        engs = self.cast_engs
        e = engs[self.cast_i % len(engs)]
        self.cast_i += 1
        if e == "act":
            self.T.add("act", lambda e_: e_.activation(out=dst, in_=src, func=AF.Copy), r, w, accw=True)
        else:
            self.T.add(e, lambda e_: e_.tensor_copy(out=dst, in_=src), r, w, accw=True)

    def load_w(self, dst, src, tok_w, nk, ncols):
        CH = 1024
        per = max(1, CH // ncols)
        k = 0
        while k < nk:
            kk = min(per, nk - k)
            if ncols > CH:
                assert per == 1
                c = 0
                while c < ncols:
                    cc = min(CH, ncols - c)
                    sap, stoks = self.stgs[self.stg_i % len(self.stgs)]
                    self.stg_i += 1
                    st = sap[:, 0:cc]
                    self.dma(st, src[:, k, c:c + cc], (), stoks)
                    self.cast(dst[:, k, c:c + cc], st, stoks, [tok_w])
                    c += cc
            else:
                sap, stoks = self.stgs[self.stg_i % len(self.stgs)]
                self.stg_i += 1
                st = sap[:, 0:kk * ncols].rearrange("p (k c) -> p k c", k=kk)
                self.dma(st, src[:, k:k + kk, :], (), stoks)
                self.cast(dst[:, k:k + kk, :], st, stoks, [tok_w])
            k += kk

    def rstd_of(self, es_tmp, src, nk, n, div, tokr, name):
        sq = self.sq_buf[:, 0:nk, 0:n]
        self.act(sq, src, AF.Square, [tokr], ["sqbuf"])
        b = self.newbank()
        for k in range(nk):
            self.mm(self.ps[b][:, 0:n], self.ones[:], sq[:, k, :], k == 0, k == nk - 1, ["sqbuf", "ones"], [("ps", b)])
        self.act(self.lnt[:, 0:n], self.ps[b][:, 0:n], AF.Ln, [("ps", b)], ["lnt"], bias=self.epsc[:, 0:1], scale=1.0 / div)
        self.act(self.rstd[:, 0:n], self.lnt[:, 0:n], AF.Exp, ["lnt"], ["rstd"], scale=-0.5)
        return self.rstd[:, 0:n]

    def norm_mod(self, xch, n, GG, SH, hT, tokx, tokh):
        rs = self.rstd_of(None, xch[:, :, 0:n], 8, n, 1024.0, tokx, "nm")
        self.tt("dve", self.xn[:, :, 0:n], xch[:, :, 0:n], bcast(rs, 1, 8), ALU.mult, [tokx, "rstd"], ["xn"])
        for k in range(8):
            self.ts("dve", hT[:, k, 0:n], self.xn[:, k, 0:n], GG[:, k:k + 1], SH[:, k:k + 1], ALU.mult, ALU.add,
                    ["xn", "mods"], [tokh])

    def build(self):
        nc = self.nc
        NL = self.nl
        xT = self.din("xT", [128, 8, NT])
        ctxT = self.din("ctxT", [128, 8, NCX])
        ccT = self.din("ccT", [128, 8, 2])
        ropeC = self.din("ropeC", [128, NT])
        ropeS = self.din("ropeS", [128, NT])
        maskA = self.din("maskA", [128, 10, 128])
        rm01 = self.din("rm01", [128, 44, 8])
        L = []
        for l in range(NL):
            d = dict(
                wada=self.din("wada%d" % l, [1024, 6144]),
                badaT=self.din("badaT%d" % l, [128, 48]),
                gains=self.din("gains%d" % l, [128, 4, 8]),
                win=self.din("win%d" % l, [1024, NWIN]),
                bgain=self.din("bgain%d" % l, [128, 4]),
                sinkT=self.din("sinkT%d" % l, [128, 6]),
                TB=self.din("TB%d" % l, [128, 2, 6 * 22 * 64]),
                wbr=self.din("wbr%d" % l, [3, 384, 1024]),
                wout=self.din("wout%d" % l, [1024, 1024]),
                w1=self.din("w1_%d" % l, [1024, 4096]),
                w2=self.din("w2_%d" % l, [4096, 1024]),
            )
            L.append(d)
        yT = nc.dram_tensor("yT", [128, 8, NT], F32, kind="ExternalOutput")
        xs = [self.dscr("xs%d" % i, [128, 8, NTOT], F32, tap=True) for i in range(2 * NL)]
        q_d = [self.dscr("q_d%d" % i, [128, 3, NTOT], BF16, tap=True) for i in range(3)]
        kaT_d = self.dscr("kaT_d", [128, NTOT], BF16, tap=True)
        va_d = self.dscr("va_d", [NTOT, 128], BF16, tap=True)
        kcT_d = self.dscr("kcT_d", [128, 3, NTOT], BF16, tap=True)
        vc_d = self.dscr("vc_d", [NTOT, 384], BF16, tap=True)
        kbcT_d = self.dscr("kbcT_d", [128, NCX], BF16, tap=True)
        vbc_d = self.dscr("vbc_d", [NCX, 128], BF16, tap=True)
        contrib = self.dscr("contrib", [960, 1024], BF16, tap=True)
        gathB = self.dscr("gathB", [4 * 512, 1024], BF16)
        gathH = self.dscr("gathH", [4 * 448, 1024], BF16)

        class G:
            @staticmethod
            def view(r, off, dims):
                if off < 524288:
                    return dview(gathB, r * 524288 + off, dims)
                return dview(gathH, r * 458752 + (off - 524288), dims)
        gath = G
        self.gathB, self.gathH = gathB, gathH
        o_d = [self.dscr("o_d%d" % i, [128, 3, NTOT], BF16, tap=True) for i in range(3)]
        mods_d = self.dscr("mods_d", [128, NL, 48, 2], F32, tap=True)

        with ExitStack() as es0:
            self.T = Trk(nc, es0)
            T = self.T
            self.bankc = 0
            self.accc = 0
            self.nsb = 4
            self.split = False
            self.stg_i = 0
            self.cast_i = 0
            self.cast_engs = ("act",)
            self.psall = es0.enter_context(nc.psum_tensor("psall", [128, 4096], F32))
            self.ps = [self.psall[:, i * 512:(i + 1) * 512] for i in range(8)]
            self.ones = self.sb(es0, "ones", [128, 128], BF16)
            self.bd = self.sb(es0, "bd", [128, 128], BF16)
            self.epsc = self.sb(es0, "epsc", [128, 1], F32)
            self.modsb = self.sb(es0, "modsb", [128, NL, 48, 2], F32)
            self.dvec = self.sb(es0, "dvec", [128, NL, 2, 6, 8], F32)
            self.gn = self.sb(es0, "gn", [128, NL, 4, 8], F32)
            self.stg = self.sb(es0, "stg", [128, 3, 1024], F32)
            self.nstg = 3
            self.stgs0 = [(self.stg[:, i, :], [("stg", i)]) for i in range(3)]
            self.stgs = self.stgs0

            with ExitStack() as es:
                self.memset("pool", self.ones[:], 1.0, ["ones"])
                self.memset("pool", self.bd[:], 0.0, ["bd"])
                self.memset("pool", self.bd[0:64, 0:64], 1.0, ["bd"])
                self.memset("pool", self.bd[64:128, 64:128], 1.0, ["bd"])
                self.memset("pool", self.epsc[:], EPS, ["epsc"])
                cc_s = self.sb(es, "cc_s", [128, 8, 2], F32)
                sil = self.sb(es, "sil", [128, 8, 2], F32)
                self.dma(cc_s[:], ccT.ap(), (), ["cc_s"])
                self.act(sil[:], cc_s[:], AF.Silu, ["cc_s"], ["sil"])
                wa = self.sb(es, "wa", [128, 2, 8, 768], F32)
                bad = self.sb(es, "bad", [128, NL, 48], F32)
                for l in range(NL):
                    self.dma(bad[:, l, :], L[l]["badaT"].ap(), (), ["bad"])
                    self.dma(self.gn[:, l], L[l]["gains"].ap(), (), ["gn"])
                    wsrc = L[l]["wada"].ap().rearrange("(k p) n -> p k n", p=128)
                    b = self.newbank()
                    for g in range(8):
                        s = g % 2
                        self.dma(wa[:, s], wsrc[:, :, g * 768:(g + 1) * 768], (), [("wa", s)])
                        for mm_ in range(6):
                            m = g * 6 + mm_
                            for k in range(8):
                                self.mm(self.ps[b][:, 2 * m:2 * m + 2], wa[:, s, k, mm_ * 128:(mm_ + 1) * 128], sil[:, k, :],
                                        k == 0, k == 7, [("wa", s), "sil"], [("ps", b)])
                    self.tt("dve", self.modsb[:, l], self.ps[b][:, 0:96].rearrange("p (m t) -> p m t", t=2),
                            bcast(bad[:, l, :], 2, 2), ALU.add, [("ps", b), "bad"], ["modsb"])
                    for t in range(2):
                        mv = self.modsb[:, l, :, t]
                        dv = self.dvec[:, l, t]
                        self.stt(dv[:, 0, :], mv[:, 8:16], 1.0, self.gn[:, l, 0, :], ALU.add, ALU.mult, ["modsb", "gn"], ["mods"])
                        self.cp("dve", dv[:, 1, :], mv[:, 0:8], ["modsb"], ["mods"])
                        self.tt("dve", dv[:, 2, :], mv[:, 16:24], self.gn[:, l, 1, :], ALU.mult, ["modsb", "gn"], ["mods"])
                        self.stt(dv[:, 3, :], mv[:, 32:40], 1.0, self.gn[:, l, 2, :], ALU.add, ALU.mult, ["modsb", "gn"], ["mods"])
                        self.cp("dve", dv[:, 4, :], mv[:, 24:32], ["modsb"], ["mods"])
                        self.tt("dve", dv[:, 5, :], mv[:, 40:48], self.gn[:, l, 3, :], ALU.mult, ["modsb", "gn"], ["mods"])
                if "mods_d" in self.taps:
                    self.dma(mods_d.ap(), self.modsb[:], ["modsb"], ["mods_d"])
                T.flush()

            for l in range(NL):
                last = (l == NL - 1)
                if l == 0:
                    xin_lat = lambda c0, n: xT.ap()[:, :, c0:c0 + n]
                    xin_ctx = ctxT.ap()
                else:
                    xin_lat = (lambda xs_: (lambda c0, n: xs_.ap()[:, :, c0:c0 + n]))(xs[2 * l - 1])
                    xin_ctx = xs[2 * l - 1].ap()[:, :, NT:NTOT]
                if self.stop == "mods":
                    break
                self.phase_proj(L[l], l, last, xin_lat, xin_ctx, ropeC, ropeS, q_d, kaT_d, va_d, kcT_d, vc_d, kbcT_d, vbc_d, contrib)
                if self.stop == "proj%d" % l or (self.stop or "").startswith("proj0"):
                    break
                self.phase_gather(contrib, gath)
                self.phase_attn_a(L[l], l, last, q_d[0], kaT_d, va_d, gath, maskA, o_d[0])
                if self.stop == "attn_a%d" % l:
                    break
                self.phase_attn_b(L[l], l, last, q_d[1], kbcT_d, vbc_d, gath, o_d[1])
                if self.stop == "attn_b%d" % l:
                    break
                self.phase_attn_c(L[l], l, last, q_d[2], kcT_d, vc_d, gath, rm01, o_d[2])
                if self.stop == "attn_c%d" % l:
                    break
                self.phase_mix(L[l], l, last, xin_lat, xin_ctx, o_d, xs[2 * l])
                if self.stop == "mix%d" % l:
                    break
                out_lat = yT if last else xs[2 * l + 1]
                self.phase_mlp(L[l], l, last, xs[2 * l], out_lat, xs[2 * l + 1])
                if self.stop == "mlp%d" % l:
                    break
            T.final_wait()
        return nc

    def phase_proj(self, Ld, l, last, xin_lat, xin_ctx, ropeC, ropeS, q_d, kaT_d, va_d, kcT_d, vc_d, kbcT_d, vbc_d, contrib):
        T = self.T
        with ExitStack() as es:
            self.split = False
            self.alloc_norm(es, 512)
            hT = self.sb(es, "hT", [128, 8, NTOT], BF16)
            xch = self.sb(es, "xch", [128, 2, 8, 512], F32)
            rC = self.sb(es, "rC", [128, NT], F32)
            rS = self.sb(es, "rS", [128, NT], F32)
            bg = self.sb(es, "bg", [128, 4], F32)
            wt = self.sb(es, "wt", [128, 2, 8, 640], BF16)
            ost = self.sb(es, "ost", [128, 4, 640], BF16)
            t1 = self.sb(es, "t1", [128, 2, 512], F32)
            t2 = self.sb(es, "t2", [128, 2, 512], F32)
            sqh = self.sb(es, "sqh", [128, 2, 512], BF16)
            lnh = self.sb(es, "lnh", [128, 2, 512], F32)
            rsh = self.sb(es, "rsh", [128, 2, 512], F32)
            self.dma(rC[:], ropeC.ap(), (), ["rC"])
            self.dma(rS[:], ropeS.ap(), (), ["rS"])
            self.dma(bg[:], Ld["bgain"].ap(), (), ["bg"])
            chunks = [(i * 512, 512, False) for i in range(4)] + [(NT, NCX, True)]
            for ci, (c0, n, isc) in enumerate(chunks):
                s = ci % 2
                src = xin_ctx if isc else xin_lat(c0, n)
                self.dma(xch[:, s, :, 0:n], src, (), [("xch", s)])
                dv = self.dvec[:, l, 1 if isc else 0]
                self.norm_mod(xch[:, s], n, dv[:, 0, :], dv[:, 1, :], hT[:, :, c0:c0 + n], ("xch", s), "hT")
            if self.stop == "proj0a":
                T.flush()
                return
            win = Ld["win"].ap().rearrange("(k p) n -> p k n", p=128)
            units = []
            units.append(("KB", 0, [C_KB, C_KBS]))
            units.append(("V", 0, None))
            units.append(("KA", 0, [C_KA, C_KAS]))
            for mi in range(3):
                units.append(("KC", mi, [C_KC + mi * 128]))
            n_kv_units = len(units)
            for mi in range(3):
                units.append(("QB", mi, [C_QB + mi * 128, C_QBS + mi * 128]))
            for mi in range(3):
                units.append(("QA", mi, [C_QA + mi * 128, C_QAS + mi * 128]))
            for mi in range(3):
                units.append(("QC", mi, [C_QC + mi * 128]))
            ctr = dict(ost=0, t=0)

            ctoks = []

            def store(srcs_dsts, tok):
                for dst, src in srcs_dsts:
                    ctr["st"] = ctr.get("st", 0) + 1
                    wtk = ("dout", ctr["st"])
                    if dst.tensor.name == contrib.name:
                        ctoks.append(wtk)
                    self.dma(dst, src, [tok], [wtk])

            if self.stop and self.stop.startswith("proj0u"):
                sel = [int(x) for x in self.stop[6:].split("_")]
                units = [units[i] for i in sel]
            def load_unit(ui):
                kind, mi, cols = units[ui]
                ws = ui % 2
                wtok = ("+wt", ws)
                if kind == "V":
                    self.load_w(wt[:, ws, :, 0:640], win[:, :, C_V:C_V + 640], wtok, 8, 640)
                else:
                    for j, c in enumerate(cols):
                        self.load_w(wt[:, ws, :, j * 128:(j + 1) * 128], win[:, :, c:c + 128], wtok, 8, 128)
            load_unit(0)
            for ui, (kind, mi, cols) in enumerate(units):
                ws = ui % 2
                wtok = ("+wt", ws)
                if ui == n_kv_units and not (self.stop or "").startswith("proj0u"):
                    self.emit_gather(contrib, list(ctoks))
                if ui + 1 < len(units):
                    load_unit(ui + 1)
                if kind == "V":
                    for tile in range(NTOT // 128):
                        t0 = tile * 128
                        b0, b1 = self.newbank(), self.newbank()
                        for k in range(8):
                            self.mm(self.ps[b0][:, 0:512], hT[:, k, t0:t0 + 128], wt[:, ws, k, 0:512], k == 0, k == 7, ["hT", wtok], [("ps", b0)])
                        for k in range(8):
                            self.mm(self.ps[b1][:, 0:128], hT[:, k, t0:t0 + 128], wt[:, ws, k, 512:640], k == 0, k == 7, ["hT", wtok], [("ps", b1)])
                        o = ctr["ost"] % 4
                        ctr["ost"] += 1
                        otok = ("ost", o)
                        self.act(ost[:, o, 0:512], self.ps[b0][:, 0:512], AF.Copy, [("ps", b0)], [otok])
                        self.cp("dve", ost[:, o, 512:640], self.ps[b1][:, 0:128], [("ps", b1)], [otok])
                        cb = contrib
                        dl = [(va_d.ap()[t0:t0 + 128, :], ost[:, o, 0:128]),
                              (vc_d.ap()[t0:t0 + 128, :], ost[:, o, 256:640])]
                        if tile < 16:
                            dl.append((dview(cb, O_VB + t0 * 128, [[128, 128], [1, 128]]), ost[:, o, 128:256]))
                            if tile == 0:
                                dl.append((dview(cb, O_VAH, [[128, 128], [1, 128]]), ost[:, o, 0:128]))
                            if tile == 15:
                                dl.append((dview(cb, O_VAT, [[128, 128], [1, 128]]), ost[:, o, 0:128]))
                            if tile < 2:
                                dl.append((dview(cb, O_VCH + tile * 128 * 384, [[384, 128], [1, 384]]), ost[:, o, 256:640]))
                            if tile >= 14:
                                dl.append((dview(cb, O_VCT + (tile - 14) * 128 * 384, [[384, 128], [1, 384]]), ost[:, o, 256:640]))
                        else:
                            dl.append((vbc_d.ap()[t0 - NT:t0 - NT + 128, :], ost[:, o, 128:256]))
                        store(dl, otok)
                    continue
                for ci, (c0, n, isc) in enumerate(chunks):
                    if isc and last and kind in ("QA", "QB", "QC"):
                        continue
                    hs = [hT[:, k, c0:c0 + n] for k in range(8)]
                    bq = self.newbank()
                    for k in range(8):
                        self.mm(self.ps[bq][:, 0:n], wt[:, ws, k, 0:128], hs[k], k == 0, k == 7, ["hT", wtok], [("ps", bq)])
                    pq = self.ps[bq][:, 0:n]
                    o = ctr["ost"] % 4
                    ctr["ost"] += 1
                    otok = ("ost", o)
                    oo = ost[:, o, 0:n]
                    need_sw = (len(cols) == 2) and not isc
                    if need_sw:
                        bs = self.newbank()
                        for k in range(8):
                            self.mm(self.ps[bs][:, 0:n], wt[:, ws, k, 128:256], hs[k], k == 0, k == 7, ["hT", wtok], [("ps", bs)])
                        psw = self.ps[bs][:, 0:n]
                    tt_ = ctr["t"] % 2
                    ctr["t"] += 1
                    a1, a2 = t1[:, tt_, 0:n], t2[:, tt_, 0:n]
                    k1, k2 = ("t1", tt_), ("t2", tt_)
                    if kind in ("QA", "KA"):
                        if isc:
                            self.act(oo, pq, AF.Copy, [("ps", bq)], [otok])
                        else:
                            self.tt("dve", a1, pq, rC[:, c0:c0 + n], ALU.mult, [("ps", bq), "rC"], [k1])
                            self.tt("dve", a2, psw, rS[:, c0:c0 + n], ALU.mult, [("ps", bs), "rS"], [k2])
                            self.tt("dve", oo, a1, a2, ALU.add, [k1, k2], [otok])
                    elif kind in ("QB", "KB"):
                        gi = 0 if kind == "QB" else 2
                        self.act(sqh[:, tt_, 0:n], pq, AF.Square, [("ps", bq)], [("sqh", tt_)])
                        bss = self.newbank()
                        self.mm(self.ps[bss][:, 0:n], self.bd[:], sqh[:, tt_, 0:n], True, True, [("sqh", tt_), "bd"], [("ps", bss)])
                        self.act(lnh[:, tt_, 0:n], self.ps[bss][:, 0:n], AF.Ln, [("ps", bss)], [("lnh", tt_)], bias=self.epsc[:, 0:1], scale=1.0 / 64)
                        self.act(rsh[:, tt_, 0:n], lnh[:, tt_, 0:n], AF.Exp, [("lnh", tt_)], [("rsh", tt_)], scale=-0.5)
                        if isc:
                            self.stt(oo, pq, bg[:, gi:gi + 1], rsh[:, tt_, 0:n], ALU.mult, ALU.mult, [("ps", bq), "bg", ("rsh", tt_)], [otok])
                        else:
                            self.stt(a1, pq, bg[:, gi:gi + 1], rC[:, c0:c0 + n], ALU.mult, ALU.mult, [("ps", bq), "bg", "rC", ("sqh", tt_)], [k1])
                            self.stt(a2, psw, bg[:, gi + 1:gi + 2], rS[:, c0:c0 + n], ALU.mult, ALU.mult, [("ps", bs), "bg", "rS"], [k2])
                            self.tt("dve", a1, a1, a2, ALU.add, [k1, k2], [k1])
                            self.tt("dve", oo, a1, rsh[:, tt_, 0:n], ALU.mult, [k1, ("rsh", tt_)], [otok])
                    else:
                        self.act(oo, pq, AF.Copy, [("ps", bq)], [otok])
                    dl = []
                    cb = contrib
                    if kind == "QA":
                        dl.append((q_d[0].ap()[:, mi, c0:c0 + n], oo))
                    elif kind == "QB":
                        dl.append((q_d[1].ap()[:, mi, c0:c0 + n], oo))
                    elif kind == "QC":
                        dl.append((q_d[2].ap()[:, mi, c0:c0 + n], oo))
                    elif kind == "KA":
                        dl.append((kaT_d.ap()[:, c0:c0 + n], oo))
                        if ci == 0:
                            dl.append((dview(cb, O_KAH, [[128, 128], [1, 128]]), ost[:, o, 0:128]))
                        if ci == 3:
                            dl.append((dview(cb, O_KAT, [[128, 128], [1, 128]]), ost[:, o, 384:512]))
                    elif kind == "KB":
                        if isc:
                            dl.append((kbcT_d.ap(), oo))
                        else:
                            dl.append((dview(cb, O_KB + c0, [[2048, 128], [1, n]]), oo))
                    elif kind == "KC":
                        dl.append((kcT_d.ap()[:, mi, c0:c0 + n], oo))
                        if ci == 0:
                            dl.append((dview(cb, O_KCH + mi * 256, [[768, 128], [1, 256]]), ost[:, o, 0:256]))
                        if ci == 3:
                            dl.append((dview(cb, O_KCT + mi * 256, [[768, 128], [1, 256]]), ost[:, o, 256:512]))
                    store(dl, otok)
            T.flush()

    def phase_gather(self, contrib, gath):
        return

    def emit_gather(self, contrib, rtoks):
        T = self.T
        gB, gH = self.gathB, self.gathH
        T.add("pool", lambda e: e.collective_compute("AllGather", ALU.bypass, replica_groups=[[0, 1, 2, 3], [4, 5, 6, 7]],
                                                     ins=[contrib.ap()[0:512, :]], outs=[gB.ap()]), rtoks, ["gathB"], cc=True)
        T.add("pool", lambda e: e.collective_compute("AllGather", ALU.bypass, replica_groups=[[0, 1, 2, 3], [4, 5, 6, 7]],
                                                     ins=[contrib.ap()[512:960, :]], outs=[gH.ap()]), rtoks, ["gathH"], cc=True)

    def attn_fin(self, bank, n, base, out_ap, rc, rctok, extra=None, extra_tok=None, shape3=None, act_recip=True):
        ob = 64 - base
        k = self.fc_i % 2
        self.fc_i += 1
        fct = ("fc", k)
        self.cp("dve", self.fc[:, k, 0:n], self.ps[bank][:, 0:n], [("ps", bank)], [fct])
        den = self.fc[ob:ob + 64, k, 0:n]
        num = self.fc[base:base + 64, k, 0:n]
        rcp = rc[base:base + 64, 0:n]
        tmp = rc[ob:ob + 64, 0:n]
        if shape3 is not None:
            a, b_ = shape3
            den = den.rearrange("p (a b) -> p a b", a=a)
            num = num.rearrange("p (a b) -> p a b", a=a)
            rcp = rcp.rearrange("p (a b) -> p a b", a=a)
            tmp = tmp.rearrange("p (a b) -> p a b", a=a)
        src, srct = den, fct
        if extra is not None:
            self.tt("dve", tmp, den, extra, ALU.add, [fct, extra_tok], [rctok])
            src, srct = tmp, rctok
        if act_recip:
            self.act(rcp, src, AF.Ln, [srct], [rctok])
            self.act(rcp, rcp, AF.Exp, [rctok], [rctok], scale=-1.0)
        else:
            self.recip(rcp, src, [srct], [rctok])
        self.tt("dve", out_ap, num, rcp, ALU.mult, [fct, rctok], ["oT"])

    def pipeline(self, items, look=3, group=1):
        groups = [items[i:i + group] for i in range(0, len(items), group)]
        banks = {}

        def qks(gi):
            for k, itm in enumerate(groups[gi]):
                banks[(gi, k)] = self.newbank()
                itm[0](banks[(gi, k)])
        for gi in range(min(look, len(groups))):
            qks(gi)
        for gi in range(len(groups)):
            if gi + look < len(groups):
                qks(gi + look)
            for k, itm in enumerate(groups[gi]):
                itm[1](banks[(gi, k)])
            for k, itm in enumerate(groups[gi]):
                itm[2]()
                if itm[3] is not None:
                    itm[3]()

    def load_vaug(self, dst, src, tokn):
        self.dma(dst, src, (), [tokn])

    def phase_attn_a(self, Ld, l, last, qa_d, kaT_d, va_d, gath, maskA, oa_d):
        T = self.T
        with ExitStack() as es:
            self.split = True
            self.nsb = 6
            self.bankc = 0
            QT = self.sb(es, "QT", [128, 3, NTOT], BF16)
            KT = self.sb(es, "KT", [128, NTOT], BF16)
            KH = self.sb(es, "KH", [128, 8, 128], BF16)
            VA = self.sb(es, "VA", [128, 18, 2, 128], BF16)
            VH = self.sb(es, "VH", [128, 8, 2, 128], BF16)
            MA = self.sb(es, "MA", [128, 10, 128], F32)
            ES = self.sb(es, "ES", [128, 6, 128], F32)
            sk = self.sb(es, "sk", [128, 6], F32)
            PT = self.sb(es, "PT", [128, 8, 512], BF16)
            tm = self.sb(es, "tm", [128, 4, 384], F32)
            rc = self.sb(es, "rc", [128, 2, 512], F32)
            self.fc = self.sb(es, "fc", [128, 2, 512], F32)
            self.fc_i = 0
            oT = self.sb(es, "oT", [128, 3, NTOT], BF16)
            nq = NT if last else NTOT
            self.dma(QT[:, :, 0:nq], qa_d.ap()[:, :, 0:nq], (), ["+QT"])
            self.dma(KT[:], kaT_d.ap(), (), ["+KT"])
            self.memset("pool", VA[:], 1.0, ["+VA"])
            self.memset("pool", VH[:], 1.0, ["+VH"])
            vsrc = va_d.ap().rearrange("(t p) c -> p t c", p=128)
            self.dma(VA[:, :, 0, 0:64], vsrc[:, :, 0:64], (), ["+VA"])
            self.dma(VA[:, :, 1, 64:128], vsrc[:, :, 64:128], (), ["+VA"])
            for r in range(4):
                pass
                self.dma(KH[:, r, :], gath.view(r, O_KAT, [[128, 128], [1, 128]]), ["gath"], ["+KH"])
                self.dma(KH[:, 4 + r, :], gath.view(r, O_KAH, [[128, 128], [1, 128]]), ["gath"], ["+KH"])
                self.dma(VH[:, r, 0, 0:64], gath.view(r, O_VAT, [[128, 128], [1, 64]]), ["gath"], ["+VH"])
                self.dma(VH[:, r, 1, 64:128], gath.view(r, O_VAT + 64, [[128, 128], [1, 64]]), ["gath"], ["+VH"])
                self.dma(VH[:, 4 + r, 0, 0:64], gath.view(r, O_VAH, [[128, 128], [1, 64]]), ["gath"], ["+VH"])
                self.dma(VH[:, 4 + r, 1, 64:128], gath.view(r, O_VAH + 64, [[128, 128], [1, 64]]), ["gath"], ["+VH"])
            self.dma(MA[:], maskA.ap(), (), ["MA"])
            self.dma(sk[:], Ld["sinkT"].ap(), (), ["sk"])
            self.act(sk[:], sk[:], AF.Exp, ["sk"], ["sk"])
            self.cp("dve", ES[:], bcast(sk[:], 2, 128), ["sk"], ["ES"])
            it = dict(p=0, t=0, r=0)
            items = []

            def run(qcols, nqc, base, tiles, shape3, out_ap, es_ap):
                n = 3 * nqc
                rhs = QT[base:base + 64, :, qcols:qcols + nqc]
                bo = self.newacc()
                nt = len(tiles)
                for ti, (kap, vap, mk) in enumerate(tiles):
                    p = it["p"] % 8
                    it["p"] += 1
                    t = it["t"] % 4
                    if mk is not None:
                        it["t"] += 1

                    def qk(b, kap=kap):
                        self.mm(self.ps[b][:, 0:n].rearrange("p (a b) -> p a b", a=3), kap, rhs, True, True, ["+QT", "+KT", "+KH"], [("ps", b)])

                    def sm(b, mk=mk, p=p, t=t):
                        pt = PT[:, p, 0:n]
                        if mk is not None:
                            self.stt(tm[:, t, 0:n].rearrange("p (a b) -> p a b", a=3), self.ps[b][:, 0:n].rearrange("p (a b) -> p a b", a=3),
                                     0.125, bcast(mk, 1, 3), ALU.mult, ALU.add, [("ps", b), "MA"], [("tm", t)])
                            self.act(pt, tm[:, t, 0:n], AF.Exp, [("tm", t)], [("PT", p)])
                        else:
                            self.act(pt, self.ps[b][:, 0:n], AF.Exp, [("ps", b)], [("PT", p)], scale=0.125)

                    def pv(vap=vap, p=p, ti=ti):
                        self.mm(self.ps[bo][:, 0:n], vap, PT[:, p, 0:n], ti == 0, ti == nt - 1, [("PT", p), "+VA", "+VH"], [("ps", bo)])
                    fin = None
                    if ti == nt - 1:
                        r_ = it["r"] % 2
                        it["r"] += 1

                        def fin(r_=r_):
                            self.attn_fin(bo, n, base, out_ap, rc[:, r_], ("rc", r_), extra=es_ap, extra_tok="ES", shape3=shape3)
                    items.append((qk, sm, pv, fin))

            for half in range(2):
                base = half * 64
                ob = 64 - base
                esl = ES[ob:ob + 64, 3 * half:3 * half + 3, :]
                for j in range(16):
                    tiles = []
                    if j > 0:
                        tiles.append((KT[base:base + 64, (j - 1) * 128:j * 128], VA[:, j - 1, half, :], MA[:, 0, :]))
                    else:
                        for r in range(4):
                            tiles.append((KH[base:base + 64, r, :], VH[:, r, half, :], MA[:, 2 + r, :]))
                    tiles.append((KT[base:base + 64, j * 128:(j + 1) * 128], VA[:, j, half, :], None))
                    if j < 15:
                        tiles.append((KT[base:base + 64, (j + 1) * 128:(j + 2) * 128], VA[:, j + 1, half, :], MA[:, 1, :]))
                    else:
                        for r in range(4):
                            tiles.append((KH[base:base + 64, 4 + r, :], VH[:, 4 + r, half, :], MA[:, 6 + r, :]))
                    for c in range(2):
                        tiles.append((KT[base:base + 64, NT + c * 128:NT + (c + 1) * 128], VA[:, 16 + c, half, :], None))
                    run(j * 128, 128, base, tiles, (3, 128), oT[base:base + 64, :, j * 128:(j + 1) * 128], esl)
                if not last:
                    for cq in range(2):
                        tiles = [(KT[base:base + 64, NT + c * 128:NT + (c + 1) * 128], VA[:, 16 + c, half, :], None) for c in range(2)]
                        q0 = NT + cq * 128
                        run(q0, 128, base, tiles, (3, 128), oT[base:base + 64, :, q0:q0 + 128], esl)
            self.pipeline(items, 5)
            self.nsb = 4
            self.dma(oa_d.ap()[:, :, 0:nq], oT[:, :, 0:nq], ["oT"], ["oa_d"])
            T.flush()

    def phase_attn_b(self, Ld, l, last, qb_d, kbcT_d, vbc_d, gath, ob_d):
        T = self.T
        with ExitStack() as es:
            NK = NCX + 4 * NT
            self.split = True
            self.nsb = 6
            self.bankc = 0
            QT = self.sb(es, "QTb", [128, 3, NTOT], BF16)
            KT = self.sb(es, "KTb", [128, NK], BF16)
            VB = self.sb(es, "VBb", [128, 66, 2, 128], BF16)
            PT = self.sb(es, "PTb", [128, 6, 512], BF16)
            rc = self.sb(es, "rcb", [128, 2, 512], F32)
            self.fc = self.sb(es, "fcb", [128, 2, 512], F32)
            self.fc_i = 0
            oT = self.sb(es, "oTb", [128, 3, NTOT], BF16)
            nq = NT if last else NTOT
            self.dma(QT[:, :, 0:nq], qb_d.ap()[:, :, 0:nq], (), ["+QT"])
            self.dma(KT[:, 0:NCX], kbcT_d.ap(), (), ["+KT"])
            self.memset("pool", VB[:, 0:33], 1.0, ["+VB"])
            self.memset("pool", VB[:, 33:66], 1.0, ["+VB"])
            vcs = vbc_d.ap().rearrange("(t p) c -> p t c", p=128)
            self.dma(VB[:, 0:2, 0, 0:64], vcs[:, :, 0:64], (), ["+VB"])
            self.dma(VB[:, 0:2, 1, 64:128], vcs[:, :, 64:128], (), ["+VB"])
            for r in range(4):
                pass
                self.dma(KT[:, NCX + r * NT:NCX + (r + 1) * NT], gath.view(r, O_KB, [[2048, 128], [1, 2048]]), ["gath"], ["+KT"])
                self.dma(VB[:, 2 + r * 16:2 + (r + 1) * 16, 0, 0:64], gath.view(r, O_VB, [[128, 128], [128 * 128, 16], [1, 64]]), ["gath"], ["+VB"])
                self.dma(VB[:, 2 + r * 16:2 + (r + 1) * 16, 1, 64:128], gath.view(r, O_VB + 64, [[128, 128], [128 * 128, 16], [1, 64]]), ["gath"], ["+VB"])
            it = dict(p=0, r=0)
            items = []

            def run(mi, q0, n, ktiles):
                bo = [self.newacc(), self.newacc()]
                nt = len(ktiles)
                for ti, kt in enumerate(ktiles):
                    for half in range(2):
                        base = half * 64
                        p = it["p"] % 6
                        it["p"] += 1

                        def qk(b, base=base, kt=kt):
                            self.mm(self.ps[b][:, 0:n], KT[base:base + 64, kt * 128:(kt + 1) * 128], QT[base:base + 64, mi, q0:q0 + n],
                                    True, True, ["+QT", "+KT"], [("ps", b)])

                        def sm(b, p=p, half=half):
                            if half == 1:
                                return
                            assert b % 2 == 0 and p % 2 == 0
                            src = self.psall[:, b * 512:(b + 2) * 512].rearrange("p (a c) -> p a c", a=2)[:, :, 0:n]
                            self.act(PT[:, p:p + 2, 0:n], src, AF.Exp, [("ps", b), ("ps", b + 1)], [("PT", p), ("PT", p + 1)], scale=0.125)

                        def pv(p=p, half=half, kt=kt, ti=ti):
                            self.mm(self.ps[bo[half]][:, 0:n], VB[:, kt, half, :], PT[:, p, 0:n], ti == 0, ti == nt - 1,
                                    [("PT", p), "+VB"], [("ps", bo[half])])
                        fin = None
                        if ti == nt - 1:
                            r_ = it["r"] % 2
                            it["r"] += 1

                            def fin(r_=r_, half=half, base=base):
                                self.attn_fin(bo[half], n, base, oT[base:base + 64, mi, q0:q0 + n], rc[:, r_], ("rc", r_), act_recip=False)
                        items.append((qk, sm, pv, fin))

            for mi in range(3):
                for qc in range(4):
                    run(mi, qc * 512, 512, list(range(66)))
                if not last:
                    run(mi, NT, NCX, [0, 1])
            self.pipeline(items, 2, 2)
            self.nsb = 4
            self.dma(ob_d.ap()[:, :, 0:nq], oT[:, :, 0:nq], ["oT"], ["ob_d"])
            T.flush()

    def phase_attn_c(self, Ld, l, last, qc_d, kcT_d, vc_d, gath, rm01, oc_d):
        T = self.T
        with ExitStack() as es:
            self.split = True
            self.nsb = 6
            self.bankc = 0
            QT = self.sb(es, "QTc", [128, 3, NTOT], BF16)
            KT = self.sb(es, "KTc", [128, 3, NTOT], BF16)
            KH = self.sb(es, "KHc", [128, 3, 8, 256], BF16)
            VC = self.sb(es, "VCc", [128, 18, 6, 128], BF16)
            VH = self.sb(es, "VHc", [128, 16, 6, 128], BF16)
            EB = self.sb(es, "EB", [128, 2, 6, 22, 64], BF16)
            RM = self.sb(es, "RM", [128, 44, 8], F32)
            RMb = self.sb(es, "RMb", [128, 44, 8], BF16)
            PT = self.sb(es, "PTc", [128, 8, 512], BF16)
            rc = self.sb(es, "rcc", [128, 2, 512], F32)
            self.fc = self.sb(es, "fcc", [128, 2, 512], F32)
            self.fc_i = 0
            oT = self.sb(es, "oTc", [128, 3, NTOT], BF16)
            nq = NT if last else NTOT
            self.dma(QT[:, :, 0:nq], qc_d.ap()[:, :, 0:nq], (), ["+QT"])
            self.dma(KT[:], kcT_d.ap(), (), ["+KT"])
            self.memset("pool", VC[:], 1.0, ["+VC"])
            self.memset("pool", VH[:], 1.0, ["+VH"])
            vsrc = vc_d.ap().rearrange("(t p) (h d) -> p t h d", p=128, d=64)
            for h in range(6):
                o = (h % 2) * 64
                self.dma(VC[:, :, h, o:o + 64], vsrc[:, :, h, :], (), ["+VC"])
            for r in range(4):
                pass
                self.dma(KH[:, :, r, :], gath.view(r, O_KCT, [[768, 128], [256, 3], [1, 256]]), ["gath"], ["+KH"])
                self.dma(KH[:, :, 4 + r, :], gath.view(r, O_KCH, [[768, 128], [256, 3], [1, 256]]), ["gath"], ["+KH"])
                for h in range(6):
                    o = (h % 2) * 64
                    self.dma(VH[:, 2 * r:2 * r + 2, h, o:o + 64], gath.view(r, O_VCT + h * 64, [[384, 128], [128 * 384, 2], [1, 64]]), ["gath"], ["+VH"])
                    self.dma(VH[:, 8 + 2 * r:8 + 2 * r + 2, h, o:o + 64], gath.view(r, O_VCH + h * 64, [[384, 128], [128 * 384, 2], [1, 64]]), ["gath"], ["+VH"])
            TBd = Ld["TB"].ap()
            for tbl in range(2):
                ebf = EB[:, tbl].rearrange("p h u q -> p (h u q)")
                c = 0
                while c < 6 * 22 * 64:
                    cc_ = min(1024, 6 * 22 * 64 - c)
                    sap, stoks = self.stgs[self.stg_i % len(self.stgs)]
                    self.stg_i += 1
                    st = sap[:, 0:cc_]
                    self.dma(st, TBd[:, tbl, c:c + cc_], (), stoks)
                    self.T.add("act", (lambda e_, o_=ebf[:, c:c + cc_], i_=st: e_.activation(out=o_, in_=i_, func=AF.Exp)),
                               stoks, ["+EB"], accw=True)
                    c += cc_
            self.dma(RM[:], rm01.ap(), (), ["RM"])
            self.cp("dve", RMb[:], RM[:], ["RM"], ["RMb"])
            it = dict(p=0, t=0, r=0)
            items = []

            def crun(q0, n, tiles, h, mi, base):
                rhs = QT[base:base + 64, mi, q0:q0 + n]
                bo = self.newacc()
                nt = len(tiles)
                for ti, (kap, vap, j, ri) in enumerate(tiles):
                    p = it["p"] % 8
                    it["p"] += 1
                    t = it["t"] % 3
                    if j is not None:
                        it["t"] += 1

                    def qk(b, kap=kap):
                        self.mm(self.ps[b][:, 0:n], kap, rhs, True, True, ["+QT", "+KT", "+KH"], [("ps", b)])

                    def sm(b, j=j, ri=ri, p=p, t=t):
                        pt = PT[:, p, 0:n]
                        self.act(pt, self.ps[b][:, 0:n], AF.Exp, [("ps", b)], [("PT", p)], scale=0.125)
                        if j is not None:
                            u0 = 14 - 2 * j
                            g_ = q0 // 512
                            tbl = 1 if g_ in (1, 2) else 0
                            pt3 = pt.rearrange("p (a b) -> p a b", a=8)
                            self.tt("dve", pt3, pt3, EB[:, tbl, h, u0:u0 + 8, :], ALU.mult, [("PT", p), "+EB"], [("PT", p)])
                            if tbl == 0:
                                self.tt("dve", pt3, pt3, bcast(RMb[:, ri, :], 2, 64), ALU.mult, [("PT", p), "RMb"], [("PT", p)])

                    def pv(vap=vap, p=p, ti=ti):
                        self.mm(self.ps[bo][:, 0:n], vap, PT[:, p, 0:n], ti == 0, ti == nt - 1, [("PT", p), "+VC", "+VH"], [("ps", bo)])
                    fin = None
                    if ti == nt - 1:
                        r_ = it["r"] % 2
                        it["r"] += 1

                        def fin(r_=r_):
                            self.attn_fin(bo, n, base, oT[base:base + 64, mi, q0:q0 + n], rc[:, r_], ("rc", r_))
                    items.append((qk, sm, pv, fin))

            for h in range(6):
                mi, half = h // 2, h % 2
                base = half * 64
                rmi = 0
                for g in range(4):
                    tiles = []
                    for j in range(8):
                        lt = g * 512 - 256 + 128 * j
                        if g == 0 and j < 2:
                            for r in range(4):
                                tiles.append((KH[base:base + 64, mi, r, j * 128:(j + 1) * 128], VH[:, 2 * r + j, h, :], j, rmi))
                                rmi += 1
                        elif g == 3 and j >= 6:
                            for r in range(4):
                                tiles.append((KH[base:base + 64, mi, 4 + r, (j - 6) * 128:(j - 5) * 128], VH[:, 8 + 2 * r + (j - 6), h, :], j, rmi))
                                rmi += 1
                        else:
                            tiles.append((KT[base:base + 64, mi, lt:lt + 128], VC[:, lt // 128, h, :], j, rmi))
                            rmi += 1
                    for c in range(2):
                        tiles.append((KT[base:base + 64, mi, NT + c * 128:NT + (c + 1) * 128], VC[:, 16 + c, h, :], None, None))
                    crun(g * 512, 512, tiles, h, mi, base)
                if not last:
                    tiles = [(KT[base:base + 64, mi, NT + c * 128:NT + (c + 1) * 128], VC[:, 16 + c, h, :], None, None) for c in range(2)]
                    crun(NT, NCX, tiles, h, mi, base)
            self.pipeline(items, 5)
            self.nsb = 4
            self.dma(oc_d.ap()[:, :, 0:nq], oT[:, :, 0:nq], ["oT"], ["oc_d"])
            T.flush()

    def post_norm_res(self, yTb, n, PG, xsrc, xdst, tok_y, tok_x, tok_o, tmpb):
        rs = self.rstd_of(None, yTb[:, :, 0:n], 8, n, 1024.0, tok_y, "pn")
        for m in range(8):
            self.stt(tmpb[:, m, 0:n], yTb[:, m, 0:n], PG[:, m:m + 1], rs, ALU.mult, ALU.mult, [tok_y, "rstd", "mods"], ["xn"])
            self.tt("dve", xdst[:, m, 0:n], xsrc[:, m, 0:n], tmpb[:, m, 0:n], ALU.add, [tok_x, "xn"], [tok_o])

    def phase_mix(self, Ld, l, last, xin_lat, xin_ctx, o_d, xs1):
        T = self.T
        with ExitStack() as es:
            self.split = False
            self.alloc_norm(es, 256)
            N = 256
            wg = self.sb(es, "wg", [128, 8, 3072], BF16)
            wbr = self.sb(es, "wbr", [128, 3, 3, 1024], BF16)
            wo = self.sb(es, "wo", [128, 8, 1024], BF16)
            xch = self.sb(es, "xchm", [128, 2, 8, N], F32)
            hT = self.sb(es, "hTm", [128, 2, 8, N], BF16)
            oc = self.sb(es, "ocm", [128, 2, 3, 3, N], BF16)
            sg = self.sb(es, "sg", [128, 2, 3, N], F32)
            ta = self.sb(es, "ta", [128, 2, 3, N], F32)
            mg = self.sb(es, "mg", [128, 8, N], BF16)
            yTb = self.sb(es, "yTb", [128, 8, N], F32)
            stg2 = self.sb(es, "stg2", [128, 6, 1024], F32)
            chunks = [(i * N, N, False) for i in range(NT // N)]
            if not last:
                chunks.append((NT, NCX, True))

            def prep(ci):
                c0, n, isc = chunks[ci]
                s_ = ci % 2
                src = xin_ctx if isc else xin_lat(c0, n)
                self.dma(xch[:, s_, :, 0:n], src, (), [("xch", s_)])
                for i in range(3):
                    self.dma(oc[:, s_, i, :, 0:n], o_d[i].ap()[:, :, c0:c0 + n], (), [("+oc", s_)])
                dv_ = self.dvec[:, l, 1 if isc else 0]
                self.norm_mod(xch[:, s_], n, dv_[:, 0, :], dv_[:, 1, :], hT[:, s_], ("xch", s_), ("hTm", s_))
            prep(0)
            win = Ld["win"].ap().rearrange("(k p) n -> p k n", p=128)
            self.cast_engs = ("act", "dve")
            self.stgs = [(stg2[:, i, :], [("stg2", i)]) for i in range(6)]
            for i in range(3):
                self.load_w(wg[:, :, i * 1024:(i + 1) * 1024], win[:, :, C_G + i * 1024:C_G + (i + 1) * 1024], ("+wg", i), 8, 1024)
                self.load_w(wbr[:, i], Ld["wbr"].ap()[i].rearrange("(k p) n -> p k n", p=128), ("+wbr", i), 3, 1024)
            self.load_w(wo[:], Ld["wout"].ap().rearrange("(k p) n -> p k n", p=128), "+wo", 8, 1024)
            self.stgs = self.stgs0
            self.cast_engs = ("act",)
            for ci, (c0, n, isc) in enumerate(chunks):
                s = ci % 2
                dv = self.dvec[:, l, 1 if isc else 0]
                for m in range(8):
                    q = m % 2
                    gb_, yb_ = [], []
                    for i in range(3):
                        b = self.newbank()
                        gb_.append(b)
                        for k in range(8):
                            self.mm(self.ps[b][:, 0:n], wg[:, k, i * 1024 + m * 128:i * 1024 + (m + 1) * 128], hT[:, s, k, 0:n], k == 0, k == 7,
                                    [("+wg", i), ("hTm", s)], [("ps", b)])
                        b = self.newbank()
                        yb_.append(b)
                        for k in range(3):
                            self.mm(self.ps[b][:, 0:n], wbr[:, i, k, m * 128:(m + 1) * 128], oc[:, s, i, k, 0:n], k == 0, k == 2,
                                    [("+wbr", i), ("+oc", s)], [("ps", b)])
                    for i in range(3):
                        self.act(sg[:, q, i, 0:n], self.ps[gb_[i]][:, 0:n], AF.Sigmoid, [("ps", gb_[i])], [("sg", q, i)])
                        self.tt("dve", ta[:, q, i, 0:n], sg[:, q, i, 0:n], self.ps[yb_[i]][:, 0:n], ALU.mult, [("sg", q, i), ("ps", yb_[i])], [("ta", q, i)])
                    self.tt("dve", ta[:, q, 0, 0:n], ta[:, q, 0, 0:n], ta[:, q, 1, 0:n], ALU.add, [("ta", q, 0), ("ta", q, 1)], [("ta", q, 0)])
                    self.tt("dve", mg[:, m, 0:n], ta[:, q, 0, 0:n], ta[:, q, 2, 0:n], ALU.add, [("ta", q, 0), ("ta", q, 2)], [("mg", m)])
                if ci + 1 < len(chunks):
                    prep(ci + 1)
                for m in range(8):
                    b = self.newbank()
                    for k in range(8):
                        self.mm(self.ps[b][:, 0:n], wo[:, k, m * 128:(m + 1) * 128], mg[:, k, 0:n], k == 0, k == 7, ["+wo", ("mg", k)], [("ps", b)])
                    self.act(yTb[:, m, 0:n], self.ps[b][:, 0:n], AF.Copy, [("ps", b)], ["yTb"])
                self.post_norm_res(yTb, n, dv[:, 2, :], xch[:, s], xch[:, s], "yTb", ("xch", s), ("xch", s), self.xn)
                self.dma(xs1.ap()[:, :, c0:c0 + n], xch[:, s, :, 0:n], [("xch", s)], [("xs1", ci)])
            T.flush()

    def phase_mlp(self, Ld, l, last, xs1, out_lat, xs2):
        T = self.T
        with ExitStack() as es:
            self.split = False
            self.alloc_norm(es, 256)
            w1 = self.sb(es, "w1", [128, 8, 4096], BF16)
            w2 = self.sb(es, "w2", [128, 32, 1024], BF16)
            xch = self.sb(es, "xchp", [128, 2, 8, 256], F32)
            h2 = self.sb(es, "h2", [128, 2, 8, 256], BF16)
            rl = self.sb(es, "rl", [128, 2, 256], F32)
            aT = self.sb(es, "aT", [128, 32, 256], BF16)
            y2 = self.sb(es, "y2", [128, 8, 256], F32)
            n = 256
            chunks = [(i * 256, False) for i in range(8)]
            if not last:
                chunks.append((NT, True))

            def prep(ci):
                c0, isc = chunks[ci]
                s_ = ci % 2
                self.dma(xch[:, s_], xs1.ap()[:, :, c0:c0 + n], ["xs1"], [("xch", s_)])
                dv_ = self.dvec[:, l, 1 if isc else 0]
                self.norm_mod(xch[:, s_], n, dv_[:, 3, :], dv_[:, 4, :], h2[:, s_], ("xch", s_), ("h2", s_))
            prep(0)
            self.cast_engs = ("act", "dve")
            aTf = aT[:].rearrange("p a b -> p (a b)").bitcast(F32).rearrange("p (s c) -> p s c", s=4)
            self.stgs = self.stgs0 + [(aTf[:, i, :], [("stgA", i)] + [("aT", j) for j in range(8 * i, 8 * i + 8)]) for i in range(4)]
            w1src = Ld["w1"].ap().rearrange("(k p) n -> p k n", p=128)
            for cc_ in range(4):
                self.load_w(w1[:, :, cc_ * 1024:(cc_ + 1) * 1024], w1src[:, :, cc_ * 1024:(cc_ + 1) * 1024], ("+w1", cc_), 8, 1024)
            self.load_w(w2[:], Ld["w2"].ap().rearrange("(k p) n -> p k n", p=128), "+w2", 32, 1024)
            self.stgs = self.stgs0
            self.cast_engs = ("act",)
            for ci, (c0, isc) in enumerate(chunks):
                s = ci % 2
                dv = self.dvec[:, l, 1 if isc else 0]
                for j in range(32):
                    b = self.newbank()
                    for k in range(8):
                        self.mm(self.ps[b][:, 0:n], w1[:, k, j * 128:(j + 1) * 128], h2[:, s, k, :], k == 0, k == 7, [("+w1", j // 8), ("h2", s)], [("ps", b)])
                    r_ = j % 2
                    self.act(rl[:, r_, :], self.ps[b][:, 0:n], AF.Relu, [("ps", b)], [("rl", r_)])
                    self.tt("dve", aT[:, j, :], rl[:, r_, :], rl[:, r_, :], ALU.mult, [("rl", r_)], [("aT", j)])
                if ci + 1 < len(chunks):
                    prep(ci + 1)
                for m in range(8):
                    b = self.newbank()
                    for j in range(32):
                        self.mm(self.ps[b][:, 0:n], w2[:, j, m * 128:(m + 1) * 128], aT[:, j, :], j == 0, j == 31, ["+w2", ("aT", j)], [("ps", b)])
                    self.act(y2[:, m, :], self.ps[b][:, 0:n], AF.Copy, [("ps", b)], ["y2"])
                self.post_norm_res(y2, n, dv[:, 5, :], xch[:, s], xch[:, s], "y2", ("xch", s), ("xch", s), self.xn)
                if isc:
                    dst = xs2.ap()[:, :, c0:c0 + n]
                else:
                    dst = out_lat.ap()[:, :, c0:c0 + n]
                self.dma(dst, xch[:, s], [("xch", s)], [("out", ci)])
            T.flush()


def _fm(v):
    v = np.asarray(v, np.float32)
    return np.ascontiguousarray(v.reshape(-1, 128).T)


def _perm_cols():
    def hc(base, heads, swap):
        out = []
        for h in heads:
            d = np.arange(64)
            if swap:
                d = d ^ 1
            out += list(base + h * 64 + d)
        return out
    p = []
    p += hc(0, HP, False) + hc(0, HP, True)
    p += hc(384, [0, 1], False) + hc(384, [0, 1], True)
    p += hc(640, HP, False) + hc(640, HP, True)
    p += hc(1024, [0, 1], False) + hc(1024, [0, 1], True)
    p += list(range(1280, 1664)) + list(range(1664, 2048))
    p += list(range(512, 640)) + list(range(1152, 1280)) + list(range(2048, 2432))
    p += list(range(2432, 5504))
    assert len(p) == NWIN
    return np.array(p)


def _rope_tables(tok0):
    pos = np.arange(tok0, tok0 + NT)
    row = (pos // 64).astype(np.float32)
    col = (pos % 64).astype(np.float32)
    freqs = (np.float32(10000.0) ** (-np.arange(16, dtype=np.float32) / np.float32(16))).astype(np.float32)
    ang = np.concatenate([row[:, None] * freqs, col[:, None] * freqs], axis=-1).astype(np.float32)
    cos, sin = np.cos(ang).astype(np.float32), np.sin(ang).astype(np.float32)
    d = np.arange(128) % 64
    C = cos[:, d // 2].T
    S = sin[:, d // 2].T * np.where(d % 2 == 0, -1.0, 1.0)[:, None]
    return np.ascontiguousarray(C, np.float32), np.ascontiguousarray(S, np.float32)


def _mask_a(rank):
    ki = np.arange(128)[:, None]
    qi = np.arange(128)[None, :]
    prev = np.where(qi <= ki, 0.0, NEG).astype(np.float32)
    nxt = np.where(ki <= qi, 0.0, NEG).astype(np.float32)
    allneg = np.full((128, 128), NEG, np.float32)
    m = np.zeros((128, 10, 128), np.float32)
    m[:, 0], m[:, 1] = prev, nxt
    for r in range(4):
        m[:, 2 + r] = prev if r == rank - 1 else allneg
        m[:, 6 + r] = nxt if r == rank + 1 else allneg
    return m


def _rm01(rank):
    out = np.zeros((128, 44, 8), np.float32)
    idx = 0
    kl = (np.arange(128) // 64)[:, None]
    ql = np.arange(8)[None, :]
    for g in range(4):
        R0 = rank * 32 + g * 8
        for j in range(8):
            kr = R0 - 4 + 2 * j + kl
            qr = R0 + ql
            rs = np.clip(qr - 4, 0, 120)
            valid = (kr >= rs) & (kr < rs + 8) & (kr >= 0) & (kr < 128)
            if g == 0 and j < 2:
                for r in range(4):
                    out[:, idx] = valid if r == rank - 1 else 0.0
                    idx += 1
            elif g == 3 and j >= 6:
                for r in range(4):
                    out[:, idx] = valid if r == rank + 1 else 0.0
                    idx += 1
            else:
                out[:, idx] = valid
                idx += 1
    assert idx == 44
    return out


def _tb_table(rpb, interior=False):
    rpb = np.asarray(rpb, np.float32)
    kc = np.arange(64)[:, None]
    qc = np.arange(64)[None, :]
    ws = np.clip(qc - 8, 0, 48)
    colv = (kc >= ws) & (kc < ws + 16)
    dc = np.clip(kc - qc + 15, 0, 30)
    tb = np.zeros((2, 64, 6, 22, 64), np.float32)
    for kl in range(2):
        for u in range(22):
            dr = 17 + kl - u
            for h in range(6):
                if 0 <= dr <= 14:
                    v = rpb[h, dr][dc]
                else:
                    v = np.zeros((64, 64), np.float32)
                if interior and not (3 <= dr <= 10):
                    tb[kl, :, h, u, :] = NEG
                else:
                    tb[kl, :, h, u, :] = np.where(colv, v, NEG)
    return np.ascontiguousarray(tb.reshape(128, 6, 22, 64))


_CACHE = {}


def _get_nc(n_layers=2, taps=()):
    key = (n_layers, tuple(taps))
    if key not in _CACHE:
        _CACHE[key] = K(n_layers, taps).build()
    return _CACHE[key]


def make_in_maps(inputs, n_layers=2):
    f = lambda a: np.asarray(a, np.float32)
    x, c, ctx, c_ctx = f(inputs["x"]), f(inputs["c"]), f(inputs["ctx"]), f(inputs["c_ctx"])
    perm = _perm_cols()
    rows_ab = np.concatenate([np.arange(h * 64, (h + 1) * 64) for h in HP])
    shared = {}
    for l in range(n_layers):
        shared["wada%d" % l] = np.ascontiguousarray(f(inputs["w_ada"])[l])
        shared["badaT%d" % l] = _fm(f(inputs["b_ada"])[l])
        shared["gains%d" % l] = np.ascontiguousarray(np.stack(
            [_fm(f(inputs[k])[l]) for k in ("norm_mix_pre", "norm_mix_post", "norm_mlp_pre", "norm_mlp_post")], axis=1))
        shared["win%d" % l] = np.ascontiguousarray(f(inputs["w_in"])[l][:, perm])
        gq, gk = f(inputs["qnorm_b"])[l], f(inputs["knorm_b"])[l]
        d = np.arange(128) % 64
        shared["bgain%d" % l] = np.ascontiguousarray(np.stack([gq[d], gq[d ^ 1], gk[d], gk[d ^ 1]], axis=1))
        shared["sinkT%d" % l] = np.ascontiguousarray(np.broadcast_to(f(inputs["sink_a"])[l][None, :], (128, 6)))
        shared["TB%d" % l] = np.ascontiguousarray(np.stack(
            [_tb_table(f(inputs["rpb_c"])[l]), _tb_table(f(inputs["rpb_c"])[l], True)], axis=1))
        shared["wbr%d" % l] = np.ascontiguousarray(np.stack(
            [f(inputs["w_br_a"])[l][rows_ab], f(inputs["w_br_b"])[l][rows_ab], f(inputs["w_br_c"])[l]], axis=0))
        shared["wout%d" % l] = np.ascontiguousarray(f(inputs["w_out"])[l])
        shared["w1_%d" % l] = np.ascontiguousarray(f(inputs["w_mlp_in"])[l])
        shared["w2_%d" % l] = np.ascontiguousarray(f(inputs["w_mlp_out"])[l])
    in_maps = []
    for core in range(8):
        b, rank = core // 4, core % 4
        tok0 = rank * NT
        m = dict(shared)
        xs = x[b, tok0:tok0 + NT, :]
        m["xT"] = np.ascontiguousarray(xs.T.reshape(8, 128, NT).transpose(1, 0, 2))
        m["ctxT"] = np.ascontiguousarray(ctx[b].T.reshape(8, 128, NCX).transpose(1, 0, 2))
        m["ccT"] = np.ascontiguousarray(np.stack([_fm(c[b]), _fm(c_ctx)], axis=2))
        C, S = _rope_tables(tok0)
        m["ropeC"], m["ropeS"] = C, S
        m["maskA"] = _mask_a(rank)
        m["rm01"] = _rm01(rank)
        in_maps.append(m)
    return in_maps


def kernel(**inputs):
    nc = _get_nc(2)
    in_maps = make_in_maps(inputs, 2)
    res = run_bass_kernel_spmd(nc, in_maps, core_ids=list(range(8)))
    out = np.zeros((2, 4 * NT, 1024), np.float32)
    for core in range(8):
        b, rank = core // 4, core % 4
        yT = np.asarray(res.results[core]["yT"])
        out[b, rank * NT:(rank + 1) * NT, :] = yT.transpose(1, 0, 2).reshape(1024, NT).T
    return out
```

```python
import numpy as np
from contextlib import ExitStack
import concourse.bass as bass
import concourse.mybir as mybir
from concourse.bass_utils import run_bass_kernel_spmd

F32 = mybir.dt.float32
BF16 = mybir.dt.bfloat16
AF = mybir.ActivationFunctionType
ALU = mybir.AluOpType

NT = 2048
NCX = 256
NTOT = NT + NCX
NEG = -30000.0
EPS = 1e-6
HP = [0, 3, 1, 4, 2, 5]
C_QA, C_QAS, C_KA, C_KAS = 0, 384, 768, 896
C_QB, C_QBS, C_KB, C_KBS = 1024, 1408, 1792, 1920
C_QC, C_KC, C_V, C_G = 2048, 2432, 2816, 3456
NWIN = 6528
O_KB, O_VB, O_KAH, O_KAT, O_VAH, O_VAT = 0, 262144, 524288, 540672, 557056, 573440
O_KCH, O_KCT, O_VCH, O_VCT, CONTRIB = 589824, 688128, 786432, 884736, 983040


class Trk:
    ENGS = ("pe", "act", "dve", "pool", "sp")
    NDSEM = 12

    def __init__(self, nc, es):
        self.nc = nc
        self.esem = {e: es.enter_context(nc.semaphore("s_" + e)) for e in self.ENGS}
        self.ecnt = {e: 0 for e in self.ENGS}
        self.dsem = {q: [es.enter_context(nc.semaphore("d_%s%d" % (q, i))) for i in range(self.NDSEM)]
                     for q in ("sp", "pool")}
        self.dval = {q: [0] * self.NDSEM for q in ("sp", "pool")}
        self.dcnt = {"sp": 0, "pool": 0}
        self.ccsem = es.enter_context(nc.semaphore("s_cc"))
        self.ccval = 0
        self.ops = []
        self.bar = None
        self.waited = {e: {} for e in self.ENGS}

    def add(self, eng, fn, r=(), w=(), dma=False, cc=False, accw=False):
        self.ops.append(dict(eng=eng, fn=fn, r=tuple(r), w=tuple(w), dma=dma, cc=cc, accw=accw))

    def flush(self):
        ops = self.ops
        self.ops = []
        if not ops:
            return
        last_w, readers = {}, {}

        def is_acc(tok):
            t0_ = tok[0] if isinstance(tok, tuple) else tok
            return isinstance(t0_, str) and t0_.startswith("+")
        for i, op in enumerate(ops):
            deps = set()
            for r in op["r"]:
                if r in last_w:
                    deps.update(last_w[r])
            for w in op["w"]:
                if w in last_w:
                    if is_acc(w) and (op["dma"] or op["accw"]):
                        deps.add(last_w[w][0])
                    else:
                        deps.update(last_w[w])
                for rd in readers.get(w, {}).values():
                    if isinstance(rd, list):
                        deps.update(rd)
                    else:
                        deps.add(rd)
            deps.discard(i)
            if op["eng"] == "pe":
                deps = {d for d in deps if ops[d]["dma"] or ops[d]["eng"] != "pe"}
            op["deps"] = deps
            for r in op["r"]:
                rr = readers.setdefault(r, {})
                if op["dma"]:
                    rr.setdefault("dma", []).append(i)
                else:
                    rr[op["eng"]] = i
            for w in op["w"]:
                if is_acc(w) and (op["dma"] or op["accw"]) and w in last_w:
                    last_w[w] = last_w[w] + [i]
                else:
                    last_w[w] = [i]
                readers[w] = {}
        for op in ops:
            op["sig"] = False
        for op in ops:
            for d in op["deps"]:
                ops[d]["sig"] = True
        last_of = {}
        for i, op in enumerate(ops):
            if not op["dma"]:
                last_of[op["eng"]] = i
        for i in last_of.values():
            ops[i]["sig"] = True
        for op in ops:
            if op["cc"]:
                self.ccval += 1
                op["done"] = (self.ccsem, self.ccval)
                op["pre"] = None
            elif op["dma"]:
                q = op["eng"]
                k = self.dcnt[q] % self.NDSEM
                self.dcnt[q] += 1
                prev = self.dval[q][k]
                self.dval[q][k] += 16
                op["done"] = (self.dsem[q][k], self.dval[q][k])
                op["pre"] = (self.dsem[q][k], prev) if prev > 0 else None
            elif op["sig"]:
                self.ecnt[op["eng"]] += 1
                op["done"] = (self.esem[op["eng"]], self.ecnt[op["eng"]])
                op["pre"] = None
            else:
                op["done"] = None
                op["pre"] = None
        per = {e: [] for e in self.ENGS}
        for op in ops:
            per[op["eng"]].append(op)
        bar = self.bar
        waited = self.waited

        def emit(ename, e):
            wd = waited[ename]

            def wait(sem, val):
                key = id(sem)
                if wd.get(key, 0) < val:
                    e.wait_ge(sem, val)
                    wd[key] = val
            if bar and per[ename]:
                for sem, val in bar:
                    wait(sem, val)
            for op in per[ename]:
                for d in op["deps"]:
                    sem, val = ops[d]["done"]
                    wait(sem, val)
                if op["pre"] is not None:
                    wait(*op["pre"])
                if op["fn"] is None:
                    continue
                ins = op["fn"](e)
                if op["done"] is not None:
                    sem, val = op["done"]
                    if op["cc"]:
                        ins.then_inc(sem, 1)
                    elif op["dma"]:
                        ins.then_inc(sem, 16)
                    else:
                        ins.then_inc(sem, 1)

        with self.nc.Block() as block:
            @block.sync
            def _(e):
                emit("sp", e)

            @block.scalar
            def _(e):
                emit("act", e)

            @block.vector
            def _(e):
                emit("dve", e)

            @block.gpsimd
            def _(e):
                emit("pool", e)

            @block.tensor
            def _(e):
                emit("pe", e)
        nb = []
        for e in self.ENGS:
            if self.ecnt[e] > 0:
                nb.append((self.esem[e], self.ecnt[e]))
        for q in ("sp", "pool"):
            for k in range(self.NDSEM):
                if self.dval[q][k] > 0:
                    nb.append((self.dsem[q][k], self.dval[q][k]))
        if self.ccval > 0:
            nb.append((self.ccsem, self.ccval))
        self.bar = nb

    def final_wait(self):
        bar = self.bar
        with self.nc.Block() as block:
            @block.sync
            def _(e):
                for sem, val in bar:
                    e.wait_ge(sem, val)


def bcast(ap, pos, n):
    l = [list(x) for x in ap.ap]
    l.insert(pos, [0, n])
    return bass.AP(ap.tensor, ap.offset, l)


def dview(t, off, dims):
    return bass.AP(t, off, [list(d) for d in dims])


class K:
    def __init__(self, n_layers=2, taps=(), stop=None):
        self.nl = n_layers
        self.taps = set(taps)
        self.stop = stop
        self.nc = bass.Bass("TRN2", target_bir_lowering=False)
        self.uid = 0

    def din(self, name, shape, dt=F32):
        return self.nc.dram_tensor(name, list(shape), dt, kind="ExternalInput")

    def dscr(self, name, shape, dt=BF16, tap=False):
        if tap and name in self.taps:
            return self.nc.dram_tensor(name, list(shape), dt, kind="ExternalOutput")
        return self.nc.dram_tensor(name, list(shape), dt)

    def sb(self, es, name, shape, dt):
        self.uid += 1
        return es.enter_context(self.nc.sbuf_tensor("%s_%d" % (name, self.uid), list(shape), dt))

    def mm(self, out, lhsT, rhs, start, stop, r, w):
        self.T.add("pe", lambda e: e.matmul(out, lhsT=lhsT, rhs=rhs, start=start, stop=stop), r, w)

    def act(self, out, in_, func, r, w, bias=None, scale=None):
        kw = {}
        if bias is not None:
            kw["bias"] = bias
        if scale is not None:
            kw["scale"] = scale
        self.T.add("act", lambda e: e.activation(out=out, in_=in_, func=func, **kw), r, w)

    def tt(self, eng, out, in0, in1, op, r, w):
        self.T.add(eng, lambda e: e.tensor_tensor(out=out, in0=in0, in1=in1, op=op), r, w)

    def ts(self, eng, out, in0, s1, s2, op0, op1, r, w):
        if op1 is None:
            self.T.add(eng, lambda e: e.tensor_scalar(out=out, in0=in0, scalar1=s1, scalar2=None, op0=op0), r, w)
        else:
            self.T.add(eng, lambda e: e.tensor_scalar(out=out, in0=in0, scalar1=s1, scalar2=s2, op0=op0, op1=op1), r, w)

    def stt(self, out, in0, scalar, in1, op0, op1, r, w):
        self.T.add("dve", lambda e: e.scalar_tensor_tensor(out=out, in0=in0, scalar=scalar, in1=in1, op0=op0, op1=op1), r, w)

    def cp(self, eng, out, in_, r, w):
        self.T.add(eng, lambda e: e.tensor_copy(out=out, in_=in_), r, w)

    def memset(self, eng, ap, val, w):
        self.T.add(eng, lambda e: e.memset(ap, val), (), w)

    def recip(self, out, in_, r, w):
        self.T.add("dve", lambda e: e.reciprocal(out=out, in_=in_), r, w)

    def dma(self, out, in_, r, w, q="sp"):
        self.T.add(q, lambda e: e.dma_start(out=out, in_=in_), r, w, dma=True)

    def newbank(self):
        if self.split:
            b = self.bankc % self.nsb
        else:
            b = self.bankc % 8
        self.bankc += 1
        return b

    def newacc(self):
        b = self.nsb + self.accc % (8 - self.nsb)
        self.accc += 1
        return b

    def alloc_norm(self, es, nmax):
        self.sq_buf = self.sb(es, "sqbuf", [128, 8, nmax], BF16)
        self.lnt = self.sb(es, "lnt", [128, nmax], F32)
        self.rstd = self.sb(es, "rstd", [128, nmax], F32)
        self.xn = self.sb(es, "xn", [128, 8, nmax], F32)

    def cast(self, dst, src, r, w):
        engs = self.cast_engs
        e = engs[self.cast_i % len(engs)]
        self.cast_i += 1
        if e == "act":
            self.T.add("act", lambda e_: e_.activation(out=dst, in_=src, func=AF.Copy), r, w, accw=True)
        else:
            self.T.add(e, lambda e_: e_.tensor_copy(out=dst, in_=src), r, w, accw=True)

    def load_w(self, dst, src, tok_w, nk, ncols):
        CH = 1024
        per = max(1, CH // ncols)
        k = 0
        while k < nk:
            kk = min(per, nk - k)
            if ncols > CH:
                assert per == 1
                c = 0
                while c < ncols:
                    cc = min(CH, ncols - c)
                    sap, stoks = self.stgs[self.stg_i % len(self.stgs)]
                    self.stg_i += 1
                    st = sap[:, 0:cc]
                    self.dma(st, src[:, k, c:c + cc], (), stoks)
                    self.cast(dst[:, k, c:c + cc], st, stoks, [tok_w])
                    c += cc
            else:
                sap, stoks = self.stgs[self.stg_i % len(self.stgs)]
                self.stg_i += 1
                st = sap[:, 0:kk * ncols].rearrange("p (k c) -> p k c", k=kk)
                self.dma(st, src[:, k:k + kk, :], (), stoks)
                self.cast(dst[:, k:k + kk, :], st, stoks, [tok_w])
            k += kk

    def rstd_of(self, es_tmp, src, nk, n, div, tokr, name):
        sq = self.sq_buf[:, 0:nk, 0:n]
        self.act(sq, src, AF.Square, [tokr], ["sqbuf"])
        b = self.newbank()
        for k in range(nk):
            self.mm(self.ps[b][:, 0:n], self.ones[:], sq[:, k, :], k == 0, k == nk - 1, ["sqbuf", "ones"], [("ps", b)])
        self.act(self.lnt[:, 0:n], self.ps[b][:, 0:n], AF.Ln, [("ps", b)], ["lnt"], bias=self.epsc[:, 0:1], scale=1.0 / div)
        self.act(self.rstd[:, 0:n], self.lnt[:, 0:n], AF.Exp, ["lnt"], ["rstd"], scale=-0.5)
        return self.rstd[:, 0:n]

    def norm_mod(self, xch, n, GG, SH, hT, tokx, tokh):
        rs = self.rstd_of(None, xch[:, :, 0:n], 8, n, 1024.0, tokx, "nm")
        self.tt("dve", self.xn[:, :, 0:n], xch[:, :, 0:n], bcast(rs, 1, 8), ALU.mult, [tokx, "rstd"], ["xn"])
        for k in range(8):
            self.ts("dve", hT[:, k, 0:n], self.xn[:, k, 0:n], GG[:, k:k + 1], SH[:, k:k + 1], ALU.mult, ALU.add,
                    ["xn", "mods"], [tokh])

    def build(self):
        nc = self.nc
        NL = self.nl
        xT = self.din("xT", [128, 8, NT])
        ctxT = self.din("ctxT", [128, 8, NCX])
        ccT = self.din("ccT", [128, 8, 2])
        ropeC = self.din("ropeC", [128, NT])
        ropeS = self.din("ropeS", [128, NT])
        maskA = self.din("maskA", [128, 10, 128])
        rm01 = self.din("rm01", [128, 44, 8])
        L = []
        for l in range(NL):
            d = dict(
                wada=self.din("wada%d" % l, [1024, 6144]),
                badaT=self.din("badaT%d" % l, [128, 48]),
                gains=self.din("gains%d" % l, [128, 4, 8]),
                win=self.din("win%d" % l, [1024, NWIN]),
                bgain=self.din("bgain%d" % l, [128, 4]),
                sinkT=self.din("sinkT%d" % l, [128, 6]),
                TB=self.din("TB%d" % l, [128, 2, 6 * 22 * 64]),
                wbr=self.din("wbr%d" % l, [3, 384, 1024]),
                wout=self.din("wout%d" % l, [1024, 1024]),
                w1=self.din("w1_%d" % l, [1024, 4096]),
                w2=self.din("w2_%d" % l, [4096, 1024]),
            )
            L.append(d)
        yT = nc.dram_tensor("yT", [128, 8, NT], F32, kind="ExternalOutput")
        xs = [self.dscr("xs%d" % i, [128, 8, NTOT], F32, tap=True) for i in range(2 * NL)]
        q_d = [self.dscr("q_d%d" % i, [128, 3, NTOT], BF16, tap=True) for i in range(3)]
        kaT_d = self.dscr("kaT_d", [128, NTOT], BF16, tap=True)
        va_d = self.dscr("va_d", [NTOT, 128], BF16, tap=True)
        kcT_d = self.dscr("kcT_d", [128, 3, NTOT], BF16, tap=True)
        vc_d = self.dscr("vc_d", [NTOT, 384], BF16, tap=True)
        kbcT_d = self.dscr("kbcT_d", [128, NCX], BF16, tap=True)
        vbc_d = self.dscr("vbc_d", [NCX, 128], BF16, tap=True)
        contrib = self.dscr("contrib", [960, 1024], BF16, tap=True)
        gathB = self.dscr("gathB", [4 * 512, 1024], BF16)
        gathH = self.dscr("gathH", [4 * 448, 1024], BF16)

        class G:
            @staticmethod
            def view(r, off, dims):
                if off < 524288:
                    return dview(gathB, r * 524288 + off, dims)
                return dview(gathH, r * 458752 + (off - 524288), dims)
        gath = G
        self.gathB, self.gathH = gathB, gathH
        o_d = [self.dscr("o_d%d" % i, [128, 3, NTOT], BF16, tap=True) for i in range(3)]
        mods_d = self.dscr("mods_d", [128, NL, 48, 2], F32, tap=True)

        with ExitStack() as es0:
            self.T = Trk(nc, es0)
            T = self.T
            self.bankc = 0
            self.accc = 0
            self.nsb = 4
            self.split = False
            self.stg_i = 0
            self.cast_i = 0
            self.cast_engs = ("act",)
            self.psall = es0.enter_context(nc.psum_tensor("psall", [128, 4096], F32))
            self.ps = [self.psall[:, i * 512:(i + 1) * 512] for i in range(8)]
            self.ones = self.sb(es0, "ones", [128, 128], BF16)
            self.bd = self.sb(es0, "bd", [128, 128], BF16)
            self.epsc = self.sb(es0, "epsc", [128, 1], F32)
            self.modsb = self.sb(es0, "modsb", [128, NL, 48, 2], F32)
            self.dvec = self.sb(es0, "dvec", [128, NL, 2, 6, 8], F32)
            self.gn = self.sb(es0, "gn", [128, NL, 4, 8], F32)
            self.stg = self.sb(es0, "stg", [128, 3, 1024], F32)
            self.nstg = 3
            self.stgs0 = [(self.stg[:, i, :], [("stg", i)]) for i in range(3)]
            self.stgs = self.stgs0

            with ExitStack() as es:
                self.memset("pool", self.ones[:], 1.0, ["ones"])
                self.memset("pool", self.bd[:], 0.0, ["bd"])
                self.memset("pool", self.bd[0:64, 0:64], 1.0, ["bd"])
                self.memset("pool", self.bd[64:128, 64:128], 1.0, ["bd"])
                self.memset("pool", self.epsc[:], EPS, ["epsc"])
                cc_s = self.sb(es, "cc_s", [128, 8, 2], F32)
                sil = self.sb(es, "sil", [128, 8, 2], F32)
                self.dma(cc_s[:], ccT.ap(), (), ["cc_s"])
                self.act(sil[:], cc_s[:], AF.Silu, ["cc_s"], ["sil"])
                wa = self.sb(es, "wa", [128, 2, 8, 768], F32)
                bad = self.sb(es, "bad", [128, NL, 48], F32)
                for l in range(NL):
                    self.dma(bad[:, l, :], L[l]["badaT"].ap(), (), ["bad"])
                    self.dma(self.gn[:, l], L[l]["gains"].ap(), (), ["gn"])
                    wsrc = L[l]["wada"].ap().rearrange("(k p) n -> p k n", p=128)
                    b = self.newbank()
                    for g in range(8):
                        s = g % 2
                        self.dma(wa[:, s], wsrc[:, :, g * 768:(g + 1) * 768], (), [("wa", s)])
                        for mm_ in range(6):
                            m = g * 6 + mm_
                            for k in range(8):
                                self.mm(self.ps[b][:, 2 * m:2 * m + 2], wa[:, s, k, mm_ * 128:(mm_ + 1) * 128], sil[:, k, :],
                                        k == 0, k == 7, [("wa", s), "sil"], [("ps", b)])
                    self.tt("dve", self.modsb[:, l], self.ps[b][:, 0:96].rearrange("p (m t) -> p m t", t=2),
                            bcast(bad[:, l, :], 2, 2), ALU.add, [("ps", b), "bad"], ["modsb"])
                    for t in range(2):
                        mv = self.modsb[:, l, :, t]
                        dv = self.dvec[:, l, t]
                        self.stt(dv[:, 0, :], mv[:, 8:16], 1.0, self.gn[:, l, 0, :], ALU.add, ALU.mult, ["modsb", "gn"], ["mods"])
                        self.cp("dve", dv[:, 1, :], mv[:, 0:8], ["modsb"], ["mods"])
                        self.tt("dve", dv[:, 2, :], mv[:, 16:24], self.gn[:, l, 1, :], ALU.mult, ["modsb", "gn"], ["mods"])
                        self.stt(dv[:, 3, :], mv[:, 32:40], 1.0, self.gn[:, l, 2, :], ALU.add, ALU.mult, ["modsb", "gn"], ["mods"])
                        self.cp("dve", dv[:, 4, :], mv[:, 24:32], ["modsb"], ["mods"])
                        self.tt("dve", dv[:, 5, :], mv[:, 40:48], self.gn[:, l, 3, :], ALU.mult, ["modsb", "gn"], ["mods"])
                if "mods_d" in self.taps:
                    self.dma(mods_d.ap(), self.modsb[:], ["modsb"], ["mods_d"])
                T.flush()

            for l in range(NL):
                last = (l == NL - 1)
                if l == 0:
                    xin_lat = lambda c0, n: xT.ap()[:, :, c0:c0 + n]
                    xin_ctx = ctxT.ap()
                else:
                    xin_lat = (lambda xs_: (lambda c0, n: xs_.ap()[:, :, c0:c0 + n]))(xs[2 * l - 1])
                    xin_ctx = xs[2 * l - 1].ap()[:, :, NT:NTOT]
                if self.stop == "mods":
                    break
                self.phase_proj(L[l], l, last, xin_lat, xin_ctx, ropeC, ropeS, q_d, kaT_d, va_d, kcT_d, vc_d, kbcT_d, vbc_d, contrib)
                if self.stop == "proj%d" % l or (self.stop or "").startswith("proj0"):
                    break
                self.phase_gather(contrib, gath)
                self.phase_attn_a(L[l], l, last, q_d[0], kaT_d, va_d, gath, maskA, o_d[0])
                if self.stop == "attn_a%d" % l:
                    break
                self.phase_attn_b(L[l], l, last, q_d[1], kbcT_d, vbc_d, gath, o_d[1])
                if self.stop == "attn_b%d" % l:
                    break
                self.phase_attn_c(L[l], l, last, q_d[2], kcT_d, vc_d, gath, rm01, o_d[2])
                if self.stop == "attn_c%d" % l:
                    break
                self.phase_mix(L[l], l, last, xin_lat, xin_ctx, o_d, xs[2 * l])
                if self.stop == "mix%d" % l:
                    break
                out_lat = yT if last else xs[2 * l + 1]
                self.phase_mlp(L[l], l, last, xs[2 * l], out_lat, xs[2 * l + 1])
                if self.stop == "mlp%d" % l:
                    break
            T.final_wait()
        return nc

    def phase_proj(self, Ld, l, last, xin_lat, xin_ctx, ropeC, ropeS, q_d, kaT_d, va_d, kcT_d, vc_d, kbcT_d, vbc_d, contrib):
        T = self.T
        with ExitStack() as es:
            self.split = False
            self.alloc_norm(es, 512)
            hT = self.sb(es, "hT", [128, 8, NTOT], BF16)
            xch = self.sb(es, "xch", [128, 2, 8, 512], F32)
            rC = self.sb(es, "rC", [128, NT], F32)
            rS = self.sb(es, "rS", [128, NT], F32)
            bg = self.sb(es, "bg", [128, 4], F32)
            wt = self.sb(es, "wt", [128, 2, 8, 640], BF16)
            ost = self.sb(es, "ost", [128, 4, 640], BF16)
            t1 = self.sb(es, "t1", [128, 2, 512], F32)
            t2 = self.sb(es, "t2", [128, 2, 512], F32)
            sqh = self.sb(es, "sqh", [128, 2, 512], BF16)
            lnh = self.sb(es, "lnh", [128, 2, 512], F32)
            rsh = self.sb(es, "rsh", [128, 2, 512], F32)
            self.dma(rC[:], ropeC.ap(), (), ["rC"])
            self.dma(rS[:], ropeS.ap(), (), ["rS"])
            self.dma(bg[:], Ld["bgain"].ap(), (), ["bg"])
            chunks = [(i * 512, 512, False) for i in range(4)] + [(NT, NCX, True)]
            for ci, (c0, n, isc) in enumerate(chunks):
                s = ci % 2
                src = xin_ctx if isc else xin_lat(c0, n)
                self.dma(xch[:, s, :, 0:n], src, (), [("xch", s)])
                dv = self.dvec[:, l, 1 if isc else 0]
                self.norm_mod(xch[:, s], n, dv[:, 0, :], dv[:, 1, :], hT[:, :, c0:c0 + n], ("xch", s), "hT")
            if self.stop == "proj0a":
                T.flush()
                return
            win = Ld["win"].ap().rearrange("(k p) n -> p k n", p=128)
            units = []
            units.append(("KB", 0, [C_KB, C_KBS]))
            units.append(("V", 0, None))
            units.append(("KA", 0, [C_KA, C_KAS]))
            for mi in range(3):
                units.append(("KC", mi, [C_KC + mi * 128]))
            n_kv_units = len(units)
            for mi in range(3):
                units.append(("QB", mi, [C_QB + mi * 128, C_QBS + mi * 128]))
            for mi in range(3):
                units.append(("QA", mi, [C_QA + mi * 128, C_QAS + mi * 128]))
            for mi in range(3):
                units.append(("QC", mi, [C_QC + mi * 128]))
            ctr = dict(ost=0, t=0)

            ctoks = []

            def store(srcs_dsts, tok):
                for dst, src in srcs_dsts:
                    ctr["st"] = ctr.get("st", 0) + 1
                    wtk = ("dout", ctr["st"])
                    if dst.tensor.name == contrib.name:
                        ctoks.append(wtk)
                    self.dma(dst, src, [tok], [wtk])

            if self.stop and self.stop.startswith("proj0u"):
                sel = [int(x) for x in self.stop[6:].split("_")]
                units = [units[i] for i in sel]
            def load_unit(ui):
                kind, mi, cols = units[ui]
                ws = ui % 2
                wtok = ("+wt", ws)
                if kind == "V":
                    self.load_w(wt[:, ws, :, 0:640], win[:, :, C_V:C_V + 640], wtok, 8, 640)
                else:
                    for j, c in enumerate(cols):
                        self.load_w(wt[:, ws, :, j * 128:(j + 1) * 128], win[:, :, c:c + 128], wtok, 8, 128)
            load_unit(0)
            for ui, (kind, mi, cols) in enumerate(units):
                ws = ui % 2
                wtok = ("+wt", ws)
                if ui == n_kv_units and not (self.stop or "").startswith("proj0u"):
                    self.emit_gather(contrib, list(ctoks))
                if ui + 1 < len(units):
                    load_unit(ui + 1)
                if kind == "V":
                    for tile in range(NTOT // 128):
                        t0 = tile * 128
                        b0, b1 = self.newbank(), self.newbank()
                        for k in range(8):
                            self.mm(self.ps[b0][:, 0:512], hT[:, k, t0:t0 + 128], wt[:, ws, k, 0:512], k == 0, k == 7, ["hT", wtok], [("ps", b0)])
                        for k in range(8):
                            self.mm(self.ps[b1][:, 0:128], hT[:, k, t0:t0 + 128], wt[:, ws, k, 512:640], k == 0, k == 7, ["hT", wtok], [("ps", b1)])
                        o = ctr["ost"] % 4
                        ctr["ost"] += 1
                        otok = ("ost", o)
                        self.act(ost[:, o, 0:512], self.ps[b0][:, 0:512], AF.Copy, [("ps", b0)], [otok])
                        self.cp("dve", ost[:, o, 512:640], self.ps[b1][:, 0:128], [("ps", b1)], [otok])
                        cb = contrib
                        dl = [(va_d.ap()[t0:t0 + 128, :], ost[:, o, 0:128]),
                              (vc_d.ap()[t0:t0 + 128, :], ost[:, o, 256:640])]
                        if tile < 16:
                            dl.append((dview(cb, O_VB + t0 * 128, [[128, 128], [1, 128]]), ost[:, o, 128:256]))
                            if tile == 0:
                                dl.append((dview(cb, O_VAH, [[128, 128], [1, 128]]), ost[:, o, 0:128]))
                            if tile == 15:
                                dl.append((dview(cb, O_VAT, [[128, 128], [1, 128]]), ost[:, o, 0:128]))
                            if tile < 2:
                                dl.append((dview(cb, O_VCH + tile * 128 * 384, [[384, 128], [1, 384]]), ost[:, o, 256:640]))
                            if tile >= 14:
                                dl.append((dview(cb, O_VCT + (tile - 14) * 128 * 384, [[384, 128], [1, 384]]), ost[:, o, 256:640]))
                        else:
                            dl.append((vbc_d.ap()[t0 - NT:t0 - NT + 128, :], ost[:, o, 128:256]))
                        store(dl, otok)
                    continue
                for ci, (c0, n, isc) in enumerate(chunks):
                    if isc and last and kind in ("QA", "QB", "QC"):
                        continue
                    hs = [hT[:, k, c0:c0 + n] for k in range(8)]
                    bq = self.newbank()
                    for k in range(8):
                        self.mm(self.ps[bq][:, 0:n], wt[:, ws, k, 0:128], hs[k], k == 0, k == 7, ["hT", wtok], [("ps", bq)])
                    pq = self.ps[bq][:, 0:n]
                    o = ctr["ost"] % 4
                    ctr["ost"] += 1
                    otok = ("ost", o)
                    oo = ost[:, o, 0:n]
                    need_sw = (len(cols) == 2) and not isc
                    if need_sw:
                        bs = self.newbank()
                        for k in range(8):
                            self.mm(self.ps[bs][:, 0:n], wt[:, ws, k, 128:256], hs[k], k == 0, k == 7, ["hT", wtok], [("ps", bs)])
                        psw = self.ps[bs][:, 0:n]
                    tt_ = ctr["t"] % 2
                    ctr["t"] += 1
                    a1, a2 = t1[:, tt_, 0:n], t2[:, tt_, 0:n]
                    k1, k2 = ("t1", tt_), ("t2", tt_)
                    if kind in ("QA", "KA"):
                        if isc:
                            self.act(oo, pq, AF.Copy, [("ps", bq)], [otok])
                        else:
                            self.tt("dve", a1, pq, rC[:, c0:c0 + n], ALU.mult, [("ps", bq), "rC"], [k1])
                            self.tt("dve", a2, psw, rS[:, c0:c0 + n], ALU.mult, [("ps", bs), "rS"], [k2])
                            self.tt("dve", oo, a1, a2, ALU.add, [k1, k2], [otok])
                    elif kind in ("QB", "KB"):
                        gi = 0 if kind == "QB" else 2
                        self.act(sqh[:, tt_, 0:n], pq, AF.Square, [("ps", bq)], [("sqh", tt_)])
                        bss = self.newbank()
                        self.mm(self.ps[bss][:, 0:n], self.bd[:], sqh[:, tt_, 0:n], True, True, [("sqh", tt_), "bd"], [("ps", bss)])
                        self.act(lnh[:, tt_, 0:n], self.ps[bss][:, 0:n], AF.Ln, [("ps", bss)], [("lnh", tt_)], bias=self.epsc[:, 0:1], scale=1.0 / 64)
                        self.act(rsh[:, tt_, 0:n], lnh[:, tt_, 0:n], AF.Exp, [("lnh", tt_)], [("rsh", tt_)], scale=-0.5)
                        if isc:
                            self.stt(oo, pq, bg[:, gi:gi + 1], rsh[:, tt_, 0:n], ALU.mult, ALU.mult, [("ps", bq), "bg", ("rsh", tt_)], [otok])
                        else:
                            self.stt(a1, pq, bg[:, gi:gi + 1], rC[:, c0:c0 + n], ALU.mult, ALU.mult, [("ps", bq), "bg", "rC", ("sqh", tt_)], [k1])
                            self.stt(a2, psw, bg[:, gi + 1:gi + 2], rS[:, c0:c0 + n], ALU.mult, ALU.mult, [("ps", bs), "bg", "rS"], [k2])
                            self.tt("dve", a1, a1, a2, ALU.add, [k1, k2], [k1])
                            self.tt("dve", oo, a1, rsh[:, tt_, 0:n], ALU.mult, [k1, ("rsh", tt_)], [otok])
                    else:
                        self.act(oo, pq, AF.Copy, [("ps", bq)], [otok])
                    dl = []
                    cb = contrib
                    if kind == "QA":
                        dl.append((q_d[0].ap()[:, mi, c0:c0 + n], oo))
                    elif kind == "QB":
                        dl.append((q_d[1].ap()[:, mi, c0:c0 + n], oo))
                    elif kind == "QC":
                        dl.append((q_d[2].ap()[:, mi, c0:c0 + n], oo))
                    elif kind == "KA":
                        dl.append((kaT_d.ap()[:, c0:c0 + n], oo))
                        if ci == 0:
                            dl.append((dview(cb, O_KAH, [[128, 128], [1, 128]]), ost[:, o, 0:128]))
                        if ci == 3:
                            dl.append((dview(cb, O_KAT, [[128, 128], [1, 128]]), ost[:, o, 384:512]))
                    elif kind == "KB":
                        if isc:
                            dl.append((kbcT_d.ap(), oo))
                        else:
                            dl.append((dview(cb, O_KB + c0, [[2048, 128], [1, n]]), oo))
                    elif kind == "KC":
                        dl.append((kcT_d.ap()[:, mi, c0:c0 + n], oo))
                        if ci == 0:
                            dl.append((dview(cb, O_KCH + mi * 256, [[768, 128], [1, 256]]), ost[:, o, 0:256]))
                        if ci == 3:
                            dl.append((dview(cb, O_KCT + mi * 256, [[768, 128], [1, 256]]), ost[:, o, 256:512]))
                    store(dl, otok)
            T.flush()

    def phase_gather(self, contrib, gath):
        return

    def emit_gather(self, contrib, rtoks):
        T = self.T
        gB, gH = self.gathB, self.gathH
        T.add("pool", lambda e: e.collective_compute("AllGather", ALU.bypass, replica_groups=[[0, 1, 2, 3], [4, 5, 6, 7]],
                                                     ins=[contrib.ap()[0:512, :]], outs=[gB.ap()]), rtoks, ["gathB"], cc=True)
        T.add("pool", lambda e: e.collective_compute("AllGather", ALU.bypass, replica_groups=[[0, 1, 2, 3], [4, 5, 6, 7]],
                                                     ins=[contrib.ap()[512:960, :]], outs=[gH.ap()]), rtoks, ["gathH"], cc=True)

    def attn_fin(self, bank, n, base, out_ap, rc, rctok, extra=None, extra_tok=None, shape3=None, act_recip=True):
        ob = 64 - base
        k = self.fc_i % 2
        self.fc_i += 1
        fct = ("fc", k)
        self.cp("dve", self.fc[:, k, 0:n], self.ps[bank][:, 0:n], [("ps", bank)], [fct])
        den = self.fc[ob:ob + 64, k, 0:n]
        num = self.fc[base:base + 64, k, 0:n]
        rcp = rc[base:base + 64, 0:n]
        tmp = rc[ob:ob + 64, 0:n]
        if shape3 is not None:
            a, b_ = shape3
            den = den.rearrange("p (a b) -> p a b", a=a)
            num = num.rearrange("p (a b) -> p a b", a=a)
            rcp = rcp.rearrange("p (a b) -> p a b", a=a)
            tmp = tmp.rearrange("p (a b) -> p a b", a=a)
        src, srct = den, fct
        if extra is not None:
            self.tt("dve", tmp, den, extra, ALU.add, [fct, extra_tok], [rctok])
            src, srct = tmp, rctok
        if act_recip:
            self.act(rcp, src, AF.Ln, [srct], [rctok])
            self.act(rcp, rcp, AF.Exp, [rctok], [rctok], scale=-1.0)
        else:
            self.recip(rcp, src, [srct], [rctok])
        self.tt("dve", out_ap, num, rcp, ALU.mult, [fct, rctok], ["oT"])

    def pipeline(self, items, look=3, group=1):
        groups = [items[i:i + group] for i in range(0, len(items), group)]
        banks = {}

        def qks(gi):
            for k, itm in enumerate(groups[gi]):
                banks[(gi, k)] = self.newbank()
                itm[0](banks[(gi, k)])
        for gi in range(min(look, len(groups))):
            qks(gi)
        for gi in range(len(groups)):
            if gi + look < len(groups):
                qks(gi + look)
            for k, itm in enumerate(groups[gi]):
                itm[1](banks[(gi, k)])
            for k, itm in enumerate(groups[gi]):
                itm[2]()
                if itm[3] is not None:
                    itm[3]()

    def load_vaug(self, dst, src, tokn):
        self.dma(dst, src, (), [tokn])

    def phase_attn_a(self, Ld, l, last, qa_d, kaT_d, va_d, gath, maskA, oa_d):
        T = self.T
        with ExitStack() as es:
            self.split = True
            self.nsb = 6
            self.bankc = 0
            QT = self.sb(es, "QT", [128, 3, NTOT], BF16)
            KT = self.sb(es, "KT", [128, NTOT], BF16)
            KH = self.sb(es, "KH", [128, 8, 128], BF16)
            VA = self.sb(es, "VA", [128, 18, 2, 128], BF16)
            VH = self.sb(es, "VH", [128, 8, 2, 128], BF16)
            MA = self.sb(es, "MA", [128, 10, 128], F32)
            ES = self.sb(es, "ES", [128, 6, 128], F32)
            sk = self.sb(es, "sk", [128, 6], F32)
            PT = self.sb(es, "PT", [128, 8, 512], BF16)
            tm = self.sb(es, "tm", [128, 4, 384], F32)
            rc = self.sb(es, "rc", [128, 2, 512], F32)
            self.fc = self.sb(es, "fc", [128, 2, 512], F32)
            self.fc_i = 0
            oT = self.sb(es, "oT", [128, 3, NTOT], BF16)
            nq = NT if last else NTOT
            self.dma(QT[:, :, 0:nq], qa_d.ap()[:, :, 0:nq], (), ["+QT"])
            self.dma(KT[:], kaT_d.ap(), (), ["+KT"])
            self.memset("pool", VA[:], 1.0, ["+VA"])
            self.memset("pool", VH[:], 1.0, ["+VH"])
            vsrc = va_d.ap().rearrange("(t p) c -> p t c", p=128)
            self.dma(VA[:, :, 0, 0:64], vsrc[:, :, 0:64], (), ["+VA"])
            self.dma(VA[:, :, 1, 64:128], vsrc[:, :, 64:128], (), ["+VA"])
            for r in range(4):
                pass
                self.dma(KH[:, r, :], gath.view(r, O_KAT, [[128, 128], [1, 128]]), ["gath"], ["+KH"])
                self.dma(KH[:, 4 + r, :], gath.view(r, O_KAH, [[128, 128], [1, 128]]), ["gath"], ["+KH"])
                self.dma(VH[:, r, 0, 0:64], gath.view(r, O_VAT, [[128, 128], [1, 64]]), ["gath"], ["+VH"])
                self.dma(VH[:, r, 1, 64:128], gath.view(r, O_VAT + 64, [[128, 128], [1, 64]]), ["gath"], ["+VH"])
                self.dma(VH[:, 4 + r, 0, 0:64], gath.view(r, O_VAH, [[128, 128], [1, 64]]), ["gath"], ["+VH"])
                self.dma(VH[:, 4 + r, 1, 64:128], gath.view(r, O_VAH + 64, [[128, 128], [1, 64]]), ["gath"], ["+VH"])
            self.dma(MA[:], maskA.ap(), (), ["MA"])
            self.dma(sk[:], Ld["sinkT"].ap(), (), ["sk"])
            self.act(sk[:], sk[:], AF.Exp, ["sk"], ["sk"])
            self.cp("dve", ES[:], bcast(sk[:], 2, 128), ["sk"], ["ES"])
            it = dict(p=0, t=0, r=0)
            items = []

            def run(qcols, nqc, base, tiles, shape3, out_ap, es_ap, items):
                n = 3 * nqc
                rhs = QT[base:base + 64, :, qcols:qcols + nqc]
                bo = self.nsb + base // 64
                nt = len(tiles)
                for ti, (kap, vap, mk) in enumerate(tiles):
                    p = it["p"] % 8
                    it["p"] += 1
                    t = it["t"] % 4
                    if mk is not None:
                        it["t"] += 1

                    def qk(b, kap=kap):
                        self.mm(self.ps[b][:, 0:n].rearrange("p (a b) -> p a b", a=3), kap, rhs, True, True, ["+QT", "+KT", "+KH"], [("ps", b)])

                    def sm(b, mk=mk, p=p, t=t):
                        pt = PT[:, p, 0:n]
                        if mk is not None:
                            self.stt(tm[:, t, 0:n].rearrange("p (a b) -> p a b", a=3), self.ps[b][:, 0:n].rearrange("p (a b) -> p a b", a=3),
                                     0.125, bcast(mk, 1, 3), ALU.mult, ALU.add, [("ps", b), "MA"], [("tm", t)])
                            self.act(pt, tm[:, t, 0:n], AF.Exp, [("tm", t)], [("PT", p)])
                        else:
                            self.act(pt, self.ps[b][:, 0:n], AF.Exp, [("ps", b)], [("PT", p)], scale=0.125)

                    def pv(vap=vap, p=p, ti=ti):
                        self.mm(self.ps[bo][:, 0:n], vap, PT[:, p, 0:n], ti == 0, ti == nt - 1, [("PT", p), "+VA", "+VH"], [("ps", bo)])
                    fin = None
                    if ti == nt - 1:
                        r_ = it["r"] % 2
                        it["r"] += 1

                        def fin(r_=r_):
                            self.attn_fin(bo, n, base, out_ap, rc[:, r_], ("rc", r_), extra=es_ap, extra_tok="ES", shape3=shape3)
                    items.append((qk, sm, pv, fin))

            all_items = items
            half_items = []
            for half in range(2):
                base = half * 64
                ob = 64 - base
                items = []
                half_items.append(items)
                esl = ES[ob:ob + 64, 3 * half:3 * half + 3, :]
                for j in range(16):
                    tiles = []
                    if j > 0:
                        tiles.append((KT[base:base + 64, (j - 1) * 128:j * 128], VA[:, j - 1, half, :], MA[:, 0, :]))
                    else:
                        for r in range(4):
                            tiles.append((KH[base:base + 64, r, :], VH[:, r, half, :], MA[:, 2 + r, :]))
                    tiles.append((KT[base:base + 64, j * 128:(j + 1) * 128], VA[:, j, half, :], None))
                    if j < 15:
                        tiles.append((KT[base:base + 64, (j + 1) * 128:(j + 2) * 128], VA[:, j + 1, half, :], MA[:, 1, :]))
                    else:
                        for r in range(4):
                            tiles.append((KH[base:base + 64, 4 + r, :], VH[:, 4 + r, half, :], MA[:, 6 + r, :]))
                    for c in range(2):
                        tiles.append((KT[base:base + 64, NT + c * 128:NT + (c + 1) * 128], VA[:, 16 + c, half, :], None))
                    run(j * 128, 128, base, tiles, (3, 128), oT[base:base + 64, :, j * 128:(j + 1) * 128], esl, items)
                if not last:
                    for cq in range(2):
                        tiles = [(KT[base:base + 64, NT + c * 128:NT + (c + 1) * 128], VA[:, 16 + c, half, :], None) for c in range(2)]
                        q0 = NT + cq * 128
                        run(q0, 128, base, tiles, (3, 128), oT[base:base + 64, :, q0:q0 + 128], esl, items)
            assert len(half_items[0]) == len(half_items[1])
            for a_, b_ in zip(half_items[0], half_items[1]):
                all_items.append(a_)
                all_items.append(b_)
            self.pipeline(all_items, 2, 2)
            self.nsb = 4
            self.dma(oa_d.ap()[:, :, 0:nq], oT[:, :, 0:nq], ["oT"], ["oa_d"])
            T.flush()

    def phase_attn_b(self, Ld, l, last, qb_d, kbcT_d, vbc_d, gath, ob_d):
        T = self.T
        with ExitStack() as es:
            NK = NCX + 4 * NT
            self.split = True
            self.nsb = 6
            self.bankc = 0
            QT = self.sb(es, "QTb", [128, 3, NTOT], BF16)
            KT = self.sb(es, "KTb", [128, NK], BF16)
            VB = self.sb(es, "VBb", [128, 66, 2, 128], BF16)
            PT = self.sb(es, "PTb", [128, 6, 512], BF16)
            rc = self.sb(es, "rcb", [128, 2, 512], F32)
            self.fc = self.sb(es, "fcb", [128, 2, 512], F32)
            self.fc_i = 0
            oT = self.sb(es, "oTb", [128, 3, NTOT], BF16)
            nq = NT if last else NTOT
            self.dma(QT[:, :, 0:nq], qb_d.ap()[:, :, 0:nq], (), ["+QT"])
            self.dma(KT[:, 0:NCX], kbcT_d.ap(), (), ["+KT"])
            self.memset("pool", VB[:, 0:33], 1.0, ["+VB"])
            self.memset("pool", VB[:, 33:66], 1.0, ["+VB"])
            vcs = vbc_d.ap().rearrange("(t p) c -> p t c", p=128)
            self.dma(VB[:, 0:2, 0, 0:64], vcs[:, :, 0:64], (), ["+VB"])
            self.dma(VB[:, 0:2, 1, 64:128], vcs[:, :, 64:128], (), ["+VB"])
            for r in range(4):
                pass
                self.dma(KT[:, NCX + r * NT:NCX + (r + 1) * NT], gath.view(r, O_KB, [[2048, 128], [1, 2048]]), ["gath"], ["+KT"])
                self.dma(VB[:, 2 + r * 16:2 + (r + 1) * 16, 0, 0:64], gath.view(r, O_VB, [[128, 128], [128 * 128, 16], [1, 64]]), ["gath"], ["+VB"])
                self.dma(VB[:, 2 + r * 16:2 + (r + 1) * 16, 1, 64:128], gath.view(r, O_VB + 64, [[128, 128], [128 * 128, 16], [1, 64]]), ["gath"], ["+VB"])
            it = dict(p=0, r=0)
            items = []

            def run(mi, q0, n, ktiles):
                bo = [self.newacc(), self.newacc()]
                nt = len(ktiles)
                for ti, kt in enumerate(ktiles):
                    for half in range(2):
                        base = half * 64
                        p = it["p"] % 6
                        it["p"] += 1

                        def qk(b, base=base, kt=kt):
                            self.mm(self.ps[b][:, 0:n], KT[base:base + 64, kt * 128:(kt + 1) * 128], QT[base:base + 64, mi, q0:q0 + n],
                                    True, True, ["+QT", "+KT"], [("ps", b)])

                        def sm(b, p=p, half=half):
                            if half == 1:
                                return
                            assert b % 2 == 0 and p % 2 == 0
                            src = self.psall[:, b * 512:(b + 2) * 512].rearrange("p (a c) -> p a c", a=2)[:, :, 0:n]
                            self.act(PT[:, p:p + 2, 0:n], src, AF.Exp, [("ps", b), ("ps", b + 1)], [("PT", p), ("PT", p + 1)], scale=0.125)

                        def pv(p=p, half=half, kt=kt, ti=ti):
                            self.mm(self.ps[bo[half]][:, 0:n], VB[:, kt, half, :], PT[:, p, 0:n], ti == 0, ti == nt - 1,
                                    [("PT", p), "+VB"], [("ps", bo[half])])
                        fin = None
                        if ti == nt - 1:
                            r_ = it["r"] % 2
                            it["r"] += 1

                            def fin(r_=r_, half=half, base=base):
                                self.attn_fin(bo[half], n, base, oT[base:base + 64, mi, q0:q0 + n], rc[:, r_], ("rc", r_), act_recip=False)
                        items.append((qk, sm, pv, fin))

            for mi in range(3):
                for qc in range(4):
                    run(mi, qc * 512, 512, list(range(66)))
                if not last:
                    run(mi, NT, NCX, [0, 1])
            self.pipeline(items, 2, 2)
            self.nsb = 4
            self.dma(ob_d.ap()[:, :, 0:nq], oT[:, :, 0:nq], ["oT"], ["ob_d"])
            T.flush()

    def phase_attn_c(self, Ld, l, last, qc_d, kcT_d, vc_d, gath, rm01, oc_d):
        T = self.T
        with ExitStack() as es:
            self.split = True
            self.nsb = 6
            self.bankc = 0
            QT = self.sb(es, "QTc", [128, 3, NTOT], BF16)
            KT = self.sb(es, "KTc", [128, 3, NTOT], BF16)
            KH = self.sb(es, "KHc", [128, 3, 8, 256], BF16)
            VC = self.sb(es, "VCc", [128, 18, 6, 128], BF16)
            VH = self.sb(es, "VHc", [128, 16, 6, 128], BF16)
            EB = self.sb(es, "EB", [128, 2, 6, 22, 64], BF16)
            RM = self.sb(es, "RM", [128, 44, 8], F32)
            RMb = self.sb(es, "RMb", [128, 44, 8], BF16)
            PT = self.sb(es, "PTc", [128, 8, 512], BF16)
            rc = self.sb(es, "rcc", [128, 2, 512], F32)
            self.fc = self.sb(es, "fcc", [128, 2, 512], F32)
            self.fc_i = 0
            oT = self.sb(es, "oTc", [128, 3, NTOT], BF16)
            nq = NT if last else NTOT
            self.dma(QT[:, :, 0:nq], qc_d.ap()[:, :, 0:nq], (), ["+QT"])
            self.dma(KT[:], kcT_d.ap(), (), ["+KT"])
            self.memset("pool", VC[:], 1.0, ["+VC"])
            self.memset("pool", VH[:], 1.0, ["+VH"])
            vsrc = vc_d.ap().rearrange("(t p) (h d) -> p t h d", p=128, d=64)
            for h in range(6):
                o = (h % 2) * 64
                self.dma(VC[:, :, h, o:o + 64], vsrc[:, :, h, :], (), ["+VC"])
            for r in range(4):
                pass
                self.dma(KH[:, :, r, :], gath.view(r, O_KCT, [[768, 128], [256, 3], [1, 256]]), ["gath"], ["+KH"])
                self.dma(KH[:, :, 4 + r, :], gath.view(r, O_KCH, [[768, 128], [256, 3], [1, 256]]), ["gath"], ["+KH"])
                for h in range(6):
                    o = (h % 2) * 64
                    self.dma(VH[:, 2 * r:2 * r + 2, h, o:o + 64], gath.view(r, O_VCT + h * 64, [[384, 128], [128 * 384, 2], [1, 64]]), ["gath"], ["+VH"])
                    self.dma(VH[:, 8 + 2 * r:8 + 2 * r + 2, h, o:o + 64], gath.view(r, O_VCH + h * 64, [[384, 128], [128 * 384, 2], [1, 64]]), ["gath"], ["+VH"])
            TBd = Ld["TB"].ap()
            for tbl in range(2):
                ebf = EB[:, tbl].rearrange("p h u q -> p (h u q)")
                c = 0
                while c < 6 * 22 * 64:
                    cc_ = min(1024, 6 * 22 * 64 - c)
                    sap, stoks = self.stgs[self.stg_i % len(self.stgs)]
                    self.stg_i += 1
                    st = sap[:, 0:cc_]
                    self.dma(st, TBd[:, tbl, c:c + cc_], (), stoks)
                    self.T.add("act", (lambda e_, o_=ebf[:, c:c + cc_], i_=st: e_.activation(out=o_, in_=i_, func=AF.Exp)),
                               stoks, ["+EB"], accw=True)
                    c += cc_
            self.dma(RM[:], rm01.ap(), (), ["RM"])
            self.cp("dve", RMb[:], RM[:], ["RM"], ["RMb"])
            it = dict(p=0, t=0, r=0)
            items = []

            def crun(q0, n, tiles, h, mi, base, items):
                rhs = QT[base:base + 64, mi, q0:q0 + n]
                bo = self.nsb + base // 64
                nt = len(tiles)
                for ti, (kap, vap, j, ri) in enumerate(tiles):
                    p = it["p"] % 8
                    it["p"] += 1
                    t = it["t"] % 3
                    if j is not None:
                        it["t"] += 1

                    def qk(b, kap=kap):
                        self.mm(self.ps[b][:, 0:n], kap, rhs, True, True, ["+QT", "+KT", "+KH"], [("ps", b)])

                    def sm(b, j=j, ri=ri, p=p, t=t):
                        pt = PT[:, p, 0:n]
                        self.act(pt, self.ps[b][:, 0:n], AF.Exp, [("ps", b)], [("PT", p)], scale=0.125)
                        if j is not None:
                            u0 = 14 - 2 * j
                            g_ = q0 // 512
                            tbl = 1 if g_ in (1, 2) else 0
                            pt3 = pt.rearrange("p (a b) -> p a b", a=8)
                            self.tt("dve", pt3, pt3, EB[:, tbl, h, u0:u0 + 8, :], ALU.mult, [("PT", p), "+EB"], [("PT", p)])
                            if tbl == 0:
                                self.tt("dve", pt3, pt3, bcast(RMb[:, ri, :], 2, 64), ALU.mult, [("PT", p), "RMb"], [("PT", p)])

                    def pv(vap=vap, p=p, ti=ti):
                        self.mm(self.ps[bo][:, 0:n], vap, PT[:, p, 0:n], ti == 0, ti == nt - 1, [("PT", p), "+VC", "+VH"], [("ps", bo)])
                    fin = None
                    if ti == nt - 1:
                        r_ = it["r"] % 2
                        it["r"] += 1

                        def fin(r_=r_):
                            self.attn_fin(bo, n, base, oT[base:base + 64, mi, q0:q0 + n], rc[:, r_], ("rc", r_))
                    items.append((qk, sm, pv, fin))

            all_items = items
            for h in range(6):
                mi, half = h // 2, h % 2
                base = half * 64
                rmi = 0
                items = []
                if half == 0:
                    ev_items = items
                else:
                    od_items = items
                for g in range(4):
                    tiles = []
                    for j in range(8):
                        lt = g * 512 - 256 + 128 * j
                        if g == 0 and j < 2:
                            for r in range(4):
                                tiles.append((KH[base:base + 64, mi, r, j * 128:(j + 1) * 128], VH[:, 2 * r + j, h, :], j, rmi))
                                rmi += 1
                        elif g == 3 and j >= 6:
                            for r in range(4):
                                tiles.append((KH[base:base + 64, mi, 4 + r, (j - 6) * 128:(j - 5) * 128], VH[:, 8 + 2 * r + (j - 6), h, :], j, rmi))
                                rmi += 1
                        else:
                            tiles.append((KT[base:base + 64, mi, lt:lt + 128], VC[:, lt // 128, h, :], j, rmi))
                            rmi += 1
                    for c in range(2):
                        tiles.append((KT[base:base + 64, mi, NT + c * 128:NT + (c + 1) * 128], VC[:, 16 + c, h, :], None, None))
                    crun(g * 512, 512, tiles, h, mi, base, items)
                if not last:
                    tiles = [(KT[base:base + 64, mi, NT + c * 128:NT + (c + 1) * 128], VC[:, 16 + c, h, :], None, None) for c in range(2)]
                    crun(NT, NCX, tiles, h, mi, base, items)
                if half == 1:
                    assert len(ev_items) == len(od_items)
                    for a_, b_ in zip(ev_items, od_items):
                        all_items.append(a_)
                        all_items.append(b_)
            self.pipeline(all_items, 2, 2)
            self.nsb = 4
            self.dma(oc_d.ap()[:, :, 0:nq], oT[:, :, 0:nq], ["oT"], ["oc_d"])
            T.flush()

    def post_norm_res(self, yTb, n, PG, xsrc, xdst, tok_y, tok_x, tok_o, tmpb):
        rs = self.rstd_of(None, yTb[:, :, 0:n], 8, n, 1024.0, tok_y, "pn")
        for m in range(8):
            self.stt(tmpb[:, m, 0:n], yTb[:, m, 0:n], PG[:, m:m + 1], rs, ALU.mult, ALU.mult, [tok_y, "rstd", "mods"], ["xn"])
            self.tt("dve", xdst[:, m, 0:n], xsrc[:, m, 0:n], tmpb[:, m, 0:n], ALU.add, [tok_x, "xn"], [tok_o])

    def phase_mix(self, Ld, l, last, xin_lat, xin_ctx, o_d, xs1):
        T = self.T
        with ExitStack() as es:
            self.split = False
            self.alloc_norm(es, 256)
            N = 256
            wg = self.sb(es, "wg", [128, 8, 3072], BF16)
            wbr = self.sb(es, "wbr", [128, 3, 3, 1024], BF16)
            wo = self.sb(es, "wo", [128, 8, 1024], BF16)
            xch = self.sb(es, "xchm", [128, 2, 8, N], F32)
            hT = self.sb(es, "hTm", [128, 2, 8, N], BF16)
            oc = self.sb(es, "ocm", [128, 2, 3, 3, N], BF16)
            sg = self.sb(es, "sg", [128, 2, 3, N], F32)
            ta = self.sb(es, "ta", [128, 2, 3, N], F32)
            mg = self.sb(es, "mg", [128, 8, N], BF16)
            yTb = self.sb(es, "yTb", [128, 8, N], F32)
            stg2 = self.sb(es, "stg2", [128, 6, 1024], F32)
            chunks = [(i * N, N, False) for i in range(NT // N)]
            if not last:
                chunks.append((NT, NCX, True))

            def prep(ci):
                c0, n, isc = chunks[ci]
                s_ = ci % 2
                src = xin_ctx if isc else xin_lat(c0, n)
                self.dma(xch[:, s_, :, 0:n], src, (), [("xch", s_)])
                for i in range(3):
                    self.dma(oc[:, s_, i, :, 0:n], o_d[i].ap()[:, :, c0:c0 + n], (), [("+oc", s_)])
                dv_ = self.dvec[:, l, 1 if isc else 0]
                self.norm_mod(xch[:, s_], n, dv_[:, 0, :], dv_[:, 1, :], hT[:, s_], ("xch", s_), ("hTm", s_))
            prep(0)
            win = Ld["win"].ap().rearrange("(k p) n -> p k n", p=128)
            self.cast_engs = ("act", "dve")
            self.stgs = [(stg2[:, i, :], [("stg2", i)]) for i in range(6)]
            for i in range(3):
                self.load_w(wg[:, :, i * 1024:(i + 1) * 1024], win[:, :, C_G + i * 1024:C_G + (i + 1) * 1024], ("+wg", i), 8, 1024)
                self.load_w(wbr[:, i], Ld["wbr"].ap()[i].rearrange("(k p) n -> p k n", p=128), ("+wbr", i), 3, 1024)
            self.load_w(wo[:], Ld["wout"].ap().rearrange("(k p) n -> p k n", p=128), "+wo", 8, 1024)
            self.stgs = self.stgs0
            self.cast_engs = ("act",)
            for ci, (c0, n, isc) in enumerate(chunks):
                s = ci % 2
                dv = self.dvec[:, l, 1 if isc else 0]
                for m in range(8):
                    q = m % 2
                    gb_, yb_ = [], []
                    for i in range(3):
                        b = self.newbank()
                        gb_.append(b)
                        for k in range(8):
                            self.mm(self.ps[b][:, 0:n], wg[:, k, i * 1024 + m * 128:i * 1024 + (m + 1) * 128], hT[:, s, k, 0:n], k == 0, k == 7,
                                    [("+wg", i), ("hTm", s)], [("ps", b)])
                        b = self.newbank()
                        yb_.append(b)
                        for k in range(3):
                            self.mm(self.ps[b][:, 0:n], wbr[:, i, k, m * 128:(m + 1) * 128], oc[:, s, i, k, 0:n], k == 0, k == 2,
                                    [("+wbr", i), ("+oc", s)], [("ps", b)])
                    for i in range(3):
                        self.act(sg[:, q, i, 0:n], self.ps[gb_[i]][:, 0:n], AF.Sigmoid, [("ps", gb_[i])], [("sg", q, i)])
                        self.tt("dve", ta[:, q, i, 0:n], sg[:, q, i, 0:n], self.ps[yb_[i]][:, 0:n], ALU.mult, [("sg", q, i), ("ps", yb_[i])], [("ta", q, i)])
                    self.tt("dve", ta[:, q, 0, 0:n], ta[:, q, 0, 0:n], ta[:, q, 1, 0:n], ALU.add, [("ta", q, 0), ("ta", q, 1)], [("ta", q, 0)])
                    self.tt("dve", mg[:, m, 0:n], ta[:, q, 0, 0:n], ta[:, q, 2, 0:n], ALU.add, [("ta", q, 0), ("ta", q, 2)], [("mg", m)])
                if ci + 1 < len(chunks):
                    prep(ci + 1)
                for m in range(8):
                    b = self.newbank()
                    for k in range(8):
                        self.mm(self.ps[b][:, 0:n], wo[:, k, m * 128:(m + 1) * 128], mg[:, k, 0:n], k == 0, k == 7, ["+wo", ("mg", k)], [("ps", b)])
                    self.act(yTb[:, m, 0:n], self.ps[b][:, 0:n], AF.Copy, [("ps", b)], ["yTb"])
                self.post_norm_res(yTb, n, dv[:, 2, :], xch[:, s], xch[:, s], "yTb", ("xch", s), ("xch", s), self.xn)
                self.dma(xs1.ap()[:, :, c0:c0 + n], xch[:, s, :, 0:n], [("xch", s)], [("xs1", ci)])
            T.flush()

    def phase_mlp(self, Ld, l, last, xs1, out_lat, xs2):
        T = self.T
        with ExitStack() as es:
            self.split = False
            self.alloc_norm(es, 256)
            w1 = self.sb(es, "w1", [128, 8, 4096], BF16)
            w2 = self.sb(es, "w2", [128, 32, 1024], BF16)
            xch = self.sb(es, "xchp", [128, 2, 8, 256], F32)
            h2 = self.sb(es, "h2", [128, 2, 8, 256], BF16)
            rl = self.sb(es, "rl", [128, 2, 256], F32)
            aT = self.sb(es, "aT", [128, 32, 256], BF16)
            y2 = self.sb(es, "y2", [128, 8, 256], F32)
            n = 256
            chunks = [(i * 256, False) for i in range(8)]
            if not last:
                chunks.append((NT, True))

            def prep(ci):
                c0, isc = chunks[ci]
                s_ = ci % 2
                self.dma(xch[:, s_], xs1.ap()[:, :, c0:c0 + n], ["xs1"], [("xch", s_)])
                dv_ = self.dvec[:, l, 1 if isc else 0]
                self.norm_mod(xch[:, s_], n, dv_[:, 3, :], dv_[:, 4, :], h2[:, s_], ("xch", s_), ("h2", s_))
            prep(0)
            self.cast_engs = ("act", "dve")
            aTf = aT[:].rearrange("p a b -> p (a b)").bitcast(F32).rearrange("p (s c) -> p s c", s=4)
            self.stgs = self.stgs0 + [(aTf[:, i, :], [("stgA", i)] + [("aT", j) for j in range(8 * i, 8 * i + 8)]) for i in range(4)]
            w1src = Ld["w1"].ap().rearrange("(k p) n -> p k n", p=128)
            for cc_ in range(4):
                self.load_w(w1[:, :, cc_ * 1024:(cc_ + 1) * 1024], w1src[:, :, cc_ * 1024:(cc_ + 1) * 1024], ("+w1", cc_), 8, 1024)
            self.load_w(w2[:], Ld["w2"].ap().rearrange("(k p) n -> p k n", p=128), "+w2", 32, 1024)
            self.stgs = self.stgs0
            self.cast_engs = ("act",)
            for ci, (c0, isc) in enumerate(chunks):
                s = ci % 2
                dv = self.dvec[:, l, 1 if isc else 0]
                for j in range(32):
                    b = self.newbank()
                    for k in range(8):
                        self.mm(self.ps[b][:, 0:n], w1[:, k, j * 128:(j + 1) * 128], h2[:, s, k, :], k == 0, k == 7, [("+w1", j // 8), ("h2", s)], [("ps", b)])
                    r_ = j % 2
                    self.act(rl[:, r_, :], self.ps[b][:, 0:n], AF.Relu, [("ps", b)], [("rl", r_)])
                    self.tt("dve", aT[:, j, :], rl[:, r_, :], rl[:, r_, :], ALU.mult, [("rl", r_)], [("aT", j)])
                if ci + 1 < len(chunks):
                    prep(ci + 1)
                for m in range(8):
                    b = self.newbank()
                    for j in range(32):
                        self.mm(self.ps[b][:, 0:n], w2[:, j, m * 128:(m + 1) * 128], aT[:, j, :], j == 0, j == 31, ["+w2", ("aT", j)], [("ps", b)])
                    self.act(y2[:, m, :], self.ps[b][:, 0:n], AF.Copy, [("ps", b)], ["y2"])
                self.post_norm_res(y2, n, dv[:, 5, :], xch[:, s], xch[:, s], "y2", ("xch", s), ("xch", s), self.xn)
                if isc:
                    dst = xs2.ap()[:, :, c0:c0 + n]
                else:
                    dst = out_lat.ap()[:, :, c0:c0 + n]
                self.dma(dst, xch[:, s], [("xch", s)], [("out", ci)])
            T.flush()


def _fm(v):
    v = np.asarray(v, np.float32)
    return np.ascontiguousarray(v.reshape(-1, 128).T)


def _perm_cols():
    def hc(base, heads, swap):
        out = []
        for h in heads:
            d = np.arange(64)
            if swap:
                d = d ^ 1
            out += list(base + h * 64 + d)
        return out
    p = []
    p += hc(0, HP, False) + hc(0, HP, True)
    p += hc(384, [0, 1], False) + hc(384, [0, 1], True)
    p += hc(640, HP, False) + hc(640, HP, True)
    p += hc(1024, [0, 1], False) + hc(1024, [0, 1], True)
    p += list(range(1280, 1664)) + list(range(1664, 2048))
    p += list(range(512, 640)) + list(range(1152, 1280)) + list(range(2048, 2432))
    p += list(range(2432, 5504))
    assert len(p) == NWIN
    return np.array(p)


def _rope_tables(tok0):
    pos = np.arange(tok0, tok0 + NT)
    row = (pos // 64).astype(np.float32)
    col = (pos % 64).astype(np.float32)
    freqs = (np.float32(10000.0) ** (-np.arange(16, dtype=np.float32) / np.float32(16))).astype(np.float32)
    ang = np.concatenate([row[:, None] * freqs, col[:, None] * freqs], axis=-1).astype(np.float32)
    cos, sin = np.cos(ang).astype(np.float32), np.sin(ang).astype(np.float32)
    d = np.arange(128) % 64
    C = cos[:, d // 2].T
    S = sin[:, d // 2].T * np.where(d % 2 == 0, -1.0, 1.0)[:, None]
    return np.ascontiguousarray(C, np.float32), np.ascontiguousarray(S, np.float32)


def _mask_a(rank):
    ki = np.arange(128)[:, None]
    qi = np.arange(128)[None, :]
    prev = np.where(qi <= ki, 0.0, NEG).astype(np.float32)
    nxt = np.where(ki <= qi, 0.0, NEG).astype(np.float32)
    allneg = np.full((128, 128), NEG, np.float32)
    m = np.zeros((128, 10, 128), np.float32)
    m[:, 0], m[:, 1] = prev, nxt
    for r in range(4):
        m[:, 2 + r] = prev if r == rank - 1 else allneg
        m[:, 6 + r] = nxt if r == rank + 1 else allneg
    return m


def _rm01(rank):
    out = np.zeros((128, 44, 8), np.float32)
    idx = 0
    kl = (np.arange(128) // 64)[:, None]
    ql = np.arange(8)[None, :]
    for g in range(4):
        R0 = rank * 32 + g * 8
        for j in range(8):
            kr = R0 - 4 + 2 * j + kl
            qr = R0 + ql
            rs = np.clip(qr - 4, 0, 120)
            valid = (kr >= rs) & (kr < rs + 8) & (kr >= 0) & (kr < 128)
            if g == 0 and j < 2:
                for r in range(4):
                    out[:, idx] = valid if r == rank - 1 else 0.0
                    idx += 1
            elif g == 3 and j >= 6:
                for r in range(4):
                    out[:, idx] = valid if r == rank + 1 else 0.0
                    idx += 1
            else:
                out[:, idx] = valid
                idx += 1
    assert idx == 44
    return out


def _tb_table(rpb, interior=False):
    rpb = np.asarray(rpb, np.float32)
    kc = np.arange(64)[:, None]
    qc = np.arange(64)[None, :]
    ws = np.clip(qc - 8, 0, 48)
    colv = (kc >= ws) & (kc < ws + 16)
    dc = np.clip(kc - qc + 15, 0, 30)
    tb = np.zeros((2, 64, 6, 22, 64), np.float32)
    for kl in range(2):
        for u in range(22):
            dr = 17 + kl - u
            for h in range(6):
                if 0 <= dr <= 14:
                    v = rpb[h, dr][dc]
                else:
                    v = np.zeros((64, 64), np.float32)
                if interior and not (3 <= dr <= 10):
                    tb[kl, :, h, u, :] = NEG
                else:
                    tb[kl, :, h, u, :] = np.where(colv, v, NEG)
    return np.ascontiguousarray(tb.reshape(128, 6, 22, 64))


_CACHE = {}


def _get_nc(n_layers=2, taps=()):
    key = (n_layers, tuple(taps))
    if key not in _CACHE:
        _CACHE[key] = K(n_layers, taps).build()
    return _CACHE[key]


def make_in_maps(inputs, n_layers=2):
    f = lambda a: np.asarray(a, np.float32)
    x, c, ctx, c_ctx = f(inputs["x"]), f(inputs["c"]), f(inputs["ctx"]), f(inputs["c_ctx"])
    perm = _perm_cols()
    rows_ab = np.concatenate([np.arange(h * 64, (h + 1) * 64) for h in HP])
    shared = {}
    for l in range(n_layers):
        shared["wada%d" % l] = np.ascontiguousarray(f(inputs["w_ada"])[l])
        shared["badaT%d" % l] = _fm(f(inputs["b_ada"])[l])
        shared["gains%d" % l] = np.ascontiguousarray(np.stack(
            [_fm(f(inputs[k])[l]) for k in ("norm_mix_pre", "norm_mix_post", "norm_mlp_pre", "norm_mlp_post")], axis=1))
        shared["win%d" % l] = np.ascontiguousarray(f(inputs["w_in"])[l][:, perm])
        gq, gk = f(inputs["qnorm_b"])[l], f(inputs["knorm_b"])[l]
        d = np.arange(128) % 64
        shared["bgain%d" % l] = np.ascontiguousarray(np.stack([gq[d], gq[d ^ 1], gk[d], gk[d ^ 1]], axis=1))
        shared["sinkT%d" % l] = np.ascontiguousarray(np.broadcast_to(f(inputs["sink_a"])[l][None, :], (128, 6)))
        shared["TB%d" % l] = np.ascontiguousarray(np.stack(
            [_tb_table(f(inputs["rpb_c"])[l]), _tb_table(f(inputs["rpb_c"])[l], True)], axis=1))
        shared["wbr%d" % l] = np.ascontiguousarray(np.stack(
            [f(inputs["w_br_a"])[l][rows_ab], f(inputs["w_br_b"])[l][rows_ab], f(inputs["w_br_c"])[l]], axis=0))
        shared["wout%d" % l] = np.ascontiguousarray(f(inputs["w_out"])[l])
        shared["w1_%d" % l] = np.ascontiguousarray(f(inputs["w_mlp_in"])[l])
        shared["w2_%d" % l] = np.ascontiguousarray(f(inputs["w_mlp_out"])[l])
    in_maps = []
    for core in range(8):
        b, rank = core // 4, core % 4
        tok0 = rank * NT
        m = dict(shared)
        xs = x[b, tok0:tok0 + NT, :]
        m["xT"] = np.ascontiguousarray(xs.T.reshape(8, 128, NT).transpose(1, 0, 2))
        m["ctxT"] = np.ascontiguousarray(ctx[b].T.reshape(8, 128, NCX).transpose(1, 0, 2))
        m["ccT"] = np.ascontiguousarray(np.stack([_fm(c[b]), _fm(c_ctx)], axis=2))
        C, S = _rope_tables(tok0)
        m["ropeC"], m["ropeS"] = C, S
        m["maskA"] = _mask_a(rank)
        m["rm01"] = _rm01(rank)
        in_maps.append(m)
    return in_maps


def kernel(**inputs):
    nc = _get_nc(2)
    in_maps = make_in_maps(inputs, 2)
    res = run_bass_kernel_spmd(nc, in_maps, core_ids=list(range(8)))
    out = np.zeros((2, 4 * NT, 1024), np.float32)
    for core in range(8):
        b, rank = core // 4, core % 4
        yT = np.asarray(res.results[core]["yT"])
        out[b, rank * NT:(rank + 1) * NT, :] = yT.transpose(1, 0, 2).reshape(1024, NT).T
    return out
```

```python
import numpy as np
from contextlib import ExitStack
import concourse.bass as bass
import concourse.mybir as mybir
from concourse.bass_utils import run_bass_kernel_spmd

F32 = mybir.dt.float32
BF16 = mybir.dt.bfloat16
F32R = mybir.dt.float32r
AF = mybir.ActivationFunctionType
ALU = mybir.AluOpType

NT = 2048
NCX = 256
NTOT = NT + NCX
NEG = -30000.0
EPS = 1e-6
HP = [0, 3, 1, 4, 2, 5]
C_QA, C_QAS, C_KA, C_KAS = 0, 384, 768, 896
C_QB, C_QBS, C_KB, C_KBS = 1024, 1408, 1792, 1920
C_QC, C_KC, C_V, C_G = 2048, 2432, 2816, 3456
NWIN = 6528
O_KB, O_VB, O_KAH, O_KAT, O_VAH, O_VAT = 0, 262144, 524288, 540672, 557056, 573440
O_KCH, O_KCT, O_VCH, O_VCT, CONTRIB = 589824, 688128, 786432, 884736, 983040


class Trk:
    ENGS = ("pe", "act", "dve", "pool", "sp")
    NDSEM = 12

    def __init__(self, nc, es):
        self.nc = nc
        self.esem = {e: es.enter_context(nc.semaphore("s_" + e)) for e in self.ENGS}
        self.ecnt = {e: 0 for e in self.ENGS}
        self.dsem = {q: [es.enter_context(nc.semaphore("d_%s%d" % (q, i))) for i in range(self.NDSEM)]
                     for q in ("sp", "pool")}
        self.dval = {q: [0] * self.NDSEM for q in ("sp", "pool")}
        self.dcnt = {"sp": 0, "pool": 0}
        self.ccsem = es.enter_context(nc.semaphore("s_cc"))
        self.ccval = 0
        self.ops = []
        self.bar = None
        self.waited = {e: {} for e in self.ENGS}

    def add(self, eng, fn, r=(), w=(), dma=False, cc=False, accw=False):
        self.ops.append(dict(eng=eng, fn=fn, r=tuple(r), w=tuple(w), dma=dma, cc=cc, accw=accw))

    def flush(self):
        ops = self.ops
        self.ops = []
        if not ops:
            return
        last_w, readers = {}, {}

        def is_acc(tok):
            t0_ = tok[0] if isinstance(tok, tuple) else tok
            return isinstance(t0_, str) and t0_.startswith("+")
        for i, op in enumerate(ops):
            deps = set()
            for r in op["r"]:
                if r in last_w:
                    deps.update(last_w[r])
            for w in op["w"]:
                if w in last_w:
                    if is_acc(w) and (op["dma"] or op["accw"]):
                        deps.add(last_w[w][0])
                    else:
                        deps.update(last_w[w])
                for rd in readers.get(w, {}).values():
                    if isinstance(rd, list):
                        deps.update(rd)
                    else:
                        deps.add(rd)
            deps.discard(i)
            if op["eng"] == "pe":
                deps = {d for d in deps if ops[d]["dma"] or ops[d]["eng"] != "pe"}
            op["deps"] = deps
            for r in op["r"]:
                rr = readers.setdefault(r, {})
                if op["dma"]:
                    rr.setdefault("dma", []).append(i)
                else:
                    rr[op["eng"]] = i
            for w in op["w"]:
                if is_acc(w) and (op["dma"] or op["accw"]) and w in last_w:
                    last_w[w] = last_w[w] + [i]
                else:
                    last_w[w] = [i]
                readers[w] = {}
        for op in ops:
            op["sig"] = False
        for op in ops:
            for d in op["deps"]:
                ops[d]["sig"] = True
        last_of = {}
        for i, op in enumerate(ops):
            if not op["dma"]:
                last_of[op["eng"]] = i
        for i in last_of.values():
            ops[i]["sig"] = True
        for op in ops:
            if op["cc"]:
                self.ccval += 1
                op["done"] = (self.ccsem, self.ccval)
                op["pre"] = None
            elif op["dma"]:
                q = op["eng"]
                k = self.dcnt[q] % self.NDSEM
                self.dcnt[q] += 1
                prev = self.dval[q][k]
                self.dval[q][k] += 16
                op["done"] = (self.dsem[q][k], self.dval[q][k])
                op["pre"] = (self.dsem[q][k], prev) if prev > 0 else None
            elif op["sig"]:
                self.ecnt[op["eng"]] += 1
                op["done"] = (self.esem[op["eng"]], self.ecnt[op["eng"]])
                op["pre"] = None
            else:
                op["done"] = None
                op["pre"] = None
        per = {e: [] for e in self.ENGS}
        for op in ops:
            per[op["eng"]].append(op)
        bar = self.bar
        waited = self.waited

        def emit(ename, e):
            wd = waited[ename]

            def wait(sem, val):
                key = id(sem)
                if wd.get(key, 0) < val:
                    e.wait_ge(sem, val)
                    wd[key] = val
            if bar and per[ename]:
                for sem, val in bar:
                    wait(sem, val)
            for op in per[ename]:
                for d in op["deps"]:
                    sem, val = ops[d]["done"]
                    wait(sem, val)
                if op["pre"] is not None:
                    wait(*op["pre"])
                if op["fn"] is None:
                    continue
                ins = op["fn"](e)
                if op["done"] is not None:
                    sem, val = op["done"]
                    if op["cc"]:
                        ins.then_inc(sem, 1)
                    elif op["dma"]:
                        ins.then_inc(sem, 16)
                    else:
                        ins.then_inc(sem, 1)

        with self.nc.Block() as block:
            @block.sync
            def _(e):
                emit("sp", e)

            @block.scalar
            def _(e):
                emit("act", e)

            @block.vector
            def _(e):
                emit("dve", e)

            @block.gpsimd
            def _(e):
                emit("pool", e)

            @block.tensor
            def _(e):
                emit("pe", e)
        nb = []
        for e in self.ENGS:
            if self.ecnt[e] > 0:
                nb.append((self.esem[e], self.ecnt[e]))
        for q in ("sp", "pool"):
            for k in range(self.NDSEM):
                if self.dval[q][k] > 0:
                    nb.append((self.dsem[q][k], self.dval[q][k]))
        if self.ccval > 0:
            nb.append((self.ccsem, self.ccval))
        self.bar = nb

    def final_wait(self):
        bar = self.bar
        with self.nc.Block() as block:
            @block.sync
            def _(e):
                for sem, val in bar:
                    e.wait_ge(sem, val)


def bcast(ap, pos, n):
    l = [list(x) for x in ap.ap]
    l.insert(pos, [0, n])
    return bass.AP(ap.tensor, ap.offset, l)


def dview(t, off, dims):
    return bass.AP(t, off, [list(d) for d in dims])


class K:
    def __init__(self, n_layers=2, taps=(), stop=None):
        self.nl = n_layers
        self.taps = set(taps)
        self.stop = stop
        self.nc = bass.Bass("TRN2", target_bir_lowering=False)
        self.uid = 0

    def din(self, name, shape, dt=F32):
        return self.nc.dram_tensor(name, list(shape), dt, kind="ExternalInput")

    def dscr(self, name, shape, dt=BF16, tap=False):
        if tap and name in self.taps:
            return self.nc.dram_tensor(name, list(shape), dt, kind="ExternalOutput")
        return self.nc.dram_tensor(name, list(shape), dt)

    def sb(self, es, name, shape, dt):
        self.uid += 1
        return es.enter_context(self.nc.sbuf_tensor("%s_%d" % (name, self.uid), list(shape), dt))

    def mm(self, out, lhsT, rhs, start, stop, r, w):
        self.T.add("pe", lambda e: e.matmul(out, lhsT=lhsT, rhs=rhs, start=start, stop=stop), r, w)

    def act(self, out, in_, func, r, w, bias=None, scale=None):
        kw = {}
        if bias is not None:
            kw["bias"] = bias
        if scale is not None:
            kw["scale"] = scale
        self.T.add("act", lambda e: e.activation(out=out, in_=in_, func=func, **kw), r, w)

    def tt(self, eng, out, in0, in1, op, r, w):
        self.T.add(eng, lambda e: e.tensor_tensor(out=out, in0=in0, in1=in1, op=op), r, w)

    def ts(self, eng, out, in0, s1, s2, op0, op1, r, w):
        if op1 is None:
            self.T.add(eng, lambda e: e.tensor_scalar(out=out, in0=in0, scalar1=s1, scalar2=None, op0=op0), r, w)
        else:
            self.T.add(eng, lambda e: e.tensor_scalar(out=out, in0=in0, scalar1=s1, scalar2=s2, op0=op0, op1=op1), r, w)

    def stt(self, out, in0, scalar, in1, op0, op1, r, w):
        self.T.add("dve", lambda e: e.scalar_tensor_tensor(out=out, in0=in0, scalar=scalar, in1=in1, op0=op0, op1=op1), r, w)

    def cp(self, eng, out, in_, r, w):
        self.T.add(eng, lambda e: e.tensor_copy(out=out, in_=in_), r, w)

    def memset(self, eng, ap, val, w):
        self.T.add(eng, lambda e: e.memset(ap, val), (), w)

    def recip(self, out, in_, r, w):
        self.T.add("dve", lambda e: e.reciprocal(out=out, in_=in_), r, w)

    def dma(self, out, in_, r, w, q="sp"):
        self.T.add(q, lambda e: e.dma_start(out=out, in_=in_), r, w, dma=True)

    def newbank(self):
        if self.split:
            b = self.bankc % self.nsb
        else:
            b = self.bankc % 8
        self.bankc += 1
        return b

    def newacc(self):
        b = self.nsb + self.accc % (8 - self.nsb)
        self.accc += 1
        return b

    def alloc_norm(self, es, nmax):
        self.sq_buf = self.sb(es, "sqbuf", [128, 8, nmax], BF16)
        self.lnt = self.sb(es, "lnt", [128, nmax], F32)
        self.rstd = self.sb(es, "rstd", [128, nmax], F32)
        self.xn = self.sb(es, "xn", [128, 8, nmax], F32)

    def cast(self, dst, src, r, w):
        engs = self.cast_engs
        e = engs[self.cast_i % len(engs)]
        self.cast_i += 1
        if e == "act":
            self.T.add("act", lambda e_: e_.activation(out=dst, in_=src, func=AF.Copy), r, w, accw=True)
        else:
            self.T.add(e, lambda e_: e_.tensor_copy(out=dst, in_=src), r, w, accw=True)

    def load_w(self, dst, src, tok_w, nk, ncols):
        CH = 1024
        per = max(1, CH // ncols)
        k = 0
        while k < nk:
            kk = min(per, nk - k)
            if ncols > CH:
                assert per == 1
                c = 0
                while c < ncols:
                    cc = min(CH, ncols - c)
                    sap, stoks = self.stgs[self.stg_i % len(self.stgs)]
                    self.stg_i += 1
                    st = sap[:, 0:cc]
                    self.dma(st, src[:, k, c:c + cc], (), stoks)
                    self.cast(dst[:, k, c:c + cc], st, stoks, [tok_w])
                    c += cc
            else:
                sap, stoks = self.stgs[self.stg_i % len(self.stgs)]
                self.stg_i += 1
                st = sap[:, 0:kk * ncols].rearrange("p (k c) -> p k c", k=kk)
                self.dma(st, src[:, k:k + kk, :], (), stoks)
                self.cast(dst[:, k:k + kk, :], st, stoks, [tok_w])
            k += kk

    def rstd_of(self, es_tmp, src, nk, n, div, tokr, name):
        sq = self.sq_buf[:, 0:nk, 0:n]
        self.act(sq, src, AF.Square, [tokr], ["sqbuf"])
        b = self.newbank()
        for k in range(nk):
            self.mm(self.ps[b][:, 0:n], self.ones[:], sq[:, k, :], k == 0, k == nk - 1, ["sqbuf", "ones"], [("ps", b)])
        self.act(self.lnt[:, 0:n], self.ps[b][:, 0:n], AF.Ln, [("ps", b)], ["lnt"], bias=self.epsc[:, 0:1], scale=1.0 / div)
        self.act(self.rstd[:, 0:n], self.lnt[:, 0:n], AF.Exp, ["lnt"], ["rstd"], scale=-0.5)
        return self.rstd[:, 0:n]

    def norm_mod(self, xch, n, GG, SH, hT, tokx, tokh):
        rs = self.rstd_of(None, xch[:, :, 0:n], 8, n, 1024.0, tokx, "nm")
        self.tt("dve", self.xn[:, :, 0:n], xch[:, :, 0:n], bcast(rs, 1, 8), ALU.mult, [tokx, "rstd"], ["xn"])
        for k in range(8):
            self.ts("dve", hT[:, k, 0:n], self.xn[:, k, 0:n], GG[:, k:k + 1], SH[:, k:k + 1], ALU.mult, ALU.add,
                    ["xn", "mods"], [tokh])

    def build(self):
        nc = self.nc
        NL = self.nl
        xT = self.din("xT", [128, 8, NT])
        ctxT = self.din("ctxT", [128, 8, NCX])
        ccT = self.din("ccT", [128, 8, 2])
        ropeC = self.din("ropeC", [128, NT])
        ropeS = self.din("ropeS", [128, NT])
        maskA = self.din("maskA", [128, 10, 128])
        rm01 = self.din("rm01", [128, 44, 8])
        L = []
        for l in range(NL):
            d = dict(
                wada=self.din("wada%d" % l, [1024, 6144]),
                badaT=self.din("badaT%d" % l, [128, 48]),
                gains=self.din("gains%d" % l, [128, 4, 8]),
                win=self.din("win%d" % l, [1024, NWIN]),
                bgain=self.din("bgain%d" % l, [128, 4]),
                sinkT=self.din("sinkT%d" % l, [128, 6]),
                TB=self.din("TB%d" % l, [128, 2, 6 * 22 * 64]),
                wbr=self.din("wbr%d" % l, [3, 384, 1024]),
                wout=self.din("wout%d" % l, [1024, 1024]),
                w1=self.din("w1_%d" % l, [1024, 4096]),
                w2=self.din("w2_%d" % l, [4096, 1024]),
            )
            L.append(d)
        yT = nc.dram_tensor("yT", [128, 8, NT], F32, kind="ExternalOutput")
        xs = [self.dscr("xs%d" % i, [128, 8, NTOT], F32, tap=True) for i in range(2 * NL)]
        q_d = [self.dscr("q_d%d" % i, [128, 3, NTOT], BF16, tap=True) for i in range(3)]
        kaT_d = self.dscr("kaT_d", [128, NTOT], BF16, tap=True)
        va_d = self.dscr("va_d", [NTOT, 128], BF16, tap=True)
        kcT_d = self.dscr("kcT_d", [128, 3, NTOT], BF16, tap=True)
        vc_d = self.dscr("vc_d", [NTOT, 384], BF16, tap=True)
        kbcT_d = self.dscr("kbcT_d", [128, NCX], BF16, tap=True)
        vbc_d = self.dscr("vbc_d", [NCX, 128], BF16, tap=True)
        contrib = self.dscr("contrib", [960, 1024], BF16, tap=True)
        gathB = self.dscr("gathB", [4 * 512, 1024], BF16)
        gathH = self.dscr("gathH", [4 * 448, 1024], BF16)

        class G:
            @staticmethod
            def view(r, off, dims):
                if off < 524288:
                    return dview(gathB, r * 524288 + off, dims)
                return dview(gathH, r * 458752 + (off - 524288), dims)
        gath = G
        self.gathB, self.gathH = gathB, gathH
        o_d = [self.dscr("o_d%d" % i, [128, 3, NTOT], BF16, tap=True) for i in range(3)]
        mods_d = self.dscr("mods_d", [128, NL, 48, 2], F32, tap=True)

        with ExitStack() as es0:
            self.T = Trk(nc, es0)
            T = self.T
            self.bankc = 0
            self.accc = 0
            self.nsb = 4
            self.split = False
            self.stg_i = 0
            self.cast_i = 0
            self.cast_engs = ("act",)
            self.psall = es0.enter_context(nc.psum_tensor("psall", [128, 4096], F32))
            self.ps = [self.psall[:, i * 512:(i + 1) * 512] for i in range(8)]
            self.ones = self.sb(es0, "ones", [128, 128], BF16)
            self.bd = self.sb(es0, "bd", [128, 128], BF16)
            self.epsc = self.sb(es0, "epsc", [128, 1], F32)
            self.modsb = self.sb(es0, "modsb", [128, NL, 48, 2], F32)
            self.dvec = self.sb(es0, "dvec", [128, NL, 2, 6, 8], F32)
            self.gn = self.sb(es0, "gn", [128, NL, 4, 8], F32)
            self.stg = self.sb(es0, "stg", [128, 3, 1024], F32)
            self.nstg = 3
            self.stgs0 = [(self.stg[:, i, :], [("stg", i)]) for i in range(3)]
            self.stgs = self.stgs0

            with ExitStack() as es:
                self.memset("pool", self.ones[:], 1.0, ["ones"])
                self.memset("pool", self.bd[:], 0.0, ["bd"])
                self.memset("pool", self.bd[0:64, 0:64], 1.0, ["bd"])
                self.memset("pool", self.bd[64:128, 64:128], 1.0, ["bd"])
                self.memset("pool", self.epsc[:], EPS, ["epsc"])
                cc_s = self.sb(es, "cc_s", [128, 8, 2], F32)
                sil = self.sb(es, "sil", [128, 8, 2], F32)
                self.dma(cc_s[:], ccT.ap(), (), ["cc_s"])
                self.act(sil[:], cc_s[:], AF.Silu, ["cc_s"], ["sil"])
                wa = self.sb(es, "wa", [128, 3, 8, 768], F32)
                bad = self.sb(es, "bad", [128, NL, 48], F32)
                for l in range(NL):
                    self.dma(bad[:, l, :], L[l]["badaT"].ap(), (), ["bad"])
                    self.dma(self.gn[:, l], L[l]["gains"].ap(), (), ["gn"])
                    wsrc = L[l]["wada"].ap().rearrange("(k p) n -> p k n", p=128)
                    b = self.newbank()
                    for g in range(8):
                        s = (l * 8 + g) % 3
                        self.dma(wa[:, s], wsrc[:, :, g * 768:(g + 1) * 768], (), [("wa", s)])
                        for mm_ in range(6):
                            m = g * 6 + mm_
                            for k in range(8):
                                self.mm(self.ps[b][:, 2 * m:2 * m + 2], wa[:, s, k, mm_ * 128:(mm_ + 1) * 128],
                                        sil[:, k, :], k == 0, k == 7, [("wa", s), "sil"], [("ps", b)])
                    self.tt("dve", self.modsb[:, l], self.ps[b][:, 0:96].rearrange("p (m t) -> p m t", t=2),
                            bcast(bad[:, l, :], 2, 2), ALU.add, [("ps", b), "bad"], ["modsb"])
                    for t in range(2):
                        mv = self.modsb[:, l, :, t]
                        dv = self.dvec[:, l, t]
                        self.stt(dv[:, 0, :], mv[:, 8:16], 1.0, self.gn[:, l, 0, :], ALU.add, ALU.mult, ["modsb", "gn"], ["mods"])
                        self.cp("dve", dv[:, 1, :], mv[:, 0:8], ["modsb"], ["mods"])
                        self.tt("dve", dv[:, 2, :], mv[:, 16:24], self.gn[:, l, 1, :], ALU.mult, ["modsb", "gn"], ["mods"])
                        self.stt(dv[:, 3, :], mv[:, 32:40], 1.0, self.gn[:, l, 2, :], ALU.add, ALU.mult, ["modsb", "gn"], ["mods"])
                        self.cp("dve", dv[:, 4, :], mv[:, 24:32], ["modsb"], ["mods"])
                        self.tt("dve", dv[:, 5, :], mv[:, 40:48], self.gn[:, l, 3, :], ALU.mult, ["modsb", "gn"], ["mods"])
                if "mods_d" in self.taps:
                    self.dma(mods_d.ap(), self.modsb[:], ["modsb"], ["mods_d"])
                T.flush()

            for l in range(NL):
                last = (l == NL - 1)
                if l == 0:
                    xin_lat = lambda c0, n: xT.ap()[:, :, c0:c0 + n]
                    xin_ctx = ctxT.ap()
                else:
                    xin_lat = (lambda xs_: (lambda c0, n: xs_.ap()[:, :, c0:c0 + n]))(xs[2 * l - 1])
                    xin_ctx = xs[2 * l - 1].ap()[:, :, NT:NTOT]
                if self.stop == "mods":
                    break
                self.phase_proj(L[l], l, last, xin_lat, xin_ctx, ropeC, ropeS, q_d, kaT_d, va_d, kcT_d, vc_d, kbcT_d, vbc_d, contrib)
                if self.stop == "proj%d" % l or (self.stop or "").startswith("proj0"):
                    break
                self.phase_gather(contrib, gath)
                self.phase_attn_a(L[l], l, last, q_d[0], kaT_d, va_d, gath, maskA, o_d[0])
                if self.stop == "attn_a%d" % l:
                    break
                self.phase_attn_b(L[l], l, last, q_d[1], kbcT_d, vbc_d, gath, o_d[1])
                if self.stop == "attn_b%d" % l:
                    break
                self.phase_attn_c(L[l], l, last, q_d[2], kcT_d, vc_d, gath, rm01, o_d[2])
                if self.stop == "attn_c%d" % l:
                    break
                self.phase_mix(L[l], l, last, xin_lat, xin_ctx, o_d, xs[2 * l])
                if self.stop == "mix%d" % l:
                    break
                out_lat = yT if last else xs[2 * l + 1]
                self.phase_mlp(L[l], l, last, xs[2 * l], out_lat, xs[2 * l + 1])
                if self.stop == "mlp%d" % l:
                    break
            T.final_wait()
        return nc

    def phase_proj(self, Ld, l, last, xin_lat, xin_ctx, ropeC, ropeS, q_d, kaT_d, va_d, kcT_d, vc_d, kbcT_d, vbc_d, contrib):
        T = self.T
        with ExitStack() as es:
            self.split = False
            self.alloc_norm(es, 512)
            hT = self.sb(es, "hT", [128, 8, NTOT], BF16)
            xch = self.sb(es, "xch", [128, 2, 8, 512], F32)
            rC = self.sb(es, "rC", [128, NT], F32)
            rS = self.sb(es, "rS", [128, NT], F32)
            bg = self.sb(es, "bg", [128, 4], F32)
            wt = self.sb(es, "wt", [128, 2, 8, 640], BF16)
            ost = self.sb(es, "ost", [128, 4, 640], BF16)
            t1 = self.sb(es, "t1", [128, 2, 512], F32)
            t2 = self.sb(es, "t2", [128, 2, 512], F32)
            sqh = self.sb(es, "sqh", [128, 2, 512], BF16)
            lnh = self.sb(es, "lnh", [128, 2, 512], F32)
            rsh = self.sb(es, "rsh", [128, 2, 512], F32)
            self.dma(rC[:], ropeC.ap(), (), ["rC"])
            self.dma(rS[:], ropeS.ap(), (), ["rS"])
            self.dma(bg[:], Ld["bgain"].ap(), (), ["bg"])
            chunks = [(i * 512, 512, False) for i in range(4)] + [(NT, NCX, True)]
            for ci, (c0, n, isc) in enumerate(chunks):
                s = ci % 2
                src = xin_ctx if isc else xin_lat(c0, n)
                self.dma(xch[:, s, :, 0:n], src, (), [("xch", s)])
                dv = self.dvec[:, l, 1 if isc else 0]
                self.norm_mod(xch[:, s], n, dv[:, 0, :], dv[:, 1, :], hT[:, :, c0:c0 + n], ("xch", s), "hT")
            if self.stop == "proj0a":
                T.flush()
                return
            win = Ld["win"].ap().rearrange("(k p) n -> p k n", p=128)
            units = []
            units.append(("KB", 0, [C_KB, C_KBS]))
            units.append(("V", 0, None))
            units.append(("KA", 0, [C_KA, C_KAS]))
            for mi in range(3):
                units.append(("KC", mi, [C_KC + mi * 128]))
            n_kv_units = len(units)
            for mi in range(3):
                units.append(("QB", mi, [C_QB + mi * 128, C_QBS + mi * 128]))
            for mi in range(3):
                units.append(("QA", mi, [C_QA + mi * 128, C_QAS + mi * 128]))
            for mi in range(3):
                units.append(("QC", mi, [C_QC + mi * 128]))
            ctr = dict(ost=0, t=0)

            ctoks = []

            def store(srcs_dsts, tok):
                for dst, src in srcs_dsts:
                    ctr["st"] = ctr.get("st", 0) + 1
                    wtk = ("dout", ctr["st"])
                    if dst.tensor.name == contrib.name:
                        ctoks.append(wtk)
                    self.dma(dst, src, [tok], [wtk])

            if self.stop and self.stop.startswith("proj0u"):
                sel = [int(x) for x in self.stop[6:].split("_")]
                units = [units[i] for i in sel]
            def load_unit(ui):
                kind, mi, cols = units[ui]
                ws = ui % 2
                wtok = ("+wt", ws)
                if kind == "V":
                    self.load_w(wt[:, ws, :, 0:640], win[:, :, C_V:C_V + 640], wtok, 8, 640)
                else:
                    for j, c in enumerate(cols):
                        self.load_w(wt[:, ws, :, j * 128:(j + 1) * 128], win[:, :, c:c + 128], wtok, 8, 128)
            load_unit(0)
            for ui, (kind, mi, cols) in enumerate(units):
                ws = ui % 2
                wtok = ("+wt", ws)
                if ui == n_kv_units and not (self.stop or "").startswith("proj0u"):
                    self.emit_gather(contrib, list(ctoks))
                if ui + 1 < len(units):
                    load_unit(ui + 1)
                if kind == "V":
                    for tile in range(NTOT // 128):
                        t0 = tile * 128
                        b0, b1 = self.newbank(), self.newbank()
                        for k in range(8):
                            self.mm(self.ps[b0][:, 0:512], hT[:, k, t0:t0 + 128], wt[:, ws, k, 0:512], k == 0, k == 7, ["hT", wtok], [("ps", b0)])
                        for k in range(8):
                            self.mm(self.ps[b1][:, 0:128], hT[:, k, t0:t0 + 128], wt[:, ws, k, 512:640], k == 0, k == 7, ["hT", wtok], [("ps", b1)])
                        o = ctr["ost"] % 4
                        ctr["ost"] += 1
                        otok = ("ost", o)
                        self.act(ost[:, o, 0:512], self.ps[b0][:, 0:512], AF.Copy, [("ps", b0)], [otok])
                        self.cp("dve", ost[:, o, 512:640], self.ps[b1][:, 0:128], [("ps", b1)], [otok])
                        cb = contrib
                        dl = [(va_d.ap()[t0:t0 + 128, :], ost[:, o, 0:128]),
                              (vc_d.ap()[t0:t0 + 128, :], ost[:, o, 256:640])]
                        if tile < 16:
                            dl.append((dview(cb, O_VB + t0 * 128, [[128, 128], [1, 128]]), ost[:, o, 128:256]))
                            if tile == 0:
                                dl.append((dview(cb, O_VAH, [[128, 128], [1, 128]]), ost[:, o, 0:128]))
                            if tile == 15:
                                dl.append((dview(cb, O_VAT, [[128, 128], [1, 128]]), ost[:, o, 0:128]))
                            if tile < 2:
                                dl.append((dview(cb, O_VCH + tile * 128 * 384, [[384, 128], [1, 384]]), ost[:, o, 256:640]))
                            if tile >= 14:
                                dl.append((dview(cb, O_VCT + (tile - 14) * 128 * 384, [[384, 128], [1, 384]]), ost[:, o, 256:640]))
                        else:
                            dl.append((vbc_d.ap()[t0 - NT:t0 - NT + 128, :], ost[:, o, 128:256]))
                        store(dl, otok)
                    continue
                for ci, (c0, n, isc) in enumerate(chunks):
                    if isc and last and kind in ("QA", "QB", "QC"):
                        continue
                    hs = [hT[:, k, c0:c0 + n] for k in range(8)]
                    bq = self.newbank()
                    for k in range(8):
                        self.mm(self.ps[bq][:, 0:n], wt[:, ws, k, 0:128], hs[k], k == 0, k == 7, ["hT", wtok], [("ps", bq)])
                    pq = self.ps[bq][:, 0:n]
                    o = ctr["ost"] % 4
                    ctr["ost"] += 1
                    otok = ("ost", o)
                    oo = ost[:, o, 0:n]
                    need_sw = (len(cols) == 2) and not isc
                    if need_sw:
                        bs = self.newbank()
                        for k in range(8):
                            self.mm(self.ps[bs][:, 0:n], wt[:, ws, k, 128:256], hs[k], k == 0, k == 7, ["hT", wtok], [("ps", bs)])
                        psw = self.ps[bs][:, 0:n]
                    tt_ = ctr["t"] % 2
                    ctr["t"] += 1
                    a1, a2 = t1[:, tt_, 0:n], t2[:, tt_, 0:n]
                    k1, k2 = ("t1", tt_), ("t2", tt_)
                    if kind in ("QA", "KA"):
                        if isc:
                            self.act(oo, pq, AF.Copy, [("ps", bq)], [otok])
                        else:
                            self.tt("dve", a1, pq, rC[:, c0:c0 + n], ALU.mult, [("ps", bq), "rC"], [k1])
                            self.tt("dve", a2, psw, rS[:, c0:c0 + n], ALU.mult, [("ps", bs), "rS"], [k2])
                            self.tt("dve", oo, a1, a2, ALU.add, [k1, k2], [otok])
                    elif kind in ("QB", "KB"):
                        gi = 0 if kind == "QB" else 2
                        self.act(sqh[:, tt_, 0:n], pq, AF.Square, [("ps", bq)], [("sqh", tt_)])
                        bss = self.newbank()
                        self.mm(self.ps[bss][:, 0:n], self.bd[:], sqh[:, tt_, 0:n], True, True, [("sqh", tt_), "bd"], [("ps", bss)])
                        self.act(lnh[:, tt_, 0:n], self.ps[bss][:, 0:n], AF.Ln, [("ps", bss)], [("lnh", tt_)], bias=self.epsc[:, 0:1], scale=1.0 / 64)
                        self.act(rsh[:, tt_, 0:n], lnh[:, tt_, 0:n], AF.Exp, [("lnh", tt_)], [("rsh", tt_)], scale=-0.5)
                        if isc:
                            self.stt(oo, pq, bg[:, gi:gi + 1], rsh[:, tt_, 0:n], ALU.mult, ALU.mult, [("ps", bq), "bg", ("rsh", tt_)], [otok])
                        else:
                            self.stt(a1, pq, bg[:, gi:gi + 1], rC[:, c0:c0 + n], ALU.mult, ALU.mult, [("ps", bq), "bg", "rC", ("sqh", tt_)], [k1])
                            self.stt(a2, psw, bg[:, gi + 1:gi + 2], rS[:, c0:c0 + n], ALU.mult, ALU.mult, [("ps", bs), "bg", "rS"], [k2])
                            self.tt("dve", a1, a1, a2, ALU.add, [k1, k2], [k1])
                            self.tt("dve", oo, a1, rsh[:, tt_, 0:n], ALU.mult, [k1, ("rsh", tt_)], [otok])
                    else:
                        self.act(oo, pq, AF.Copy, [("ps", bq)], [otok])
                    dl = []
                    cb = contrib
                    if kind == "QA":
                        dl.append((q_d[0].ap()[:, mi, c0:c0 + n], oo))
                    elif kind == "QB":
                        dl.append((q_d[1].ap()[:, mi, c0:c0 + n], oo))
                    elif kind == "QC":
                        dl.append((q_d[2].ap()[:, mi, c0:c0 + n], oo))
                    elif kind == "KA":
                        dl.append((kaT_d.ap()[:, c0:c0 + n], oo))
                        if ci == 0:
                            dl.append((dview(cb, O_KAH, [[128, 128], [1, 128]]), ost[:, o, 0:128]))
                        if ci == 3:
                            dl.append((dview(cb, O_KAT, [[128, 128], [1, 128]]), ost[:, o, 384:512]))
                    elif kind == "KB":
                        if isc:
                            dl.append((kbcT_d.ap(), oo))
                        else:
                            dl.append((dview(cb, O_KB + c0, [[2048, 128], [1, n]]), oo))
                    elif kind == "KC":
                        dl.append((kcT_d.ap()[:, mi, c0:c0 + n], oo))
                        if ci == 0:
                            dl.append((dview(cb, O_KCH + mi * 256, [[768, 128], [1, 256]]), ost[:, o, 0:256]))
                        if ci == 3:
                            dl.append((dview(cb, O_KCT + mi * 256, [[768, 128], [1, 256]]), ost[:, o, 256:512]))
                    store(dl, otok)
            T.flush()

    def phase_gather(self, contrib, gath):
        return

    def emit_gather(self, contrib, rtoks):
        T = self.T
        gB, gH = self.gathB, self.gathH
        T.add("pool", lambda e: e.collective_compute("AllGather", ALU.bypass, replica_groups=[[0, 1, 2, 3], [4, 5, 6, 7]],
                                                     ins=[contrib.ap()[0:512, :]], outs=[gB.ap()]), rtoks, ["gathB"], cc=True)
        T.add("pool", lambda e: e.collective_compute("AllGather", ALU.bypass, replica_groups=[[0, 1, 2, 3], [4, 5, 6, 7]],
                                                     ins=[contrib.ap()[512:960, :]], outs=[gH.ap()]), rtoks, ["gathH"], cc=True)

    def attn_fin(self, bank, n, base, out_ap, rc, rctok, extra=None, extra_tok=None, shape3=None, act_recip=True):
        ob = 64 - base
        k = self.fc_i % 2
        self.fc_i += 1
        fct = ("fc", k)
        self.cp("dve", self.fc[:, k, 0:n], self.ps[bank][:, 0:n], [("ps", bank)], [fct])
        den = self.fc[ob:ob + 64, k, 0:n]
        num = self.fc[base:base + 64, k, 0:n]
        rcp = rc[base:base + 64, 0:n]
        tmp = rc[ob:ob + 64, 0:n]
        if shape3 is not None:
            a, b_ = shape3
            den = den.rearrange("p (a b) -> p a b", a=a)
            num = num.rearrange("p (a b) -> p a b", a=a)
            rcp = rcp.rearrange("p (a b) -> p a b", a=a)
            tmp = tmp.rearrange("p (a b) -> p a b", a=a)
        src, srct = den, fct
        if extra is not None:
            self.tt("dve", tmp, den, extra, ALU.add, [fct, extra_tok], [rctok])
            src, srct = tmp, rctok
        if act_recip:
            self.act(rcp, src, AF.Ln, [srct], [rctok])
            self.act(rcp, rcp, AF.Exp, [rctok], [rctok], scale=-1.0)
        else:
            self.recip(rcp, src, [srct], [rctok])
        self.tt("dve", out_ap, num, rcp, ALU.mult, [fct, rctok], ["oT"])

    def pipeline(self, items, look=3, group=1):
        groups = [items[i:i + group] for i in range(0, len(items), group)]
        banks = {}

        def qks(gi):
            for k, itm in enumerate(groups[gi]):
                banks[(gi, k)] = self.newbank()
                itm[0](banks[(gi, k)])
        for gi in range(min(look, len(groups))):
            qks(gi)
        for gi in range(len(groups)):
            if gi + look < len(groups):
                qks(gi + look)
            for k, itm in enumerate(groups[gi]):
                itm[1](banks[(gi, k)])
            for k, itm in enumerate(groups[gi]):
                itm[2]()
                if itm[3] is not None:
                    itm[3]()

    def load_vaug(self, dst, src, tokn):
        self.dma(dst, src, (), [tokn])

    def phase_attn_a(self, Ld, l, last, qa_d, kaT_d, va_d, gath, maskA, oa_d):
        T = self.T
        with ExitStack() as es:
            self.split = True
            self.nsb = 6
            self.bankc = 0
            QT = self.sb(es, "QT", [128, 3, NTOT], BF16)
            KT = self.sb(es, "KT", [128, NTOT], BF16)
            KH = self.sb(es, "KH", [128, 8, 128], BF16)
            VA = self.sb(es, "VA", [128, 18, 2, 128], BF16)
            VH = self.sb(es, "VH", [128, 8, 2, 128], BF16)
            MA = self.sb(es, "MA", [128, 10, 128], F32)
            ES = self.sb(es, "ES", [128, 6, 128], F32)
            sk = self.sb(es, "sk", [128, 6], F32)
            PT = self.sb(es, "PT", [128, 8, 512], BF16)
            tm = self.sb(es, "tm", [128, 4, 384], F32)
            rc = self.sb(es, "rc", [128, 2, 512], F32)
            self.fc = self.sb(es, "fc", [128, 2, 512], F32)
            self.fc_i = 0
            oT = self.sb(es, "oT", [128, 3, NTOT], BF16)
            nq = NT if last else NTOT
            self.dma(QT[:, :, 0:nq], qa_d.ap()[:, :, 0:nq], (), ["+QT"])
            self.dma(KT[:], kaT_d.ap(), (), ["+KT"])
            self.memset("pool", VA[:], 1.0, ["+VA"])
            self.memset("pool", VH[:], 1.0, ["+VH"])
            vsrc = va_d.ap().rearrange("(t p) c -> p t c", p=128)
            self.dma(VA[:, :, 0, 0:64], vsrc[:, :, 0:64], (), ["+VA"])
            self.dma(VA[:, :, 1, 64:128], vsrc[:, :, 64:128], (), ["+VA"])
            for r in range(4):
                pass
                self.dma(KH[:, r, :], gath.view(r, O_KAT, [[128, 128], [1, 128]]), ["gath"], ["+KH"])
                self.dma(KH[:, 4 + r, :], gath.view(r, O_KAH, [[128, 128], [1, 128]]), ["gath"], ["+KH"])
                self.dma(VH[:, r, 0, 0:64], gath.view(r, O_VAT, [[128, 128], [1, 64]]), ["gath"], ["+VH"])
                self.dma(VH[:, r, 1, 64:128], gath.view(r, O_VAT + 64, [[128, 128], [1, 64]]), ["gath"], ["+VH"])
                self.dma(VH[:, 4 + r, 0, 0:64], gath.view(r, O_VAH, [[128, 128], [1, 64]]), ["gath"], ["+VH"])
                self.dma(VH[:, 4 + r, 1, 64:128], gath.view(r, O_VAH + 64, [[128, 128], [1, 64]]), ["gath"], ["+VH"])
            self.dma(MA[:], maskA.ap(), (), ["MA"])
            self.dma(sk[:], Ld["sinkT"].ap(), (), ["sk"])
            self.act(sk[:], sk[:], AF.Exp, ["sk"], ["sk"])
            self.cp("dve", ES[:], bcast(sk[:], 2, 128), ["sk"], ["ES"])
            it = dict(p=0, t=0, r=0)
            items = []

            def run(qcols, nqc, base, tiles, shape3, out_ap, es_ap, items):
                n = 3 * nqc
                rhs = QT[base:base + 64, :, qcols:qcols + nqc]
                bo = self.nsb + base // 64
                nt = len(tiles)
                for ti, (kap, vap, mk) in enumerate(tiles):
                    p = it["p"] % 8
                    it["p"] += 1
                    t = it["t"] % 4
                    if mk is not None:
                        it["t"] += 1

                    def qk(b, kap=kap):
                        self.mm(self.ps[b][:, 0:n].rearrange("p (a b) -> p a b", a=3), kap, rhs, True, True, ["+QT", "+KT", "+KH"], [("ps", b)])

                    def sm(b, mk=mk, p=p, t=t):
                        pt = PT[:, p, 0:n]
                        if mk is not None:
                            self.stt(tm[:, t, 0:n].rearrange("p (a b) -> p a b", a=3), self.ps[b][:, 0:n].rearrange("p (a b) -> p a b", a=3),
                                     0.125, bcast(mk, 1, 3), ALU.mult, ALU.add, [("ps", b), "MA"], [("tm", t)])
                            self.act(pt, tm[:, t, 0:n], AF.Exp, [("tm", t)], [("PT", p)])
                        else:
                            self.act(pt, self.ps[b][:, 0:n], AF.Exp, [("ps", b)], [("PT", p)], scale=0.125)

                    def pv(vap=vap, p=p, ti=ti):
                        self.mm(self.ps[bo][:, 0:n], vap, PT[:, p, 0:n], ti == 0, ti == nt - 1, [("PT", p), "+VA", "+VH"], [("ps", bo)])
                    fin = None
                    if ti == nt - 1:
                        r_ = it["r"] % 2
                        it["r"] += 1

                        def fin(r_=r_):
                            self.attn_fin(bo, n, base, out_ap, rc[:, r_], ("rc", r_), extra=es_ap, extra_tok="ES", shape3=shape3)
                    items.append((qk, sm, pv, fin))

            all_items = items
            half_items = []
            for half in range(2):
                base = half * 64
                ob = 64 - base
                items = []
                half_items.append(items)
                esl = ES[ob:ob + 64, 3 * half:3 * half + 3, :]
                for j in range(16):
                    tiles = []
                    if j > 0:
                        tiles.append((KT[base:base + 64, (j - 1) * 128:j * 128], VA[:, j - 1, half, :], MA[:, 0, :]))
                    else:
                        for r in range(4):
                            tiles.append((KH[base:base + 64, r, :], VH[:, r, half, :], MA[:, 2 + r, :]))
                    tiles.append((KT[base:base + 64, j * 128:(j + 1) * 128], VA[:, j, half, :], None))
                    if j < 15:
                        tiles.append((KT[base:base + 64, (j + 1) * 128:(j + 2) * 128], VA[:, j + 1, half, :], MA[:, 1, :]))
                    else:
                        for r in range(4):
                            tiles.append((KH[base:base + 64, 4 + r, :], VH[:, 4 + r, half, :], MA[:, 6 + r, :]))
                    for c in range(2):
                        tiles.append((KT[base:base + 64, NT + c * 128:NT + (c + 1) * 128], VA[:, 16 + c, half, :], None))
                    run(j * 128, 128, base, tiles, (3, 128), oT[base:base + 64, :, j * 128:(j + 1) * 128], esl, items)
                if not last:
                    for cq in range(2):
                        tiles = [(KT[base:base + 64, NT + c * 128:NT + (c + 1) * 128], VA[:, 16 + c, half, :], None) for c in range(2)]
                        q0 = NT + cq * 128
                        run(q0, 128, base, tiles, (3, 128), oT[base:base + 64, :, q0:q0 + 128], esl, items)
            assert len(half_items[0]) == len(half_items[1])
            for a_, b_ in zip(half_items[0], half_items[1]):
                all_items.append(a_)
                all_items.append(b_)
            self.pipeline(all_items, 2, 2)
            self.nsb = 4
            self.dma(oa_d.ap()[:, :, 0:nq], oT[:, :, 0:nq], ["oT"], ["oa_d"])
            T.flush()

    def phase_attn_b(self, Ld, l, last, qb_d, kbcT_d, vbc_d, gath, ob_d):
        T = self.T
        with ExitStack() as es:
            NK = NCX + 4 * NT
            self.split = True
            self.nsb = 6
            self.bankc = 0
            QT = self.sb(es, "QTb", [128, 3, NTOT], BF16)
            KT = self.sb(es, "KTb", [128, NK], BF16)
            VB = self.sb(es, "VBb", [128, 66, 2, 128], BF16)
            PT = self.sb(es, "PTb", [128, 6, 512], BF16)
            rc = self.sb(es, "rcb", [128, 2, 512], F32)
            self.fc = self.sb(es, "fcb", [128, 2, 512], F32)
            self.fc_i = 0
            oT = self.sb(es, "oTb", [128, 3, NTOT], BF16)
            nq = NT if last else NTOT
            self.dma(QT[:, :, 0:nq], qb_d.ap()[:, :, 0:nq], (), ["+QT"])
            self.dma(KT[:, 0:NCX], kbcT_d.ap(), (), ["+KT"])
            self.memset("pool", VB[:, 0:33], 1.0, ["+VB"])
            self.memset("pool", VB[:, 33:66], 1.0, ["+VB"])
            vcs = vbc_d.ap().rearrange("(t p) c -> p t c", p=128)
            self.dma(VB[:, 0:2, 0, 0:64], vcs[:, :, 0:64], (), ["+VB"])
            self.dma(VB[:, 0:2, 1, 64:128], vcs[:, :, 64:128], (), ["+VB"])
            for r in range(4):
                pass
                self.dma(KT[:, NCX + r * NT:NCX + (r + 1) * NT], gath.view(r, O_KB, [[2048, 128], [1, 2048]]), ["gath"], ["+KT"])
                self.dma(VB[:, 2 + r * 16:2 + (r + 1) * 16, 0, 0:64], gath.view(r, O_VB, [[128, 128], [128 * 128, 16], [1, 64]]), ["gath"], ["+VB"])
                self.dma(VB[:, 2 + r * 16:2 + (r + 1) * 16, 1, 64:128], gath.view(r, O_VB + 64, [[128, 128], [128 * 128, 16], [1, 64]]), ["gath"], ["+VB"])
            it = dict(p=0, r=0)
            items = []

            def run(mi, q0, n, ktiles):
                bo = [self.newacc(), self.newacc()]
                nt = len(ktiles)
                for ti, kt in enumerate(ktiles):
                    for half in range(2):
                        base = half * 64
                        p = it["p"] % 6
                        it["p"] += 1

                        def qk(b, base=base, kt=kt):
                            self.mm(self.ps[b][:, 0:n], KT[base:base + 64, kt * 128:(kt + 1) * 128], QT[base:base + 64, mi, q0:q0 + n],
                                    True, True, ["+QT", "+KT"], [("ps", b)])

                        def sm(b, p=p, half=half):
                            if half == 1:
                                return
                            assert b % 2 == 0 and p % 2 == 0
                            src = self.psall[:, b * 512:(b + 2) * 512].rearrange("p (a c) -> p a c", a=2)[:, :, 0:n]
                            self.act(PT[:, p:p + 2, 0:n], src, AF.Exp, [("ps", b), ("ps", b + 1)], [("PT", p), ("PT", p + 1)], scale=0.125)

                        def pv(p=p, half=half, kt=kt, ti=ti):
                            self.mm(self.ps[bo[half]][:, 0:n], VB[:, kt, half, :], PT[:, p, 0:n], ti == 0, ti == nt - 1,
                                    [("PT", p), "+VB"], [("ps", bo[half])])
                        fin = None
                        if ti == nt - 1:
                            r_ = it["r"] % 2
                            it["r"] += 1

                            def fin(r_=r_, half=half, base=base):
                                self.attn_fin(bo[half], n, base, oT[base:base + 64, mi, q0:q0 + n], rc[:, r_], ("rc", r_), act_recip=False)
                        items.append((qk, sm, pv, fin))

            for mi in range(3):
                for qc in range(4):
                    run(mi, qc * 512, 512, list(range(66)))
                if not last:
                    run(mi, NT, NCX, [0, 1])
            self.pipeline(items, 2, 2)
            self.nsb = 4
            self.dma(ob_d.ap()[:, :, 0:nq], oT[:, :, 0:nq], ["oT"], ["ob_d"])
            T.flush()

    def phase_attn_c(self, Ld, l, last, qc_d, kcT_d, vc_d, gath, rm01, oc_d):
        T = self.T
        with ExitStack() as es:
            self.split = True
            self.nsb = 6
            self.bankc = 0
            QT = self.sb(es, "QTc", [128, 3, NTOT], BF16)
            KT = self.sb(es, "KTc", [128, 3, NTOT], BF16)
            KH = self.sb(es, "KHc", [128, 3, 8, 256], BF16)
            VC = self.sb(es, "VCc", [128, 18, 6, 128], BF16)
            VH = self.sb(es, "VHc", [128, 16, 6, 128], BF16)
            EB = self.sb(es, "EB", [128, 2, 6, 22, 64], BF16)
            RM = self.sb(es, "RM", [128, 44, 8], F32)
            RMb = self.sb(es, "RMb", [128, 44, 8], BF16)
            PT = self.sb(es, "PTc", [128, 8, 512], BF16)
            rc = self.sb(es, "rcc", [128, 2, 512], F32)
            self.fc = self.sb(es, "fcc", [128, 2, 512], F32)
            self.fc_i = 0
            oT = self.sb(es, "oTc", [128, 3, NTOT], BF16)
            nq = NT if last else NTOT
            self.dma(QT[:, :, 0:nq], qc_d.ap()[:, :, 0:nq], (), ["+QT"])
            self.dma(KT[:], kcT_d.ap(), (), ["+KT"])
            self.memset("pool", VC[:], 1.0, ["+VC"])
            self.memset("pool", VH[:], 1.0, ["+VH"])
            vsrc = vc_d.ap().rearrange("(t p) (h d) -> p t h d", p=128, d=64)
            for h in range(6):
                o = (h % 2) * 64
                self.dma(VC[:, :, h, o:o + 64], vsrc[:, :, h, :], (), ["+VC"])
            for r in range(4):
                pass
                self.dma(KH[:, :, r, :], gath.view(r, O_KCT, [[768, 128], [256, 3], [1, 256]]), ["gath"], ["+KH"])
                self.dma(KH[:, :, 4 + r, :], gath.view(r, O_KCH, [[768, 128], [256, 3], [1, 256]]), ["gath"], ["+KH"])
                for h in range(6):
                    o = (h % 2) * 64
                    self.dma(VH[:, 2 * r:2 * r + 2, h, o:o + 64], gath.view(r, O_VCT + h * 64, [[384, 128], [128 * 384, 2], [1, 64]]), ["gath"], ["+VH"])
                    self.dma(VH[:, 8 + 2 * r:8 + 2 * r + 2, h, o:o + 64], gath.view(r, O_VCH + h * 64, [[384, 128], [128 * 384, 2], [1, 64]]), ["gath"], ["+VH"])
            TBd = Ld["TB"].ap()
            for tbl in range(2):
                ebf = EB[:, tbl].rearrange("p h u q -> p (h u q)")
                c = 0
                while c < 6 * 22 * 64:
                    cc_ = min(1024, 6 * 22 * 64 - c)
                    sap, stoks = self.stgs[self.stg_i % len(self.stgs)]
                    self.stg_i += 1
                    st = sap[:, 0:cc_]
                    self.dma(st, TBd[:, tbl, c:c + cc_], (), stoks)
                    self.T.add("act", (lambda e_, o_=ebf[:, c:c + cc_], i_=st: e_.activation(out=o_, in_=i_, func=AF.Exp)),
                               stoks, ["+EB"], accw=True)
                    c += cc_
            self.dma(RM[:], rm01.ap(), (), ["RM"])
            self.cp("dve", RMb[:], RM[:], ["RM"], ["RMb"])
            it = dict(p=0, t=0, r=0)
            items = []

            def crun(q0, n, tiles, h, mi, base, items):
                rhs = QT[base:base + 64, mi, q0:q0 + n]
                bo = self.nsb + base // 64
                nt = len(tiles)
                for ti, (kap, vap, j, ri) in enumerate(tiles):
                    p = it["p"] % 8
                    it["p"] += 1
                    t = it["t"] % 3
                    if j is not None:
                        it["t"] += 1

                    def qk(b, kap=kap):
                        self.mm(self.ps[b][:, 0:n], kap, rhs, True, True, ["+QT", "+KT", "+KH"], [("ps", b)])

                    def sm(b, j=j, ri=ri, p=p, t=t):
                        pt = PT[:, p, 0:n]
                        self.act(pt, self.ps[b][:, 0:n], AF.Exp, [("ps", b)], [("PT", p)], scale=0.125)
                        if j is not None:
                            u0 = 14 - 2 * j
                            g_ = q0 // 512
                            tbl = 1 if g_ in (1, 2) else 0
                            pt3 = pt.rearrange("p (a b) -> p a b", a=8)
                            self.tt("dve", pt3, pt3, EB[:, tbl, h, u0:u0 + 8, :], ALU.mult, [("PT", p), "+EB"], [("PT", p)])
                            if tbl == 0:
                                self.tt("dve", pt3, pt3, bcast(RMb[:, ri, :], 2, 64), ALU.mult, [("PT", p), "RMb"], [("PT", p)])

                    def pv(vap=vap, p=p, ti=ti):
                        self.mm(self.ps[bo][:, 0:n], vap, PT[:, p, 0:n], ti == 0, ti == nt - 1, [("PT", p), "+VC", "+VH"], [("ps", bo)])
                    fin = None
                    if ti == nt - 1:
                        r_ = it["r"] % 2
                        it["r"] += 1

                        def fin(r_=r_):
                            self.attn_fin(bo, n, base, oT[base:base + 64, mi, q0:q0 + n], rc[:, r_], ("rc", r_))
                    items.append((qk, sm, pv, fin))

            all_items = items
            for h in range(6):
                mi, half = h // 2, h % 2
                base = half * 64
                rmi = 0
                items = []
                if half == 0:
                    ev_items = items
                else:
                    od_items = items
                for g in range(4):
                    tiles = []
                    for j in range(8):
                        lt = g * 512 - 256 + 128 * j
                        if g == 0 and j < 2:
                            for r in range(4):
                                tiles.append((KH[base:base + 64, mi, r, j * 128:(j + 1) * 128], VH[:, 2 * r + j, h, :], j, rmi))
                                rmi += 1
                        elif g == 3 and j >= 6:
                            for r in range(4):
                                tiles.append((KH[base:base + 64, mi, 4 + r, (j - 6) * 128:(j - 5) * 128], VH[:, 8 + 2 * r + (j - 6), h, :], j, rmi))
                                rmi += 1
                        else:
                            tiles.append((KT[base:base + 64, mi, lt:lt + 128], VC[:, lt // 128, h, :], j, rmi))
                            rmi += 1
                    for c in range(2):
                        tiles.append((KT[base:base + 64, mi, NT + c * 128:NT + (c + 1) * 128], VC[:, 16 + c, h, :], None, None))
                    crun(g * 512, 512, tiles, h, mi, base, items)
                if not last:
                    tiles = [(KT[base:base + 64, mi, NT + c * 128:NT + (c + 1) * 128], VC[:, 16 + c, h, :], None, None) for c in range(2)]
                    crun(NT, NCX, tiles, h, mi, base, items)
                if half == 1:
                    assert len(ev_items) == len(od_items)
                    for a_, b_ in zip(ev_items, od_items):
                        all_items.append(a_)
                        all_items.append(b_)
            self.pipeline(all_items, 2, 2)
            self.nsb = 4
            self.dma(oc_d.ap()[:, :, 0:nq], oT[:, :, 0:nq], ["oT"], ["oc_d"])
            T.flush()

    def post_norm_res(self, yTb, n, PG, xsrc, xdst, tok_y, tok_x, tok_o, tmpb):
        rs = self.rstd_of(None, yTb[:, :, 0:n], 8, n, 1024.0, tok_y, "pn")
        for m in range(8):
            self.stt(tmpb[:, m, 0:n], yTb[:, m, 0:n], PG[:, m:m + 1], rs, ALU.mult, ALU.mult, [tok_y, "rstd", "mods"], ["xn"])
            self.tt("dve", xdst[:, m, 0:n], xsrc[:, m, 0:n], tmpb[:, m, 0:n], ALU.add, [tok_x, "xn"], [tok_o])

    def phase_mix(self, Ld, l, last, xin_lat, xin_ctx, o_d, xs1):
        T = self.T
        with ExitStack() as es:
            self.split = False
            self.alloc_norm(es, 256)
            N = 256
            wg = self.sb(es, "wg", [128, 8, 3072], BF16)
            wbr = self.sb(es, "wbr", [128, 3, 3, 1024], BF16)
            wo = self.sb(es, "wo", [128, 8, 1024], BF16)
            xch = self.sb(es, "xchm", [128, 2, 8, N], F32)
            hT = self.sb(es, "hTm", [128, 2, 8, N], BF16)
            oc = self.sb(es, "ocm", [128, 2, 3, 3, N], BF16)
            sg = self.sb(es, "sg", [128, 2, 3, N], F32)
            ta = self.sb(es, "ta", [128, 2, 3, N], F32)
            mg = self.sb(es, "mg", [128, 8, N], BF16)
            yTb = self.sb(es, "yTb", [128, 8, N], F32)
            stg2 = self.sb(es, "stg2", [128, 6, 1024], F32)
            chunks = [(i * N, N, False) for i in range(NT // N)]
            if not last:
                chunks.append((NT, NCX, True))

            def prep(ci):
                c0, n, isc = chunks[ci]
                s_ = ci % 2
                src = xin_ctx if isc else xin_lat(c0, n)
                self.dma(xch[:, s_, :, 0:n], src, (), [("xch", s_)])
                for i in range(3):
                    self.dma(oc[:, s_, i, :, 0:n], o_d[i].ap()[:, :, c0:c0 + n], (), [("+oc", s_)])
                dv_ = self.dvec[:, l, 1 if isc else 0]
                self.norm_mod(xch[:, s_], n, dv_[:, 0, :], dv_[:, 1, :], hT[:, s_], ("xch", s_), ("hTm", s_))
            prep(0)
            win = Ld["win"].ap().rearrange("(k p) n -> p k n", p=128)
            self.cast_engs = ("act", "dve")
            self.stgs = [(stg2[:, i, :], [("stg2", i)]) for i in range(6)]
            for i in range(3):
                self.load_w(wg[:, :, i * 1024:(i + 1) * 1024], win[:, :, C_G + i * 1024:C_G + (i + 1) * 1024], ("+wg", i), 8, 1024)
                self.load_w(wbr[:, i], Ld["wbr"].ap()[i].rearrange("(k p) n -> p k n", p=128), ("+wbr", i), 3, 1024)
            self.load_w(wo[:], Ld["wout"].ap().rearrange("(k p) n -> p k n", p=128), "+wo", 8, 1024)
            self.stgs = self.stgs0
            self.cast_engs = ("act",)
            for ci, (c0, n, isc) in enumerate(chunks):
                s = ci % 2
                dv = self.dvec[:, l, 1 if isc else 0]
                for m in range(8):
                    q = m % 2
                    gb_, yb_ = [], []
                    for i in range(3):
                        b = self.newbank()
                        gb_.append(b)
                        for k in range(8):
                            self.mm(self.ps[b][:, 0:n], wg[:, k, i * 1024 + m * 128:i * 1024 + (m + 1) * 128], hT[:, s, k, 0:n], k == 0, k == 7,
                                    [("+wg", i), ("hTm", s)], [("ps", b)])
                        b = self.newbank()
                        yb_.append(b)
                        for k in range(3):
                            self.mm(self.ps[b][:, 0:n], wbr[:, i, k, m * 128:(m + 1) * 128], oc[:, s, i, k, 0:n], k == 0, k == 2,
                                    [("+wbr", i), ("+oc", s)], [("ps", b)])
                    for i in range(3):
                        self.act(sg[:, q, i, 0:n], self.ps[gb_[i]][:, 0:n], AF.Sigmoid, [("ps", gb_[i])], [("sg", q, i)])
                        self.tt("dve", ta[:, q, i, 0:n], sg[:, q, i, 0:n], self.ps[yb_[i]][:, 0:n], ALU.mult, [("sg", q, i), ("ps", yb_[i])], [("ta", q, i)])
                    self.tt("dve", ta[:, q, 0, 0:n], ta[:, q, 0, 0:n], ta[:, q, 1, 0:n], ALU.add, [("ta", q, 0), ("ta", q, 1)], [("ta", q, 0)])
                    self.tt("dve", mg[:, m, 0:n], ta[:, q, 0, 0:n], ta[:, q, 2, 0:n], ALU.add, [("ta", q, 0), ("ta", q, 2)], [("mg", m)])
                if ci + 1 < len(chunks):
                    prep(ci + 1)
                for m in range(8):
                    b = self.newbank()
                    for k in range(8):
                        self.mm(self.ps[b][:, 0:n], wo[:, k, m * 128:(m + 1) * 128], mg[:, k, 0:n], k == 0, k == 7, ["+wo", ("mg", k)], [("ps", b)])
                    self.act(yTb[:, m, 0:n], self.ps[b][:, 0:n], AF.Copy, [("ps", b)], ["yTb"])
                self.post_norm_res(yTb, n, dv[:, 2, :], xch[:, s], xch[:, s], "yTb", ("xch", s), ("xch", s), self.xn)
                self.dma(xs1.ap()[:, :, c0:c0 + n], xch[:, s, :, 0:n], [("xch", s)], [("xs1", ci)])
            T.flush()

    def phase_mlp(self, Ld, l, last, xs1, out_lat, xs2):
        T = self.T
        with ExitStack() as es:
            self.split = False
            self.alloc_norm(es, 256)
            w1 = self.sb(es, "w1", [128, 8, 4096], BF16)
            w2 = self.sb(es, "w2", [128, 32, 1024], BF16)
            xch = self.sb(es, "xchp", [128, 2, 8, 256], F32)
            h2 = self.sb(es, "h2", [128, 2, 8, 256], BF16)
            rl = self.sb(es, "rl", [128, 2, 256], F32)
            aT = self.sb(es, "aT", [128, 32, 256], BF16)
            y2 = self.sb(es, "y2", [128, 8, 256], F32)
            n = 256
            chunks = [(i * 256, False) for i in range(8)]
            if not last:
                chunks.append((NT, True))

            def prep(ci):
                c0, isc = chunks[ci]
                s_ = ci % 2
                self.dma(xch[:, s_], xs1.ap()[:, :, c0:c0 + n], ["xs1"], [("xch", s_)])
                dv_ = self.dvec[:, l, 1 if isc else 0]
                self.norm_mod(xch[:, s_], n, dv_[:, 3, :], dv_[:, 4, :], h2[:, s_], ("xch", s_), ("h2", s_))
            prep(0)
            self.cast_engs = ("act", "dve")
            aTf = aT[:].rearrange("p a b -> p (a b)").bitcast(F32).rearrange("p (s c) -> p s c", s=4)
            self.stgs = self.stgs0 + [(aTf[:, i, :], [("stgA", i)] + [("aT", j) for j in range(8 * i, 8 * i + 8)]) for i in range(4)]
            w1src = Ld["w1"].ap().rearrange("(k p) n -> p k n", p=128)
            for cc_ in range(4):
                self.load_w(w1[:, :, cc_ * 1024:(cc_ + 1) * 1024], w1src[:, :, cc_ * 1024:(cc_ + 1) * 1024], ("+w1", cc_), 8, 1024)
            self.load_w(w2[:], Ld["w2"].ap().rearrange("(k p) n -> p k n", p=128), "+w2", 32, 1024)
            self.stgs = self.stgs0
            self.cast_engs = ("act",)
            for ci, (c0, isc) in enumerate(chunks):
                s = ci % 2
                dv = self.dvec[:, l, 1 if isc else 0]
                for j in range(32):
                    b = self.newbank()
                    for k in range(8):
                        self.mm(self.ps[b][:, 0:n], w1[:, k, j * 128:(j + 1) * 128], h2[:, s, k, :], k == 0, k == 7, [("+w1", j // 8), ("h2", s)], [("ps", b)])
                    r_ = j % 2
                    self.act(rl[:, r_, :], self.ps[b][:, 0:n], AF.Relu, [("ps", b)], [("rl", r_)])
                    self.tt("dve", aT[:, j, :], rl[:, r_, :], rl[:, r_, :], ALU.mult, [("rl", r_)], [("aT", j)])
                if ci + 1 < len(chunks):
                    prep(ci + 1)
                for m in range(8):
                    b = self.newbank()
                    for j in range(32):
                        self.mm(self.ps[b][:, 0:n], w2[:, j, m * 128:(m + 1) * 128], aT[:, j, :], j == 0, j == 31, ["+w2", ("aT", j)], [("ps", b)])
                    self.act(y2[:, m, :], self.ps[b][:, 0:n], AF.Copy, [("ps", b)], ["y2"])
                self.post_norm_res(y2, n, dv[:, 5, :], xch[:, s], xch[:, s], "y2", ("xch", s), ("xch", s), self.xn)
                if isc:
                    dst = xs2.ap()[:, :, c0:c0 + n]
                else:
                    dst = out_lat.ap()[:, :, c0:c0 + n]
                self.dma(dst, xch[:, s], [("xch", s)], [("out", ci)])
            T.flush()


def _fm(v):
    v = np.asarray(v, np.float32)
    return np.ascontiguousarray(v.reshape(-1, 128).T)


def _perm_cols():
    def hc(base, heads, swap):
        out = []
        for h in heads:
            d = np.arange(64)
            if swap:
                d = d ^ 1
            out += list(base + h * 64 + d)
        return out
    p = []
    p += hc(0, HP, False) + hc(0, HP, True)
    p += hc(384, [0, 1], False) + hc(384, [0, 1], True)
    p += hc(640, HP, False) + hc(640, HP, True)
    p += hc(1024, [0, 1], False) + hc(1024, [0, 1], True)
    p += list(range(1280, 1664)) + list(range(1664, 2048))
    p += list(range(512, 640)) + list(range(1152, 1280)) + list(range(2048, 2432))
    p += list(range(2432, 5504))
    assert len(p) == NWIN
    return np.array(p)


def _rope_tables(tok0):
    pos = np.arange(tok0, tok0 + NT)
    row = (pos // 64).astype(np.float32)
    col = (pos % 64).astype(np.float32)
    freqs = (np.float32(10000.0) ** (-np.arange(16, dtype=np.float32) / np.float32(16))).astype(np.float32)
    ang = np.concatenate([row[:, None] * freqs, col[:, None] * freqs], axis=-1).astype(np.float32)
    cos, sin = np.cos(ang).astype(np.float32), np.sin(ang).astype(np.float32)
    d = np.arange(128) % 64
    C = cos[:, d // 2].T
    S = sin[:, d // 2].T * np.where(d % 2 == 0, -1.0, 1.0)[:, None]
    return np.ascontiguousarray(C, np.float32), np.ascontiguousarray(S, np.float32)


def _mask_a(rank):
    ki = np.arange(128)[:, None]
    qi = np.arange(128)[None, :]
    prev = np.where(qi <= ki, 0.0, NEG).astype(np.float32)
    nxt = np.where(ki <= qi, 0.0, NEG).astype(np.float32)
    allneg = np.full((128, 128), NEG, np.float32)
    m = np.zeros((128, 10, 128), np.float32)
    m[:, 0], m[:, 1] = prev, nxt
    for r in range(4):
        m[:, 2 + r] = prev if r == rank - 1 else allneg
        m[:, 6 + r] = nxt if r == rank + 1 else allneg
    return m


def _rm01(rank):
    out = np.zeros((128, 44, 8), np.float32)
    idx = 0
    kl = (np.arange(128) // 64)[:, None]
    ql = np.arange(8)[None, :]
    for g in range(4):
        R0 = rank * 32 + g * 8
        for j in range(8):
            kr = R0 - 4 + 2 * j + kl
            qr = R0 + ql
            rs = np.clip(qr - 4, 0, 120)
            valid = (kr >= rs) & (kr < rs + 8) & (kr >= 0) & (kr < 128)
            if g == 0 and j < 2:
                for r in range(4):
                    out[:, idx] = valid if r == rank - 1 else 0.0
                    idx += 1
            elif g == 3 and j >= 6:
                for r in range(4):
                    out[:, idx] = valid if r == rank + 1 else 0.0
                    idx += 1
            else:
                out[:, idx] = valid
                idx += 1
    assert idx == 44
    return out


def _tb_table(rpb, interior=False):
    rpb = np.asarray(rpb, np.float32)
    kc = np.arange(64)[:, None]
    qc = np.arange(64)[None, :]
    ws = np.clip(qc - 8, 0, 48)
    colv = (kc >= ws) & (kc < ws + 16)
    dc = np.clip(kc - qc + 15, 0, 30)
    tb = np.zeros((2, 64, 6, 22, 64), np.float32)
    for kl in range(2):
        for u in range(22):
            dr = 17 + kl - u
            for h in range(6):
                if 0 <= dr <= 14:
                    v = rpb[h, dr][dc]
                else:
                    v = np.zeros((64, 64), np.float32)
                if interior and not (3 <= dr <= 10):
                    tb[kl, :, h, u, :] = NEG
                else:
                    tb[kl, :, h, u, :] = np.where(colv, v, NEG)
    return np.ascontiguousarray(tb.reshape(128, 6, 22, 64))


_CACHE = {}


def _get_nc(n_layers=2, taps=()):
    key = (n_layers, tuple(taps))
    if key not in _CACHE:
        _CACHE[key] = K(n_layers, taps).build()
    return _CACHE[key]


def make_in_maps(inputs, n_layers=2):
    f = lambda a: np.asarray(a, np.float32)
    x, c, ctx, c_ctx = f(inputs["x"]), f(inputs["c"]), f(inputs["ctx"]), f(inputs["c_ctx"])
    perm = _perm_cols()
    rows_ab = np.concatenate([np.arange(h * 64, (h + 1) * 64) for h in HP])
    shared = {}
    for l in range(n_layers):
        shared["wada%d" % l] = np.ascontiguousarray(f(inputs["w_ada"])[l])
        shared["badaT%d" % l] = _fm(f(inputs["b_ada"])[l])
        shared["gains%d" % l] = np.ascontiguousarray(np.stack(
            [_fm(f(inputs[k])[l]) for k in ("norm_mix_pre", "norm_mix_post", "norm_mlp_pre", "norm_mlp_post")], axis=1))
        shared["win%d" % l] = np.ascontiguousarray(f(inputs["w_in"])[l][:, perm])
        gq, gk = f(inputs["qnorm_b"])[l], f(inputs["knorm_b"])[l]
        d = np.arange(128) % 64
        shared["bgain%d" % l] = np.ascontiguousarray(np.stack([gq[d], gq[d ^ 1], gk[d], gk[d ^ 1]], axis=1))
        shared["sinkT%d" % l] = np.ascontiguousarray(np.broadcast_to(f(inputs["sink_a"])[l][None, :], (128, 6)))
        shared["TB%d" % l] = np.ascontiguousarray(np.stack(
            [_tb_table(f(inputs["rpb_c"])[l]), _tb_table(f(inputs["rpb_c"])[l], True)], axis=1))
        shared["wbr%d" % l] = np.ascontiguousarray(np.stack(
            [f(inputs["w_br_a"])[l][rows_ab], f(inputs["w_br_b"])[l][rows_ab], f(inputs["w_br_c"])[l]], axis=0))
        shared["wout%d" % l] = np.ascontiguousarray(f(inputs["w_out"])[l])
        shared["w1_%d" % l] = np.ascontiguousarray(f(inputs["w_mlp_in"])[l])
        shared["w2_%d" % l] = np.ascontiguousarray(f(inputs["w_mlp_out"])[l])
    in_maps = []
    for core in range(8):
        b, rank = core // 4, core % 4
        tok0 = rank * NT
        m = dict(shared)
        xs = x[b, tok0:tok0 + NT, :]
        m["xT"] = np.ascontiguousarray(xs.T.reshape(8, 128, NT).transpose(1, 0, 2))
        m["ctxT"] = np.ascontiguousarray(ctx[b].T.reshape(8, 128, NCX).transpose(1, 0, 2))
        m["ccT"] = np.ascontiguousarray(np.stack([_fm(c[b]), _fm(c_ctx)], axis=2))
        C, S = _rope_tables(tok0)
        m["ropeC"], m["ropeS"] = C, S
        m["maskA"] = _mask_a(rank)
        m["rm01"] = _rm01(rank)
        in_maps.append(m)
    return in_maps


def kernel(**inputs):
    nc = _get_nc(2)
    in_maps = make_in_maps(inputs, 2)
    res = run_bass_kernel_spmd(nc, in_maps, core_ids=list(range(8)))
    out = np.zeros((2, 4 * NT, 1024), np.float32)
    for core in range(8):
        b, rank = core // 4, core % 4
        yT = np.asarray(res.results[core]["yT"])
        out[b, rank * NT:(rank + 1) * NT, :] = yT.transpose(1, 0, 2).reshape(1024, NT).T
    return out
```

```python
import numpy as np
from contextlib import ExitStack
import concourse.bass as bass
import concourse.mybir as mybir
from concourse.bass_utils import run_bass_kernel_spmd

F32 = mybir.dt.float32
BF16 = mybir.dt.bfloat16
F32R = mybir.dt.float32r
AF = mybir.ActivationFunctionType
ALU = mybir.AluOpType

NT = 2048
NCX = 256
NTOT = NT + NCX
NEG = -30000.0
EPS = 1e-6
HP = [0, 3, 1, 4, 2, 5]
C_QA, C_QAS, C_KA, C_KAS = 0, 384, 768, 896
C_QB, C_QBS, C_KB, C_KBS = 1024, 1408, 1792, 1920
C_QC, C_KC, C_V, C_G = 2048, 2432, 2816, 3456
NWIN = 6528
O_KB, O_VB, O_KAH, O_KAT, O_VAH, O_VAT = 0, 262144, 524288, 540672, 557056, 573440
O_KCH, O_KCT, O_VCH, O_VCT, CONTRIB = 589824, 688128, 786432, 884736, 983040


class Trk:
    ENGS = ("pe", "act", "dve", "pool", "sp")
    NDSEM = 12

    def __init__(self, nc, es):
        self.nc = nc
        self.esem = {e: es.enter_context(nc.semaphore("s_" + e)) for e in self.ENGS}
        self.ecnt = {e: 0 for e in self.ENGS}
        self.dsem = {q: [es.enter_context(nc.semaphore("d_%s%d" % (q, i))) for i in range(self.NDSEM)]
                     for q in ("sp", "pool")}
        self.dval = {q: [0] * self.NDSEM for q in ("sp", "pool")}
        self.dcnt = {"sp": 0, "pool": 0}
        self.ccsem = es.enter_context(nc.semaphore("s_cc"))
        self.ccval = 0
        self.ops = []
        self.bar = None
        self.waited = {e: {} for e in self.ENGS}

    def add(self, eng, fn, r=(), w=(), dma=False, cc=False, accw=False):
        self.ops.append(dict(eng=eng, fn=fn, r=tuple(r), w=tuple(w), dma=dma, cc=cc, accw=accw))

    def flush(self):
        ops = self.ops
        self.ops = []
        if not ops:
            return
        last_w, readers = {}, {}

        def is_acc(tok):
            t0_ = tok[0] if isinstance(tok, tuple) else tok
            return isinstance(t0_, str) and t0_.startswith("+")
        for i, op in enumerate(ops):
            deps = set()
            for r in op["r"]:
                if r in last_w:
                    deps.update(last_w[r])
            for w in op["w"]:
                if w in last_w:
                    if is_acc(w) and (op["dma"] or op["accw"]):
                        deps.add(last_w[w][0])
                    else:
                        deps.update(last_w[w])
                for rd in readers.get(w, {}).values():
                    if isinstance(rd, list):
                        deps.update(rd)
                    else:
                        deps.add(rd)
            deps.discard(i)
            if op["eng"] == "pe":
                deps = {d for d in deps if ops[d]["dma"] or ops[d]["eng"] != "pe"}
            op["deps"] = deps
            for r in op["r"]:
                rr = readers.setdefault(r, {})
                if op["dma"]:
                    rr.setdefault("dma", []).append(i)
                else:
                    rr[op["eng"]] = i
            for w in op["w"]:
                if is_acc(w) and (op["dma"] or op["accw"]) and w in last_w:
                    last_w[w] = last_w[w] + [i]
                else:
                    last_w[w] = [i]
                readers[w] = {}
        for op in ops:
            op["sig"] = False
        for op in ops:
            for d in op["deps"]:
                ops[d]["sig"] = True
        last_of = {}
        for i, op in enumerate(ops):
            if not op["dma"]:
                last_of[op["eng"]] = i
        for i in last_of.values():
            ops[i]["sig"] = True
        for op in ops:
            if op["cc"]:
                self.ccval += 1
                op["done"] = (self.ccsem, self.ccval)
                op["pre"] = None
            elif op["dma"]:
                q = op["eng"]
                k = self.dcnt[q] % self.NDSEM
                self.dcnt[q] += 1
                prev = self.dval[q][k]
                self.dval[q][k] += 16
                op["done"] = (self.dsem[q][k], self.dval[q][k])
                op["pre"] = (self.dsem[q][k], prev) if prev > 0 else None
            elif op["sig"]:
                self.ecnt[op["eng"]] += 1
                op["done"] = (self.esem[op["eng"]], self.ecnt[op["eng"]])
                op["pre"] = None
            else:
                op["done"] = None
                op["pre"] = None
        per = {e: [] for e in self.ENGS}
        for op in ops:
            per[op["eng"]].append(op)
        bar = self.bar
        waited = self.waited

        def emit(ename, e):
            wd = waited[ename]

            def wait(sem, val):
                key = id(sem)
                if wd.get(key, 0) < val:
                    e.wait_ge(sem, val)
                    wd[key] = val
            if bar and per[ename]:
                for sem, val in bar:
                    wait(sem, val)
            for op in per[ename]:
                for d in op["deps"]:
                    sem, val = ops[d]["done"]
                    wait(sem, val)
                if op["pre"] is not None:
                    wait(*op["pre"])
                if op["fn"] is None:
                    continue
                ins = op["fn"](e)
                if op["done"] is not None:
                    sem, val = op["done"]
                    if op["cc"]:
                        ins.then_inc(sem, 1)
                    elif op["dma"]:
                        ins.then_inc(sem, 16)
                    else:
                        ins.then_inc(sem, 1)

        with self.nc.Block() as block:
            @block.sync
            def _(e):
                emit("sp", e)

            @block.scalar
            def _(e):
                emit("act", e)

            @block.vector
            def _(e):
                emit("dve", e)

            @block.gpsimd
            def _(e):
                emit("pool", e)

            @block.tensor
            def _(e):
                emit("pe", e)
        nb = []
        for e in self.ENGS:
            if self.ecnt[e] > 0:
                nb.append((self.esem[e], self.ecnt[e]))
        for q in ("sp", "pool"):
            for k in range(self.NDSEM):
                if self.dval[q][k] > 0:
                    nb.append((self.dsem[q][k], self.dval[q][k]))
        if self.ccval > 0:
            nb.append((self.ccsem, self.ccval))
        self.bar = nb

    def final_wait(self):
        bar = self.bar
        with self.nc.Block() as block:
            @block.sync
            def _(e):
                for sem, val in bar:
                    e.wait_ge(sem, val)


def bcast(ap, pos, n):
    l = [list(x) for x in ap.ap]
    l.insert(pos, [0, n])
    return bass.AP(ap.tensor, ap.offset, l)


def dview(t, off, dims):
    return bass.AP(t, off, [list(d) for d in dims])


class K:
    def __init__(self, n_layers=2, taps=(), stop=None):
        self.nl = n_layers
        self.taps = set(taps)
        self.stop = stop
        self.nc = bass.Bass("TRN2", target_bir_lowering=False)
        self.uid = 0

    def din(self, name, shape, dt=F32):
        return self.nc.dram_tensor(name, list(shape), dt, kind="ExternalInput")

    def dscr(self, name, shape, dt=BF16, tap=False):
        if tap and name in self.taps:
            return self.nc.dram_tensor(name, list(shape), dt, kind="ExternalOutput")
        return self.nc.dram_tensor(name, list(shape), dt)

    def sb(self, es, name, shape, dt):
        self.uid += 1
        return es.enter_context(self.nc.sbuf_tensor("%s_%d" % (name, self.uid), list(shape), dt))

    def mm(self, out, lhsT, rhs, start, stop, r, w):
        self.T.add("pe", lambda e: e.matmul(out, lhsT=lhsT, rhs=rhs, start=start, stop=stop), r, w)

    def act(self, out, in_, func, r, w, bias=None, scale=None):
        kw = {}
        if bias is not None:
            kw["bias"] = bias
        if scale is not None:
            kw["scale"] = scale
        self.T.add("act", lambda e: e.activation(out=out, in_=in_, func=func, **kw), r, w)

    def tt(self, eng, out, in0, in1, op, r, w):
        self.T.add(eng, lambda e: e.tensor_tensor(out=out, in0=in0, in1=in1, op=op), r, w)

    def ts(self, eng, out, in0, s1, s2, op0, op1, r, w):
        if op1 is None:
            self.T.add(eng, lambda e: e.tensor_scalar(out=out, in0=in0, scalar1=s1, scalar2=None, op0=op0), r, w)
        else:
            self.T.add(eng, lambda e: e.tensor_scalar(out=out, in0=in0, scalar1=s1, scalar2=s2, op0=op0, op1=op1), r, w)

    def stt(self, out, in0, scalar, in1, op0, op1, r, w):
        self.T.add("dve", lambda e: e.scalar_tensor_tensor(out=out, in0=in0, scalar=scalar, in1=in1, op0=op0, op1=op1), r, w)

    def cp(self, eng, out, in_, r, w):
        self.T.add(eng, lambda e: e.tensor_copy(out=out, in_=in_), r, w)

    def memset(self, eng, ap, val, w):
        self.T.add(eng, lambda e: e.memset(ap, val), (), w)

    def recip(self, out, in_, r, w):
        self.T.add("dve", lambda e: e.reciprocal(out=out, in_=in_), r, w)

    def dma(self, out, in_, r, w, q="sp"):
        self.T.add(q, lambda e: e.dma_start(out=out, in_=in_), r, w, dma=True)

    def newbank(self):
        if self.split:
            b = self.bankc % self.nsb
        else:
            b = self.bankc % 8
        self.bankc += 1
        return b

    def newacc(self):
        b = self.nsb + self.accc % (8 - self.nsb)
        self.accc += 1
        return b

    def alloc_norm(self, es, nmax):
        self.sq_buf = self.sb(es, "sqbuf", [128, 8, nmax], BF16)
        self.lnt = self.sb(es, "lnt", [128, nmax], F32)
        self.rstd = self.sb(es, "rstd", [128, nmax], F32)
        self.xn = self.sb(es, "xn", [128, 8, nmax], F32)

    def cast(self, dst, src, r, w):
        engs = self.cast_engs
        e = engs[self.cast_i % len(engs)]
        self.cast_i += 1
        if e == "act":
            self.T.add("act", lambda e_: e_.activation(out=dst, in_=src, func=AF.Copy), r, w, accw=True)
        else:
            self.T.add(e, lambda e_: e_.tensor_copy(out=dst, in_=src), r, w, accw=True)

    def load_w(self, dst, src, tok_w, nk, ncols):
        CH = 1024
        per = max(1, CH // ncols)
        k = 0
        while k < nk:
            kk = min(per, nk - k)
            if ncols > CH:
                assert per == 1
                c = 0
                while c < ncols:
                    cc = min(CH, ncols - c)
                    sap, stoks = self.stgs[self.stg_i % len(self.stgs)]
                    self.stg_i += 1
                    st = sap[:, 0:cc]
                    self.dma(st, src[:, k, c:c + cc], (), stoks)
                    self.cast(dst[:, k, c:c + cc], st, stoks, [tok_w])
                    c += cc
            else:
                sap, stoks = self.stgs[self.stg_i % len(self.stgs)]
                self.stg_i += 1
                st = sap[:, 0:kk * ncols].rearrange("p (k c) -> p k c", k=kk)
                self.dma(st, src[:, k:k + kk, :], (), stoks)
                self.cast(dst[:, k:k + kk, :], st, stoks, [tok_w])
            k += kk

    def rstd_of(self, es_tmp, src, nk, n, div, tokr, name):
        sq = self.sq_buf[:, 0:nk, 0:n]
        self.act(sq, src, AF.Square, [tokr], ["sqbuf"])
        b = self.newbank()
        for k in range(nk):
            self.mm(self.ps[b][:, 0:n], self.ones[:], sq[:, k, :], k == 0, k == nk - 1, ["sqbuf", "ones"], [("ps", b)])
        self.act(self.lnt[:, 0:n], self.ps[b][:, 0:n], AF.Ln, [("ps", b)], ["lnt"], bias=self.epsc[:, 0:1], scale=1.0 / div)
        self.act(self.rstd[:, 0:n], self.lnt[:, 0:n], AF.Exp, ["lnt"], ["rstd"], scale=-0.5)
        return self.rstd[:, 0:n]

    def norm_mod(self, xch, n, GG, SH, hT, tokx, tokh):
        rs = self.rstd_of(None, xch[:, :, 0:n], 8, n, 1024.0, tokx, "nm")
        self.tt("dve", self.xn[:, :, 0:n], xch[:, :, 0:n], bcast(rs, 1, 8), ALU.mult, [tokx, "rstd"], ["xn"])
        for k in range(8):
            self.ts("dve", hT[:, k, 0:n], self.xn[:, k, 0:n], GG[:, k:k + 1], SH[:, k:k + 1], ALU.mult, ALU.add,
                    ["xn", "mods"], [tokh])

    def build(self):
        nc = self.nc
        NL = self.nl
        xT = self.din("xT", [128, 8, NT])
        ctxT = self.din("ctxT", [128, 8, NCX])
        ccT = self.din("ccT", [128, 8, 2])
        ropeC = self.din("ropeC", [128, NT])
        ropeS = self.din("ropeS", [128, NT])
        maskA = self.din("maskA", [128, 10, 128])
        rm01 = self.din("rm01", [128, 44, 8])
        L = []
        for l in range(NL):
            d = dict(
                wada=self.din("wada%d" % l, [1024, 6144]),
                badaT=self.din("badaT%d" % l, [128, 48]),
                gains=self.din("gains%d" % l, [128, 4, 8]),
                win=self.din("win%d" % l, [1024, NWIN]),
                bgain=self.din("bgain%d" % l, [128, 4]),
                sinkT=self.din("sinkT%d" % l, [128, 6]),
                TB=self.din("TB%d" % l, [128, 2, 6 * 22 * 64]),
                wbr=self.din("wbr%d" % l, [3, 384, 1024]),
                wout=self.din("wout%d" % l, [1024, 1024]),
                w1=self.din("w1_%d" % l, [1024, 4096]),
                w2=self.din("w2_%d" % l, [4096, 1024]),
            )
            L.append(d)
        yT = nc.dram_tensor("yT", [128, 8, NT], F32, kind="ExternalOutput")
        xs = [self.dscr("xs%d" % i, [128, 8, NTOT], F32, tap=True) for i in range(2 * NL)]
        q_d = [self.dscr("q_d%d" % i, [128, 3, NTOT], BF16, tap=True) for i in range(3)]
        kaT_d = self.dscr("kaT_d", [128, NTOT], BF16, tap=True)
        va_d = self.dscr("va_d", [NTOT, 128], BF16, tap=True)
        kcT_d = self.dscr("kcT_d", [128, 3, NTOT], BF16, tap=True)
        vc_d = self.dscr("vc_d", [NTOT, 384], BF16, tap=True)
        kbcT_d = self.dscr("kbcT_d", [128, NCX], BF16, tap=True)
        vbc_d = self.dscr("vbc_d", [NCX, 128], BF16, tap=True)
        contrib = self.dscr("contrib", [960, 1024], BF16, tap=True)
        gathB = self.dscr("gathB", [4 * 512, 1024], BF16)
        gathH = self.dscr("gathH", [4 * 448, 1024], BF16)

        class G:
            @staticmethod
            def view(r, off, dims):
                if off < 524288:
                    return dview(gathB, r * 524288 + off, dims)
                return dview(gathH, r * 458752 + (off - 524288), dims)
        gath = G
        self.gathB, self.gathH = gathB, gathH
        o_d = [self.dscr("o_d%d" % i, [128, 3, NTOT], BF16, tap=True) for i in range(3)]
        mods_d = self.dscr("mods_d", [128, NL, 48, 2], F32, tap=True)

        with ExitStack() as es0:
            self.T = Trk(nc, es0)
            T = self.T
            self.bankc = 0
            self.accc = 0
            self.nsb = 4
            self.split = False
            self.stg_i = 0
            self.cast_i = 0
            self.cast_engs = ("act",)
            self.psall = es0.enter_context(nc.psum_tensor("psall", [128, 4096], F32))
            self.ps = [self.psall[:, i * 512:(i + 1) * 512] for i in range(8)]
            self.ones = self.sb(es0, "ones", [128, 128], BF16)
            self.bd = self.sb(es0, "bd", [128, 128], BF16)
            self.epsc = self.sb(es0, "epsc", [128, 1], F32)
            self.modsb = self.sb(es0, "modsb", [128, NL, 48, 2], F32)
            self.dvec = self.sb(es0, "dvec", [128, NL, 2, 6, 8], F32)
            self.gn = self.sb(es0, "gn", [128, NL, 4, 8], F32)
            self.stg = self.sb(es0, "stg", [128, 3, 1024], F32)
            self.nstg = 3
            self.stgs0 = [(self.stg[:, i, :], [("stg", i)]) for i in range(3)]
            self.stgs = self.stgs0

            with ExitStack() as es:
                self.memset("pool", self.ones[:], 1.0, ["ones"])
                self.memset("pool", self.bd[:], 0.0, ["bd"])
                self.memset("pool", self.bd[0:64, 0:64], 1.0, ["bd"])
                self.memset("pool", self.bd[64:128, 64:128], 1.0, ["bd"])
                self.memset("pool", self.epsc[:], EPS, ["epsc"])
                cc_s = self.sb(es, "cc_s", [128, 8, 2], F32)
                sil = self.sb(es, "sil", [128, 8, 2], F32)
                self.dma(cc_s[:], ccT.ap(), (), ["cc_s"])
                self.act(sil[:], cc_s[:], AF.Silu, ["cc_s"], ["sil"])
                wa = self.sb(es, "wa", [128, 2, 8, 768], F32)
                bad = self.sb(es, "bad", [128, NL, 48], F32)
                for l in range(NL):
                    self.dma(bad[:, l, :], L[l]["badaT"].ap(), (), ["bad"])
                    self.dma(self.gn[:, l], L[l]["gains"].ap(), (), ["gn"])
                    wsrc = L[l]["wada"].ap().rearrange("(k p) n -> p k n", p=128)
                    b = self.newbank()
                    for g in range(8):
                        s = g % 2
                        self.dma(wa[:, s], wsrc[:, :, g * 768:(g + 1) * 768], (), [("wa", s)])
                        for mm_ in range(6):
                            m = g * 6 + mm_
                            for k in range(8):
                                self.mm(self.ps[b][:, 2 * m:2 * m + 2], wa[:, s, k, mm_ * 128:(mm_ + 1) * 128],
                                        sil[:, k, :], k == 0, k == 7, [("wa", s), "sil"], [("ps", b)])
                    self.tt("dve", self.modsb[:, l], self.ps[b][:, 0:96].rearrange("p (m t) -> p m t", t=2),
                            bcast(bad[:, l, :], 2, 2), ALU.add, [("ps", b), "bad"], ["modsb"])
                    for t in range(2):
                        mv = self.modsb[:, l, :, t]
                        dv = self.dvec[:, l, t]
                        self.stt(dv[:, 0, :], mv[:, 8:16], 1.0, self.gn[:, l, 0, :], ALU.add, ALU.mult, ["modsb", "gn"], ["mods"])
                        self.cp("dve", dv[:, 1, :], mv[:, 0:8], ["modsb"], ["mods"])
                        self.tt("dve", dv[:, 2, :], mv[:, 16:24], self.gn[:, l, 1, :], ALU.mult, ["modsb", "gn"], ["mods"])
                        self.stt(dv[:, 3, :], mv[:, 32:40], 1.0, self.gn[:, l, 2, :], ALU.add, ALU.mult, ["modsb", "gn"], ["mods"])
                        self.cp("dve", dv[:, 4, :], mv[:, 24:32], ["modsb"], ["mods"])
                        self.tt("dve", dv[:, 5, :], mv[:, 40:48], self.gn[:, l, 3, :], ALU.mult, ["modsb", "gn"], ["mods"])
                if "mods_d" in self.taps:
                    self.dma(mods_d.ap(), self.modsb[:], ["modsb"], ["mods_d"])
                T.flush()

            for l in range(NL):
                last = (l == NL - 1)
                if l == 0:
                    xin_lat = lambda c0, n: xT.ap()[:, :, c0:c0 + n]
                    xin_ctx = ctxT.ap()
                else:
                    xin_lat = (lambda xs_: (lambda c0, n: xs_.ap()[:, :, c0:c0 + n]))(xs[2 * l - 1])
                    xin_ctx = xs[2 * l - 1].ap()[:, :, NT:NTOT]
                if self.stop == "mods":
                    break
                self.phase_proj(L[l], l, last, xin_lat, xin_ctx, ropeC, ropeS, q_d, kaT_d, va_d, kcT_d, vc_d, kbcT_d, vbc_d, contrib)
                if self.stop == "proj%d" % l or (self.stop or "").startswith("proj0"):
                    break
                self.phase_gather(contrib, gath)
                self.phase_attn_a(L[l], l, last, q_d[0], kaT_d, va_d, gath, maskA, o_d[0])
                if self.stop == "attn_a%d" % l:
                    break
                with ExitStack() as esc:
                    cb = self.c_alloc(esc)
                    Ll = L[l]
                    self.phase_attn_b(Ll, l, last, q_d[1], kbcT_d, vbc_d, gath, o_d[1],
                                      pre=lambda: self.c_loads(cb, last, q_d[2], kcT_d, vc_d, gath))
                    self.phase_attn_c(Ll, l, last, q_d[2], kcT_d, vc_d, gath, rm01, o_d[2], cb)
                self.phase_mix(L[l], l, last, xin_lat, xin_ctx, o_d, xs[2 * l])
                if self.stop == "mix%d" % l:
                    break
                out_lat = yT if last else xs[2 * l + 1]
                self.phase_mlp(L[l], l, last, xs[2 * l], out_lat, xs[2 * l + 1])
                if self.stop == "mlp%d" % l:
                    break
            T.final_wait()
        return nc

    def phase_proj(self, Ld, l, last, xin_lat, xin_ctx, ropeC, ropeS, q_d, kaT_d, va_d, kcT_d, vc_d, kbcT_d, vbc_d, contrib):
        T = self.T
        with ExitStack() as es:
            self.split = False
            self.alloc_norm(es, 512)
            hT = self.sb(es, "hT", [128, 8, NTOT], BF16)
            xch = self.sb(es, "xch", [128, 2, 8, 512], F32)
            rC = self.sb(es, "rC", [128, NT], F32)
            rS = self.sb(es, "rS", [128, NT], F32)
            bg = self.sb(es, "bg", [128, 4], F32)
            wt = self.sb(es, "wt", [128, 2, 8, 640], BF16)
            ost = self.sb(es, "ost", [128, 4, 640], BF16)
            t1 = self.sb(es, "t1", [128, 2, 512], F32)
            t2 = self.sb(es, "t2", [128, 2, 512], F32)
            sqh = self.sb(es, "sqh", [128, 2, 512], BF16)
            lnh = self.sb(es, "lnh", [128, 2, 512], F32)
            rsh = self.sb(es, "rsh", [128, 2, 512], F32)
            self.dma(rC[:], ropeC.ap(), (), ["rC"])
            self.dma(rS[:], ropeS.ap(), (), ["rS"])
            self.dma(bg[:], Ld["bgain"].ap(), (), ["bg"])
            chunks = [(i * 512, 512, False) for i in range(4)] + [(NT, NCX, True)]
            for ci, (c0, n, isc) in enumerate(chunks):
                s = ci % 2
                src = xin_ctx if isc else xin_lat(c0, n)
                self.dma(xch[:, s, :, 0:n], src, (), [("xch", s)])
                dv = self.dvec[:, l, 1 if isc else 0]
                self.norm_mod(xch[:, s], n, dv[:, 0, :], dv[:, 1, :], hT[:, :, c0:c0 + n], ("xch", s), ("hT", ci))
            if self.stop == "proj0a":
                T.flush()
                return
            win = Ld["win"].ap().rearrange("(k p) n -> p k n", p=128)
            units = []
            units.append(("KB", 0, [C_KB, C_KBS]))
            units.append(("V", 0, None))
            units.append(("KA", 0, [C_KA, C_KAS]))
            for mi in range(3):
                units.append(("KC", mi, [C_KC + mi * 128]))
            n_kv_units = len(units)
            for mi in range(3):
                units.append(("QB", mi, [C_QB + mi * 128, C_QBS + mi * 128]))
            for mi in range(3):
                units.append(("QA", mi, [C_QA + mi * 128, C_QAS + mi * 128]))
            for mi in range(3):
                units.append(("QC", mi, [C_QC + mi * 128]))
            ctr = dict(ost=0, t=0)

            ctoks = []

            def store(srcs_dsts, tok):
                for dst, src in srcs_dsts:
                    ctr["st"] = ctr.get("st", 0) + 1
                    wtk = ("dout", ctr["st"])
                    if dst.tensor.name == contrib.name:
                        ctoks.append(wtk)
                    self.dma(dst, src, [tok], [wtk])

            if self.stop and self.stop.startswith("proj0u"):
                sel = [int(x) for x in self.stop[6:].split("_")]
                units = [units[i] for i in sel]
            def load_unit(ui):
                kind, mi, cols = units[ui]
                ws = ui % 2
                wtok = ("+wt", ws)
                if kind == "V":
                    self.load_w(wt[:, ws, :, 0:640], win[:, :, C_V:C_V + 640], wtok, 8, 640)
                else:
                    for j, c in enumerate(cols):
                        self.load_w(wt[:, ws, :, j * 128:(j + 1) * 128], win[:, :, c:c + 128], wtok, 8, 128)
            load_unit(0)
            for ui, (kind, mi, cols) in enumerate(units):
                ws = ui % 2
                wtok = ("+wt", ws)
                if ui == n_kv_units and not (self.stop or "").startswith("proj0u"):
                    self.emit_gather(contrib, list(ctoks))
                if ui + 1 < len(units):
                    load_unit(ui + 1)
                if kind == "V":
                    for tile in range(NTOT // 128):
                        t0 = tile * 128
                        b0, b1 = self.newbank(), self.newbank()
                        for k in range(8):
                            self.mm(self.ps[b0][:, 0:512], hT[:, k, t0:t0 + 128], wt[:, ws, k, 0:512], k == 0, k == 7, [("hT", min(tile // 4, 4)), wtok], [("ps", b0)])
                        for k in range(8):
                            self.mm(self.ps[b1][:, 0:128], hT[:, k, t0:t0 + 128], wt[:, ws, k, 512:640], k == 0, k == 7, [("hT", min(tile // 4, 4)), wtok], [("ps", b1)])
                        o = ctr["ost"] % 4
                        ctr["ost"] += 1
                        otok = ("ost", o)
                        self.act(ost[:, o, 0:512], self.ps[b0][:, 0:512], AF.Copy, [("ps", b0)], [otok])
                        self.cp("dve", ost[:, o, 512:640], self.ps[b1][:, 0:128], [("ps", b1)], [otok])
                        cb = contrib
                        dl = [(va_d.ap()[t0:t0 + 128, :], ost[:, o, 0:128]),
                              (vc_d.ap()[t0:t0 + 128, :], ost[:, o, 256:640])]
                        if tile < 16:
                            dl.append((dview(cb, O_VB + t0 * 128, [[128, 128], [1, 128]]), ost[:, o, 128:256]))
                            if tile == 0:
                                dl.append((dview(cb, O_VAH, [[128, 128], [1, 128]]), ost[:, o, 0:128]))
                            if tile == 15:
                                dl.append((dview(cb, O_VAT, [[128, 128], [1, 128]]), ost[:, o, 0:128]))
                            if tile < 2:
                                dl.append((dview(cb, O_VCH + tile * 128 * 384, [[384, 128], [1, 384]]), ost[:, o, 256:640]))
                            if tile >= 14:
                                dl.append((dview(cb, O_VCT + (tile - 14) * 128 * 384, [[384, 128], [1, 384]]), ost[:, o, 256:640]))
                        else:
                            dl.append((vbc_d.ap()[t0 - NT:t0 - NT + 128, :], ost[:, o, 128:256]))
                        store(dl, otok)
                    continue
                for ci, (c0, n, isc) in enumerate(chunks):
                    if isc and last and kind in ("QA", "QB", "QC"):
                        continue
                    hs = [hT[:, k, c0:c0 + n] for k in range(8)]
                    bq = self.newbank()
                    for k in range(8):
                        self.mm(self.ps[bq][:, 0:n], wt[:, ws, k, 0:128], hs[k], k == 0, k == 7, [("hT", ci), wtok], [("ps", bq)])
                    pq = self.ps[bq][:, 0:n]
                    o = ctr["ost"] % 4
                    ctr["ost"] += 1
                    otok = ("ost", o)
                    oo = ost[:, o, 0:n]
                    need_sw = (len(cols) == 2) and not isc
                    if need_sw:
                        bs = self.newbank()
                        for k in range(8):
                            self.mm(self.ps[bs][:, 0:n], wt[:, ws, k, 128:256], hs[k], k == 0, k == 7, [("hT", ci), wtok], [("ps", bs)])
                        psw = self.ps[bs][:, 0:n]
                    tt_ = ctr["t"] % 2
                    ctr["t"] += 1
                    a1, a2 = t1[:, tt_, 0:n], t2[:, tt_, 0:n]
                    k1, k2 = ("t1", tt_), ("t2", tt_)
                    if kind in ("QA", "KA"):
                        if isc:
                            self.act(oo, pq, AF.Copy, [("ps", bq)], [otok])
                        else:
                            self.tt("dve", a1, pq, rC[:, c0:c0 + n], ALU.mult, [("ps", bq), "rC"], [k1])
                            self.tt("dve", a2, psw, rS[:, c0:c0 + n], ALU.mult, [("ps", bs), "rS"], [k2])
                            self.tt("dve", oo, a1, a2, ALU.add, [k1, k2], [otok])
                    elif kind in ("QB", "KB"):
                        gi = 0 if kind == "QB" else 2
                        self.act(sqh[:, tt_, 0:n], pq, AF.Square, [("ps", bq)], [("sqh", tt_)])
                        bss = self.newbank()
                        self.mm(self.ps[bss][:, 0:n], self.bd[:], sqh[:, tt_, 0:n], True, True, [("sqh", tt_), "bd"], [("ps", bss)])
                        self.act(lnh[:, tt_, 0:n], self.ps[bss][:, 0:n], AF.Ln, [("ps", bss)], [("lnh", tt_)], bias=self.epsc[:, 0:1], scale=1.0 / 64)
                        self.act(rsh[:, tt_, 0:n], lnh[:, tt_, 0:n], AF.Exp, [("lnh", tt_)], [("rsh", tt_)], scale=-0.5)
                        if isc:
                            self.stt(oo, pq, bg[:, gi:gi + 1], rsh[:, tt_, 0:n], ALU.mult, ALU.mult, [("ps", bq), "bg", ("rsh", tt_)], [otok])
                        else:
                            self.stt(a1, pq, bg[:, gi:gi + 1], rC[:, c0:c0 + n], ALU.mult, ALU.mult, [("ps", bq), "bg", "rC", ("sqh", tt_)], [k1])
                            self.stt(a2, psw, bg[:, gi + 1:gi + 2], rS[:, c0:c0 + n], ALU.mult, ALU.mult, [("ps", bs), "bg", "rS"], [k2])
                            self.tt("dve", a1, a1, a2, ALU.add, [k1, k2], [k1])
                            self.tt("dve", oo, a1, rsh[:, tt_, 0:n], ALU.mult, [k1, ("rsh", tt_)], [otok])
                    else:
                        self.act(oo, pq, AF.Copy, [("ps", bq)], [otok])
                    dl = []
                    cb = contrib
                    if kind == "QA":
                        dl.append((q_d[0].ap()[:, mi, c0:c0 + n], oo))
                    elif kind == "QB":
                        dl.append((q_d[1].ap()[:, mi, c0:c0 + n], oo))
                    elif kind == "QC":
                        dl.append((q_d[2].ap()[:, mi, c0:c0 + n], oo))
                    elif kind == "KA":
                        dl.append((kaT_d.ap()[:, c0:c0 + n], oo))
                        if ci == 0:
                            dl.append((dview(cb, O_KAH, [[128, 128], [1, 128]]), ost[:, o, 0:128]))
                        if ci == 3:
                            dl.append((dview(cb, O_KAT, [[128, 128], [1, 128]]), ost[:, o, 384:512]))
                    elif kind == "KB":
                        if isc:
                            dl.append((kbcT_d.ap(), oo))
                        else:
                            dl.append((dview(cb, O_KB + c0, [[2048, 128], [1, n]]), oo))
                    elif kind == "KC":
                        dl.append((kcT_d.ap()[:, mi, c0:c0 + n], oo))
                        if ci == 0:
                            dl.append((dview(cb, O_KCH + mi * 256, [[768, 128], [1, 256]]), ost[:, o, 0:256]))
                        if ci == 3:
                            dl.append((dview(cb, O_KCT + mi * 256, [[768, 128], [1, 256]]), ost[:, o, 256:512]))
                    store(dl, otok)
            T.flush()

    def phase_gather(self, contrib, gath):
        return

    def emit_gather(self, contrib, rtoks):
        T = self.T
        gB, gH = self.gathB, self.gathH
        T.add("pool", lambda e: e.collective_compute("AllGather", ALU.bypass, replica_groups=[[0, 1, 2, 3], [4, 5, 6, 7]],
                                                     ins=[contrib.ap()[0:512, :]], outs=[gB.ap()]), rtoks, ["gathB"], cc=True)
        T.add("pool", lambda e: e.collective_compute("AllGather", ALU.bypass, replica_groups=[[0, 1, 2, 3], [4, 5, 6, 7]],
                                                     ins=[contrib.ap()[512:960, :]], outs=[gH.ap()]), rtoks, ["gathH"], cc=True)

    def attn_fin(self, bank, n, base, out_ap, rc, rctok, extra=None, extra_tok=None, shape3=None, act_recip=True):
        ob = 64 - base
        k = self.fc_i % 2
        self.fc_i += 1
        fct = ("fc", k)
        self.cp("dve", self.fc[:, k, 0:n], self.ps[bank][:, 0:n], [("ps", bank)], [fct])
        den = self.fc[ob:ob + 64, k, 0:n]
        num = self.fc[base:base + 64, k, 0:n]
        rcp = rc[base:base + 64, 0:n]
        tmp = rc[ob:ob + 64, 0:n]
        if shape3 is not None:
            a, b_ = shape3
            den = den.rearrange("p (a b) -> p a b", a=a)
            num = num.rearrange("p (a b) -> p a b", a=a)
            rcp = rcp.rearrange("p (a b) -> p a b", a=a)
            tmp = tmp.rearrange("p (a b) -> p a b", a=a)
        src, srct = den, fct
        if extra is not None:
            self.tt("dve", tmp, den, extra, ALU.add, [fct, extra_tok], [rctok])
            src, srct = tmp, rctok
        if act_recip:
            self.act(rcp, src, AF.Ln, [srct], [rctok])
            self.act(rcp, rcp, AF.Exp, [rctok], [rctok], scale=-1.0)
        else:
            self.recip(rcp, src, [srct], [rctok])
        self.tt("dve", out_ap, num, rcp, ALU.mult, [fct, rctok], ["oT"])

    def pipeline(self, items, look=3, group=1):
        groups = [items[i:i + group] for i in range(0, len(items), group)]
        banks = {}

        def qks(gi):
            for k, itm in enumerate(groups[gi]):
                banks[(gi, k)] = self.newbank()
                itm[0](banks[(gi, k)])
        for gi in range(min(look, len(groups))):
            qks(gi)
        for gi in range(len(groups)):
            if gi + look < len(groups):
                qks(gi + look)
            for k, itm in enumerate(groups[gi]):
                itm[1](banks[(gi, k)])
            for k, itm in enumerate(groups[gi]):
                itm[2]()
                if itm[3] is not None:
                    itm[3]()

    def load_vaug(self, dst, src, tokn):
        self.dma(dst, src, (), [tokn])

    def phase_attn_a(self, Ld, l, last, qa_d, kaT_d, va_d, gath, maskA, oa_d):
        T = self.T
        with ExitStack() as es:
            self.split = True
            self.nsb = 6
            self.bankc = 0
            QT = self.sb(es, "QT", [128, 3, NTOT], BF16)
            KT = self.sb(es, "KT", [128, NTOT], BF16)
            KH = self.sb(es, "KH", [128, 8, 128], BF16)
            VA = self.sb(es, "VA", [128, 18, 2, 128], BF16)
            VH = self.sb(es, "VH", [128, 8, 2, 128], BF16)
            MA = self.sb(es, "MA", [128, 10, 128], F32)
            ES = self.sb(es, "ES", [128, 6, 128], F32)
            sk = self.sb(es, "sk", [128, 6], F32)
            PT = self.sb(es, "PT", [128, 8, 512], BF16)
            tm = self.sb(es, "tm", [128, 4, 384], F32)
            rc = self.sb(es, "rc", [128, 2, 512], F32)
            self.fc = self.sb(es, "fc", [128, 2, 512], F32)
            self.fc_i = 0
            oT = self.sb(es, "oT", [128, 3, NTOT], BF16)
            nq = NT if last else NTOT
            self.dma(QT[:, :, 0:nq], qa_d.ap()[:, :, 0:nq], (), ["+QT"])
            self.dma(KT[:], kaT_d.ap(), (), ["+KT"])
            self.memset("pool", VA[:], 1.0, ["+VA"])
            self.memset("pool", VH[:], 1.0, ["+VH"])
            vsrc = va_d.ap().rearrange("(t p) c -> p t c", p=128)
            self.dma(VA[:, :, 0, 0:64], vsrc[:, :, 0:64], (), ["+VA"])
            self.dma(VA[:, :, 1, 64:128], vsrc[:, :, 64:128], (), ["+VA"])
            for r in range(4):
                pass
                self.dma(KH[:, r, :], gath.view(r, O_KAT, [[128, 128], [1, 128]]), ["gath"], ["+KH"])
                self.dma(KH[:, 4 + r, :], gath.view(r, O_KAH, [[128, 128], [1, 128]]), ["gath"], ["+KH"])
                self.dma(VH[:, r, 0, 0:64], gath.view(r, O_VAT, [[128, 128], [1, 64]]), ["gath"], ["+VH"])
                self.dma(VH[:, r, 1, 64:128], gath.view(r, O_VAT + 64, [[128, 128], [1, 64]]), ["gath"], ["+VH"])
                self.dma(VH[:, 4 + r, 0, 0:64], gath.view(r, O_VAH, [[128, 128], [1, 64]]), ["gath"], ["+VH"])
                self.dma(VH[:, 4 + r, 1, 64:128], gath.view(r, O_VAH + 64, [[128, 128], [1, 64]]), ["gath"], ["+VH"])
            self.dma(MA[:], maskA.ap(), (), ["MA"])
            self.dma(sk[:], Ld["sinkT"].ap(), (), ["sk"])
            self.act(sk[:], sk[:], AF.Exp, ["sk"], ["sk"])
            self.cp("dve", ES[:], bcast(sk[:], 2, 128), ["sk"], ["ES"])
            it = dict(p=0, t=0, r=0)
            items = []

            def run(qcols, nqc, base, tiles, shape3, out_ap, es_ap, items):
                n = 3 * nqc
                rhs = QT[base:base + 64, :, qcols:qcols + nqc]
                bo = self.nsb + base // 64
                nt = len(tiles)
                for ti, (kap, vap, mk) in enumerate(tiles):
                    p = it["p"] % 8
                    it["p"] += 1
                    t = it["t"] % 4
                    if mk is not None:
                        it["t"] += 1

                    def qk(b, kap=kap):
                        self.mm(self.ps[b][:, 0:n].rearrange("p (a b) -> p a b", a=3), kap, rhs, True, True, ["+QT", "+KT", "+KH"], [("ps", b)])

                    def sm(b, mk=mk, p=p, t=t):
                        pt = PT[:, p, 0:n]
                        if mk is not None:
                            self.stt(tm[:, t, 0:n].rearrange("p (a b) -> p a b", a=3), self.ps[b][:, 0:n].rearrange("p (a b) -> p a b", a=3),
                                     0.125, bcast(mk, 1, 3), ALU.mult, ALU.add, [("ps", b), "MA"], [("tm", t)])
                            self.act(pt, tm[:, t, 0:n], AF.Exp, [("tm", t)], [("PT", p)])
                        else:
                            self.act(pt, self.ps[b][:, 0:n], AF.Exp, [("ps", b)], [("PT", p)], scale=0.125)

                    def pv(vap=vap, p=p, ti=ti):
                        self.mm(self.ps[bo][:, 0:n], vap, PT[:, p, 0:n], ti == 0, ti == nt - 1, [("PT", p), "+VA", "+VH"], [("ps", bo)])
                    fin = None
                    if ti == nt - 1:
                        r_ = it["r"] % 2
                        it["r"] += 1

                        def fin(r_=r_):
                            self.attn_fin(bo, n, base, out_ap, rc[:, r_], ("rc", r_), extra=es_ap, extra_tok="ES", shape3=shape3)
                    items.append((qk, sm, pv, fin))

            all_items = items
            half_items = []
            for half in range(2):
                base = half * 64
                ob = 64 - base
                items = []
                half_items.append(items)
                esl = ES[ob:ob + 64, 3 * half:3 * half + 3, :]
                for j in range(16):
                    tiles = []
                    if j > 0:
                        tiles.append((KT[base:base + 64, (j - 1) * 128:j * 128], VA[:, j - 1, half, :], MA[:, 0, :]))
                    else:
                        for r in range(4):
                            tiles.append((KH[base:base + 64, r, :], VH[:, r, half, :], MA[:, 2 + r, :]))
                    tiles.append((KT[base:base + 64, j * 128:(j + 1) * 128], VA[:, j, half, :], None))
                    if j < 15:
                        tiles.append((KT[base:base + 64, (j + 1) * 128:(j + 2) * 128], VA[:, j + 1, half, :], MA[:, 1, :]))
                    else:
                        for r in range(4):
                            tiles.append((KH[base:base + 64, 4 + r, :], VH[:, 4 + r, half, :], MA[:, 6 + r, :]))
                    for c in range(2):
                        tiles.append((KT[base:base + 64, NT + c * 128:NT + (c + 1) * 128], VA[:, 16 + c, half, :], None))
                    run(j * 128, 128, base, tiles, (3, 128), oT[base:base + 64, :, j * 128:(j + 1) * 128], esl, items)
                if not last:
                    for cq in range(2):
                        tiles = [(KT[base:base + 64, NT + c * 128:NT + (c + 1) * 128], VA[:, 16 + c, half, :], None) for c in range(2)]
                        q0 = NT + cq * 128
                        run(q0, 128, base, tiles, (3, 128), oT[base:base + 64, :, q0:q0 + 128], esl, items)
            assert len(half_items[0]) == len(half_items[1])
            for a_, b_ in zip(half_items[0], half_items[1]):
                all_items.append(a_)
                all_items.append(b_)
            self.pipeline(all_items, 2, 2)
            self.nsb = 4
            self.dma(oa_d.ap()[:, :, 0:nq], oT[:, :, 0:nq], ["oT"], ["oa_d"])
            T.flush()

    def phase_attn_b(self, Ld, l, last, qb_d, kbcT_d, vbc_d, gath, ob_d, pre=None):
        T = self.T
        with ExitStack() as es:
            NK = NCX + 4 * NT
            self.split = True
            self.nsb = 6
            self.bankc = 0
            QT = self.sb(es, "QTb", [128, 3, NTOT], BF16)
            KT = self.sb(es, "KTb", [128, NK], BF16)
            VB = self.sb(es, "VBb", [128, 66, 2, 128], BF16)
            PT = self.sb(es, "PTb", [128, 6, 512], BF16)
            rc = self.sb(es, "rcb", [128, 2, 512], F32)
            self.fc = self.sb(es, "fcb", [128, 2, 512], F32)
            self.fc_i = 0
            oT = self.sb(es, "oTb", [128, 3, NTOT], BF16)
            nq = NT if last else NTOT
            self.dma(QT[:, :, 0:nq], qb_d.ap()[:, :, 0:nq], (), ["+QT"])
            self.dma(KT[:, 0:NCX], kbcT_d.ap(), (), ["+KT"])
            self.memset("pool", VB[:, 0:33], 1.0, ["+VB"])
            self.memset("pool", VB[:, 33:66], 1.0, ["+VB"])
            vcs = vbc_d.ap().rearrange("(t p) c -> p t c", p=128)
            self.dma(VB[:, 0:2, 0, 0:64], vcs[:, :, 0:64], (), ["+VB"])
            self.dma(VB[:, 0:2, 1, 64:128], vcs[:, :, 64:128], (), ["+VB"])
            for r in range(4):
                pass
                self.dma(KT[:, NCX + r * NT:NCX + (r + 1) * NT], gath.view(r, O_KB, [[2048, 128], [1, 2048]]), ["gath"], ["+KT"])
                self.dma(VB[:, 2 + r * 16:2 + (r + 1) * 16, 0, 0:64], gath.view(r, O_VB, [[128, 128], [128 * 128, 16], [1, 64]]), ["gath"], ["+VB"])
                self.dma(VB[:, 2 + r * 16:2 + (r + 1) * 16, 1, 64:128], gath.view(r, O_VB + 64, [[128, 128], [128 * 128, 16], [1, 64]]), ["gath"], ["+VB"])
            if pre is not None:
                pre()
            it = dict(p=0, r=0)
            items = []

            def run(mi, q0, n, ktiles):
                bo = [self.newacc(), self.newacc()]
                nt = len(ktiles)
                for ti, kt in enumerate(ktiles):
                    for half in range(2):
                        base = half * 64
                        p = it["p"] % 6
                        it["p"] += 1

                        def qk(b, base=base, kt=kt):
                            self.mm(self.ps[b][:, 0:n], KT[base:base + 64, kt * 128:(kt + 1) * 128], QT[base:base + 64, mi, q0:q0 + n],
                                    True, True, ["+QT", "+KT"], [("ps", b)])

                        def sm(b, p=p, half=half):
                            if half == 1:
                                return
                            assert b % 2 == 0 and p % 2 == 0
                            src = self.psall[:, b * 512:(b + 2) * 512].rearrange("p (a c) -> p a c", a=2)[:, :, 0:n]
                            self.act(PT[:, p:p + 2, 0:n], src, AF.Exp, [("ps", b), ("ps", b + 1)], [("PT", p), ("PT", p + 1)], scale=0.125)

                        def pv(p=p, half=half, kt=kt, ti=ti):
                            self.mm(self.ps[bo[half]][:, 0:n], VB[:, kt, half, :], PT[:, p, 0:n], ti == 0, ti == nt - 1,
                                    [("PT", p), "+VB"], [("ps", bo[half])])
                        fin = None
                        if ti == nt - 1:
                            r_ = it["r"] % 2
                            it["r"] += 1

                            def fin(r_=r_, half=half, base=base):
                                self.attn_fin(bo[half], n, base, oT[base:base + 64, mi, q0:q0 + n], rc[:, r_], ("rc", r_), act_recip=False)
                        items.append((qk, sm, pv, fin))

            for mi in range(3):
                for qc in range(4):
                    run(mi, qc * 512, 512, list(range(66)))
                if not last:
                    run(mi, NT, NCX, [0, 1])
            self.pipeline(items, 2, 2)
            self.nsb = 4
            self.dma(ob_d.ap()[:, :, 0:nq], oT[:, :, 0:nq], ["oT"], ["ob_d"])
            T.flush()

    def c_alloc(self, es):
        return dict(
            QT=self.sb(es, "QTc", [128, 3, NTOT], BF16),
            KT=self.sb(es, "KTc", [128, 3, NTOT], BF16),
            KH=self.sb(es, "KHc", [128, 3, 8, 256], BF16),
            VC=self.sb(es, "VCc", [128, 18, 6, 128], BF16),
            VH=self.sb(es, "VHc", [128, 16, 6, 128], BF16),
        )

    def c_loads(self, cb, last, qc_d, kcT_d, vc_d, gath):
        QT, KT, KH, VC, VH = cb["QT"], cb["KT"], cb["KH"], cb["VC"], cb["VH"]
        nq = NT if last else NTOT
        self.dma(QT[:, :, 0:nq], qc_d.ap()[:, :, 0:nq], (), ["+QTc"])
        self.dma(KT[:], kcT_d.ap(), (), ["+KTc"])
        self.memset("pool", VC[:], 1.0, ["+VCc"])
        self.memset("pool", VH[:], 1.0, ["+VHc"])
        vsrc = vc_d.ap().rearrange("(t p) (h d) -> p t h d", p=128, d=64)
        for h in range(6):
            o = (h % 2) * 64
            self.dma(VC[:, :, h, o:o + 64], vsrc[:, :, h, :], (), ["+VCc"])
        for r in range(4):
            self.dma(KH[:, :, r, :], gath.view(r, O_KCT, [[768, 128], [256, 3], [1, 256]]), ["gath"], ["+KHc"])
            self.dma(KH[:, :, 4 + r, :], gath.view(r, O_KCH, [[768, 128], [256, 3], [1, 256]]), ["gath"], ["+KHc"])
            for h in range(6):
                o = (h % 2) * 64
                self.dma(VH[:, 2 * r:2 * r + 2, h, o:o + 64], gath.view(r, O_VCT + h * 64, [[384, 128], [128 * 384, 2], [1, 64]]), ["gath"], ["+VHc"])
                self.dma(VH[:, 8 + 2 * r:8 + 2 * r + 2, h, o:o + 64], gath.view(r, O_VCH + h * 64, [[384, 128], [128 * 384, 2], [1, 64]]), ["gath"], ["+VHc"])

    def phase_attn_c(self, Ld, l, last, qc_d, kcT_d, vc_d, gath, rm01, oc_d, cb):
        T = self.T
        with ExitStack() as es:
            self.split = True
            self.nsb = 6
            self.bankc = 0
            QT, KT, KH, VC, VH = cb["QT"], cb["KT"], cb["KH"], cb["VC"], cb["VH"]
            EB = self.sb(es, "EB", [128, 2, 6, 22, 64], BF16)
            RM = self.sb(es, "RM", [128, 44, 8], F32)
            RMb = self.sb(es, "RMb", [128, 44, 8], BF16)
            PT = self.sb(es, "PTc", [128, 8, 512], BF16)
            rc = self.sb(es, "rcc", [128, 2, 512], F32)
            self.fc = self.sb(es, "fcc", [128, 2, 512], F32)
            self.fc_i = 0
            oT = self.sb(es, "oTc", [128, 3, NTOT], BF16)
            nq = NT if last else NTOT
            TBd = Ld["TB"].ap()
            for tbl in range(2):
                ebf = EB[:, tbl].rearrange("p h u q -> p (h u q)")
                c = 0
                while c < 6 * 22 * 64:
                    cc_ = min(1024, 6 * 22 * 64 - c)
                    sap, stoks = self.stgs[self.stg_i % len(self.stgs)]
                    self.stg_i += 1
                    st = sap[:, 0:cc_]
                    self.dma(st, TBd[:, tbl, c:c + cc_], (), stoks)
                    self.T.add("act", (lambda e_, o_=ebf[:, c:c + cc_], i_=st: e_.activation(out=o_, in_=i_, func=AF.Exp)),
                               stoks, ["+EB"], accw=True)
                    c += cc_
            self.dma(RM[:], rm01.ap(), (), ["RM"])
            self.cp("dve", RMb[:], RM[:], ["RM"], ["RMb"])
            it = dict(p=0, t=0, r=0)
            items = []

            def crun(q0, n, tiles, h, mi, base, items):
                rhs = QT[base:base + 64, mi, q0:q0 + n]
                bo = self.nsb + base // 64
                nt = len(tiles)
                for ti, (kap, vap, j, ri) in enumerate(tiles):
                    p = it["p"] % 8
                    it["p"] += 1
                    t = it["t"] % 3
                    if j is not None:
                        it["t"] += 1

                    def qk(b, kap=kap):
                        self.mm(self.ps[b][:, 0:n], kap, rhs, True, True, ["+QTc", "+KTc", "+KHc"], [("ps", b)])

                    def sm(b, j=j, ri=ri, p=p, t=t):
                        pt = PT[:, p, 0:n]
                        self.act(pt, self.ps[b][:, 0:n], AF.Exp, [("ps", b)], [("PT", p)], scale=0.125)
                        if j is not None:
                            u0 = 14 - 2 * j
                            g_ = q0 // 512
                            tbl = 1 if g_ in (1, 2) else 0
                            pt3 = pt.rearrange("p (a b) -> p a b", a=8)
                            self.tt("dve", pt3, pt3, EB[:, tbl, h, u0:u0 + 8, :], ALU.mult, [("PT", p), "+EB"], [("PT", p)])
                            if tbl == 0:
                                self.tt("dve", pt3, pt3, bcast(RMb[:, ri, :], 2, 64), ALU.mult, [("PT", p), "RMb"], [("PT", p)])

                    def pv(vap=vap, p=p, ti=ti):
                        self.mm(self.ps[bo][:, 0:n], vap, PT[:, p, 0:n], ti == 0, ti == nt - 1, [("PT", p), "+VCc", "+VHc"], [("ps", bo)])
                    fin = None
                    if ti == nt - 1:
                        r_ = it["r"] % 2
                        it["r"] += 1

                        def fin(r_=r_):
                            self.attn_fin(bo, n, base, oT[base:base + 64, mi, q0:q0 + n], rc[:, r_], ("rc", r_))
                    items.append((qk, sm, pv, fin))

            all_items = items
            for h in range(6):
                mi, half = h // 2, h % 2
                base = half * 64
                rmi = 0
                items = []
                if half == 0:
                    ev_items = items
                else:
                    od_items = items
                for g in range(4):
                    tiles = []
                    for j in range(8):
                        lt = g * 512 - 256 + 128 * j
                        if g == 0 and j < 2:
                            for r in range(4):
                                tiles.append((KH[base:base + 64, mi, r, j * 128:(j + 1) * 128], VH[:, 2 * r + j, h, :], j, rmi))
                                rmi += 1
                        elif g == 3 and j >= 6:
                            for r in range(4):
                                tiles.append((KH[base:base + 64, mi, 4 + r, (j - 6) * 128:(j - 5) * 128], VH[:, 8 + 2 * r + (j - 6), h, :], j, rmi))
                                rmi += 1
                        else:
                            tiles.append((KT[base:base + 64, mi, lt:lt + 128], VC[:, lt // 128, h, :], j, rmi))
                            rmi += 1
                    for c in range(2):
                        tiles.append((KT[base:base + 64, mi, NT + c * 128:NT + (c + 1) * 128], VC[:, 16 + c, h, :], None, None))
                    crun(g * 512, 512, tiles, h, mi, base, items)
                if not last:
                    tiles = [(KT[base:base + 64, mi, NT + c * 128:NT + (c + 1) * 128], VC[:, 16 + c, h, :], None, None) for c in range(2)]
                    crun(NT, NCX, tiles, h, mi, base, items)
                if half == 1:
                    assert len(ev_items) == len(od_items)
                    for a_, b_ in zip(ev_items, od_items):
                        all_items.append(a_)
                        all_items.append(b_)
            self.pipeline(all_items, 2, 2)
            self.nsb = 4
            self.dma(oc_d.ap()[:, :, 0:nq], oT[:, :, 0:nq], ["oT"], ["oc_d"])
            T.flush()

    def post_norm_res(self, yTb, n, PG, xsrc, xdst, tok_y, tok_x, tok_o, tmpb):
        rs = self.rstd_of(None, yTb[:, :, 0:n], 8, n, 1024.0, tok_y, "pn")
        for m in range(8):
            self.stt(tmpb[:, m, 0:n], yTb[:, m, 0:n], PG[:, m:m + 1], rs, ALU.mult, ALU.mult, [tok_y, "rstd", "mods"], ["xn"])
            self.tt("dve", xdst[:, m, 0:n], xsrc[:, m, 0:n], tmpb[:, m, 0:n], ALU.add, [tok_x, "xn"], [tok_o])

    def phase_mix(self, Ld, l, last, xin_lat, xin_ctx, o_d, xs1):
        T = self.T
        with ExitStack() as es:
            self.split = False
            self.alloc_norm(es, 256)
            N = 256
            wg = self.sb(es, "wg", [128, 8, 3072], BF16)
            wbr = self.sb(es, "wbr", [128, 3, 3, 1024], BF16)
            wo = self.sb(es, "wo", [128, 8, 1024], BF16)
            xch = self.sb(es, "xchm", [128, 2, 8, N], F32)
            hT = self.sb(es, "hTm", [128, 2, 8, N], BF16)
            oc = self.sb(es, "ocm", [128, 2, 3, 3, N], BF16)
            sg = self.sb(es, "sg", [128, 2, 3, N], F32)
            ta = self.sb(es, "ta", [128, 2, 3, N], F32)
            mg = self.sb(es, "mg", [128, 8, N], BF16)
            yTb = self.sb(es, "yTb", [128, 8, N], F32)
            stg2 = self.sb(es, "stg2", [128, 6, 1024], F32)
            chunks = [(i * N, N, False) for i in range(NT // N)]
            if not last:
                chunks.append((NT, NCX, True))

            def prep(ci):
                c0, n, isc = chunks[ci]
                s_ = ci % 2
                src = xin_ctx if isc else xin_lat(c0, n)
                self.dma(xch[:, s_, :, 0:n], src, (), [("xch", s_)])
                for i in range(3):
                    self.dma(oc[:, s_, i, :, 0:n], o_d[i].ap()[:, :, c0:c0 + n], (), [("+oc", s_)])
                dv_ = self.dvec[:, l, 1 if isc else 0]
                self.norm_mod(xch[:, s_], n, dv_[:, 0, :], dv_[:, 1, :], hT[:, s_], ("xch", s_), ("hTm", s_))
            prep(0)
            win = Ld["win"].ap().rearrange("(k p) n -> p k n", p=128)
            self.cast_engs = ("act", "dve")
            self.stgs = [(stg2[:, i, :], [("stg2", i)]) for i in range(6)]
            for i in range(3):
                self.load_w(wg[:, :, i * 1024:(i + 1) * 1024], win[:, :, C_G + i * 1024:C_G + (i + 1) * 1024], ("+wg", i), 8, 1024)
                self.load_w(wbr[:, i], Ld["wbr"].ap()[i].rearrange("(k p) n -> p k n", p=128), ("+wbr", i), 3, 1024)
            self.load_w(wo[:], Ld["wout"].ap().rearrange("(k p) n -> p k n", p=128), "+wo", 8, 1024)
            self.stgs = self.stgs0
            self.cast_engs = ("act",)
            for ci, (c0, n, isc) in enumerate(chunks):
                s = ci % 2
                dv = self.dvec[:, l, 1 if isc else 0]
                for m in range(8):
                    q = m % 2
                    gb_, yb_ = [], []
                    for i in range(3):
                        b = self.newbank()
                        gb_.append(b)
                        for k in range(8):
                            self.mm(self.ps[b][:, 0:n], wg[:, k, i * 1024 + m * 128:i * 1024 + (m + 1) * 128], hT[:, s, k, 0:n], k == 0, k == 7,
                                    [("+wg", i), ("hTm", s)], [("ps", b)])
                        b = self.newbank()
                        yb_.append(b)
                        for k in range(3):
                            self.mm(self.ps[b][:, 0:n], wbr[:, i, k, m * 128:(m + 1) * 128], oc[:, s, i, k, 0:n], k == 0, k == 2,
                                    [("+wbr", i), ("+oc", s)], [("ps", b)])
                    for i in range(3):
                        self.act(sg[:, q, i, 0:n], self.ps[gb_[i]][:, 0:n], AF.Sigmoid, [("ps", gb_[i])], [("sg", q, i)])
                        self.tt("dve", ta[:, q, i, 0:n], sg[:, q, i, 0:n], self.ps[yb_[i]][:, 0:n], ALU.mult, [("sg", q, i), ("ps", yb_[i])], [("ta", q, i)])
                    self.tt("dve", ta[:, q, 0, 0:n], ta[:, q, 0, 0:n], ta[:, q, 1, 0:n], ALU.add, [("ta", q, 0), ("ta", q, 1)], [("ta", q, 0)])
                    self.tt("dve", mg[:, m, 0:n], ta[:, q, 0, 0:n], ta[:, q, 2, 0:n], ALU.add, [("ta", q, 0), ("ta", q, 2)], [("mg", m)])
                if ci + 1 < len(chunks):
                    prep(ci + 1)
                for m in range(8):
                    b = self.newbank()
                    for k in range(8):
                        self.mm(self.ps[b][:, 0:n], wo[:, k, m * 128:(m + 1) * 128], mg[:, k, 0:n], k == 0, k == 7, ["+wo", ("mg", k)], [("ps", b)])
                    self.act(yTb[:, m, 0:n], self.ps[b][:, 0:n], AF.Copy, [("ps", b)], ["yTb"])
                self.post_norm_res(yTb, n, dv[:, 2, :], xch[:, s], xch[:, s], "yTb", ("xch", s), ("xch", s), self.xn)
                self.dma(xs1.ap()[:, :, c0:c0 + n], xch[:, s, :, 0:n], [("xch", s)], [("xs1", ci)])
            T.flush()

    def phase_mlp(self, Ld, l, last, xs1, out_lat, xs2):
        T = self.T
        with ExitStack() as es:
            self.split = False
            self.alloc_norm(es, 256)
            w1 = self.sb(es, "w1", [128, 8, 4096], BF16)
            w2 = self.sb(es, "w2", [128, 32, 1024], BF16)
            xch = self.sb(es, "xchp", [128, 2, 8, 256], F32)
            h2 = self.sb(es, "h2", [128, 2, 8, 256], BF16)
            rl = self.sb(es, "rl", [128, 2, 256], F32)
            aT = self.sb(es, "aT", [128, 32, 256], BF16)
            y2 = self.sb(es, "y2", [128, 8, 256], F32)
            n = 256
            chunks = [(i * 256, False) for i in range(8)]
            if not last:
                chunks.append((NT, True))

            def prep(ci):
                c0, isc = chunks[ci]
                s_ = ci % 2
                self.dma(xch[:, s_], xs1.ap()[:, :, c0:c0 + n], ["xs1"], [("xch", s_)])
                dv_ = self.dvec[:, l, 1 if isc else 0]
                self.norm_mod(xch[:, s_], n, dv_[:, 3, :], dv_[:, 4, :], h2[:, s_], ("xch", s_), ("h2", s_))
            prep(0)
            self.cast_engs = ("act", "dve")
            aTf = aT[:].rearrange("p a b -> p (a b)").bitcast(F32).rearrange("p (s c) -> p s c", s=4)
            self.stgs = self.stgs0 + [(aTf[:, i, :], [("stgA", i)] + [("aT", j) for j in range(8 * i, 8 * i + 8)]) for i in range(4)]
            w1src = Ld["w1"].ap().rearrange("(k p) n -> p k n", p=128)
            for cc_ in range(4):
                self.load_w(w1[:, :, cc_ * 1024:(cc_ + 1) * 1024], w1src[:, :, cc_ * 1024:(cc_ + 1) * 1024], ("+w1", cc_), 8, 1024)
            self.load_w(w2[:], Ld["w2"].ap().rearrange("(k p) n -> p k n", p=128), "+w2", 32, 1024)
            self.stgs = self.stgs0
            self.cast_engs = ("act",)
            for ci, (c0, isc) in enumerate(chunks):
                s = ci % 2
                dv = self.dvec[:, l, 1 if isc else 0]
                for j in range(32):
                    b = self.newbank()
                    for k in range(8):
                        self.mm(self.ps[b][:, 0:n], w1[:, k, j * 128:(j + 1) * 128], h2[:, s, k, :], k == 0, k == 7, [("+w1", j // 8), ("h2", s)], [("ps", b)])
                    r_ = j % 2
                    self.act(rl[:, r_, :], self.ps[b][:, 0:n], AF.Relu, [("ps", b)], [("rl", r_)])
                    self.tt("dve", aT[:, j, :], rl[:, r_, :], rl[:, r_, :], ALU.mult, [("rl", r_)], [("aT", j)])
                if ci + 1 < len(chunks):
                    prep(ci + 1)
                for m in range(8):
                    b = self.newbank()
                    for j in range(32):
                        self.mm(self.ps[b][:, 0:n], w2[:, j, m * 128:(m + 1) * 128], aT[:, j, :], j == 0, j == 31, ["+w2", ("aT", j)], [("ps", b)])
                    self.act(y2[:, m, :], self.ps[b][:, 0:n], AF.Copy, [("ps", b)], ["y2"])
                self.post_norm_res(y2, n, dv[:, 5, :], xch[:, s], xch[:, s], "y2", ("xch", s), ("xch", s), self.xn)
                if isc:
                    dst = xs2.ap()[:, :, c0:c0 + n]
                else:
                    dst = out_lat.ap()[:, :, c0:c0 + n]
                self.dma(dst, xch[:, s], [("xch", s)], [("out", ci)])
            T.flush()


def _fm(v):
    v = np.asarray(v, np.float32)
    return np.ascontiguousarray(v.reshape(-1, 128).T)


def _perm_cols():
    def hc(base, heads, swap):
        out = []
        for h in heads:
            d = np.arange(64)
            if swap:
                d = d ^ 1
            out += list(base + h * 64 + d)
        return out
    p = []
    p += hc(0, HP, False) + hc(0, HP, True)
    p += hc(384, [0, 1], False) + hc(384, [0, 1], True)
    p += hc(640, HP, False) + hc(640, HP, True)
    p += hc(1024, [0, 1], False) + hc(1024, [0, 1], True)
    p += list(range(1280, 1664)) + list(range(1664, 2048))
    p += list(range(512, 640)) + list(range(1152, 1280)) + list(range(2048, 2432))
    p += list(range(2432, 5504))
    assert len(p) == NWIN
    return np.array(p)


def _rope_tables(tok0):
    pos = np.arange(tok0, tok0 + NT)
    row = (pos // 64).astype(np.float32)
    col = (pos % 64).astype(np.float32)
    freqs = (np.float32(10000.0) ** (-np.arange(16, dtype=np.float32) / np.float32(16))).astype(np.float32)
    ang = np.concatenate([row[:, None] * freqs, col[:, None] * freqs], axis=-1).astype(np.float32)
    cos, sin = np.cos(ang).astype(np.float32), np.sin(ang).astype(np.float32)
    d = np.arange(128) % 64
    C = cos[:, d // 2].T
    S = sin[:, d // 2].T * np.where(d % 2 == 0, -1.0, 1.0)[:, None]
    return np.ascontiguousarray(C, np.float32), np.ascontiguousarray(S, np.float32)


def _mask_a(rank):
    ki = np.arange(128)[:, None]
    qi = np.arange(128)[None, :]
    prev = np.where(qi <= ki, 0.0, NEG).astype(np.float32)
    nxt = np.where(ki <= qi, 0.0, NEG).astype(np.float32)
    allneg = np.full((128, 128), NEG, np.float32)
    m = np.zeros((128, 10, 128), np.float32)
    m[:, 0], m[:, 1] = prev, nxt
    for r in range(4):
        m[:, 2 + r] = prev if r == rank - 1 else allneg
        m[:, 6 + r] = nxt if r == rank + 1 else allneg
    return m


def _rm01(rank):
    out = np.zeros((128, 44, 8), np.float32)
    idx = 0
    kl = (np.arange(128) // 64)[:, None]
    ql = np.arange(8)[None, :]
    for g in range(4):
        R0 = rank * 32 + g * 8
        for j in range(8):
            kr = R0 - 4 + 2 * j + kl
            qr = R0 + ql
            rs = np.clip(qr - 4, 0, 120)
            valid = (kr >= rs) & (kr < rs + 8) & (kr >= 0) & (kr < 128)
            if g == 0 and j < 2:
                for r in range(4):
                    out[:, idx] = valid if r == rank - 1 else 0.0
                    idx += 1
            elif g == 3 and j >= 6:
                for r in range(4):
                    out[:, idx] = valid if r == rank + 1 else 0.0
                    idx += 1
            else:
                out[:, idx] = valid
                idx += 1
    assert idx == 44
    return out


def _tb_table(rpb, interior=False):
    rpb = np.asarray(rpb, np.float32)
    kc = np.arange(64)[:, None]
    qc = np.arange(64)[None, :]
    ws = np.clip(qc - 8, 0, 48)
    colv = (kc >= ws) & (kc < ws + 16)
    dc = np.clip(kc - qc + 15, 0, 30)
    tb = np.zeros((2, 64, 6, 22, 64), np.float32)
    for kl in range(2):
        for u in range(22):
            dr = 17 + kl - u
            for h in range(6):
                if 0 <= dr <= 14:
                    v = rpb[h, dr][dc]
                else:
                    v = np.zeros((64, 64), np.float32)
                if interior and not (3 <= dr <= 10):
                    tb[kl, :, h, u, :] = NEG
                else:
                    tb[kl, :, h, u, :] = np.where(colv, v, NEG)
    return np.ascontiguousarray(tb.reshape(128, 6, 22, 64))


_CACHE = {}


def _get_nc(n_layers=2, taps=()):
    key = (n_layers, tuple(taps))
    if key not in _CACHE:
        _CACHE[key] = K(n_layers, taps).build()
    return _CACHE[key]


def make_in_maps(inputs, n_layers=2):
    f = lambda a: np.asarray(a, np.float32)
    x, c, ctx, c_ctx = f(inputs["x"]), f(inputs["c"]), f(inputs["ctx"]), f(inputs["c_ctx"])
    perm = _perm_cols()
    rows_ab = np.concatenate([np.arange(h * 64, (h + 1) * 64) for h in HP])
    shared = {}
    for l in range(n_layers):
        shared["wada%d" % l] = np.ascontiguousarray(f(inputs["w_ada"])[l])
        shared["badaT%d" % l] = _fm(f(inputs["b_ada"])[l])
        shared["gains%d" % l] = np.ascontiguousarray(np.stack(
            [_fm(f(inputs[k])[l]) for k in ("norm_mix_pre", "norm_mix_post", "norm_mlp_pre", "norm_mlp_post")], axis=1))
        shared["win%d" % l] = np.ascontiguousarray(f(inputs["w_in"])[l][:, perm])
        gq, gk = f(inputs["qnorm_b"])[l], f(inputs["knorm_b"])[l]
        d = np.arange(128) % 64
        shared["bgain%d" % l] = np.ascontiguousarray(np.stack([gq[d], gq[d ^ 1], gk[d], gk[d ^ 1]], axis=1))
        shared["sinkT%d" % l] = np.ascontiguousarray(np.broadcast_to(f(inputs["sink_a"])[l][None, :], (128, 6)))
        shared["TB%d" % l] = np.ascontiguousarray(np.stack(
            [_tb_table(f(inputs["rpb_c"])[l]), _tb_table(f(inputs["rpb_c"])[l], True)], axis=1))
        shared["wbr%d" % l] = np.ascontiguousarray(np.stack(
            [f(inputs["w_br_a"])[l][rows_ab], f(inputs["w_br_b"])[l][rows_ab], f(inputs["w_br_c"])[l]], axis=0))
        shared["wout%d" % l] = np.ascontiguousarray(f(inputs["w_out"])[l])
        shared["w1_%d" % l] = np.ascontiguousarray(f(inputs["w_mlp_in"])[l])
        shared["w2_%d" % l] = np.ascontiguousarray(f(inputs["w_mlp_out"])[l])
    in_maps = []
    for core in range(8):
        b, rank = core // 4, core % 4
        tok0 = rank * NT
        m = dict(shared)
        xs = x[b, tok0:tok0 + NT, :]
        m["xT"] = np.ascontiguousarray(xs.T.reshape(8, 128, NT).transpose(1, 0, 2))
        m["ctxT"] = np.ascontiguousarray(ctx[b].T.reshape(8, 128, NCX).transpose(1, 0, 2))
        m["ccT"] = np.ascontiguousarray(np.stack([_fm(c[b]), _fm(c_ctx)], axis=2))
        C, S = _rope_tables(tok0)
        m["ropeC"], m["ropeS"] = C, S
        m["maskA"] = _mask_a(rank)
        m["rm01"] = _rm01(rank)
        in_maps.append(m)
    return in_maps


def kernel(**inputs):
    nc = _get_nc(2)
    in_maps = make_in_maps(inputs, 2)
    res = run_bass_kernel_spmd(nc, in_maps, core_ids=list(range(8)))
    out = np.zeros((2, 4 * NT, 1024), np.float32)
    for core in range(8):
        b, rank = core // 4, core % 4
        yT = np.asarray(res.results[core]["yT"])
        out[b, rank * NT:(rank + 1) * NT, :] = yT.transpose(1, 0, 2).reshape(1024, NT).T
    return out
```
